# Optimizing a Trainium2 kernel written in Bass

```python
import math
import jax, jax.numpy as jnp
from jax import lax
import numpy as np

D_MODEL = 1024
BATCH = 8
SEQ = 2048
DEPTH = 2

N_MIXERS = 2
N_A = (DEPTH + 1) // 2
N_B = DEPTH // 2

HY_ORDER = 2
HY_STREAMS = HY_ORDER + 1
HY_EMB = 33
HY_BANDS = (HY_EMB - 1) // 2
HY_FILT = 64
HY_DECAY_TARGET = 1e-2
HY_FAST_DECAY = 0.3
HY_SLOW_DECAY = 1.5
HY_FILTER_OUT_SCALE = 0.05

HEAD_DIM = 64
N_HEADS = D_MODEL // HEAD_DIM
N_KV = N_HEADS // 4
GROUP = N_HEADS // N_KV
WINDOW = 128
BLOCK = 128
ROT_DIM = HEAD_DIM // 4
ROPE_THETA = 500000.0

D_FF = 4 * D_MODEL
EPS = 1e-6
NEG = -1e30

kernel_name = "hybrid_hyena_swa_sandwich_encoder"


def rms_norm(x, g):
    xf = x.astype(jnp.float32)
    y = xf * lax.rsqrt(jnp.mean(xf * xf, axis=-1, keepdims=True) + EPS)
    return (y * g.astype(jnp.float32)).astype(x.dtype)


def short_conv(u, w, b):
    L = u.shape[1]
    up = jnp.pad(u, ((0, 0), (1, 1), (0, 0)))
    return up[:, :L] * w[0] + up[:, 1:L + 1] * w[1] + up[:, 2:] * w[2] + b


def hyena_filters(L, f_w1, f_b1, f_w2, f_b2, f_w3, f_b3, f_freq, f_wout):
    f32 = jnp.float32
    t = jnp.linspace(0.0, 1.0, L, dtype=f32)[:, None]
    w = (2.0 * math.pi / L) * jnp.arange(L, dtype=f32)[:, None]
    f = jnp.linspace(1e-4, HY_BANDS - 1, HY_BANDS, dtype=f32)[None, :]
    z = jnp.concatenate([t, jnp.cos(f * w), -jnp.sin(f * w)], axis=-1)
    freq = f_freq.astype(f32)
    h = jnp.sin(freq * (z @ f_w1.astype(f32) + f_b1.astype(f32)))
    h = jnp.sin(freq * (h @ f_w2.astype(f32) + f_b2.astype(f32)))
    h = jnp.sin(freq * (h @ f_w3.astype(f32) + f_b3.astype(f32)))
    h = (h @ f_wout.astype(f32)).reshape(L, 2, HY_ORDER, D_MODEL)
    max_decay = math.log(HY_DECAY_TARGET) / HY_FAST_DECAY
    min_decay = math.log(HY_DECAY_TARGET) / HY_SLOW_DECAY
    deltas = jnp.linspace(min_decay, max_decay, D_MODEL, dtype=f32)
    decay = jnp.exp(-t * jnp.abs(deltas)[None, :])
    return h * decay[:, None, None, :]


def long_conv(u, k_hat, bias):
    L = u.shape[1]
    u_hat = jnp.fft.rfft(u, n=2 * L, axis=1)
    y = jnp.fft.irfft(u_hat * k_hat[None], n=2 * L, axis=1)[:, :L]
    return y + u * bias


def hyena_mixer(x, w_in, conv_w, conv_b, f_w1, f_b1, f_w2, f_b2, f_w3, f_b3,
                f_freq, f_wout, bias, w_out):
    f32 = jnp.float32
    B, L, _ = x.shape
    u = (x @ w_in).astype(f32)
    u = short_conv(u, conv_w.astype(f32), conv_b.astype(f32))
    v, x1, x2 = jnp.split(u, HY_STREAMS, axis=-1)
    h = hyena_filters(L, f_w1, f_b1, f_w2, f_b2, f_w3, f_b3, f_freq, f_wout)
    k_full = jnp.concatenate(
        [h[:, 0], jnp.zeros((1, HY_ORDER, D_MODEL), f32), h[:0:-1, 1]], axis=0)
    k_hat = jnp.fft.rfft(k_full, axis=0)
    bias = bias.astype(f32)
    gates = (x1, x2)
    z = v
    for n in range(HY_ORDER):
        z = gates[n] * long_conv(z, k_hat[:, n], bias[n])
    return z.astype(x.dtype) @ w_out


def rope_partial(t, pos):
    f32 = jnp.float32
    half = ROT_DIM // 2
    inv = ROPE_THETA ** (-jnp.arange(0, ROT_DIM, 2, dtype=f32) / ROT_DIM)
    ang = pos.astype(f32)[:, None] * inv[None, :]
    cos = jnp.cos(ang)[None, :, None, :]
    sin = jnp.sin(ang)[None, :, None, :]
    tr = t[..., :ROT_DIM].astype(f32)
    a, b = tr[..., :half], tr[..., half:]
    rot = jnp.concatenate([a * cos - b * sin, b * cos + a * sin], axis=-1)
    return jnp.concatenate([rot.astype(t.dtype), t[..., ROT_DIM:]], axis=-1)


def window_attention(x, w_qkv, sink, w_o):
    f32 = jnp.float32
    B, S, _ = x.shape
    nb = S // BLOCK
    qkv = x @ w_qkv
    q, k, v = jnp.split(qkv, [N_HEADS * HEAD_DIM, (N_HEADS + N_KV) * HEAD_DIM], axis=-1)
    q = q.reshape(B, S, N_HEADS, HEAD_DIM)
    k = k.reshape(B, S, N_KV, HEAD_DIM)
    v = v.reshape(B, S, N_KV, HEAD_DIM)
    pos = jnp.arange(S)
    q = rope_partial(q, pos)
    k = rope_partial(k, pos)
    qb = q.reshape(B, nb, BLOCK, N_KV, GROUP, HEAD_DIM).astype(f32)

    def band(t):
        tp = jnp.pad(t, ((0, 0), (BLOCK, BLOCK), (0, 0), (0, 0)))
        tp = tp.reshape(B, nb + 2, BLOCK, N_KV, HEAD_DIM)
        return jnp.concatenate([tp[:, :-2], tp[:, 1:-1], tp[:, 2:]], axis=2).astype(f32)

    kb, vb = band(k), band(v)
    s = jnp.einsum('bnqhgd,bnkhd->bnhgqk', qb, kb) * (HEAD_DIM ** -0.5)
    qpos = jnp.arange(nb)[:, None] * BLOCK + jnp.arange(BLOCK)[None, :]
    kpos = jnp.arange(nb)[:, None] * BLOCK - BLOCK + jnp.arange(3 * BLOCK)[None, :]
    rel = kpos[:, None, :] - qpos[:, :, None]
    valid = (jnp.abs(rel) <= WINDOW) & (kpos[:, None, :] >= 0) & (kpos[:, None, :] < S)
    s = jnp.where(valid[None, :, None, None, :, :], s, NEG)
    sink_l = sink.astype(f32).reshape(1, 1, N_KV, GROUP, 1, 1)
    sink_b = jnp.broadcast_to(sink_l, s.shape[:-1] + (1,))
    p = jax.nn.softmax(jnp.concatenate([s, sink_b], axis=-1), axis=-1)[..., :-1]
    o = jnp.einsum('bnhgqk,bnkhd->bnqhgd', p, vb)
    o = o.reshape(B, S, N_HEADS * HEAD_DIM).astype(x.dtype)
    return o @ w_o


def squared_relu_mlp(x, w_up, w_down):
    return jnp.square(jax.nn.relu(x @ w_up)) @ w_down


def setup_inputs(seed: int = 0) -> dict:
    key = jax.random.key(seed)
    ks = jax.random.split(key, 32)
    f32 = jnp.float32
    D = D_MODEL

    def nrm(k, shape, scale):
        return jax.random.normal(k, shape, f32) * scale

    def gain(k, shape):
        return 1.0 + 0.05 * jax.random.normal(k, shape, f32)

    return {
        "x": jax.random.normal(ks[0], (BATCH, SEQ, D), f32),
        "norm_mix_pre": gain(ks[1], (DEPTH, D)),
        "norm_mix_post": gain(ks[2], (DEPTH, D)),
        "norm_mlp_pre": gain(ks[3], (DEPTH, D)),
        "norm_mlp_post": gain(ks[4], (DEPTH, D)),
        "w_up": nrm(ks[5], (DEPTH, D, D_FF), D ** -0.5),
        "w_down": nrm(ks[6], (DEPTH, D_FF, D), D_FF ** -0.5),
        "hy_w_in": nrm(ks[7], (N_A, D, HY_STREAMS * D), D ** -0.5),
        "hy_conv_w": nrm(ks[8], (N_A, 3, HY_STREAMS * D), 3 ** -0.5),
        "hy_conv_b": nrm(ks[9], (N_A, HY_STREAMS * D), 0.02),
        "hy_f_w1": nrm(ks[10], (N_A, HY_EMB, HY_FILT), HY_EMB ** -0.5),
        "hy_f_b1": nrm(ks[11], (N_A, HY_FILT), 0.1),
        "hy_f_w2": nrm(ks[12], (N_A, HY_FILT, HY_FILT), HY_FILT ** -0.5),
        "hy_f_b2": nrm(ks[13], (N_A, HY_FILT), 0.1),
        "hy_f_w3": nrm(ks[14], (N_A, HY_FILT, HY_FILT), HY_FILT ** -0.5),
        "hy_f_b3": nrm(ks[15], (N_A, HY_FILT), 0.1),
        "hy_f_freq": gain(ks[16], (N_A, HY_FILT)),
        "hy_f_wout": nrm(ks[17], (N_A, HY_FILT, 2 * HY_ORDER * D), HY_FILTER_OUT_SCALE * HY_FILT ** -0.5),
        "hy_bias": nrm(ks[18], (N_A, HY_ORDER, D), 0.1),
        "hy_w_out": nrm(ks[19], (N_A, D, D), D ** -0.5),
        "at_w_qkv": nrm(ks[20], (N_B, D, (N_HEADS + 2 * N_KV) * HEAD_DIM), D ** -0.5),
        "at_sink": nrm(ks[21], (N_B, N_HEADS), 0.5),
        "at_w_o": nrm(ks[22], (N_B, N_HEADS * HEAD_DIM, D), (N_HEADS * HEAD_DIM) ** -0.5),
    }


def reference(x, norm_mix_pre, norm_mix_post, norm_mlp_pre, norm_mlp_post, w_up, w_down,
              hy_w_in, hy_conv_w, hy_conv_b, hy_f_w1, hy_f_b1, hy_f_w2, hy_f_b2, hy_f_w3,
              hy_f_b3, hy_f_freq, hy_f_wout, hy_bias, hy_w_out, at_w_qkv, at_sink, at_w_o):
    h = x
    for i in range(DEPTH):
        hn = rms_norm(h, norm_mix_pre[i])
        j = i // N_MIXERS
        if i % N_MIXERS == 0:
            m = hyena_mixer(hn, hy_w_in[j], hy_conv_w[j], hy_conv_b[j], hy_f_w1[j], hy_f_b1[j],
                            hy_f_w2[j], hy_f_b2[j], hy_f_w3[j], hy_f_b3[j], hy_f_freq[j],
                            hy_f_wout[j], hy_bias[j], hy_w_out[j])
        else:
            m = window_attention(hn, at_w_qkv[j], at_sink[j], at_w_o[j])
        h = h + rms_norm(m, norm_mix_post[i])
        hn = rms_norm(h, norm_mlp_pre[i])
        h = h + rms_norm(squared_relu_mlp(hn, w_up[i], w_down[i]), norm_mlp_post[i])
    return h
```

```python
import math
import bisect
import numpy as np
import ml_dtypes
import concourse.bass as bass
import concourse.mybir as mybir
from concourse.bass_utils import run_bass_kernel_spmd

F32 = mybir.dt.float32
BF16 = mybir.dt.bfloat16
AF = mybir.ActivationFunctionType
ALU = mybir.AluOpType

S = 2048
D = 1024
NT = 16
DFF = 4096
EPS = 1e-6
NCORES = 8
PI = math.pi

SBUF_WORDS = 52800


class Sched:
    ENGS = ("pe", "act", "dve", "pool", "sp")

    def __init__(self):
        self.ops = []
        self.lastw = {}
        self.readers = {}

    def op(self, eng, fn, r=(), w=(), slot=None):
        idx = len(self.ops)
        deps = set()
        if getattr(self, "fence", None):
            for k in w:
                if k not in self.known:
                    deps.update(self.fence)
                    self.known.add(k)
        for k in r:
            if k in self.lastw:
                deps.add(self.lastw[k])
        for k in w:
            if k in self.lastw:
                deps.add(self.lastw[k])
            deps.update(self.readers.get(k, ()))
        for k in r:
            self.readers.setdefault(k, []).append(idx)
        for k in w:
            self.lastw[k] = idx
            self.readers[k] = []
        deps.discard(idx)
        self.ops.append(dict(eng=eng, fn=fn, deps=deps, slot=slot, sem=None, val=None, sig=False))
        return idx

    def fence_all(self):
        last = {}
        for i, o in enumerate(self.ops):
            last[(o["eng"], o["slot"])] = i
        self.fence = set(last.values())
        self.known = set(self.lastw.keys()) | set(self.readers.keys())

    def retire(self, old_keys, new_keys):
        acc = set()
        for k in old_keys:
            if k in self.lastw:
                acc.add(self.lastw[k])
            acc.update(self.readers.get(k, ()))
        for k in new_keys:
            self.readers.setdefault(k, []).extend(acc)

    def emit(self, nc, stack, same_engine_sync=("act", "dve", "pool")):
        ops = self.ops
        for i, o in enumerate(ops):
            for d in o["deps"]:
                y = ops[d]
                if y["slot"] is not None:
                    continue
                if y["eng"] == o["eng"] and y["eng"] not in same_engine_sync:
                    continue
                y["sig"] = True
        SEM_MAX = 30000
        eng_sems = {}
        counters = {}
        for e in ("pe", "act", "dve", "pool"):
            eng_sems[e] = []
            counters[e] = SEM_MAX
        slot_sem = {}
        slot_list = {}
        for i, o in enumerate(ops):
            if o["slot"] is not None:
                sl = o["slot"]
                if sl not in slot_sem:
                    slot_sem[sl] = stack.enter_context(nc.semaphore("d_" + str(len(slot_sem))))
                    slot_list[sl] = []
                slot_list[sl].append(i)
                o["sem"] = slot_sem[sl]
                o["val"] = 16 * len(slot_list[sl])
            elif o["sig"]:
                e = o["eng"]
                if counters[e] >= SEM_MAX:
                    eng_sems[e].append(stack.enter_context(nc.semaphore("e_%s_%d" % (e, len(eng_sems[e])))))
                    counters[e] = 0
                counters[e] += 1
                o["sem"] = eng_sems[e][-1]
                o["val"] = counters[e]
        self.nsem = len(slot_sem) + sum(len(v) for v in eng_sems.values())
        block = stack.enter_context(nc.Block())
        per_eng = {e: [] for e in self.ENGS}
        for i, o in enumerate(ops):
            per_eng[o["eng"]].append(i)

        def run(engname, eng):
            waited = {}
            for i in per_eng[engname]:
                o = ops[i]
                need = {}
                for d in o["deps"]:
                    y = ops[d]
                    if y["slot"] is not None:
                        lst = slot_list[y["slot"]]
                        pos = bisect.bisect_left(lst, i)
                        sem, val = y["sem"], 16 * pos
                    else:
                        if y["eng"] == engname and engname not in same_engine_sync:
                            continue
                        sem, val = y["sem"], y["val"]
                    key = id(sem)
                    if key not in need or need[key][1] < val:
                        need[key] = (sem, val)
                for key, (sem, val) in need.items():
                    if waited.get(key, 0) >= val:
                        continue
                    eng.wait_ge(sem, val)
                    waited[key] = val
                ins = o["fn"](eng)
                if o["slot"] is not None:
                    ins.then_inc(o["sem"], 16)
                elif o["sig"]:
                    ins.then_inc(o["sem"], 1)

        @block.tensor
        def _(e):
            run("pe", e)

        @block.scalar
        def _(e):
            run("act", e)

        @block.vector
        def _(e):
            run("dve", e)

        @block.gpsimd
        def _(e):
            run("pool", e)

        @block.sync
        def _(e):
            run("sp", e)


_CONST_CACHE = {}


def _dft_tables():
    if "dft" in _CONST_CACHE:
        return _CONST_CACHE["dft"]
    n = np.arange(S, dtype=np.int64)
    prod = (n[:, None] * n[None, :]) % 4096
    ang = prod.astype(np.float64) * (2.0 * np.pi / 4096.0)
    C = np.cos(ang)
    Sm = np.sin(ang)
    sgn = np.where(n % 2 == 0, 1.0, -1.0)
    Sp = Sm.copy()
    Sp[:, 0] = sgn
    SpT = Sp.T.copy()

    def panelize(M):
        return M.reshape(16, 128, 16, 128).transpose(2, 1, 0, 3)

    fwd = np.stack([panelize(C), panelize(Sp)], axis=2)
    inv = np.stack([panelize(C), panelize(SpT)], axis=2)
    fwd = np.ascontiguousarray(fwd).astype(ml_dtypes.bfloat16)
    inv = np.ascontiguousarray(inv).astype(ml_dtypes.bfloat16)
    kk = np.arange(16)[:, None]
    pp = np.arange(128)[None, :]
    sperm = (2 * (128 * (kk % 8) + pp) + (kk // 8)).reshape(-1)
    f1 = np.arange(1024, dtype=np.int64)
    angk = ((sperm[:, None].astype(np.int64) * f1[None, :]) % 4096).astype(np.float64) * (2.0 * np.pi / 4096.0)

    def panelize_k(M):
        return M.reshape(16, 128, 8, 128).transpose(2, 1, 0, 3)

    dk = np.stack([panelize_k(np.cos(angk)), panelize_k(np.sin(angk))], axis=2)
    dk = np.ascontiguousarray(dk).astype(ml_dtypes.bfloat16)
    _CONST_CACHE["dft"] = (fwd, inv, dk)
    return fwd, inv, dk


def _zfeat():
    L = S
    t = np.linspace(0.0, 1.0, L, dtype=np.float32)[:, None]
    w = (np.float32(2.0 * math.pi / L) * np.arange(L, dtype=np.float32))[:, None]
    f = np.linspace(1e-4, 15.0, 16, dtype=np.float32)[None, :]
    z = np.concatenate([t, np.cos(f * w), -np.sin(f * w)], axis=-1).astype(np.float32)
    return z


def _attn_consts():
    a = np.zeros((128, 4736), np.float32)
    kl = np.arange(128)[:, None]
    ql = np.arange(128)[None, :]
    a[:, 0:128] = np.where(kl <= ql, 0.0, -30000.0)
    a[:, 128:256] = 0.0
    a[:, 256:384] = np.where(ql <= kl, 0.0, -30000.0)
    pr = np.zeros((128, 128), np.float32)
    for hb in (0, 64):
        for i in range(8):
            pr[hb + 8 + i, hb + i] = 1.0
            pr[hb + i, hb + 8 + i] = 1.0
    a[:, 384:512] = pr
    ph = np.zeros((128, 128), np.float32)
    for m in range(128):
        ph[(m + 64) % 128, m] = 1.0
    a[:, 512:640] = ph
    inv = (500000.0 ** (-np.arange(0, 16, 2, dtype=np.float32) / 16.0)).astype(np.float32)
    ang = np.arange(S, dtype=np.float32)[None, :] * inv[:, None]
    cos = np.ones((128, S), np.float32)
    sin = np.zeros((128, S), np.float32)
    for hb in (0, 64):
        cos[hb:hb + 8] = np.cos(ang); cos[hb + 8:hb + 16] = np.cos(ang)
        sin[hb:hb + 8] = -np.sin(ang); sin[hb + 8:hb + 16] = np.sin(ang)
    a[:, 640:2688] = cos
    a[:, 2688:4736] = sin
    return a.astype(ml_dtypes.bfloat16)


def _absdelta():
    max_decay = math.log(1e-2) / 0.3
    min_decay = math.log(1e-2) / 1.5
    deltas = np.linspace(min_decay, max_decay, D, dtype=np.float32)
    return np.abs(deltas).astype(np.float32)


CF_G = 0
CF_CW = 64
CF_FB = 160
CF_TNEG = 164
CF_FW1 = 180
CF_FW2 = 244
CF_FW3 = 308
CF_FBF = 372
CF_N = 384

CB_ID = 0
CB_ONES = 128
CB_JREV = 256
CB_E00 = 384
CB_SGN = 512
CB_N = 640


def _host_consts(inp):
    cf = np.zeros((128, CF_N), np.float32)
    gl = ["norm_mix_pre", "norm_mix_post", "norm_mlp_pre", "norm_mlp_post"]
    for layer in range(2):
        for gi, gname in enumerate(gl):
            v = np.asarray(inp[gname])[layer]
            cf[:, CF_G + (layer * 4 + gi) * 8: CF_G + (layer * 4 + gi) * 8 + 8] = v.reshape(8, 128).T
    cw = np.asarray(inp["hy_conv_w"])[0]
    cbias = np.asarray(inp["hy_conv_b"])[0]
    for k in range(3):
        cf[:, CF_CW + 24 * k: CF_CW + 24 * k + 24] = cw[k].reshape(24, 128).T
    cf[:, CF_CW + 72: CF_CW + 96] = cbias.reshape(24, 128).T
    cf[0:64, CF_FB + 0] = np.asarray(inp["hy_f_b1"])[0]
    cf[0:64, CF_FB + 1] = np.asarray(inp["hy_f_b2"])[0]
    cf[0:64, CF_FB + 2] = np.asarray(inp["hy_f_b3"])[0]
    cf[0:64, CF_FB + 3] = np.asarray(inp["hy_f_freq"])[0]
    t = np.linspace(0.0, 1.0, S, dtype=np.float32)
    for k in range(16):
        sidx = 2 * (128 * (k % 8) + np.arange(128)) + (k // 8)
        cf[:, CF_TNEG + k] = -t[sidx]
    cf[0:33, CF_FW1: CF_FW1 + 64] = np.asarray(inp["hy_f_w1"])[0]
    cf[0:64, CF_FW2: CF_FW2 + 64] = np.asarray(inp["hy_f_w2"])[0]
    cf[0:64, CF_FW3: CF_FW3 + 64] = np.asarray(inp["hy_f_w3"])[0]
    cb = np.zeros((128, CB_N), np.float32)
    cb[:, CB_ID:CB_ID + 128] = np.eye(128)
    cb[:, CB_ONES:CB_ONES + 128] = 1.0
    for q in range(1, 128):
        cb[128 - q, CB_JREV + q] = 1.0
    cb[0, CB_E00] = 1.0
    cb[:, CB_SGN] = np.where(np.arange(128) % 2 == 0, 1.0, -1.0)
    cb = cb.astype(ml_dtypes.bfloat16)
    zf = np.zeros((33, S), np.float32)
    zf[:, :] = _zfeat().T
    return cf, cb, zf


def build_program(stage=4):
    from contextlib import ExitStack
    nc = bass.Bass("TRN2", target_bir_lowering=False)
    dt = nc.dram_tensor
    xT = dt("xT", [D, S], F32, kind="ExternalInput").ap()
    cf_d = dt("cf", [128, CF_N], F32, kind="ExternalInput").ap()
    cb_d = dt("cb", [128, CB_N], BF16, kind="ExternalInput").ap()
    zf_d = dt("zf", [33, S], F32, kind="ExternalInput").ap()
    adl_d = dt("adl", [1, D], F32, kind="ExternalInput").ap()
    hyb_d = dt("hyb", [1, 2 * D], F32, kind="ExternalInput").ap()
    dftF = dt("dftF", [16, 128, 2, 16, 128], BF16, kind="ExternalInput").ap()
    dftI = dt("dftI", [16, 128, 2, 16, 128], BF16, kind="ExternalInput").ap()
    dftK = dt("dftK", [8, 128, 2, 16, 128], BF16, kind="ExternalInput").ap()
    w_in_d = dt("hy_w_in", [D, 3 * D], F32, kind="ExternalInput").ap()
    wout_f_d = dt("hy_f_wout", [64, 4 * D], F32, kind="ExternalInput").ap()
    w_out_d = dt("hy_w_out", [D, D], F32, kind="ExternalInput").ap()
    w_up_d = dt("w_up", [2, D, DFF], F32, kind="ExternalInput").ap()
    w_down_d = dt("w_down", [2, DFF, D], F32, kind="ExternalInput").ap()
    wqkv_d = dt("at_w_qkv", [D, 1536], F32, kind="ExternalInput").ap()
    wo2_d = dt("at_w_o", [D, D], F32, kind="ExternalInput").ap()
    sink_d = dt("sink", [1, 16], F32, kind="ExternalInput").ap()
    atc_d = dt("atc", [128, 4736], BF16, kind="ExternalInput").ap()
    kco_d = dt("kco", [2, 16, 128, 2, D], BF16, kind="Internal").ap()
    outT = dt("outT", [D, S], F32, kind="ExternalOutput").ap()

    stack = ExitStack()
    big = stack.enter_context(nc.sbuf_tensor("big", [128, SBUF_WORDS], F32))
    PS = stack.enter_context(nc.psum_tensor("PS", [128, 8, 512], F32))
    sch = Sched()

    def V(off, words, dtype=F32, pat=None, **kw):
        ap = big[:, off:off + words]
        if dtype != F32:
            ap = ap.bitcast(dtype)
        if pat is not None:
            ap = ap.rearrange(pat, **kw)
        return ap

    def psb(b):
        return PS[:, b, :]

    def psb16(b):
        return PS[:, b, :].bitcast(BF16)

    o = 0
    O_CF = o; o += CF_N
    O_CB = o; o += CB_N // 2
    O_RSTD = o; o += 2 * 512
    O_LN = o; o += 512
    O_SQ = o; o += 2048
    O_KNY = o; o += 1024
    P_H = o; o += 16384
    P_Z2T = o; o += 8192
    P_T0 = o; o += 4096
    P_T1 = o; o += 4096
    P_RAW = o; o += 2052
    P_TT = o; o += 2048
    P_TMP = o; o += 2048
    P_KST = o; o += 1024
    P_WIN = o; o += 2048
    P_PAN = o; o += 4096
    P_X = o; o += 1024
    assert o <= SBUF_WORDS, o

    CF = V(O_CF, CF_N)
    CB = V(O_CB, CB_N // 2, BF16)
    IDENT = CB[:, CB_ID:CB_ID + 128]
    ONES = CB[:, CB_ONES:CB_ONES + 128]
    JREV = CB[:, CB_JREV:CB_JREV + 128]
    E00 = CB[:, CB_E00:CB_E00 + 128]
    SGNC = CB[:, CB_SGN:CB_SGN + 1]
    RSTD = [V(O_RSTD + 512 * i, 512) for i in range(2)]
    LNB = V(O_LN, 512)
    SQ = V(O_SQ, 2048, BF16, "p (j s) -> p j s", j=8)
    KNY = V(O_KNY, 1024, BF16)

    def gcol(layer, gi, j):
        c = CF_G + (layer * 4 + gi) * 8 + j
        return CF[:, c:c + 1]

    PAN = [V(P_PAN + 2048 * i, 2048, BF16, "p (a k j) -> p a k j", a=2, k=16) for i in range(2)]

    sch.op("sp", lambda e: e.dma_start(out=CF, in_=cf_d), w=["CF"], slot="cf")
    sch.op("sp", lambda e: e.dma_start(out=CB, in_=cb_d), w=["CB"], slot="cb")

    F0 = P_Z2T
    ZF = V(F0, 2048)
    HB = [V(F0 + 2048, 2048), V(F0 + 4096, 2048)]
    H3 = V(F0 + 6144, 1024, BF16)
    WF16 = V(F0 + 7168, 1024, BF16)
    WOF = V(F0 + 8192, 4096)
    WSUM = V(F0 + 12288, 1024, BF16)
    WDIF = V(F0 + 13312, 1024, BF16)
    ARG = V(F0 + 14336, 512)
    ADL = V(F0 + 14848, 1024)
    DEC = V(F0 + 15872, 1024)
    HYB = V(F0 + 16896, 2048)
    TROW = V(F0 + 18944, 512)
    ARG2 = V(F0 + 19456, 512)
    assert F0 + 19968 <= P_TMP
    sch.op("sp", lambda e: e.dma_start(out=ZF[0:33, :], in_=zf_d), w=["ZF"], slot="zf")
    sch.op("sp", lambda e: e.dma_start(out=WOF[0:64, :], in_=wout_f_d), w=["WOF"], slot="wof")
    sch.op("sp", lambda e: e.dma_start(out=ADL, in_=adl_d.partition_broadcast(128)), w=["ADL"], slot="adl")
    sch.op("sp", lambda e: e.dma_start(out=HYB[0:1, :], in_=hyb_d), w=["HYB"], slot="hyb")

    for l in range(3):
        sch.op("dve", lambda e, l=l: e.tensor_tensor(out=CF[0:64, CF_FBF + l:CF_FBF + l + 1],
                                                     in0=CF[0:64, CF_FB + l:CF_FB + l + 1],
                                                     in1=CF[0:64, CF_FB + 3:CF_FB + 4], op=ALU.mult),
               r=["CF"], w=[("fbf", l)])
    sch.op("dve", lambda e: e.tensor_tensor(out=WSUM[0:64, :], in0=WOF[0:64, 0:2048], in1=WOF[0:64, 2048:4096], op=ALU.add),
           r=["WOF"], w=["WSUM"])
    sch.op("dve", lambda e: e.tensor_tensor(out=WDIF[0:64, :], in0=WOF[0:64, 2048:4096], in1=WOF[0:64, 0:2048], op=ALU.subtract),
           r=["WOF"], w=["WDIF"])
    sch.op("dve", lambda e: e.tensor_copy(out=WF16[0:64, :], in_=WOF[0:64, 0:2048]), r=["WOF"], w=["WF16"])

    FWOFF = [CF_FW1, CF_FW2, CF_FW3]
    FK = [33, 64, 64]
    fcnt = 0
    for l in range(3):
        src = ZF if l == 0 else HB[(l - 1) % 2]
        srckey = "ZF" if l == 0 else ("HB", (l - 1) % 2)
        for sc in range(4):
            bank = 5 + (fcnt % 2)
            fcnt += 1
            sch.op("pe", lambda e, l=l, sc=sc, bank=bank, src=src: e.matmul(
                psb(bank)[0:64, :], CF[0:FK[l], FWOFF[l]:FWOFF[l] + 64], src[0:FK[l], sc * 512:(sc + 1) * 512],
                start=True, stop=True), r=["CF", (srckey, sc) if l else "ZF"], w=[("ps", bank)])
            sch.op("act", lambda e, l=l, bank=bank: e.activation(
                out=ARG[0:64, :], in_=psb(bank)[0:64, :], func=AF.Identity,
                scale=CF[0:64, CF_FB + 3:CF_FB + 4], bias=CF[0:64, CF_FBF + l:CF_FBF + l + 1]),
                r=[("ps", bank), ("fbf", l), "CF"], w=["ARG"])
            sch.op("dve", lambda e: e.tensor_scalar(out=ARG2[0:64, :], in0=ARG[0:64, :], scalar1=PI, scalar2=2 * PI,
                                                    op0=ALU.is_gt, op1=ALU.mult), r=["ARG"], w=["ARG2"])
            sch.op("dve", lambda e: e.tensor_tensor(out=ARG[0:64, :], in0=ARG[0:64, :], in1=ARG2[0:64, :], op=ALU.subtract),
                   r=["ARG", "ARG2"], w=["ARG"])
            sch.op("dve", lambda e: e.tensor_scalar(out=ARG2[0:64, :], in0=ARG[0:64, :], scalar1=-PI, scalar2=2 * PI,
                                                    op0=ALU.is_lt, op1=ALU.mult), r=["ARG"], w=["ARG2"])
            sch.op("dve", lambda e: e.tensor_tensor(out=ARG[0:64, :], in0=ARG[0:64, :], in1=ARG2[0:64, :], op=ALU.add),
                   r=["ARG", "ARG2"], w=["ARG"])
            if l < 2:
                dst = HB[l % 2][0:64, sc * 512:(sc + 1) * 512]
                dkey = (("HB", l % 2), sc)
            else:
                dst = H3[0:64, sc * 512:(sc + 1) * 512]
                dkey = ("H3", sc)
            sch.op("act", lambda e, dst=dst: e.activation(out=dst, in_=ARG[0:64, :], func=AF.Sin),
                   r=["ARG"], w=[dkey])


    ABv = [[V(P_H + 8192 * sl + 4096 * w_, 4096, BF16, "p (t c) -> p t c", t=16) for w_ in range(2)] for sl in range(2)]
    kcnt = [0]
    panel_seq = []
    for _p in range(4):
        panel_seq += [(dftK, m) for m in range(7, -1, -1)]
    for _h in range(2):
        for _c in range(2):
            panel_seq += [(dftF, m) for m in range(NT)]
            panel_seq += [(dftI, m) for m in range(NT)]
    pst = {"use": 0, "issued": 0}

    def _issue_panel():
        i = pst["issued"]
        if i >= len(panel_seq):
            return
        src_d, m = panel_seq[i]
        sl = i % 2
        pst["issued"] += 1
        sch.op("sp", lambda e, sl=sl, m=m, src_d=src_d: e.dma_start(out=PAN[sl], in_=src_d[m]), w=[("PAN", sl)], slot=("pan", sl))

    def load_panel(src_d, m):
        i = pst["use"]
        assert panel_seq[i][1] == m
        while pst["issued"] <= i:
            _issue_panel()
        if i == 0:
            _issue_panel()
        pst["use"] += 1
        return i % 2

    def filt_gen_tile(pss, st):
        od, hf = pss // 2, pss % 2
        sl = pss % 2
        A_, B_ = ABv[sl]
        c0 = od * 1024 + hf * 512
        sch.op("act", lambda e, st=st, hf=hf: e.activation(out=DEC[:, 0:512], in_=ADL[:, hf * 512:(hf + 1) * 512], func=AF.Exp,
                                                           scale=CF[:, CF_TNEG + st:CF_TNEG + st + 1]),
               r=["ADL", "CF"], w=["DEC"])
        for which in range(2):
            bank = 5 + (kcnt[0] % 2)
            kcnt[0] += 1
            wsrc = WSUM if which == 0 else WDIF
            dst = (A_, B_)[which]
            sch.op("pe", lambda e, st=st, bank=bank, wsrc=wsrc, c0=c0: e.matmul(
                psb(bank), H3[0:64, 256 * (st % 8) + st // 8:256 * (st % 8) + st // 8 + 255:2], wsrc[0:64, c0:c0 + 512],
                start=True, stop=True), r=[("H3", (st % 8) // 2), "WSUM", "WDIF"], w=[("ps", bank)])
            sch.op("dve", lambda e, st=st, bank=bank, dst=dst: e.tensor_tensor(
                out=dst[:, st, :], in0=psb(bank), in1=DEC[:, 0:512], op=ALU.mult),
                r=[("ps", bank), "DEC"], w=[("AB", sl, which, st)])
        if st == 0:
            bank = 5 + (kcnt[0] % 2)
            kcnt[0] += 1
            sch.op("pe", lambda e, bank=bank, c0=c0: e.matmul(
                psb(bank)[0:1, :], H3[0:64, 0:1], WF16[0:64, c0:c0 + 512], start=True, stop=True),
                r=[("H3", 0), "WF16"], w=[("ps", bank)])
            sch.op("dve", lambda e, bank=bank, c0=c0: e.tensor_tensor(
                out=TROW[0:1, :], in0=psb(bank)[0:1, :], in1=HYB[0:1, c0:c0 + 512], op=ALU.add),
                r=[("ps", bank), "HYB"], w=["TROW"])
            sch.op("dve", lambda e, A_=A_: e.tensor_copy(out=A_[0:1, 0, :], in_=TROW[0:1, :]),
                   r=["TROW"], w=[("AB", sl, 0, 0)])
            sch.op("dve", lambda e, B_=B_: e.tensor_scalar(
                out=B_[0:1, 0, :], in0=TROW[0:1, :], scalar1=-1.0, scalar2=None, op0=ALU.mult),
                r=["TROW"], w=[("AB", sl, 1, 0)])

    KB0 = P_TMP
    STGP = [V(KB0 + 512 * i, 512, BF16, "p (a c) -> p a c", a=2) for i in range(2)]
    STGM = [V(KB0 + 1024 + 512 * i, 512, BF16, "p (a c) -> p a c", a=2) for i in range(3)]
    STGR = [V(KB0 + 2560 + 512 * i, 512, BF16, "p (a c) -> p a c", a=2) for i in range(2)]
    OSB = [V(KB0 + 3584 + 512 * i, 512) for i in range(2)]
    SPEC = V(KB0 + 4608, 512, BF16, "p (a c) -> p a c", a=2)
    assert KB0 + 5120 <= P_PAN
    S11 = 2.0 ** -11
    for st in range(NT):
        filt_gen_tile(0, st)
    sch.op("dve", lambda e: e.memset(SPEC, 0.0), w=["SPEC"])
    gcount = [0]
    for pss in range(4):
        od, hf = pss // 2, pss % 2
        sl = pss % 2
        A_, B_ = ABv[sl]
        c0 = od * 1024 + hf * 512
        for a_, (src_, k0) in enumerate(((A_, 0), (B_, 8))):
            bk = 4 if a_ == 0 else 7
            for k in range(8):
                sch.op("pe", lambda e, k=k, k0=k0, src_=src_, bk=bk: e.matmul(
                    psb(bk)[0:1, :], SGNC, src_[:, k0 + k, :], start=(k == 0), stop=(k == 7)),
                    r=["CB", ("AB", sl, a_, k0 + k)], w=[("ps", bk)])
            sch.op("act", lambda e, a_=a_, bk=bk: e.activation(out=SPEC[0:1, a_, :], in_=psb(bk)[0:1, :], func=AF.Copy, scale=S11),
                   r=[("ps", bk)], w=["SPEC"])
        prev_m = None
        pending = None

        def emit_rev(pend):
            m_, ss_, sm_, prev_t, prev_keys = pend
            for a_ in range(2):
                bk = 4 if a_ == 0 else 7
                sch.op("pe", lambda e, a_=a_, bk=bk: e.matmul(psb(bk), JREV, STGM[sm_][:, a_, :], start=True, stop=False),
                       r=["CB", ("STGM", sm_, a_)], w=[("ps", bk)])
                sch.op("pe", lambda e, a_=a_, bk=bk: e.matmul(psb(bk), E00, prev_t[:, a_, :], start=False, stop=True),
                       r=["CB"] + prev_keys, w=[("ps", bk)])
                sch.op("act", lambda e, a_=a_, bk=bk: e.activation(out=STGR[ss_][:, a_, :], in_=psb(bk), func=AF.Copy),
                       r=[("ps", bk)], w=[("STGR", ss_, a_)])
            sch.op("act", lambda e, od_=od, hf_=hf: e.dma_start(out=kco_d[od_, 15 - m_, :, :, hf_ * 512:(hf_ + 1) * 512], in_=STGR[ss_]),
                   r=[("STGR", ss_, 0), ("STGR", ss_, 1)], w=[("kco", od, 15 - m_, hf)], slot=("kst_outr", ss_))

        for m in range(7, -1, -1):
            psl = load_panel(dftK, m)
            g = gcount[0]
            gcount[0] += 1
            ss = g % 2
            sm = g % 3
            for a_, src_ in enumerate((A_, B_)):
                be, bo = 2 * a_, 2 * a_ + 1
                for k in range(8):
                    sch.op("pe", lambda e, k=k, psl=psl, be=be, a_=a_, src_=src_: e.matmul(
                        psb(be), PAN[psl][:, a_, k, :], src_[:, k, :], start=(k == 0), stop=(k == 7)),
                        r=[("PAN", psl), ("AB", sl, a_, k)], w=[("ps", be)])
                for k in range(8, 16):
                    sch.op("pe", lambda e, k=k, psl=psl, bo=bo, a_=a_, src_=src_: e.matmul(
                        psb(bo), PAN[psl][:, a_, k, :], src_[:, k, :], start=(k == 8), stop=(k == 15)),
                        r=[("PAN", psl), ("AB", sl, a_, k)], w=[("ps", bo)])
                if a_ == 1:
                    _issue_panel()
                    if pending is not None:
                        emit_rev(pending)
                        pending = None
                sch.op("act", lambda e, bo=bo, a_=a_: e.activation(out=OSB[a_], in_=psb(bo), func=AF.Copy, scale=S11),
                       r=[("ps", bo)], w=[("OSB", a_)])
                sch.op("dve", lambda e, be=be, a_=a_, ss=ss: e.scalar_tensor_tensor(
                    out=STGP[ss][:, a_, :], in0=psb(be), scalar=S11, in1=OSB[a_], op0=ALU.mult, op1=ALU.add),
                    r=[("ps", be), ("OSB", a_)], w=[("STGP", ss, a_)])
                if a_ == 0:
                    sch.op("dve", lambda e, be=be, sm=sm: e.scalar_tensor_tensor(
                        out=STGM[sm][:, 0, :], in0=psb(be), scalar=S11, in1=OSB[0], op0=ALU.mult, op1=ALU.subtract),
                        r=[("ps", be), ("OSB", 0)], w=[("STGM", sm, 0)])
                else:
                    sch.op("dve", lambda e, be=be, sm=sm: e.scalar_tensor_tensor(
                        out=STGM[sm][:, 1, :], in0=psb(be), scalar=-S11, in1=OSB[1], op0=ALU.mult, op1=ALU.add),
                        r=[("ps", be), ("OSB", 1)], w=[("STGM", sm, 1)])
            if m == 0:
                sch.op("dve", lambda e, ss=ss: e.tensor_scalar(out=STGP[ss][0:1, 0, :], in0=STGP[ss][0:1, 0, :],
                                                               scalar1=0.5, scalar2=None, op0=ALU.mult),
                       r=[("STGP", ss, 0)], w=[("STGP", ss, 0)])
                sch.op("dve", lambda e, ss=ss: e.memset(STGP[ss][0:1, 1, :], 0.0), w=[("STGP", ss, 1)])
                sch.op("dve", lambda e, sm=sm, c0=c0: e.tensor_scalar(out=KNY[0:1, c0:c0 + 512], in0=STGM[sm][0:1, 0, :],
                                                                      scalar1=0.5, scalar2=None, op0=ALU.mult),
                       r=[("STGM", sm, 0)], w=[("KNY", pss)])
            sch.op("act", lambda e, ss=ss, od=od, m=m, hf=hf: e.dma_start(
                out=kco_d[od, m, :, :, hf * 512:(hf + 1) * 512], in_=STGP[ss]),
                r=[("STGP", ss, 0), ("STGP", ss, 1)], w=[("kco", od, m, hf)], slot=("kst_out", ss))
            prev_t = SPEC if prev_m is None else STGM[prev_m]
            prev_keys = ["SPEC"] if prev_m is None else [("STGM", prev_m, 0), ("STGM", prev_m, 1)]
            pending = (m, ss, sm, prev_t, prev_keys)
            prev_m = sm
            if pss + 1 < 4:
                filt_gen_tile(pss + 1, 2 * (7 - m))
                filt_gen_tile(pss + 1, 2 * (7 - m) + 1)
        emit_rev(pending)

    if stage == 0:
        sch.op("sp", lambda e: e.dma_start(out=outT[0:128, 0:1024].bitcast(BF16).rearrange("p (a c) -> p a c", a=2),
                                           in_=kco_d[0, 1, :, :, :]), r=[("kco", 0, 1, 0), ("kco", 0, 1, 1)], w=["out"], slot="out")
        sch.op("sp", lambda e: e.dma_start(out=outT[128:256, 0:1024].bitcast(BF16).rearrange("p (a c) -> p a c", a=2),
                                           in_=kco_d[1, 0, :, :, :]), r=[("kco", 1, 0, 0), ("kco", 1, 0, 1)], w=["out2"], slot="out")
        sch.op("sp", lambda e: e.dma_start(out=outT[256:257, 0:16], in_=outT[257:258, 0:16]), r=["out", "out2"], slot="fin")
        sch.emit(nc, stack)
        return nc, stack


    sch.fence_all()
    HNT = V(P_H, 8192, BF16, "p (j s) -> p j s", j=8)
    YA = V(P_H + 8192, 4096, BF16, "p (t c) -> p t c", t=16)
    YB = V(P_H + 12288, 4096, BF16, "p (t c) -> p t c", t=16)
    Z2T = V(P_Z2T, 8192, BF16, "p (j s) -> p j s", j=8)
    T0 = V(P_T0, 4096, BF16, "p (t c) -> p t c", t=16)
    T1 = V(P_T1, 4096, BF16, "p (t c) -> p t c", t=16)
    RAW = V(P_RAW, 2052)
    TT = V(P_TT, 2048)
    TMP = [V(P_TMP + 512 * i, 512) for i in range(4)]
    KST = [V(P_KST + 512 * i, 512, BF16, "p (a c) -> p a c", a=2) for i in range(2)]
    WIN = [V(P_WIN + 1024 * i, 1024, BF16, "p (j c) -> p j c", j=8) for i in range(2)]
    XC = [V(P_T0 + 4096 * i, 4096, F32, "p (j s) -> p j s", j=8) for i in range(2)]
    H = V(P_H, 16384, F32, "p (j s) -> p j s", j=8)
    MB = V(P_PAN, 4096, F32, "p (j s) -> p j s", j=8)
    xT_v = xT.rearrange("(j p) s -> p j s", p=128)
    outT_v = outT.rearrange("(j p) s -> p j s", p=128)
    cnt = {"mb": 0, "rs": 0, "win": 0, "kst": 0, "pq": 0, "py": 0}

    def mbank():
        b = 6 + (cnt["mb"] % 2)
        cnt["mb"] += 1
        return b

    def norm_chunk(src, src_keys, layer, gi, dst_of_j, dst_keys_of_j):
        rs = cnt["rs"] % 2
        cnt["rs"] += 1
        sch.op("act", lambda e: e.activation(out=SQ, in_=src, func=AF.Square), r=src_keys, w=[("SQj", j) for j in range(8)])
        bank = mbank()
        for j in range(8):
            sch.op("pe", lambda e, j=j, bank=bank: e.matmul(psb(bank), ONES, SQ[:, j, :], start=(j == 0), stop=(j == 7)),
                   r=[("SQj", j), "CB"], w=[("ps", bank)])
        sch.op("act", lambda e, bank=bank: e.activation(out=LNB, in_=psb(bank), func=AF.Ln, scale=1.0 / D, bias=EPSC),
               r=[("ps", bank), "EPSC"], w=["LNB"])
        sch.op("act", lambda e, rs=rs: e.activation(out=RSTD[rs], in_=LNB, func=AF.Exp, scale=-0.5), r=["LNB"], w=[("RSTD", rs)])
        for j in range(8):
            sch.op("dve", lambda e, j=j, rs=rs: e.scalar_tensor_tensor(
                out=dst_of_j(j), in0=src[:, j, :], scalar=gcol(layer, gi, j), in1=RSTD[rs], op0=ALU.mult, op1=ALU.mult),
                r=src_keys + [("RSTD", rs), "CF"], w=dst_keys_of_j(j))

    def branch_finish(sc, m_keys_ready, layer, gi, res_src, res_keys, tok0):
        rs = cnt["rs"] % 2
        cnt["rs"] += 1
        bank = mbank()
        for j in range(8):
            sch.op("pe", lambda e, j=j, bank=bank: e.matmul(psb(bank), ONES, SQ[:, j, :], start=(j == 0), stop=(j == 7)),
                   r=[("SQj", j), "CB"], w=[("ps", bank)])
        sch.op("act", lambda e, bank=bank: e.activation(out=LNB, in_=psb(bank), func=AF.Ln, scale=1.0 / D, bias=EPSC),
               r=[("ps", bank), "EPSC"], w=["LNB"])
        sch.op("act", lambda e, rs=rs: e.activation(out=RSTD[rs], in_=LNB, func=AF.Exp, scale=-0.5), r=["LNB"], w=[("RSTD", rs)])
        for j in range(8):
            sch.op("dve", lambda e, j=j, rs=rs: e.scalar_tensor_tensor(
                out=MB[:, j, :], in0=MB[:, j, :], scalar=gcol(layer, gi, j), in1=RSTD[rs], op0=ALU.mult, op1=ALU.mult),
                r=[("MB", j), ("RSTD", rs), "CF"], w=[("MB", j)])
            sch.op("dve", lambda e, j=j: e.tensor_tensor(
                out=H[:, j, tok0:tok0 + 512], in0=res_src(j), in1=MB[:, j, :], op=ALU.add),
                r=[("MB", j)] + res_keys(j), w=[("H", j, tok0 // 512)])

    def evac_m(bank, j):
        sch.op("act", lambda e: e.activation(out=MB[:, j, :], in_=psb(bank), func=AF.Copy), r=[("ps", bank)], w=[("MB", j)])
        sch.op("act", lambda e: e.activation(out=SQ[:, j, :], in_=psb(bank), func=AF.Square), r=[("ps", bank)], w=[("SQj", j)])

    EPSC = V(O_LN + 0, 512)[:, 0:1] if False else CF[:, CF_FBF + 3:CF_FBF + 4]
    sch.op("dve", lambda e: e.memset(EPSC, EPS), r=["CF"], w=["EPSC"])

    for sc in range(4):
        xs = sc % 2
        sch.op("sp", lambda e, sc=sc, xs=xs: e.dma_start(out=XC[xs], in_=xT_v[:, :, sc * 512:(sc + 1) * 512]),
               w=[("XC", xs)], slot=("xc", xs))
        norm_chunk(XC[xs], [("XC", xs)], 0, 0,
                   lambda j, sc=sc: HNT[:, j, sc * 512:(sc + 1) * 512], lambda j, sc=sc: [("HNT", sc)])

    sch.op("dve", lambda e: e.memset(RAW[:, 0:1], 0.0), w=["RAWpad"])
    sch.op("dve", lambda e: e.memset(RAW[:, 2049:2050], 0.0), w=["RAWpad2"])
    w_in_v = w_in_d.rearrange("(j p) n -> p j n", p=128)

    def inproj(strm, hf):
        c30 = strm * 1024 + hf * 512
        for sb in range(2):
            ws = cnt["win"] % 2
            cnt["win"] += 1
            sch.op("pool", lambda e, ws=ws, c=c30 + 256 * sb: e.dma_start(out=WIN[ws], in_=w_in_v[:, :, c:c + 256]),
                   w=[("WIN", ws)], slot=("win", ws))
            for q2 in range(2):
                q4 = sb * 2 + q2
                q = c30 // 128 + q4
                for sc in range(4):
                    bank = mbank()
                    for j in range(8):
                        sch.op("pe", lambda e, j=j, ws=ws, q2=q2, sc=sc, bank=bank: e.matmul(
                            psb(bank), WIN[ws][:, j, q2 * 128:(q2 + 1) * 128], HNT[:, j, sc * 512:(sc + 1) * 512],
                            start=(j == 0), stop=(j == 7)), r=[("WIN", ws), ("HNT", sc)], w=[("ps", bank)])
                    sch.op("act", lambda e, sc=sc, bank=bank: e.activation(out=RAW[:, 1 + sc * 512:1 + (sc + 1) * 512], in_=psb(bank), func=AF.Copy),
                           r=[("ps", bank)], w=[("RAW", sc)])
                rawk = [("RAW", i) for i in range(4)] + ["RAWpad", "RAWpad2"]
                sch.op("act", lambda e, q=q: e.activation(out=TT, in_=RAW[:, 0:2048], func=AF.Identity,
                                                          scale=CF[:, CF_CW + q:CF_CW + q + 1], bias=CF[:, CF_CW + 72 + q:CF_CW + 72 + q + 1]),
                       r=rawk + ["CF"], w=["TT"])
                sch.op("dve", lambda e, q=q: e.scalar_tensor_tensor(out=TT, in0=RAW[:, 1:2049], scalar=CF[:, CF_CW + 24 + q:CF_CW + 24 + q + 1],
                                                                    in1=TT, op0=ALU.mult, op1=ALU.add), r=rawk + ["TT", "CF"], w=["TT"])
                sch.op("dve", lambda e, q=q, zc=hf * 4 + q4: e.scalar_tensor_tensor(
                    out=Z2T[:, zc, :], in0=RAW[:, 2:2050], scalar=CF[:, CF_CW + 48 + q:CF_CW + 48 + q + 1],
                    in1=TT, op0=ALU.mult, op1=ALU.add), r=rawk + ["TT", "CF"], w=[("Z2T", hf * 4 + q4)])

    def transp_to(hf, T, Tn):
        for q4 in range(4):
            for stg in range(2):
                bank = mbank()
                pv = psb16(bank)
                for i in range(8):
                    st = stg * 8 + i
                    sch.op("pe", lambda e, i=i, st=st, q4=q4, pv=pv: e.transpose(
                        pv[:, i * 128:(i + 1) * 128], Z2T[:, hf * 4 + q4, st * 128:(st + 1) * 128], IDENT),
                        r=[("Z2T", hf * 4 + q4), "CB"], w=[("ps", bank)])
                sch.op("act", lambda e, pv=pv, stg=stg, q4=q4: e.activation(
                    out=T[:, stg * 8:(stg + 1) * 8, q4 * 128:(q4 + 1) * 128], in_=pv.rearrange("p (i c) -> p i c", i=8), func=AF.Copy),
                    r=[("ps", bank)], w=[(Tn, st_) for st_ in range(stg * 8, stg * 8 + 8)])

    def transp_back(hf, T, Tn):
        for q4 in range(4):
            for stg in range(2):
                bank = mbank()
                pv = psb16(bank)
                for i in range(8):
                    st = stg * 8 + i
                    sch.op("pe", lambda e, i=i, st=st, q4=q4, pv=pv: e.transpose(
                        pv[:, i * 128:(i + 1) * 128], T[:, st, q4 * 128:(q4 + 1) * 128], IDENT),
                        r=[(Tn, st), "CB"], w=[("ps", bank)])
                sch.op("act", lambda e, pv=pv, stg=stg, q4=q4: e.activation(
                    out=Z2T[:, hf * 4 + q4, stg * 1024:(stg + 1) * 1024], in_=pv, func=AF.Copy),
                    r=[("ps", bank)], w=[("Z2T", hf * 4 + q4)])

    def fwd_dft(T, Tn, od, hf):
        c0 = od * 1024 + hf * 512
        for ft in range(NT):
            psl = load_panel(dftF, ft)
            ks = cnt["kst"] % 2
            cnt["kst"] += 1
            sch.op("sp", lambda e, ks=ks, ft=ft: e.dma_start(out=KST[ks], in_=kco_d[od, ft, :, :, hf * 512:(hf + 1) * 512]),
                   r=[("kco", od, ft, hf)], w=[("KST", ks)], slot=("kst", ks))
            pq = cnt["pq"] % 2
            cnt["pq"] += 1
            bp, bq = 2 * pq, 2 * pq + 1
            for st in range(NT):
                sch.op("pe", lambda e, st=st, psl=psl, bp=bp: e.matmul(
                    psb(bp), PAN[psl][:, 0, st, :], T[:, st, :], start=(st == 0), stop=(st == NT - 1)),
                    r=[("PAN", psl), (Tn, st)], w=[("ps", bp)])
            for st in range(NT):
                sch.op("pe", lambda e, st=st, psl=psl, bq=bq: e.matmul(
                    psb(bq), PAN[psl][:, 1, st, :], T[:, st, :], start=(st == 0), stop=(st == NT - 1)),
                    r=[("PAN", psl), (Tn, st)], w=[("ps", bq)])
            _issue_panel()
            Ka, Kb = KST[ks][:, 0, :], KST[ks][:, 1, :]
            kk = [("KST", ks)]
            tt = sch.op
            tt("dve", lambda e, bp=bp, Ka=Ka: e.tensor_tensor(out=TMP[0], in0=psb(bp), in1=Ka, op=ALU.mult), r=[("ps", bp)] + kk, w=[("TMP", 0)])
            tt("dve", lambda e, bq=bq, Kb=Kb: e.tensor_tensor(out=TMP[1], in0=psb(bq), in1=Kb, op=ALU.mult), r=[("ps", bq)] + kk, w=[("TMP", 1)])
            tt("dve", lambda e, ft=ft: e.tensor_tensor(out=YA[:, ft, :], in0=TMP[0], in1=TMP[1], op=ALU.add),
               r=[("TMP", 0), ("TMP", 1)], w=[("YA", ft)])
            tt("dve", lambda e, bq=bq, Ka=Ka: e.tensor_tensor(out=TMP[2], in0=psb(bq), in1=Ka, op=ALU.mult), r=[("ps", bq)] + kk, w=[("TMP", 2)])
            tt("dve", lambda e, bp=bp, Kb=Kb: e.tensor_tensor(out=TMP[3], in0=psb(bp), in1=Kb, op=ALU.mult), r=[("ps", bp)] + kk, w=[("TMP", 3)])
            tt("dve", lambda e, ft=ft: e.tensor_tensor(out=YB[:, ft, :], in0=TMP[2], in1=TMP[3], op=ALU.subtract),
               r=[("TMP", 2), ("TMP", 3)], w=[("YB", ft)])
            if ft == 0:
                tt("dve", lambda e, bq=bq: e.tensor_tensor(out=YB[0:1, 0, :], in0=psb(bq)[0:1, :], in1=KNY[0:1, c0:c0 + 512], op=ALU.mult),
                   r=[("ps", bq), ("KNY", od * 2 + hf)], w=[("YB", 0)])

    def inv_dft(T, Tn):
        for tt_ in range(NT):
            psl = load_panel(dftI, tt_)
            by = 4 + (cnt["py"] % 2)
            cnt["py"] += 1
            for ft in range(NT):
                sch.op("pe", lambda e, ft=ft, psl=psl, by=by: e.matmul(
                    psb(by), PAN[psl][:, 0, ft, :], YA[:, ft, :], start=(ft == 0), stop=False),
                    r=[("PAN", psl), ("YA", ft)], w=[("ps", by)])
                sch.op("pe", lambda e, ft=ft, psl=psl, by=by: e.matmul(
                    psb(by), PAN[psl][:, 1, ft, :], YB[:, ft, :], start=False, stop=(ft == NT - 1)),
                    r=[("PAN", psl), ("YB", ft)], w=[("ps", by)])
            _issue_panel()
            sch.op("dve", lambda e, tt_=tt_, by=by: e.tensor_tensor(out=T[:, tt_, :], in0=psb(by), in1=T[:, tt_, :], op=ALU.mult),
                   r=[("ps", by), (Tn, tt_)], w=[(Tn, tt_)])

    for hf in range(2):
        inproj(0, hf)
        transp_to(hf, T0, "T0")
        inproj(1, hf)
        fwd_dft(T0, "T0", 0, hf)
        transp_to(hf, T1, "T1")
        inv_dft(T1, "T1")
        inproj(2, hf)
        fwd_dft(T1, "T1", 1, hf)
        transp_to(hf, T0, "T0")
        inv_dft(T0, "T0")
        transp_back(hf, T0, "T0")

    sch.fence_all()
    WO = V(P_RAW, 4096, BF16, "p (j d) -> p j d", j=8)
    sch.op("pool", lambda e: e.dma_start(out=WO, in_=w_out_d.rearrange("(j p) d -> p j d", p=128)), w=["WO"], slot="wo")
    XC2 = [V(P_T0 + 4096 * i, 4096, F32, "p (j s) -> p j s", j=8) for i in range(2)]
    for sc in range(4):
        xs = sc % 2
        sch.op("sp", lambda e, sc=sc, xs=xs: e.dma_start(out=XC2[xs], in_=xT_v[:, :, sc * 512:(sc + 1) * 512]),
               w=[("XC2", xs)], slot=("xc2", xs))
        for jd in range(8):
            bank = mbank()
            for cj in range(8):
                sch.op("pe", lambda e, cj=cj, jd=jd, sc=sc, bank=bank: e.matmul(
                    psb(bank), WO[:, cj, jd * 128:(jd + 1) * 128], Z2T[:, cj, sc * 512:(sc + 1) * 512],
                    start=(cj == 0), stop=(cj == 7)), r=["WO", ("Z2T", cj)], w=[("ps", bank)])
            evac_m(bank, jd)
        branch_finish(sc, None, 0, 1, lambda j, xs=xs: XC2[xs][:, j, :], lambda j, xs=xs: [("XC2", xs)], sc * 512)

    def write_out():
        for j in range(8):
            sch.op("sp", lambda e, j=j: e.dma_start(out=outT_v[:, j, :], in_=H[:, j, :]),
                   r=[("H", j, i) for i in range(4)], w=[("out", j)], slot=("out", j % 4))
        sch.op("sp", lambda e: e.dma_start(out=kco_d[0, 0, 0:1, 0, 0:8], in_=kco_d[0, 0, 1:2, 0, 0:8]),
               r=[("out", j) for j in range(8)], slot="fin")

    if stage == 1:
        write_out()
        sch.emit(nc, stack)
        return nc, stack


    def mlp(layer):
        sch.fence_all()
        tag = "L%d" % layer
        HNC = V(P_RAW, 4096, BF16, "p (j s) -> p j s", j=8)
        ACTB = V(P_Z2T, 16384, BF16, "p (f s) -> p f s", f=32)
        WU = [V(o_, 1024, BF16, "p (j c) -> p j c", j=8) for o_ in (P_TMP, P_TMP + 1024, P_X)]
        WD = [V(P_KST + 1024 * i, 1024, BF16, "p (f d) -> p f d", f=4) for i in range(3)]
        RL = [V(O_KNY + 256 * i, 256, BF16) for i in range(2)]
        wu_v = w_up_d[layer].rearrange("(j p) f -> p j f", p=128)
        wd_v = w_down_d[layer].rearrange("(f p) d -> p f d", p=128)
        c = {"wu": 0, "wd": 0, "rl": 0, "ub": 0}
        for tc in range(2):
            t0 = tc * 1024
            for sc2 in range(2):
                tok = t0 + sc2 * 512
                norm_chunk(H[:, :, tok:tok + 512], [("H", j, tok // 512) for j in range(8)], layer, 2,
                           lambda j, sc2=sc2: HNC[:, j, sc2 * 512:(sc2 + 1) * 512], lambda j, sc2=sc2: [(tag + "HNC", sc2)])
            for slab in range(16):
                ws = c["wu"] % 3
                c["wu"] += 1
                sch.op("pool", lambda e, ws=ws, slab=slab: e.dma_start(out=WU[ws], in_=wu_v[:, :, slab * 256:(slab + 1) * 256]),
                       w=[(tag + "WU", ws)], slot=(tag + "wu", ws))
                for q2 in range(2):
                    ffc = slab * 2 + q2
                    for sc2 in range(2):
                        bank = 4 + (c["ub"] % 2)
                        c["ub"] += 1
                        for j in range(8):
                            sch.op("pe", lambda e, j=j, ws=ws, q2=q2, sc2=sc2, bank=bank: e.matmul(
                                psb(bank), WU[ws][:, j, q2 * 128:(q2 + 1) * 128], HNC[:, j, sc2 * 512:(sc2 + 1) * 512],
                                start=(j == 0), stop=(j == 7)), r=[(tag + "WU", ws), (tag + "HNC", sc2)], w=[("ps", bank)])
                        rl = c["rl"] % 2
                        c["rl"] += 1
                        sch.op("act", lambda e, bank=bank, rl=rl: e.activation(out=RL[rl], in_=psb(bank), func=AF.Relu),
                               r=[("ps", bank)], w=[(tag + "RL", rl)])
                        sch.op("dve", lambda e, rl=rl, ffc=ffc, sc2=sc2: e.tensor_tensor(
                            out=ACTB[:, ffc, sc2 * 512:(sc2 + 1) * 512], in0=RL[rl], in1=RL[rl], op=ALU.mult),
                            r=[(tag + "RL", rl)], w=[(tag + "ACT", ffc, sc2)])
            for sc2 in range(2):
                tok = t0 + sc2 * 512
                for jh in range(2):
                    for slab in range(8):
                        ws = c["wd"] % 3
                        c["wd"] += 1
                        sch.op("pool", lambda e, ws=ws, slab=slab, jh=jh: e.dma_start(
                            out=WD[ws], in_=wd_v[:, slab * 4:(slab + 1) * 4, jh * 512:(jh + 1) * 512]),
                            w=[(tag + "WD", ws)], slot=(tag + "wd", ws))
                        for f4 in range(4):
                            ffc = slab * 4 + f4
                            for jq in range(4):
                                sch.op("pe", lambda e, ws=ws, f4=f4, jq=jq, ffc=ffc, sc2=sc2: e.matmul(
                                    psb(jq), WD[ws][:, f4, jq * 128:(jq + 1) * 128], ACTB[:, ffc, sc2 * 512:(sc2 + 1) * 512],
                                    start=(ffc == 0), stop=(ffc == 31)), r=[(tag + "WD", ws), (tag + "ACT", ffc, sc2)], w=[("ps", jq)])
                    for jq in range(4):
                        evac_m(jq, jh * 4 + jq)
                branch_finish(None, None, layer, 3, lambda j, tok=tok: H[:, j, tok:tok + 512],
                              lambda j, tok=tok: [("H", j, tok // 512)], tok)

    mlp(0)
    if stage == 2:
        write_out()
        sch.emit(nc, stack)
        return nc, stack


    sch.fence_all()
    HNT2 = V(P_Z2T, 8192, BF16, "p (j s) -> p j s", j=8)
    QKT = V(P_T0, 12288, BF16, "p (c s) -> p c s", c=12)
    VTOK = V(P_TMP, 2112, BF16, "p (t g e) -> p t g e", t=16, g=4)
    WQ = [V(P_WIN + 1024 * i, 1024, BF16, "p (j c) -> p j c", j=8) for i in range(2)]
    ATC = V(P_PAN, 2368, BF16)
    MASK = ATC[:, 0:384]
    PERMR = ATC[:, 384:512]
    PERMH = ATC[:, 512:640]
    COS = ATC[:, 640:2688]
    SIN = ATC[:, 2688:4736]
    PTS = [V(P_PAN + 2368 + 192 * i, 192, BF16) for i in range(8)]
    ESK = V(P_PAN + 3904, 16)
    RDN = V(P_PAN + 3920, 16)
    XB = [V(O_KNY + 256 * i, 256, BF16) for i in range(2)]
    XS = [V(O_KNY + 512 + 256 * i, 256, BF16) for i in range(2)]
    wqkv_v = wqkv_d.rearrange("(j p) n -> p j n", p=128)
    sch.op("sp", lambda e: e.dma_start(out=ATC, in_=atc_d), w=["ATC"], slot="atc")
    sch.op("sp", lambda e: e.dma_start(out=ESK, in_=sink_d.partition_broadcast(128)), w=["ESK"], slot="esk")
    sch.op("act", lambda e: e.activation(out=ESK, in_=ESK, func=AF.Exp), r=["ESK"], w=["ESK"])
    for sc in range(4):
        norm_chunk(H[:, :, sc * 512:(sc + 1) * 512], [("H", j, sc) for j in range(8)], 1, 0,
                   lambda j, sc=sc: HNT2[:, j, sc * 512:(sc + 1) * 512], lambda j, sc=sc: [("HNT2", sc)])
    ac = {"wq": 0, "xb": 0, "pt": 0, "sb": 0, "pv": 0}
    KZ = [[QKT[:, 8, :], QKT[:, 9, :]], [QKT[:, 10, :], QKT[:, 11, :]],
          [V(O_SQ, 1024, BF16), V(O_SQ + 1024, 1024, BF16)], [V(O_RSTD, 1024, BF16), V(P_X, 1024, BF16)]]
    sch.retire([("SQj", j) for j in range(8)] + [("RSTD", 0), ("RSTD", 1)],
               [("KZ", g_, v_, s_) for g_ in (2, 3) for v_ in range(2) for s_ in range(4)] + [("KZz", g_, v_) for g_ in (2, 3) for v_ in range(2)])
    for g_ in range(4):
        sch.op("dve", lambda e, g_=g_: e.memset(KZ[g_][0][64:128, :], 0.0), w=[("KZz", g_, 0)])
        sch.op("dve", lambda e, g_=g_: e.memset(KZ[g_][1][0:64, :], 0.0), w=[("KZz", g_, 1)])
    dst_chunk = [0, 1, 2, 3, 4, 5, 6, 7, 8, 10]
    for slab in range(5):
        ws = ac["wq"] % 2
        ac["wq"] += 1
        sch.op("pool", lambda e, ws=ws, slab=slab: e.dma_start(out=WQ[ws], in_=wqkv_v[:, :, slab * 256:(slab + 1) * 256]),
               w=[("WQ", ws)], slot=("wq", ws))
        for q2 in range(2):
            dc = dst_chunk[slab * 2 + q2]
            for sc in range(4):
                bank = mbank()
                for j in range(8):
                    sch.op("pe", lambda e, j=j, ws=ws, q2=q2, sc=sc, bank=bank: e.matmul(
                        psb(bank), WQ[ws][:, j, q2 * 128:(q2 + 1) * 128], HNT2[:, j, sc * 512:(sc + 1) * 512],
                        start=(j == 0), stop=(j == 7)), r=[("WQ", ws), ("HNT2", sc)], w=[("ps", bank)])
                xb = ac["xb"] % 2
                ac["xb"] += 1
                sch.op("act", lambda e, bank=bank, xb=xb: e.activation(out=XB[xb], in_=psb(bank), func=AF.Copy),
                       r=[("ps", bank)], w=[("XB", xb)])
                bank2 = mbank()
                sch.op("pe", lambda e, bank2=bank2, xb=xb: e.matmul(psb(bank2), PERMR, XB[xb], start=True, stop=True),
                       r=[("XB", xb), "ATC"], w=[("ps", bank2)])
                sch.op("dve", lambda e, bank2=bank2, xb=xb, sc=sc: e.tensor_tensor(
                    out=XS[xb], in0=psb(bank2), in1=SIN[:, sc * 512:(sc + 1) * 512], op=ALU.mult),
                    r=[("ps", bank2), "ATC"], w=[("XS", xb)])
                sch.op("dve", lambda e, xb=xb, sc=sc: e.tensor_tensor(
                    out=XB[xb], in0=XB[xb], in1=COS[:, sc * 512:(sc + 1) * 512], op=ALU.mult),
                    r=[("XB", xb), "ATC"], w=[("XB", xb)])
                if dc < 8:
                    sch.op("dve", lambda e, xb=xb, sc=sc, dc=dc: e.tensor_tensor(
                        out=QKT[:, dc, sc * 512:(sc + 1) * 512], in0=XB[xb], in1=XS[xb], op=ALU.add),
                        r=[("XB", xb), ("XS", xb)], w=[("QKT", dc, sc)])
                else:
                    g0, g1 = (0, 1) if dc == 8 else (2, 3)
                    cs = slice(sc * 512, (sc + 1) * 512)
                    sch.op("dve", lambda e, xb=xb: e.tensor_tensor(out=XS[xb], in0=XB[xb], in1=XS[xb], op=ALU.add),
                           r=[("XB", xb), ("XS", xb)], w=[("XS", xb)])
                    sch.op("act", lambda e, xb=xb, g0=g0, cs=cs: e.activation(out=KZ[g0][0][0:64, cs], in_=XS[xb][0:64, :], func=AF.Copy),
                           r=[("XS", xb)], w=[("KZ", g0, 0, sc)])
                    sch.op("act", lambda e, xb=xb, g1=g1, cs=cs: e.activation(out=KZ[g1][1][64:128, cs], in_=XS[xb][64:128, :], func=AF.Copy),
                           r=[("XS", xb)], w=[("KZ", g1, 1, sc)])
                    bank3 = mbank()
                    sch.op("pe", lambda e, bank3=bank3, xb=xb: e.matmul(psb(bank3), PERMH, XS[xb], start=True, stop=True),
                           r=[("XS", xb), "ATC"], w=[("ps", bank3)])
                    sch.op("act", lambda e, bank3=bank3, g1=g1, cs=cs: e.activation(out=KZ[g1][0][0:64, cs], in_=psb(bank3)[0:64, :], func=AF.Copy),
                           r=[("ps", bank3)], w=[("KZ", g1, 0, sc)])
                    sch.op("act", lambda e, bank3=bank3, g0=g0, cs=cs: e.activation(out=KZ[g0][1][64:128, cs], in_=psb(bank3)[64:128, :], func=AF.Copy),
                           r=[("ps", bank3)], w=[("KZ", g0, 1, sc)])
    ws = ac["wq"] % 2
    ac["wq"] += 1
    sch.op("pool", lambda e, ws=ws: e.dma_start(out=WQ[ws], in_=wqkv_v[:, :, 1280:1536]), w=[("WQ", ws)], slot=("wq", ws))
    sch.op("dve", lambda e: e.memset(VTOK[:, :, :, 64:66], 1.0), w=["VONE"])
    for st in range(NT):
        bank = mbank()
        for j in range(8):
            sch.op("pe", lambda e, j=j, ws=ws, st=st, bank=bank: e.matmul(
                psb(bank)[:, 0:256], HNT2[:, j, st * 128:(st + 1) * 128], WQ[ws][:, j, :],
                start=(j == 0), stop=(j == 7)), r=[("WQ", ws), ("HNT2", st // 4)], w=[("ps", bank)])
        sch.op("act", lambda e, bank=bank, st=st: e.activation(
            out=VTOK[:, st, :, 0:64], in_=psb(bank)[:, 0:256].rearrange("p (g e) -> p g e", g=4), func=AF.Copy),
            r=[("ps", bank)], w=[("VTOK", st)])

    OTOK = V(P_Z2T, 8192, BF16, "p (t c) -> p t c", t=16)
    sch.retire([("HNT2", i) for i in range(4)], [("OTOK", i) for i in range(16)])

    NSLOT = 8
    LAG = 3
    tidx = lambda h, j: h * NT + j

    def pv(h, i):
        g = h // 4
        kbs = [kb for kb in (i - 1, i, i + 1) if 0 <= kb < NT]
        pb = ac["pv"] % 4
        ac["pv"] += 1
        for n_, kb in enumerate(kbs):
            qlo = max(kb - 1, 0)
            slot = tidx(h, kb) % NSLOT
            off = (i - qlo) * 128
            sch.op("pe", lambda e, slot=slot, off=off, kb=kb, pb=pb, n_=n_: e.matmul(
                psb(pb)[:, 0:65], PTS[slot][:, off:off + 128], VTOK[:, kb, g, 0:65],
                start=(n_ == 0), stop=(n_ == len(kbs) - 1)),
                r=[("PT", slot), ("VTOK", kb), "VONE"], w=[("ps", pb)])
        rd = ac["pv"] % 2
        sch.op("dve", lambda e, pb=pb, h=h, rd=rd: e.tensor_scalar(out=RDN[:, 2 * rd:2 * rd + 1], in0=psb(pb)[:, 64:65], scalar1=ESK[:, h:h + 1],
                                                               scalar2=None, op0=ALU.add), r=[("ps", pb), "ESK"], w=[("RDN", rd)])
        sch.op("dve", lambda e, rd=rd: e.reciprocal(out=RDN[:, 2 * rd + 1:2 * rd + 2], in_=RDN[:, 2 * rd:2 * rd + 1]), r=[("RDN", rd)], w=[("RDN2", rd)])
        sch.op("dve", lambda e, pb=pb, h=h, i=i, rd=rd: e.tensor_scalar(out=OTOK[:, i, h * 64:(h + 1) * 64], in0=psb(pb)[:, 0:64],
                                                                       scalar1=RDN[:, 2 * rd + 1:2 * rd + 2], scalar2=None, op0=ALU.mult),
               r=[("ps", pb), ("RDN2", rd)], w=[("OTOK", i)])

    def scores(h, j):
        g = h // 4
        v = h % 2
        qc = h // 2
        qlo, qhi = max(j - 1, 0), min(j + 1, NT - 1)
        nq = qhi - qlo + 1
        bank = 4 + (ac["sb"] % 4)
        ac["sb"] += 1
        slot = tidx(h, j) % NSLOT
        sch.op("pe", lambda e: e.matmul(
            psb(bank)[:, 0:nq * 128], KZ[g][v][:, j * 128:(j + 1) * 128], QKT[:, qc, qlo * 128:(qhi + 1) * 128],
            start=True, stop=False),
            r=[("KZ", g, v, j // 4), ("KZz", g, v)] + [("QKT", qc, s_) for s_ in range(qlo // 4, qhi // 4 + 1)], w=[("ps", bank)])
        m0 = 128 if j == 0 else 0
        sch.op("pe", lambda e: e.matmul(psb(bank)[:, 0:nq * 128], IDENT, MASK[:, m0:m0 + nq * 128], start=False, stop=True),
               r=["ATC", "CB"], w=[("ps", bank)])
        sch.op("act", lambda e: e.activation(
            out=PTS[slot][:, 0:nq * 128], in_=psb(bank)[:, 0:nq * 128], func=AF.Exp, scale=0.125),
            r=[("ps", bank)], w=[("PT", slot)])

    pv_tasks = []
    for h in range(16):
        for i in range(NT):
            pv_tasks.append((tidx(h, min(i + 1, NT - 1)), h, i))
    pvi = 0
    for t in range(16 * NT + LAG):
        if t < 16 * NT:
            scores(t // NT, t % NT)
        while pvi < len(pv_tasks) and pv_tasks[pvi][0] + LAG <= t:
            pv(pv_tasks[pvi][1], pv_tasks[pvi][2])
            pvi += 1
    assert pvi == len(pv_tasks)

    OT = V(P_T0, 8192, BF16, "p (j s) -> p j s", j=8)
    sch.retire([("QKT", c, s_) for c in range(8) for s_ in range(4)] + [("KZ", g_, v_, s_) for g_ in range(2) for v_ in range(2) for s_ in range(4)]
               + [("KZz", g_, v_) for g_ in range(2) for v_ in range(2)], [("OT", j) for j in range(8)] + ["WO2"])
    WO2 = V(P_RAW, 4096, BF16, "p (j d) -> p j d", j=8)
    sch.op("pool", lambda e: e.dma_start(out=WO2, in_=wo2_d.rearrange("(j p) d -> p j d", p=128)), w=["WO2"], slot="wo2")
    for cj in range(8):
        for stg in range(2):
            bank = mbank()
            pv_ = psb16(bank)
            for i in range(8):
                st = stg * 8 + i
                sch.op("pe", lambda e, i=i, st=st, cj=cj, pv_=pv_: e.transpose(
                    pv_[:, i * 128:(i + 1) * 128], OTOK[:, st, cj * 128:(cj + 1) * 128], IDENT),
                    r=[("OTOK", st), "CB"], w=[("ps", bank)])
            sch.op("act", lambda e, pv_=pv_, stg=stg, cj=cj: e.activation(
                out=OT[:, cj, stg * 1024:(stg + 1) * 1024], in_=pv_, func=AF.Copy), r=[("ps", bank)], w=[("OT", cj)])
    sch.fence_all()
    for sc in range(4):
        for jd in range(8):
            bank = mbank()
            for cj in range(8):
                sch.op("pe", lambda e, cj=cj, jd=jd, sc=sc, bank=bank: e.matmul(
                    psb(bank), WO2[:, cj, jd * 128:(jd + 1) * 128], OT[:, cj, sc * 512:(sc + 1) * 512],
                    start=(cj == 0), stop=(cj == 7)), r=["WO2", ("OT", cj)], w=[("ps", bank)])
            evac_m(bank, jd)
        branch_finish(sc, None, 1, 1, lambda j, sc=sc: H[:, j, sc * 512:(sc + 1) * 512], lambda j, sc=sc: [("H", j, sc)], sc * 512)
    if stage == 3:
        write_out()
        sch.emit(nc, stack)
        return nc, stack
    mlp(1)
    write_out()
    sch.emit(nc, stack)
    return nc, stack


def make_in_maps(inp):
    cf, cb, zf = _host_consts(inp)
    fwd, inv, dk = _dft_tables()
    adl = _absdelta().reshape(1, D)
    hyb = np.ascontiguousarray(np.asarray(inp["hy_bias"], np.float32)[0].reshape(1, 2 * D))
    x = np.asarray(inp["x"], np.float32)
    common = {
        "cf": cf, "cb": cb, "zf": zf, "adl": adl, "hyb": hyb, "dftF": fwd, "dftI": inv, "dftK": dk,
        "hy_w_in": np.ascontiguousarray(np.asarray(inp["hy_w_in"], np.float32)[0]),
        "hy_f_wout": np.ascontiguousarray(np.asarray(inp["hy_f_wout"], np.float32)[0]),
        "hy_w_out": np.ascontiguousarray(np.asarray(inp["hy_w_out"], np.float32)[0]),
        "w_up": np.ascontiguousarray(np.asarray(inp["w_up"], np.float32)),
        "w_down": np.ascontiguousarray(np.asarray(inp["w_down"], np.float32)),
    }
    common["at_w_qkv"] = np.ascontiguousarray(np.asarray(inp["at_w_qkv"], np.float32)[0])
    common["at_w_o"] = np.ascontiguousarray(np.asarray(inp["at_w_o"], np.float32)[0])
    common["sink"] = np.ascontiguousarray(np.asarray(inp["at_sink"], np.float32)[0].reshape(1, 16))
    common["atc"] = _attn_consts()
    maps = []
    for c in range(NCORES):
        m = dict(common)
        m["xT"] = np.ascontiguousarray(x[c].T)
        maps.append(m)
    return maps


_PROG = {}


def kernel(**inputs):
    inp = {k: np.asarray(v) for k, v in inputs.items()}
    if "nc" not in _PROG:
        _PROG["nc"] = build_program(stage=4)
    nc, _stack = _PROG["nc"]
    maps = make_in_maps(inp)
    res = run_bass_kernel_spmd(nc, maps, core_ids=list(range(NCORES)))
    out = np.stack([np.ascontiguousarray(r["outT"].T) for r in res.results], axis=0)
    return out.astype(np.float32)
```

```python
import math
import bisect
import numpy as np
import ml_dtypes
import concourse.bass as bass
import concourse.mybir as mybir
from concourse.bass_utils import run_bass_kernel_spmd

F32 = mybir.dt.float32
BF16 = mybir.dt.bfloat16
AF = mybir.ActivationFunctionType
ALU = mybir.AluOpType

S = 2048
D = 1024
NT = 16
DFF = 4096
EPS = 1e-6
NCORES = 8
PI = math.pi

SBUF_WORDS = 52800


class Sched:
    ENGS = ("pe", "act", "dve", "pool", "sp")

    def __init__(self):
        self.ops = []
        self.lastw = {}
        self.readers = {}

    def op(self, eng, fn, r=(), w=(), slot=None):
        idx = len(self.ops)
        deps = set()
        if getattr(self, "fence", None):
            for k in w:
                if k not in self.known:
                    deps.update(self.fence)
                    self.known.add(k)
        for k in r:
            if k in self.lastw:
                deps.add(self.lastw[k])
        for k in w:
            if k in self.lastw:
                deps.add(self.lastw[k])
            deps.update(self.readers.get(k, ()))
        for k in r:
            self.readers.setdefault(k, []).append(idx)
        for k in w:
            self.lastw[k] = idx
            self.readers[k] = []
        deps.discard(idx)
        self.ops.append(dict(eng=eng, fn=fn, deps=deps, slot=slot, sem=None, val=None, sig=False))
        return idx

    def fence_all(self):
        last = {}
        for i, o in enumerate(self.ops):
            last[(o["eng"], o["slot"])] = i
        self.fence = set(last.values())
        self.known = set(self.lastw.keys()) | set(self.readers.keys())

    def retire(self, old_keys, new_keys):
        acc = set()
        for k in old_keys:
            if k in self.lastw:
                acc.add(self.lastw[k])
            acc.update(self.readers.get(k, ()))
        for k in new_keys:
            self.readers.setdefault(k, []).extend(acc)

    def emit(self, nc, stack, same_engine_sync=("act", "dve", "pool")):
        ops = self.ops
        for i, o in enumerate(ops):
            for d in o["deps"]:
                y = ops[d]
                if y["slot"] is not None:
                    continue
                if y["eng"] == o["eng"] and y["eng"] not in same_engine_sync:
                    continue
                y["sig"] = True
        SEM_MAX = 30000
        eng_sems = {}
        counters = {}
        for e in ("pe", "act", "dve", "pool"):
            eng_sems[e] = []
            counters[e] = SEM_MAX
        slot_sem = {}
        slot_list = {}
        for i, o in enumerate(ops):
            if o["slot"] is not None:
                sl = o["slot"]
                if sl not in slot_sem:
                    slot_sem[sl] = stack.enter_context(nc.semaphore("d_" + str(len(slot_sem))))
                    slot_list[sl] = []
                slot_list[sl].append(i)
                o["sem"] = slot_sem[sl]
                o["val"] = 16 * len(slot_list[sl])
            elif o["sig"]:
                e = o["eng"]
                if counters[e] >= SEM_MAX:
                    eng_sems[e].append(stack.enter_context(nc.semaphore("e_%s_%d" % (e, len(eng_sems[e])))))
                    counters[e] = 0
                counters[e] += 1
                o["sem"] = eng_sems[e][-1]
                o["val"] = counters[e]
        self.nsem = len(slot_sem) + sum(len(v) for v in eng_sems.values())
        block = stack.enter_context(nc.Block())
        per_eng = {e: [] for e in self.ENGS}
        for i, o in enumerate(ops):
            per_eng[o["eng"]].append(i)

        def run(engname, eng):
            waited = {}
            for i in per_eng[engname]:
                o = ops[i]
                need = {}
                for d in o["deps"]:
                    y = ops[d]
                    if y["slot"] is not None:
                        lst = slot_list[y["slot"]]
                        pos = bisect.bisect_left(lst, i)
                        sem, val = y["sem"], 16 * pos
                    else:
                        if y["eng"] == engname and engname not in same_engine_sync:
                            continue
                        sem, val = y["sem"], y["val"]
                    key = id(sem)
                    if key not in need or need[key][1] < val:
                        need[key] = (sem, val)
                for key, (sem, val) in need.items():
                    if waited.get(key, 0) >= val:
                        continue
                    eng.wait_ge(sem, val)
                    waited[key] = val
                ins = o["fn"](eng)
                if o["slot"] is not None:
                    ins.then_inc(o["sem"], 16)
                elif o["sig"]:
                    ins.then_inc(o["sem"], 1)

        @block.tensor
        def _(e):
            run("pe", e)

        @block.scalar
        def _(e):
            run("act", e)

        @block.vector
        def _(e):
            run("dve", e)

        @block.gpsimd
        def _(e):
            run("pool", e)

        @block.sync
        def _(e):
            run("sp", e)


_CONST_CACHE = {}


def _dft_tables():
    if "dft" in _CONST_CACHE:
        return _CONST_CACHE["dft"]
    n = np.arange(S, dtype=np.int64)
    prod = (n[:, None] * n[None, :]) % 4096
    ang = prod.astype(np.float64) * (2.0 * np.pi / 4096.0)
    C = np.cos(ang)
    Sm = np.sin(ang)
    sgn = np.where(n % 2 == 0, 1.0, -1.0)
    Sp = Sm.copy()
    Sp[:, 0] = sgn
    SpT = Sp.T.copy()

    def panelize(M):
        return M.reshape(16, 128, 16, 128).transpose(2, 1, 0, 3)

    fwd = np.stack([panelize(C), panelize(Sp)], axis=2)
    inv = np.stack([panelize(C), panelize(SpT)], axis=2)
    fwd = np.ascontiguousarray(fwd).astype(ml_dtypes.bfloat16)
    inv = np.ascontiguousarray(inv).astype(ml_dtypes.bfloat16)
    kk = np.arange(16)[:, None]
    pp = np.arange(128)[None, :]
    sperm = (2 * (128 * (kk % 8) + pp) + (kk // 8)).reshape(-1)
    f1 = np.arange(1024, dtype=np.int64)
    angk = ((sperm[:, None].astype(np.int64) * f1[None, :]) % 4096).astype(np.float64) * (2.0 * np.pi / 4096.0)

    def panelize_k(M):
        return M.reshape(16, 128, 8, 128).transpose(2, 1, 0, 3)

    dk = np.stack([panelize_k(np.cos(angk)), panelize_k(np.sin(angk))], axis=2)
    dk = np.ascontiguousarray(dk).astype(ml_dtypes.bfloat16)
    _CONST_CACHE["dft"] = (fwd, inv, dk)
    return fwd, inv, dk


def _zfeat():
    L = S
    t = np.linspace(0.0, 1.0, L, dtype=np.float32)[:, None]
    w = (np.float32(2.0 * math.pi / L) * np.arange(L, dtype=np.float32))[:, None]
    f = np.linspace(1e-4, 15.0, 16, dtype=np.float32)[None, :]
    z = np.concatenate([t, np.cos(f * w), -np.sin(f * w)], axis=-1).astype(np.float32)
    return z


def _attn_consts():
    a = np.zeros((128, 4736), np.float32)
    kl = np.arange(128)[:, None]
    ql = np.arange(128)[None, :]
    a[:, 0:128] = np.where(kl <= ql, 0.0, -30000.0)
    a[:, 128:256] = 0.0
    a[:, 256:384] = np.where(ql <= kl, 0.0, -30000.0)
    pr = np.zeros((128, 128), np.float32)
    for hb in (0, 64):
        for i in range(8):
            pr[hb + 8 + i, hb + i] = 1.0
            pr[hb + i, hb + 8 + i] = 1.0
    a[:, 384:512] = pr
    ph = np.zeros((128, 128), np.float32)
    for m in range(128):
        ph[(m + 64) % 128, m] = 1.0
    a[:, 512:640] = ph
    inv = (500000.0 ** (-np.arange(0, 16, 2, dtype=np.float32) / 16.0)).astype(np.float32)
    ang = np.arange(S, dtype=np.float32)[None, :] * inv[:, None]
    cos = np.ones((128, S), np.float32)
    sin = np.zeros((128, S), np.float32)
    for hb in (0, 64):
        cos[hb:hb + 8] = np.cos(ang); cos[hb + 8:hb + 16] = np.cos(ang)
        sin[hb:hb + 8] = -np.sin(ang); sin[hb + 8:hb + 16] = np.sin(ang)
    a[:, 640:2688] = cos
    a[:, 2688:4736] = sin
    return a.astype(ml_dtypes.bfloat16)


def _absdelta():
    max_decay = math.log(1e-2) / 0.3
    min_decay = math.log(1e-2) / 1.5
    deltas = np.linspace(min_decay, max_decay, D, dtype=np.float32)
    return np.abs(deltas).astype(np.float32)


CF_G = 0
CF_CW = 64
CF_FB = 160
CF_TNEG = 164
CF_FW1 = 180
CF_FW2 = 244
CF_FW3 = 308
CF_FBF = 372
CF_N = 384

CB_ID = 0
CB_ONES = 128
CB_JREV = 256
CB_E00 = 384
CB_SGN = 512
CB_N = 640


def _host_consts(inp):
    cf = np.zeros((128, CF_N), np.float32)
    gl = ["norm_mix_pre", "norm_mix_post", "norm_mlp_pre", "norm_mlp_post"]
    for layer in range(2):
        for gi, gname in enumerate(gl):
            v = np.asarray(inp[gname])[layer]
            cf[:, CF_G + (layer * 4 + gi) * 8: CF_G + (layer * 4 + gi) * 8 + 8] = v.reshape(8, 128).T
    cw = np.asarray(inp["hy_conv_w"])[0]
    cbias = np.asarray(inp["hy_conv_b"])[0]
    for k in range(3):
        cf[:, CF_CW + 24 * k: CF_CW + 24 * k + 24] = cw[k].reshape(24, 128).T
    cf[:, CF_CW + 72: CF_CW + 96] = cbias.reshape(24, 128).T
    cf[0:64, CF_FB + 0] = np.asarray(inp["hy_f_b1"])[0]
    cf[0:64, CF_FB + 1] = np.asarray(inp["hy_f_b2"])[0]
    cf[0:64, CF_FB + 2] = np.asarray(inp["hy_f_b3"])[0]
    cf[0:64, CF_FB + 3] = np.asarray(inp["hy_f_freq"])[0]
    t = np.linspace(0.0, 1.0, S, dtype=np.float32)
    for k in range(16):
        sidx = 2 * (128 * (k % 8) + np.arange(128)) + (k // 8)
        cf[:, CF_TNEG + k] = -t[sidx]
    cf[0:33, CF_FW1: CF_FW1 + 64] = np.asarray(inp["hy_f_w1"])[0]
    cf[0:64, CF_FW2: CF_FW2 + 64] = np.asarray(inp["hy_f_w2"])[0]
    cf[0:64, CF_FW3: CF_FW3 + 64] = np.asarray(inp["hy_f_w3"])[0]
    cb = np.zeros((128, CB_N), np.float32)
    cb[:, CB_ID:CB_ID + 128] = np.eye(128)
    cb[:, CB_ONES:CB_ONES + 128] = 1.0
    for q in range(1, 128):
        cb[128 - q, CB_JREV + q] = 1.0
    cb[0, CB_E00] = 1.0
    cb[:, CB_SGN] = np.where(np.arange(128) % 2 == 0, 1.0, -1.0)
    cb = cb.astype(ml_dtypes.bfloat16)
    zf = np.zeros((33, S), np.float32)
    zf[:, :] = _zfeat().T
    return cf, cb, zf


def build_program(stage=4):
    from contextlib import ExitStack
    nc = bass.Bass("TRN2", target_bir_lowering=False)
    dt = nc.dram_tensor
    xT = dt("xT", [D, S], F32, kind="ExternalInput").ap()
    cf_d = dt("cf", [128, CF_N], F32, kind="ExternalInput").ap()
    cb_d = dt("cb", [128, CB_N], BF16, kind="ExternalInput").ap()
    zf_d = dt("zf", [33, S], F32, kind="ExternalInput").ap()
    adl_d = dt("adl", [1, D], F32, kind="ExternalInput").ap()
    hyb_d = dt("hyb", [1, 2 * D], F32, kind="ExternalInput").ap()
    dftF = dt("dftF", [16, 128, 2, 16, 128], BF16, kind="ExternalInput").ap()
    dftI = dt("dftI", [16, 128, 2, 16, 128], BF16, kind="ExternalInput").ap()
    dftK = dt("dftK", [8, 128, 2, 16, 128], BF16, kind="ExternalInput").ap()
    w_in_d = dt("hy_w_in", [D, 3 * D], F32, kind="ExternalInput").ap()
    wout_f_d = dt("hy_f_wout", [64, 4 * D], F32, kind="ExternalInput").ap()
    w_out_d = dt("hy_w_out", [D, D], F32, kind="ExternalInput").ap()
    w_up_d = dt("w_up", [2, D, DFF], F32, kind="ExternalInput").ap()
    w_down_d = dt("w_down", [2, DFF, D], F32, kind="ExternalInput").ap()
    wqkv_d = dt("at_w_qkv", [D, 1536], F32, kind="ExternalInput").ap()
    wo2_d = dt("at_w_o", [D, D], F32, kind="ExternalInput").ap()
    sink_d = dt("sink", [1, 16], F32, kind="ExternalInput").ap()
    atc_d = dt("atc", [128, 4736], BF16, kind="ExternalInput").ap()
    kco_d = dt("kco", [2, 16, 128, 2, D], BF16, kind="Internal").ap()
    outT = dt("outT", [D, S], F32, kind="ExternalOutput").ap()

    stack = ExitStack()
    big = stack.enter_context(nc.sbuf_tensor("big", [128, SBUF_WORDS], F32))
    PS = stack.enter_context(nc.psum_tensor("PS", [128, 8, 512], F32))
    sch = Sched()

    def V(off, words, dtype=F32, pat=None, **kw):
        ap = big[:, off:off + words]
        if dtype != F32:
            ap = ap.bitcast(dtype)
        if pat is not None:
            ap = ap.rearrange(pat, **kw)
        return ap

    def psb(b):
        return PS[:, b, :]

    def psb16(b):
        return PS[:, b, :].bitcast(BF16)

    o = 0
    O_CF = o; o += CF_N
    O_CB = o; o += CB_N // 2
    O_RSTD = o; o += 2 * 512
    O_LN = o; o += 512
    O_SQ = o; o += 2048
    O_KNY = o; o += 1024
    P_H = o; o += 16384
    P_Z2T = o; o += 8192
    P_T0 = o; o += 4096
    P_T1 = o; o += 4096
    P_RAW = o; o += 2052
    P_TT = o; o += 2048
    P_TMP = o; o += 2048
    P_KST = o; o += 1024
    P_WIN = o; o += 2048
    P_PAN = o; o += 4096
    P_X = o; o += 1024
    assert o <= SBUF_WORDS, o

    CF = V(O_CF, CF_N)
    CB = V(O_CB, CB_N // 2, BF16)
    IDENT = CB[:, CB_ID:CB_ID + 128]
    ONES = CB[:, CB_ONES:CB_ONES + 128]
    JREV = CB[:, CB_JREV:CB_JREV + 128]
    E00 = CB[:, CB_E00:CB_E00 + 128]
    SGNC = CB[:, CB_SGN:CB_SGN + 1]
    RSTD = [V(O_RSTD + 512 * i, 512) for i in range(2)]
    LNB = V(O_LN, 512)
    SQ = V(O_SQ, 2048, BF16, "p (j s) -> p j s", j=8)
    KNY = V(O_KNY, 1024, BF16)

    def gcol(layer, gi, j):
        c = CF_G + (layer * 4 + gi) * 8 + j
        return CF[:, c:c + 1]

    PAN = [V(P_PAN + 2048 * i, 2048, BF16, "p (a k j) -> p a k j", a=2, k=16) for i in range(2)]

    sch.op("sp", lambda e: e.dma_start(out=CF, in_=cf_d), w=["CF"], slot="cf")
    sch.op("sp", lambda e: e.dma_start(out=CB, in_=cb_d), w=["CB"], slot="cb")

    F0 = P_Z2T
    ZF = V(F0, 2048)
    HB = [V(F0 + 2048, 2048), V(F0 + 4096, 2048)]
    H3 = V(F0 + 6144, 1024, BF16)
    WF16 = V(F0 + 7168, 1024, BF16)
    WOF = V(F0 + 8192, 4096)
    WSUM = V(F0 + 12288, 1024, BF16)
    WDIF = V(F0 + 13312, 1024, BF16)
    ARG = V(F0 + 14336, 512)
    ADL = V(F0 + 14848, 1024)
    DEC = V(F0 + 15872, 1024)
    HYB = V(F0 + 16896, 2048)
    TROW = V(F0 + 18944, 512)
    ARG2 = V(F0 + 19456, 512)
    assert F0 + 19968 <= P_TMP
    sch.op("sp", lambda e: e.dma_start(out=ZF[0:33, :], in_=zf_d), w=["ZF"], slot="zf")
    sch.op("sp", lambda e: e.dma_start(out=WOF[0:64, :], in_=wout_f_d), w=["WOF"], slot="wof")
    sch.op("sp", lambda e: e.dma_start(out=ADL, in_=adl_d.partition_broadcast(128)), w=["ADL"], slot="adl")
    sch.op("sp", lambda e: e.dma_start(out=HYB[0:1, :], in_=hyb_d), w=["HYB"], slot="hyb")

    for l in range(3):
        sch.op("dve", lambda e, l=l: e.tensor_tensor(out=CF[0:64, CF_FBF + l:CF_FBF + l + 1],
                                                     in0=CF[0:64, CF_FB + l:CF_FB + l + 1],
                                                     in1=CF[0:64, CF_FB + 3:CF_FB + 4], op=ALU.mult),
               r=["CF"], w=[("fbf", l)])
    sch.op("dve", lambda e: e.tensor_tensor(out=WSUM[0:64, :], in0=WOF[0:64, 0:2048], in1=WOF[0:64, 2048:4096], op=ALU.add),
           r=["WOF"], w=["WSUM"])
    sch.op("dve", lambda e: e.tensor_tensor(out=WDIF[0:64, :], in0=WOF[0:64, 2048:4096], in1=WOF[0:64, 0:2048], op=ALU.subtract),
           r=["WOF"], w=["WDIF"])
    sch.op("dve", lambda e: e.tensor_copy(out=WF16[0:64, :], in_=WOF[0:64, 0:2048]), r=["WOF"], w=["WF16"])

    FWOFF = [CF_FW1, CF_FW2, CF_FW3]
    FK = [33, 64, 64]
    fcnt = 0
    for l in range(3):
        src = ZF if l == 0 else HB[(l - 1) % 2]
        srckey = "ZF" if l == 0 else ("HB", (l - 1) % 2)
        for sc in range(4):
            bank = 5 + (fcnt % 2)
            fcnt += 1
            sch.op("pe", lambda e, l=l, sc=sc, bank=bank, src=src: e.matmul(
                psb(bank)[0:64, :], CF[0:FK[l], FWOFF[l]:FWOFF[l] + 64], src[0:FK[l], sc * 512:(sc + 1) * 512],
                start=True, stop=True), r=["CF", (srckey, sc) if l else "ZF"], w=[("ps", bank)])
            sch.op("act", lambda e, l=l, bank=bank: e.activation(
                out=ARG[0:64, :], in_=psb(bank)[0:64, :], func=AF.Identity,
                scale=CF[0:64, CF_FB + 3:CF_FB + 4], bias=CF[0:64, CF_FBF + l:CF_FBF + l + 1]),
                r=[("ps", bank), ("fbf", l), "CF"], w=["ARG"])
            sch.op("dve", lambda e: e.tensor_scalar(out=ARG2[0:64, :], in0=ARG[0:64, :], scalar1=PI, scalar2=2 * PI,
                                                    op0=ALU.is_gt, op1=ALU.mult), r=["ARG"], w=["ARG2"])
            sch.op("dve", lambda e: e.tensor_tensor(out=ARG[0:64, :], in0=ARG[0:64, :], in1=ARG2[0:64, :], op=ALU.subtract),
                   r=["ARG", "ARG2"], w=["ARG"])
            sch.op("dve", lambda e: e.tensor_scalar(out=ARG2[0:64, :], in0=ARG[0:64, :], scalar1=-PI, scalar2=2 * PI,
                                                    op0=ALU.is_lt, op1=ALU.mult), r=["ARG"], w=["ARG2"])
            sch.op("dve", lambda e: e.tensor_tensor(out=ARG[0:64, :], in0=ARG[0:64, :], in1=ARG2[0:64, :], op=ALU.add),
                   r=["ARG", "ARG2"], w=["ARG"])
            if l < 2:
                dst = HB[l % 2][0:64, sc * 512:(sc + 1) * 512]
                dkey = (("HB", l % 2), sc)
            else:
                dst = H3[0:64, sc * 512:(sc + 1) * 512]
                dkey = ("H3", sc)
            sch.op("act", lambda e, dst=dst: e.activation(out=dst, in_=ARG[0:64, :], func=AF.Sin),
                   r=["ARG"], w=[dkey])


    ABv = [[V(P_H + 8192 * sl + 4096 * w_, 4096, BF16, "p (t c) -> p t c", t=16) for w_ in range(2)] for sl in range(2)]
    kcnt = [0]
    panel_seq = []
    for _p in range(4):
        panel_seq += [(dftK, m) for m in range(7, -1, -1)]
    for _h in range(2):
        for _c in range(2):
            panel_seq += [(dftF, m) for m in range(NT)]
            panel_seq += [(dftI, m) for m in range(NT)]
    pst = {"use": 0, "issued": 0}

    def _issue_panel():
        i = pst["issued"]
        if i >= len(panel_seq):
            return
        src_d, m = panel_seq[i]
        sl = i % 2
        pst["issued"] += 1
        sch.op("sp", lambda e, sl=sl, m=m, src_d=src_d: e.dma_start(out=PAN[sl], in_=src_d[m]), w=[("PAN", sl)], slot=("pan", sl))

    def load_panel(src_d, m):
        i = pst["use"]
        assert panel_seq[i][1] == m
        while pst["issued"] <= i:
            _issue_panel()
        if i == 0:
            _issue_panel()
        pst["use"] += 1
        return i % 2

    def filt_steps(pss, st):
        return [lambda: filt_gen_tile(pss, st, 0), lambda: filt_gen_tile(pss, st, 1)]

    def filt_gen_tile(pss, st, only=None):
        od, hf = pss // 2, pss % 2
        sl = pss % 2
        A_, B_ = ABv[sl]
        c0 = od * 1024 + hf * 512
        if only in (None, 0):
            dsl = st % 2
            sch.op("act", lambda e, st=st, hf=hf, dsl=dsl: e.activation(out=DEC[:, 512 * dsl:512 * dsl + 512], in_=ADL[:, hf * 512:(hf + 1) * 512], func=AF.Exp,
                                                                       scale=CF[:, CF_TNEG + st:CF_TNEG + st + 1]),
                   r=["ADL", "CF"], w=[("DEC", dsl)])
        dsl = st % 2
        for which in ((0, 1) if only is None else (only,)):
            bank = 5 + (kcnt[0] % 2)
            kcnt[0] += 1
            wsrc = WSUM if which == 0 else WDIF
            dst = (A_, B_)[which]
            sch.op("pe", lambda e, st=st, bank=bank, wsrc=wsrc, c0=c0: e.matmul(
                psb(bank), H3[0:64, 256 * (st % 8) + st // 8:256 * (st % 8) + st // 8 + 255:2], wsrc[0:64, c0:c0 + 512],
                start=True, stop=True), r=[("H3", (st % 8) // 2), "WSUM", "WDIF"], w=[("ps", bank)])
            sch.op("dve", lambda e, st=st, bank=bank, dst=dst, dsl=dsl: e.tensor_tensor(
                out=dst[:, st, :], in0=psb(bank), in1=DEC[:, 512 * dsl:512 * dsl + 512], op=ALU.mult),
                r=[("ps", bank), ("DEC", dsl)], w=[("AB", sl, which, st)])
        if st == 0 and only in (None, 1):
            bank = 5 + (kcnt[0] % 2)
            kcnt[0] += 1
            sch.op("pe", lambda e, bank=bank, c0=c0: e.matmul(
                psb(bank)[0:1, :], H3[0:64, 0:1], WF16[0:64, c0:c0 + 512], start=True, stop=True),
                r=[("H3", 0), "WF16"], w=[("ps", bank)])
            sch.op("dve", lambda e, bank=bank, c0=c0: e.tensor_tensor(
                out=TROW[0:1, :], in0=psb(bank)[0:1, :], in1=HYB[0:1, c0:c0 + 512], op=ALU.add),
                r=[("ps", bank), "HYB"], w=["TROW"])
            sch.op("dve", lambda e, A_=A_: e.tensor_copy(out=A_[0:1, 0, :], in_=TROW[0:1, :]),
                   r=["TROW"], w=[("AB", sl, 0, 0)])
            sch.op("dve", lambda e, B_=B_: e.tensor_scalar(
                out=B_[0:1, 0, :], in0=TROW[0:1, :], scalar1=-1.0, scalar2=None, op0=ALU.mult),
                r=["TROW"], w=[("AB", sl, 1, 0)])

    KB0 = P_TMP
    STGP = [V(KB0 + 512 * i, 512, BF16, "p (a c) -> p a c", a=2) for i in range(2)]
    STGM = [V(KB0 + 1024 + 512 * i, 512, BF16, "p (a c) -> p a c", a=2) for i in range(3)]
    STGR = [V(KB0 + 2560 + 512 * i, 512, BF16, "p (a c) -> p a c", a=2) for i in range(2)]
    OSB = [V(KB0 + 3584 + 512 * i, 512) for i in range(2)]
    SPEC = V(KB0 + 4608, 512, BF16, "p (a c) -> p a c", a=2)
    assert KB0 + 5120 <= P_PAN
    S11 = 2.0 ** -11
    for st in range(NT):
        filt_gen_tile(0, st)
    sch.op("dve", lambda e: e.memset(SPEC, 0.0), w=["SPEC"])
    gcount = [0]
    for pss in range(4):
        od, hf = pss // 2, pss % 2
        sl = pss % 2
        A_, B_ = ABv[sl]
        c0 = od * 1024 + hf * 512
        for a_, (src_, k0) in enumerate(((A_, 0), (B_, 8))):
            bk = 4 if a_ == 0 else 7
            for k in range(8):
                sch.op("pe", lambda e, k=k, k0=k0, src_=src_, bk=bk: e.matmul(
                    psb(bk)[0:1, :], SGNC, src_[:, k0 + k, :], start=(k == 0), stop=(k == 7)),
                    r=["CB", ("AB", sl, a_, k0 + k)], w=[("ps", bk)])
            sch.op("act", lambda e, a_=a_, bk=bk: e.activation(out=SPEC[0:1, a_, :], in_=psb(bk)[0:1, :], func=AF.Copy, scale=S11),
                   r=[("ps", bk)], w=["SPEC"])
        prev_m = None
        pending = None

        def emit_rev(pend):
            m_, ss_, sm_, prev_t, prev_keys = pend
            for a_ in range(2):
                bk = 4 if a_ == 0 else 7
                sch.op("pe", lambda e, a_=a_, bk=bk: e.matmul(psb(bk), JREV, STGM[sm_][:, a_, :], start=True, stop=False),
                       r=["CB", ("STGM", sm_, a_)], w=[("ps", bk)])
                sch.op("pe", lambda e, a_=a_, bk=bk: e.matmul(psb(bk), E00, prev_t[:, a_, :], start=False, stop=True),
                       r=["CB"] + prev_keys, w=[("ps", bk)])
                sch.op("act", lambda e, a_=a_, bk=bk: e.activation(out=STGR[ss_][:, a_, :], in_=psb(bk), func=AF.Copy),
                       r=[("ps", bk)], w=[("STGR", ss_, a_)])
            sch.op("act", lambda e, od_=od, hf_=hf: e.dma_start(out=kco_d[od_, 15 - m_, :, :, hf_ * 512:(hf_ + 1) * 512], in_=STGR[ss_]),
                   r=[("STGR", ss_, 0), ("STGR", ss_, 1)], w=[("kco", od, 15 - m_, hf)], slot=("kst_outr", ss_))

        fq = []
        if pss + 1 < 4:
            for st_ in range(NT):
                fq += filt_steps(pss + 1, st_)
        for m in range(7, -1, -1):
            psl = load_panel(dftK, m)
            g = gcount[0]
            gcount[0] += 1
            ss = g % 2
            sm = g % 3
            for a_, src_ in enumerate((A_, B_)):
                be, bo = 2 * a_, 2 * a_ + 1
                for k in range(8):
                    sch.op("pe", lambda e, k=k, psl=psl, be=be, a_=a_, src_=src_: e.matmul(
                        psb(be), PAN[psl][:, a_, k, :], src_[:, k, :], start=(k == 0), stop=(k == 7)),
                        r=[("PAN", psl), ("AB", sl, a_, k)], w=[("ps", be)])
                if fq:
                    fq.pop(0)()
                for k in range(8, 16):
                    sch.op("pe", lambda e, k=k, psl=psl, bo=bo, a_=a_, src_=src_: e.matmul(
                        psb(bo), PAN[psl][:, a_, k, :], src_[:, k, :], start=(k == 8), stop=(k == 15)),
                        r=[("PAN", psl), ("AB", sl, a_, k)], w=[("ps", bo)])
                if fq:
                    fq.pop(0)()
                if a_ == 1:
                    _issue_panel()
                    if pending is not None:
                        emit_rev(pending)
                        pending = None
                sch.op("act", lambda e, bo=bo, a_=a_: e.activation(out=OSB[a_], in_=psb(bo), func=AF.Copy, scale=S11),
                       r=[("ps", bo)], w=[("OSB", a_)])
                sch.op("dve", lambda e, be=be, a_=a_, ss=ss: e.scalar_tensor_tensor(
                    out=STGP[ss][:, a_, :], in0=psb(be), scalar=S11, in1=OSB[a_], op0=ALU.mult, op1=ALU.add),
                    r=[("ps", be), ("OSB", a_)], w=[("STGP", ss, a_)])
                if a_ == 0:
                    sch.op("dve", lambda e, be=be, sm=sm: e.scalar_tensor_tensor(
                        out=STGM[sm][:, 0, :], in0=psb(be), scalar=S11, in1=OSB[0], op0=ALU.mult, op1=ALU.subtract),
                        r=[("ps", be), ("OSB", 0)], w=[("STGM", sm, 0)])
                else:
                    sch.op("dve", lambda e, be=be, sm=sm: e.scalar_tensor_tensor(
                        out=STGM[sm][:, 1, :], in0=psb(be), scalar=-S11, in1=OSB[1], op0=ALU.mult, op1=ALU.add),
                        r=[("ps", be), ("OSB", 1)], w=[("STGM", sm, 1)])
            if m == 0:
                sch.op("dve", lambda e, ss=ss: e.tensor_scalar(out=STGP[ss][0:1, 0, :], in0=STGP[ss][0:1, 0, :],
                                                               scalar1=0.5, scalar2=None, op0=ALU.mult),
                       r=[("STGP", ss, 0)], w=[("STGP", ss, 0)])
                sch.op("dve", lambda e, ss=ss: e.memset(STGP[ss][0:1, 1, :], 0.0), w=[("STGP", ss, 1)])
                sch.op("dve", lambda e, sm=sm, c0=c0: e.tensor_scalar(out=KNY[0:1, c0:c0 + 512], in0=STGM[sm][0:1, 0, :],
                                                                      scalar1=0.5, scalar2=None, op0=ALU.mult),
                       r=[("STGM", sm, 0)], w=[("KNY", pss)])
            sch.op("act", lambda e, ss=ss, od=od, m=m, hf=hf: e.dma_start(
                out=kco_d[od, m, :, :, hf * 512:(hf + 1) * 512], in_=STGP[ss]),
                r=[("STGP", ss, 0), ("STGP", ss, 1)], w=[("kco", od, m, hf)], slot=("kst_out", ss))
            prev_t = SPEC if prev_m is None else STGM[prev_m]
            prev_keys = ["SPEC"] if prev_m is None else [("STGM", prev_m, 0), ("STGM", prev_m, 1)]
            pending = (m, ss, sm, prev_t, prev_keys)
            prev_m = sm
        while fq:
            fq.pop(0)()
        emit_rev(pending)

    if stage == 0:
        sch.op("sp", lambda e: e.dma_start(out=outT[0:128, 0:1024].bitcast(BF16).rearrange("p (a c) -> p a c", a=2),
                                           in_=kco_d[0, 1, :, :, :]), r=[("kco", 0, 1, 0), ("kco", 0, 1, 1)], w=["out"], slot="out")
        sch.op("sp", lambda e: e.dma_start(out=outT[128:256, 0:1024].bitcast(BF16).rearrange("p (a c) -> p a c", a=2),
                                           in_=kco_d[1, 0, :, :, :]), r=[("kco", 1, 0, 0), ("kco", 1, 0, 1)], w=["out2"], slot="out")
        sch.op("sp", lambda e: e.dma_start(out=outT[256:257, 0:16], in_=outT[257:258, 0:16]), r=["out", "out2"], slot="fin")
        sch.emit(nc, stack)
        return nc, stack


    sch.fence_all()
    HNT = V(P_H, 8192, BF16, "p (j s) -> p j s", j=8)
    YA = V(P_H + 8192, 4096, BF16, "p (t c) -> p t c", t=16)
    YB = V(P_H + 12288, 4096, BF16, "p (t c) -> p t c", t=16)
    Z2T = V(P_Z2T, 8192, BF16, "p (j s) -> p j s", j=8)
    T0 = V(P_T0, 4096, BF16, "p (t c) -> p t c", t=16)
    T1 = V(P_T1, 4096, BF16, "p (t c) -> p t c", t=16)
    RAW = V(P_RAW, 2052)
    TT = V(P_TT, 2048)
    TMP = [V(P_TMP + 512 * i, 512) for i in range(4)]
    KST = [V(P_KST + 512 * i, 512, BF16, "p (a c) -> p a c", a=2) for i in range(2)]
    WIN = [V(P_WIN + 1024 * i, 1024, BF16, "p (j c) -> p j c", j=8) for i in range(2)]
    XC = [V(P_T0 + 4096 * i, 4096, F32, "p (j s) -> p j s", j=8) for i in range(2)]
    H = V(P_H, 16384, F32, "p (j s) -> p j s", j=8)
    MB = V(P_PAN, 4096, F32, "p (j s) -> p j s", j=8)
    xT_v = xT.rearrange("(j p) s -> p j s", p=128)
    outT_v = outT.rearrange("(j p) s -> p j s", p=128)
    cnt = {"mb": 0, "rs": 0, "win": 0, "kst": 0, "pq": 0, "py": 0}

    def mbank():
        b = 6 + (cnt["mb"] % 2)
        cnt["mb"] += 1
        return b

    def norm_chunk(src, src_keys, layer, gi, dst_of_j, dst_keys_of_j):
        rs = cnt["rs"] % 2
        cnt["rs"] += 1
        sch.op("act", lambda e: e.activation(out=SQ, in_=src, func=AF.Square), r=src_keys, w=[("SQj", j) for j in range(8)])
        bank = mbank()
        for j in range(8):
            sch.op("pe", lambda e, j=j, bank=bank: e.matmul(psb(bank), ONES, SQ[:, j, :], start=(j == 0), stop=(j == 7)),
                   r=[("SQj", j), "CB"], w=[("ps", bank)])
        sch.op("act", lambda e, bank=bank: e.activation(out=LNB, in_=psb(bank), func=AF.Ln, scale=1.0 / D, bias=EPSC),
               r=[("ps", bank), "EPSC"], w=["LNB"])
        sch.op("act", lambda e, rs=rs: e.activation(out=RSTD[rs], in_=LNB, func=AF.Exp, scale=-0.5), r=["LNB"], w=[("RSTD", rs)])
        for j in range(8):
            sch.op("dve", lambda e, j=j, rs=rs: e.scalar_tensor_tensor(
                out=dst_of_j(j), in0=src[:, j, :], scalar=gcol(layer, gi, j), in1=RSTD[rs], op0=ALU.mult, op1=ALU.mult),
                r=src_keys + [("RSTD", rs), "CF"], w=dst_keys_of_j(j))

    def branch_finish(sc, m_keys_ready, layer, gi, res_src, res_keys, tok0):
        rs = cnt["rs"] % 2
        cnt["rs"] += 1
        bank = mbank()
        for j in range(8):
            sch.op("pe", lambda e, j=j, bank=bank: e.matmul(psb(bank), ONES, SQ[:, j, :], start=(j == 0), stop=(j == 7)),
                   r=[("SQj", j), "CB"], w=[("ps", bank)])
        sch.op("act", lambda e, bank=bank: e.activation(out=LNB, in_=psb(bank), func=AF.Ln, scale=1.0 / D, bias=EPSC),
               r=[("ps", bank), "EPSC"], w=["LNB"])
        sch.op("act", lambda e, rs=rs: e.activation(out=RSTD[rs], in_=LNB, func=AF.Exp, scale=-0.5), r=["LNB"], w=[("RSTD", rs)])
        for j in range(8):
            sch.op("dve", lambda e, j=j, rs=rs: e.scalar_tensor_tensor(
                out=MB[:, j, :], in0=MB[:, j, :], scalar=gcol(layer, gi, j), in1=RSTD[rs], op0=ALU.mult, op1=ALU.mult),
                r=[("MB", j), ("RSTD", rs), "CF"], w=[("MB", j)])
            sch.op("dve", lambda e, j=j: e.tensor_tensor(
                out=H[:, j, tok0:tok0 + 512], in0=res_src(j), in1=MB[:, j, :], op=ALU.add),
                r=[("MB", j)] + res_keys(j), w=[("H", j, tok0 // 512)])

    def evac_m(bank, j):
        sch.op("act", lambda e: e.activation(out=MB[:, j, :], in_=psb(bank), func=AF.Copy), r=[("ps", bank)], w=[("MB", j)])
        sch.op("act", lambda e: e.activation(out=SQ[:, j, :], in_=psb(bank), func=AF.Square), r=[("ps", bank)], w=[("SQj", j)])

    EPSC = V(O_LN + 0, 512)[:, 0:1] if False else CF[:, CF_FBF + 3:CF_FBF + 4]
    sch.op("dve", lambda e: e.memset(EPSC, EPS), r=["CF"], w=["EPSC"])

    for sc in range(4):
        xs = sc % 2
        sch.op("sp", lambda e, sc=sc, xs=xs: e.dma_start(out=XC[xs], in_=xT_v[:, :, sc * 512:(sc + 1) * 512]),
               w=[("XC", xs)], slot=("xc", xs))
        norm_chunk(XC[xs], [("XC", xs)], 0, 0,
                   lambda j, sc=sc: HNT[:, j, sc * 512:(sc + 1) * 512], lambda j, sc=sc: [("HNT", sc)])

    sch.op("dve", lambda e: e.memset(RAW[:, 0:1], 0.0), w=["RAWpad"])
    sch.op("dve", lambda e: e.memset(RAW[:, 2049:2050], 0.0), w=["RAWpad2"])
    w_in_v = w_in_d.rearrange("(j p) n -> p j n", p=128)

    def inproj(strm, hf):
        c30 = strm * 1024 + hf * 512
        for sb in range(2):
            ws = cnt["win"] % 2
            cnt["win"] += 1
            sch.op("pool", lambda e, ws=ws, c=c30 + 256 * sb: e.dma_start(out=WIN[ws], in_=w_in_v[:, :, c:c + 256]),
                   w=[("WIN", ws)], slot=("win", ws))
            for q2 in range(2):
                q4 = sb * 2 + q2
                q = c30 // 128 + q4
                for sc in range(4):
                    bank = mbank()
                    for j in range(8):
                        sch.op("pe", lambda e, j=j, ws=ws, q2=q2, sc=sc, bank=bank: e.matmul(
                            psb(bank), WIN[ws][:, j, q2 * 128:(q2 + 1) * 128], HNT[:, j, sc * 512:(sc + 1) * 512],
                            start=(j == 0), stop=(j == 7)), r=[("WIN", ws), ("HNT", sc)], w=[("ps", bank)])
                    sch.op("act", lambda e, sc=sc, bank=bank: e.activation(out=RAW[:, 1 + sc * 512:1 + (sc + 1) * 512], in_=psb(bank), func=AF.Copy),
                           r=[("ps", bank)], w=[("RAW", sc)])
                rawk = [("RAW", i) for i in range(4)] + ["RAWpad", "RAWpad2"]
                sch.op("act", lambda e, q=q: e.activation(out=TT, in_=RAW[:, 0:2048], func=AF.Identity,
                                                          scale=CF[:, CF_CW + q:CF_CW + q + 1], bias=CF[:, CF_CW + 72 + q:CF_CW + 72 + q + 1]),
                       r=rawk + ["CF"], w=["TT"])
                sch.op("dve", lambda e, q=q: e.scalar_tensor_tensor(out=TT, in0=RAW[:, 1:2049], scalar=CF[:, CF_CW + 24 + q:CF_CW + 24 + q + 1],
                                                                    in1=TT, op0=ALU.mult, op1=ALU.add), r=rawk + ["TT", "CF"], w=["TT"])
                sch.op("dve", lambda e, q=q, zc=hf * 4 + q4: e.scalar_tensor_tensor(
                    out=Z2T[:, zc, :], in0=RAW[:, 2:2050], scalar=CF[:, CF_CW + 48 + q:CF_CW + 48 + q + 1],
                    in1=TT, op0=ALU.mult, op1=ALU.add), r=rawk + ["TT", "CF"], w=[("Z2T", hf * 4 + q4)])

    def transp_to(hf, T, Tn):
        for q4 in range(4):
            for stg in range(2):
                bank = mbank()
                pv = psb16(bank)
                for i in range(8):
                    st = stg * 8 + i
                    sch.op("pe", lambda e, i=i, st=st, q4=q4, pv=pv: e.transpose(
                        pv[:, i * 128:(i + 1) * 128], Z2T[:, hf * 4 + q4, st * 128:(st + 1) * 128], IDENT),
                        r=[("Z2T", hf * 4 + q4), "CB"], w=[("ps", bank)])
                sch.op("act", lambda e, pv=pv, stg=stg, q4=q4: e.activation(
                    out=T[:, stg * 8:(stg + 1) * 8, q4 * 128:(q4 + 1) * 128], in_=pv.rearrange("p (i c) -> p i c", i=8), func=AF.Copy),
                    r=[("ps", bank)], w=[(Tn, st_) for st_ in range(stg * 8, stg * 8 + 8)])

    def transp_back(hf, T, Tn):
        for q4 in range(4):
            for stg in range(2):
                bank = mbank()
                pv = psb16(bank)
                for i in range(8):
                    st = stg * 8 + i
                    sch.op("pe", lambda e, i=i, st=st, q4=q4, pv=pv: e.transpose(
                        pv[:, i * 128:(i + 1) * 128], T[:, st, q4 * 128:(q4 + 1) * 128], IDENT),
                        r=[(Tn, st), "CB"], w=[("ps", bank)])
                sch.op("act", lambda e, pv=pv, stg=stg, q4=q4: e.activation(
                    out=Z2T[:, hf * 4 + q4, stg * 1024:(stg + 1) * 1024], in_=pv, func=AF.Copy),
                    r=[("ps", bank)], w=[("Z2T", hf * 4 + q4)])

    def fwd_dft(T, Tn, od, hf):
        c0 = od * 1024 + hf * 512
        for ft in range(NT):
            psl = load_panel(dftF, ft)
            ks = cnt["kst"] % 2
            cnt["kst"] += 1
            sch.op("sp", lambda e, ks=ks, ft=ft: e.dma_start(out=KST[ks], in_=kco_d[od, ft, :, :, hf * 512:(hf + 1) * 512]),
                   r=[("kco", od, ft, hf)], w=[("KST", ks)], slot=("kst", ks))
            pq = cnt["pq"] % 2
            cnt["pq"] += 1
            bp, bq = 2 * pq, 2 * pq + 1
            for st in range(NT):
                sch.op("pe", lambda e, st=st, psl=psl, bp=bp: e.matmul(
                    psb(bp), PAN[psl][:, 0, st, :], T[:, st, :], start=(st == 0), stop=(st == NT - 1)),
                    r=[("PAN", psl), (Tn, st)], w=[("ps", bp)])
            for st in range(NT):
                sch.op("pe", lambda e, st=st, psl=psl, bq=bq: e.matmul(
                    psb(bq), PAN[psl][:, 1, st, :], T[:, st, :], start=(st == 0), stop=(st == NT - 1)),
                    r=[("PAN", psl), (Tn, st)], w=[("ps", bq)])
            _issue_panel()
            Ka, Kb = KST[ks][:, 0, :], KST[ks][:, 1, :]
            kk = [("KST", ks)]
            tt = sch.op
            tt("dve", lambda e, bp=bp, Ka=Ka: e.tensor_tensor(out=TMP[0], in0=psb(bp), in1=Ka, op=ALU.mult), r=[("ps", bp)] + kk, w=[("TMP", 0)])
            tt("dve", lambda e, bq=bq, Kb=Kb: e.tensor_tensor(out=TMP[1], in0=psb(bq), in1=Kb, op=ALU.mult), r=[("ps", bq)] + kk, w=[("TMP", 1)])
            tt("dve", lambda e, ft=ft: e.tensor_tensor(out=YA[:, ft, :], in0=TMP[0], in1=TMP[1], op=ALU.add),
               r=[("TMP", 0), ("TMP", 1)], w=[("YA", ft)])
            tt("dve", lambda e, bq=bq, Ka=Ka: e.tensor_tensor(out=TMP[2], in0=psb(bq), in1=Ka, op=ALU.mult), r=[("ps", bq)] + kk, w=[("TMP", 2)])
            tt("dve", lambda e, bp=bp, Kb=Kb: e.tensor_tensor(out=TMP[3], in0=psb(bp), in1=Kb, op=ALU.mult), r=[("ps", bp)] + kk, w=[("TMP", 3)])
            tt("dve", lambda e, ft=ft: e.tensor_tensor(out=YB[:, ft, :], in0=TMP[2], in1=TMP[3], op=ALU.subtract),
               r=[("TMP", 2), ("TMP", 3)], w=[("YB", ft)])
            if ft == 0:
                tt("dve", lambda e, bq=bq: e.tensor_tensor(out=YB[0:1, 0, :], in0=psb(bq)[0:1, :], in1=KNY[0:1, c0:c0 + 512], op=ALU.mult),
                   r=[("ps", bq), ("KNY", od * 2 + hf)], w=[("YB", 0)])

    def inv_dft(T, Tn):
        for tt_ in range(NT):
            psl = load_panel(dftI, tt_)
            by = 4 + (cnt["py"] % 2)
            cnt["py"] += 1
            for ft in range(NT):
                sch.op("pe", lambda e, ft=ft, psl=psl, by=by: e.matmul(
                    psb(by), PAN[psl][:, 0, ft, :], YA[:, ft, :], start=(ft == 0), stop=False),
                    r=[("PAN", psl), ("YA", ft)], w=[("ps", by)])
                sch.op("pe", lambda e, ft=ft, psl=psl, by=by: e.matmul(
                    psb(by), PAN[psl][:, 1, ft, :], YB[:, ft, :], start=False, stop=(ft == NT - 1)),
                    r=[("PAN", psl), ("YB", ft)], w=[("ps", by)])
            _issue_panel()
            sch.op("dve", lambda e, tt_=tt_, by=by: e.tensor_tensor(out=T[:, tt_, :], in0=psb(by), in1=T[:, tt_, :], op=ALU.mult),
                   r=[("ps", by), (Tn, tt_)], w=[(Tn, tt_)])

    for hf in range(2):
        inproj(0, hf)
        transp_to(hf, T0, "T0")
        inproj(1, hf)
        fwd_dft(T0, "T0", 0, hf)
        transp_to(hf, T1, "T1")
        inv_dft(T1, "T1")
        inproj(2, hf)
        fwd_dft(T1, "T1", 1, hf)
        transp_to(hf, T0, "T0")
        inv_dft(T0, "T0")
        transp_back(hf, T0, "T0")

    sch.fence_all()
    WO = V(P_RAW, 4096, BF16, "p (j d) -> p j d", j=8)
    sch.op("pool", lambda e: e.dma_start(out=WO, in_=w_out_d.rearrange("(j p) d -> p j d", p=128)), w=["WO"], slot="wo")
    XC2 = [V(P_T0 + 4096 * i, 4096, F32, "p (j s) -> p j s", j=8) for i in range(2)]
    for sc in range(4):
        xs = sc % 2
        sch.op("sp", lambda e, sc=sc, xs=xs: e.dma_start(out=XC2[xs], in_=xT_v[:, :, sc * 512:(sc + 1) * 512]),
               w=[("XC2", xs)], slot=("xc2", xs))
        for jd in range(8):
            bank = mbank()
            for cj in range(8):
                sch.op("pe", lambda e, cj=cj, jd=jd, sc=sc, bank=bank: e.matmul(
                    psb(bank), WO[:, cj, jd * 128:(jd + 1) * 128], Z2T[:, cj, sc * 512:(sc + 1) * 512],
                    start=(cj == 0), stop=(cj == 7)), r=["WO", ("Z2T", cj)], w=[("ps", bank)])
            evac_m(bank, jd)
        branch_finish(sc, None, 0, 1, lambda j, xs=xs: XC2[xs][:, j, :], lambda j, xs=xs: [("XC2", xs)], sc * 512)

    def write_out():
        for j in range(8):
            sch.op("sp", lambda e, j=j: e.dma_start(out=outT_v[:, j, :], in_=H[:, j, :]),
                   r=[("H", j, i) for i in range(4)], w=[("out", j)], slot=("out", j % 4))
        sch.op("sp", lambda e: e.dma_start(out=kco_d[0, 0, 0:1, 0, 0:8], in_=kco_d[0, 0, 1:2, 0, 0:8]),
               r=[("out", j) for j in range(8)], slot="fin")

    if stage == 1:
        write_out()
        sch.emit(nc, stack)
        return nc, stack


    def mlp(layer):
        sch.fence_all()
        tag = "L%d" % layer
        HNC = V(P_RAW, 4096, BF16, "p (j s) -> p j s", j=8)
        ACTB = V(P_Z2T, 16384, BF16, "p (f s) -> p f s", f=32)
        WU = [V(o_, 1024, BF16, "p (j c) -> p j c", j=8) for o_ in (P_TMP, P_TMP + 1024, P_X)]
        WD = [V(P_KST + 1024 * i, 1024, BF16, "p (f d) -> p f d", f=4) for i in range(3)]
        RL = [V(O_KNY + 256 * i, 256, BF16) for i in range(2)]
        wu_v = w_up_d[layer].rearrange("(j p) f -> p j f", p=128)
        wd_v = w_down_d[layer].rearrange("(f p) d -> p f d", p=128)
        c = {"wu": 0, "wd": 0, "rl": 0, "ub": 0}
        for tc in range(2):
            t0 = tc * 1024
            for sc2 in range(2):
                tok = t0 + sc2 * 512
                norm_chunk(H[:, :, tok:tok + 512], [("H", j, tok // 512) for j in range(8)], layer, 2,
                           lambda j, sc2=sc2: HNC[:, j, sc2 * 512:(sc2 + 1) * 512], lambda j, sc2=sc2: [(tag + "HNC", sc2)])
            for slab in range(16):
                ws = c["wu"] % 3
                c["wu"] += 1
                sch.op("pool", lambda e, ws=ws, slab=slab: e.dma_start(out=WU[ws], in_=wu_v[:, :, slab * 256:(slab + 1) * 256]),
                       w=[(tag + "WU", ws)], slot=(tag + "wu", ws))
                for q2 in range(2):
                    ffc = slab * 2 + q2
                    for sc2 in range(2):
                        bank = 4 + (c["ub"] % 2)
                        c["ub"] += 1
                        for j in range(8):
                            sch.op("pe", lambda e, j=j, ws=ws, q2=q2, sc2=sc2, bank=bank: e.matmul(
                                psb(bank), WU[ws][:, j, q2 * 128:(q2 + 1) * 128], HNC[:, j, sc2 * 512:(sc2 + 1) * 512],
                                start=(j == 0), stop=(j == 7)), r=[(tag + "WU", ws), (tag + "HNC", sc2)], w=[("ps", bank)])
                        rl = c["rl"] % 2
                        c["rl"] += 1
                        sch.op("act", lambda e, bank=bank, rl=rl: e.activation(out=RL[rl], in_=psb(bank), func=AF.Relu),
                               r=[("ps", bank)], w=[(tag + "RL", rl)])
                        sch.op("dve", lambda e, rl=rl, ffc=ffc, sc2=sc2: e.tensor_tensor(
                            out=ACTB[:, ffc, sc2 * 512:(sc2 + 1) * 512], in0=RL[rl], in1=RL[rl], op=ALU.mult),
                            r=[(tag + "RL", rl)], w=[(tag + "ACT", ffc, sc2)])
            for sc2 in range(2):
                tok = t0 + sc2 * 512
                for jh in range(2):
                    for slab in range(8):
                        ws = c["wd"] % 3
                        c["wd"] += 1
                        sch.op("pool", lambda e, ws=ws, slab=slab, jh=jh: e.dma_start(
                            out=WD[ws], in_=wd_v[:, slab * 4:(slab + 1) * 4, jh * 512:(jh + 1) * 512]),
                            w=[(tag + "WD", ws)], slot=(tag + "wd", ws))
                        for f4 in range(4):
                            ffc = slab * 4 + f4
                            for jq in range(4):
                                sch.op("pe", lambda e, ws=ws, f4=f4, jq=jq, ffc=ffc, sc2=sc2: e.matmul(
                                    psb(jq), WD[ws][:, f4, jq * 128:(jq + 1) * 128], ACTB[:, ffc, sc2 * 512:(sc2 + 1) * 512],
                                    start=(ffc == 0), stop=(ffc == 31)), r=[(tag + "WD", ws), (tag + "ACT", ffc, sc2)], w=[("ps", jq)])
                    for jq in range(4):
                        evac_m(jq, jh * 4 + jq)
                branch_finish(None, None, layer, 3, lambda j, tok=tok: H[:, j, tok:tok + 512],
                              lambda j, tok=tok: [("H", j, tok // 512)], tok)

    mlp(0)
    if stage == 2:
        write_out()
        sch.emit(nc, stack)
        return nc, stack


    sch.fence_all()
    HNT2 = V(P_Z2T, 8192, BF16, "p (j s) -> p j s", j=8)
    QKT = V(P_T0, 12288, BF16, "p (c s) -> p c s", c=12)
    VTOK = V(P_TMP, 2112, BF16, "p (t g e) -> p t g e", t=16, g=4)
    WQ = [V(P_WIN + 1024 * i, 1024, BF16, "p (j c) -> p j c", j=8) for i in range(2)]
    ATC = V(P_PAN, 2368, BF16)
    MASK = ATC[:, 0:384]
    PERMR = ATC[:, 384:512]
    PERMH = ATC[:, 512:640]
    COS = ATC[:, 640:2688]
    SIN = ATC[:, 2688:4736]
    PTS = [V(P_PAN + 2368 + 192 * i, 192, BF16) for i in range(8)]
    ESK = V(P_PAN + 3904, 16)
    RDN = V(P_PAN + 3920, 16)
    XB = [V(O_KNY + 256 * i, 256, BF16) for i in range(2)]
    XS = [V(O_KNY + 512 + 256 * i, 256, BF16) for i in range(2)]
    wqkv_v = wqkv_d.rearrange("(j p) n -> p j n", p=128)
    sch.op("sp", lambda e: e.dma_start(out=ATC, in_=atc_d), w=["ATC"], slot="atc")
    sch.op("sp", lambda e: e.dma_start(out=ESK, in_=sink_d.partition_broadcast(128)), w=["ESK"], slot="esk")
    sch.op("act", lambda e: e.activation(out=ESK, in_=ESK, func=AF.Exp), r=["ESK"], w=["ESK"])
    for sc in range(4):
        norm_chunk(H[:, :, sc * 512:(sc + 1) * 512], [("H", j, sc) for j in range(8)], 1, 0,
                   lambda j, sc=sc: HNT2[:, j, sc * 512:(sc + 1) * 512], lambda j, sc=sc: [("HNT2", sc)])
    ac = {"wq": 0, "xb": 0, "pt": 0, "sb": 0, "pv": 0}
    KZ = [[QKT[:, 8, :], QKT[:, 9, :]], [QKT[:, 10, :], QKT[:, 11, :]],
          [V(O_SQ, 1024, BF16), V(O_SQ + 1024, 1024, BF16)], [V(O_RSTD, 1024, BF16), V(P_X, 1024, BF16)]]
    sch.retire([("SQj", j) for j in range(8)] + [("RSTD", 0), ("RSTD", 1)],
               [("KZ", g_, v_, s_) for g_ in (2, 3) for v_ in range(2) for s_ in range(4)] + [("KZz", g_, v_) for g_ in (2, 3) for v_ in range(2)])
    for g_ in range(4):
        sch.op("dve", lambda e, g_=g_: e.memset(KZ[g_][0][64:128, :], 0.0), w=[("KZz", g_, 0)])
        sch.op("dve", lambda e, g_=g_: e.memset(KZ[g_][1][0:64, :], 0.0), w=[("KZz", g_, 1)])
    dst_chunk = [0, 1, 2, 3, 4, 5, 6, 7, 8, 10]
    for slab in range(5):
        ws = ac["wq"] % 2
        ac["wq"] += 1
        sch.op("pool", lambda e, ws=ws, slab=slab: e.dma_start(out=WQ[ws], in_=wqkv_v[:, :, slab * 256:(slab + 1) * 256]),
               w=[("WQ", ws)], slot=("wq", ws))
        for q2 in range(2):
            dc = dst_chunk[slab * 2 + q2]
            for sc in range(4):
                bank = mbank()
                for j in range(8):
                    sch.op("pe", lambda e, j=j, ws=ws, q2=q2, sc=sc, bank=bank: e.matmul(
                        psb(bank), WQ[ws][:, j, q2 * 128:(q2 + 1) * 128], HNT2[:, j, sc * 512:(sc + 1) * 512],
                        start=(j == 0), stop=(j == 7)), r=[("WQ", ws), ("HNT2", sc)], w=[("ps", bank)])
                xb = ac["xb"] % 2
                ac["xb"] += 1
                sch.op("act", lambda e, bank=bank, xb=xb: e.activation(out=XB[xb], in_=psb(bank), func=AF.Copy),
                       r=[("ps", bank)], w=[("XB", xb)])
                bank2 = mbank()
                sch.op("pe", lambda e, bank2=bank2, xb=xb: e.matmul(psb(bank2), PERMR, XB[xb], start=True, stop=True),
                       r=[("XB", xb), "ATC"], w=[("ps", bank2)])
                sch.op("dve", lambda e, bank2=bank2, xb=xb, sc=sc: e.tensor_tensor(
                    out=XS[xb], in0=psb(bank2), in1=SIN[:, sc * 512:(sc + 1) * 512], op=ALU.mult),
                    r=[("ps", bank2), "ATC"], w=[("XS", xb)])
                sch.op("dve", lambda e, xb=xb, sc=sc: e.tensor_tensor(
                    out=XB[xb], in0=XB[xb], in1=COS[:, sc * 512:(sc + 1) * 512], op=ALU.mult),
                    r=[("XB", xb), "ATC"], w=[("XB", xb)])
                if dc < 8:
                    sch.op("dve", lambda e, xb=xb, sc=sc, dc=dc: e.tensor_tensor(
                        out=QKT[:, dc, sc * 512:(sc + 1) * 512], in0=XB[xb], in1=XS[xb], op=ALU.add),
                        r=[("XB", xb), ("XS", xb)], w=[("QKT", dc, sc)])
                else:
                    g0, g1 = (0, 1) if dc == 8 else (2, 3)
                    cs = slice(sc * 512, (sc + 1) * 512)
                    sch.op("dve", lambda e, xb=xb: e.tensor_tensor(out=XS[xb], in0=XB[xb], in1=XS[xb], op=ALU.add),
                           r=[("XB", xb), ("XS", xb)], w=[("XS", xb)])
                    sch.op("act", lambda e, xb=xb, g0=g0, cs=cs: e.activation(out=KZ[g0][0][0:64, cs], in_=XS[xb][0:64, :], func=AF.Copy),
                           r=[("XS", xb)], w=[("KZ", g0, 0, sc)])
                    sch.op("act", lambda e, xb=xb, g1=g1, cs=cs: e.activation(out=KZ[g1][1][64:128, cs], in_=XS[xb][64:128, :], func=AF.Copy),
                           r=[("XS", xb)], w=[("KZ", g1, 1, sc)])
                    bank3 = mbank()
                    sch.op("pe", lambda e, bank3=bank3, xb=xb: e.matmul(psb(bank3), PERMH, XS[xb], start=True, stop=True),
                           r=[("XS", xb), "ATC"], w=[("ps", bank3)])
                    sch.op("act", lambda e, bank3=bank3, g1=g1, cs=cs: e.activation(out=KZ[g1][0][0:64, cs], in_=psb(bank3)[0:64, :], func=AF.Copy),
                           r=[("ps", bank3)], w=[("KZ", g1, 0, sc)])
                    sch.op("act", lambda e, bank3=bank3, g0=g0, cs=cs: e.activation(out=KZ[g0][1][64:128, cs], in_=psb(bank3)[64:128, :], func=AF.Copy),
                           r=[("ps", bank3)], w=[("KZ", g0, 1, sc)])
    ws = ac["wq"] % 2
    ac["wq"] += 1
    sch.op("pool", lambda e, ws=ws: e.dma_start(out=WQ[ws], in_=wqkv_v[:, :, 1280:1536]), w=[("WQ", ws)], slot=("wq", ws))
    sch.op("dve", lambda e: e.memset(VTOK[:, :, :, 64:66], 1.0), w=["VONE"])
    for st in range(NT):
        bank = mbank()
        for j in range(8):
            sch.op("pe", lambda e, j=j, ws=ws, st=st, bank=bank: e.matmul(
                psb(bank)[:, 0:256], HNT2[:, j, st * 128:(st + 1) * 128], WQ[ws][:, j, :],
                start=(j == 0), stop=(j == 7)), r=[("WQ", ws), ("HNT2", st // 4)], w=[("ps", bank)])
        sch.op("act", lambda e, bank=bank, st=st: e.activation(
            out=VTOK[:, st, :, 0:64], in_=psb(bank)[:, 0:256].rearrange("p (g e) -> p g e", g=4), func=AF.Copy),
            r=[("ps", bank)], w=[("VTOK", st)])

    OTOK = V(P_Z2T, 8192, BF16, "p (t c) -> p t c", t=16)
    sch.retire([("HNT2", i) for i in range(4)], [("OTOK", i) for i in range(16)])

    NSLOT = 8
    LAG = 3
    tidx = lambda h, j: h * NT + j

    def pv(h, i):
        g = h // 4
        kbs = [kb for kb in (i - 1, i, i + 1) if 0 <= kb < NT]
        pb = ac["pv"] % 4
        ac["pv"] += 1
        for n_, kb in enumerate(kbs):
            qlo = max(kb - 1, 0)
            slot = tidx(h, kb) % NSLOT
            off = (i - qlo) * 128
            sch.op("pe", lambda e, slot=slot, off=off, kb=kb, pb=pb, n_=n_: e.matmul(
                psb(pb)[:, 0:65], PTS[slot][:, off:off + 128], VTOK[:, kb, g, 0:65],
                start=(n_ == 0), stop=(n_ == len(kbs) - 1)),
                r=[("PT", slot), ("VTOK", kb), "VONE"], w=[("ps", pb)])
        rd = ac["pv"] % 2
        sch.op("dve", lambda e, pb=pb, h=h, rd=rd: e.tensor_scalar(out=RDN[:, 2 * rd:2 * rd + 1], in0=psb(pb)[:, 64:65], scalar1=ESK[:, h:h + 1],
                                                               scalar2=None, op0=ALU.add), r=[("ps", pb), "ESK"], w=[("RDN", rd)])
        sch.op("dve", lambda e, rd=rd: e.reciprocal(out=RDN[:, 2 * rd + 1:2 * rd + 2], in_=RDN[:, 2 * rd:2 * rd + 1]), r=[("RDN", rd)], w=[("RDN2", rd)])
        sch.op("dve", lambda e, pb=pb, h=h, i=i, rd=rd: e.tensor_scalar(out=OTOK[:, i, h * 64:(h + 1) * 64], in0=psb(pb)[:, 0:64],
                                                                       scalar1=RDN[:, 2 * rd + 1:2 * rd + 2], scalar2=None, op0=ALU.mult),
               r=[("ps", pb), ("RDN2", rd)], w=[("OTOK", i)])

    def scores(h, j):
        g = h // 4
        v = h % 2
        qc = h // 2
        qlo, qhi = max(j - 1, 0), min(j + 1, NT - 1)
        nq = qhi - qlo + 1
        bank = 4 + (ac["sb"] % 4)
        ac["sb"] += 1
        slot = tidx(h, j) % NSLOT
        sch.op("pe", lambda e: e.matmul(
            psb(bank)[:, 0:nq * 128], KZ[g][v][:, j * 128:(j + 1) * 128], QKT[:, qc, qlo * 128:(qhi + 1) * 128],
            start=True, stop=False),
            r=[("KZ", g, v, j // 4), ("KZz", g, v)] + [("QKT", qc, s_) for s_ in range(qlo // 4, qhi // 4 + 1)], w=[("ps", bank)])
        m0 = 128 if j == 0 else 0
        sch.op("pe", lambda e: e.matmul(psb(bank)[:, 0:nq * 128], IDENT, MASK[:, m0:m0 + nq * 128], start=False, stop=True),
               r=["ATC", "CB"], w=[("ps", bank)])
        sch.op("act", lambda e: e.activation(
            out=PTS[slot][:, 0:nq * 128], in_=psb(bank)[:, 0:nq * 128], func=AF.Exp, scale=0.125),
            r=[("ps", bank)], w=[("PT", slot)])

    pv_tasks = []
    for h in range(16):
        for i in range(NT):
            pv_tasks.append((tidx(h, min(i + 1, NT - 1)), h, i))
    pvi = 0
    for t in range(16 * NT + LAG):
        if t < 16 * NT:
            scores(t // NT, t % NT)
        while pvi < len(pv_tasks) and pv_tasks[pvi][0] + LAG <= t:
            pv(pv_tasks[pvi][1], pv_tasks[pvi][2])
            pvi += 1
    assert pvi == len(pv_tasks)

    OT = V(P_T0, 8192, BF16, "p (j s) -> p j s", j=8)
    sch.retire([("QKT", c, s_) for c in range(8) for s_ in range(4)] + [("KZ", g_, v_, s_) for g_ in range(2) for v_ in range(2) for s_ in range(4)]
               + [("KZz", g_, v_) for g_ in range(2) for v_ in range(2)], [("OT", j) for j in range(8)] + ["WO2"])
    WO2 = V(P_RAW, 4096, BF16, "p (j d) -> p j d", j=8)
    sch.op("pool", lambda e: e.dma_start(out=WO2, in_=wo2_d.rearrange("(j p) d -> p j d", p=128)), w=["WO2"], slot="wo2")
    for cj in range(8):
        for stg in range(2):
            bank = mbank()
            pv_ = psb16(bank)
            for i in range(8):
                st = stg * 8 + i
                sch.op("pe", lambda e, i=i, st=st, cj=cj, pv_=pv_: e.transpose(
                    pv_[:, i * 128:(i + 1) * 128], OTOK[:, st, cj * 128:(cj + 1) * 128], IDENT),
                    r=[("OTOK", st), "CB"], w=[("ps", bank)])
            sch.op("act", lambda e, pv_=pv_, stg=stg, cj=cj: e.activation(
                out=OT[:, cj, stg * 1024:(stg + 1) * 1024], in_=pv_, func=AF.Copy), r=[("ps", bank)], w=[("OT", cj)])
    sch.fence_all()
    for sc in range(4):
        for jd in range(8):
            bank = mbank()
            for cj in range(8):
                sch.op("pe", lambda e, cj=cj, jd=jd, sc=sc, bank=bank: e.matmul(
                    psb(bank), WO2[:, cj, jd * 128:(jd + 1) * 128], OT[:, cj, sc * 512:(sc + 1) * 512],
                    start=(cj == 0), stop=(cj == 7)), r=["WO2", ("OT", cj)], w=[("ps", bank)])
            evac_m(bank, jd)
        branch_finish(sc, None, 1, 1, lambda j, sc=sc: H[:, j, sc * 512:(sc + 1) * 512], lambda j, sc=sc: [("H", j, sc)], sc * 512)
    if stage == 3:
        write_out()
        sch.emit(nc, stack)
        return nc, stack
    mlp(1)
    write_out()
    sch.emit(nc, stack)
    return nc, stack


def make_in_maps(inp):
    cf, cb, zf = _host_consts(inp)
    fwd, inv, dk = _dft_tables()
    adl = _absdelta().reshape(1, D)
    hyb = np.ascontiguousarray(np.asarray(inp["hy_bias"], np.float32)[0].reshape(1, 2 * D))
    x = np.asarray(inp["x"], np.float32)
    common = {
        "cf": cf, "cb": cb, "zf": zf, "adl": adl, "hyb": hyb, "dftF": fwd, "dftI": inv, "dftK": dk,
        "hy_w_in": np.ascontiguousarray(np.asarray(inp["hy_w_in"], np.float32)[0]),
        "hy_f_wout": np.ascontiguousarray(np.asarray(inp["hy_f_wout"], np.float32)[0]),
        "hy_w_out": np.ascontiguousarray(np.asarray(inp["hy_w_out"], np.float32)[0]),
        "w_up": np.ascontiguousarray(np.asarray(inp["w_up"], np.float32)),
        "w_down": np.ascontiguousarray(np.asarray(inp["w_down"], np.float32)),
    }
    common["at_w_qkv"] = np.ascontiguousarray(np.asarray(inp["at_w_qkv"], np.float32)[0])
    common["at_w_o"] = np.ascontiguousarray(np.asarray(inp["at_w_o"], np.float32)[0])
    common["sink"] = np.ascontiguousarray(np.asarray(inp["at_sink"], np.float32)[0].reshape(1, 16))
    common["atc"] = _attn_consts()
    maps = []
    for c in range(NCORES):
        m = dict(common)
        m["xT"] = np.ascontiguousarray(x[c].T)
        maps.append(m)
    return maps


_PROG = {}


def kernel(**inputs):
    inp = {k: np.asarray(v) for k, v in inputs.items()}
    if "nc" not in _PROG:
        _PROG["nc"] = build_program(stage=4)
    nc, _stack = _PROG["nc"]
    maps = make_in_maps(inp)
    res = run_bass_kernel_spmd(nc, maps, core_ids=list(range(NCORES)))
    out = np.stack([np.ascontiguousarray(r["outT"].T) for r in res.results], axis=0)
    return out.astype(np.float32)
```

```python
import math
import bisect
import numpy as np
import ml_dtypes
import concourse.bass as bass
import concourse.mybir as mybir
from concourse.bass_utils import run_bass_kernel_spmd

F32 = mybir.dt.float32
BF16 = mybir.dt.bfloat16
AF = mybir.ActivationFunctionType
ALU = mybir.AluOpType

S = 2048
D = 1024
NT = 16
DFF = 4096
EPS = 1e-6
NCORES = 8
PI = math.pi

SBUF_WORDS = 52800


class Sched:
    ENGS = ("pe", "act", "dve", "pool", "sp")

    def __init__(self):
        self.ops = []
        self.lastw = {}
        self.readers = {}

    def op(self, eng, fn, r=(), w=(), slot=None):
        idx = len(self.ops)
        deps = set()
        if getattr(self, "fence", None):
            for k in w:
                if k not in self.known:
                    deps.update(self.fence)
                    self.known.add(k)
        for k in r:
            if k in self.lastw:
                deps.add(self.lastw[k])
        for k in w:
            if k in self.lastw:
                deps.add(self.lastw[k])
            deps.update(self.readers.get(k, ()))
        for k in r:
            self.readers.setdefault(k, []).append(idx)
        for k in w:
            self.lastw[k] = idx
            self.readers[k] = []
        deps.discard(idx)
        self.ops.append(dict(eng=eng, fn=fn, deps=deps, slot=slot, sem=None, val=None, sig=False))
        return idx

    def fence_all(self):
        last = {}
        for i, o in enumerate(self.ops):
            last[(o["eng"], o["slot"])] = i
        self.fence = set(last.values())
        self.known = set(self.lastw.keys()) | set(self.readers.keys())

    def retire(self, old_keys, new_keys):
        acc = set()
        for k in old_keys:
            if k in self.lastw:
                acc.add(self.lastw[k])
            acc.update(self.readers.get(k, ()))
        for k in new_keys:
            self.readers.setdefault(k, []).extend(acc)

    def emit(self, nc, stack, same_engine_sync=("act", "dve", "pool")):
        ops = self.ops
        for i, o in enumerate(ops):
            for d in o["deps"]:
                y = ops[d]
                if y["slot"] is not None:
                    continue
                if y["eng"] == o["eng"] and y["eng"] not in same_engine_sync:
                    continue
                y["sig"] = True
        SEM_MAX = 30000
        eng_sems = {}
        counters = {}
        for e in ("pe", "act", "dve", "pool"):
            eng_sems[e] = []
            counters[e] = SEM_MAX
        slot_sem = {}
        slot_list = {}
        for i, o in enumerate(ops):
            if o["slot"] is not None:
                sl = o["slot"]
                if sl not in slot_sem:
                    slot_sem[sl] = stack.enter_context(nc.semaphore("d_" + str(len(slot_sem))))
                    slot_list[sl] = []
                slot_list[sl].append(i)
                o["sem"] = slot_sem[sl]
                o["val"] = 16 * len(slot_list[sl])
            elif o["sig"]:
                e = o["eng"]
                if counters[e] >= SEM_MAX:
                    eng_sems[e].append(stack.enter_context(nc.semaphore("e_%s_%d" % (e, len(eng_sems[e])))))
                    counters[e] = 0
                counters[e] += 1
                o["sem"] = eng_sems[e][-1]
                o["val"] = counters[e]
        self.nsem = len(slot_sem) + sum(len(v) for v in eng_sems.values())
        block = stack.enter_context(nc.Block())
        per_eng = {e: [] for e in self.ENGS}
        for i, o in enumerate(ops):
            per_eng[o["eng"]].append(i)

        def run(engname, eng):
            waited = {}
            for i in per_eng[engname]:
                o = ops[i]
                need = {}
                for d in o["deps"]:
                    y = ops[d]
                    if y["slot"] is not None:
                        lst = slot_list[y["slot"]]
                        pos = bisect.bisect_left(lst, i)
                        sem, val = y["sem"], 16 * pos
                    else:
                        if y["eng"] == engname and engname not in same_engine_sync:
                            continue
                        sem, val = y["sem"], y["val"]
                    key = id(sem)
                    if key not in need or need[key][1] < val:
                        need[key] = (sem, val)
                for key, (sem, val) in need.items():
                    if waited.get(key, 0) >= val:
                        continue
                    eng.wait_ge(sem, val)
                    waited[key] = val
                ins = o["fn"](eng)
                if o["slot"] is not None:
                    ins.then_inc(o["sem"], 16)
                elif o["sig"]:
                    ins.then_inc(o["sem"], 1)

        @block.tensor
        def _(e):
            run("pe", e)

        @block.scalar
        def _(e):
            run("act", e)

        @block.vector
        def _(e):
            run("dve", e)

        @block.gpsimd
        def _(e):
            run("pool", e)

        @block.sync
        def _(e):
            run("sp", e)


_CONST_CACHE = {}


def _dft_tables():
    if "dft" in _CONST_CACHE:
        return _CONST_CACHE["dft"]
    n = np.arange(S, dtype=np.int64)
    prod = (n[:, None] * n[None, :]) % 4096
    ang = prod.astype(np.float64) * (2.0 * np.pi / 4096.0)
    C = np.cos(ang)
    Sm = np.sin(ang)
    sgn = np.where(n % 2 == 0, 1.0, -1.0)
    Sp = Sm.copy()
    Sp[:, 0] = sgn
    SpT = Sp.T.copy()

    def panelize(M):
        return M.reshape(16, 128, 16, 128).transpose(2, 1, 0, 3)

    fwd = np.stack([panelize(C), panelize(Sp)], axis=2)
    inv = np.stack([panelize(C), panelize(SpT)], axis=2)
    fwd = np.ascontiguousarray(fwd).astype(ml_dtypes.bfloat16)
    inv = np.ascontiguousarray(inv).astype(ml_dtypes.bfloat16)
    kk = np.arange(16)[:, None]
    pp = np.arange(128)[None, :]
    sperm = (2 * (128 * (kk % 8) + pp) + (kk // 8)).reshape(-1)
    f1 = np.arange(1024, dtype=np.int64)
    angk = ((sperm[:, None].astype(np.int64) * f1[None, :]) % 4096).astype(np.float64) * (2.0 * np.pi / 4096.0)

    def panelize_k(M):
        return M.reshape(16, 128, 8, 128).transpose(2, 1, 0, 3)

    dk = np.stack([panelize_k(np.cos(angk)), panelize_k(np.sin(angk))], axis=2)
    dk = np.ascontiguousarray(dk).astype(ml_dtypes.bfloat16)
    _CONST_CACHE["dft"] = (fwd, inv, dk)
    return fwd, inv, dk


def _zfeat():
    L = S
    t = np.linspace(0.0, 1.0, L, dtype=np.float32)[:, None]
    w = (np.float32(2.0 * math.pi / L) * np.arange(L, dtype=np.float32))[:, None]
    f = np.linspace(1e-4, 15.0, 16, dtype=np.float32)[None, :]
    z = np.concatenate([t, np.cos(f * w), -np.sin(f * w)], axis=-1).astype(np.float32)
    return z


def _attn_consts():
    a = np.zeros((128, 4736), np.float32)
    kl = np.arange(128)[:, None]
    ql = np.arange(128)[None, :]
    a[:, 0:128] = np.where(kl <= ql, 0.0, -30000.0)
    a[:, 128:256] = 0.0
    a[:, 256:384] = np.where(ql <= kl, 0.0, -30000.0)
    pr = np.zeros((128, 128), np.float32)
    for hb in (0, 64):
        for i in range(8):
            pr[hb + 8 + i, hb + i] = 1.0
            pr[hb + i, hb + 8 + i] = 1.0
    a[:, 384:512] = pr
    ph = np.zeros((128, 128), np.float32)
    for m in range(128):
        ph[(m + 64) % 128, m] = 1.0
    a[:, 512:640] = ph
    inv = (500000.0 ** (-np.arange(0, 16, 2, dtype=np.float32) / 16.0)).astype(np.float32)
    ang = np.arange(S, dtype=np.float32)[None, :] * inv[:, None]
    cos = np.ones((128, S), np.float32)
    sin = np.zeros((128, S), np.float32)
    for hb in (0, 64):
        cos[hb:hb + 8] = np.cos(ang); cos[hb + 8:hb + 16] = np.cos(ang)
        sin[hb:hb + 8] = -np.sin(ang); sin[hb + 8:hb + 16] = np.sin(ang)
    a[:, 640:2688] = cos
    a[:, 2688:4736] = sin
    return a.astype(ml_dtypes.bfloat16)


def _absdelta():
    max_decay = math.log(1e-2) / 0.3
    min_decay = math.log(1e-2) / 1.5
    deltas = np.linspace(min_decay, max_decay, D, dtype=np.float32)
    return np.abs(deltas).astype(np.float32)


CF_G = 0
CF_CW = 64
CF_FB = 160
CF_TNEG = 164
CF_FW1 = 180
CF_FW2 = 244
CF_FW3 = 308
CF_FBF = 372
CF_N = 384

CB_ID = 0
CB_ONES = 128
CB_JREV = 256
CB_E00 = 384
CB_SGN = 512
CB_N = 640


def _host_consts(inp):
    cf = np.zeros((128, CF_N), np.float32)
    gl = ["norm_mix_pre", "norm_mix_post", "norm_mlp_pre", "norm_mlp_post"]
    for layer in range(2):
        for gi, gname in enumerate(gl):
            v = np.asarray(inp[gname])[layer]
            cf[:, CF_G + (layer * 4 + gi) * 8: CF_G + (layer * 4 + gi) * 8 + 8] = v.reshape(8, 128).T
    cw = np.asarray(inp["hy_conv_w"])[0]
    cbias = np.asarray(inp["hy_conv_b"])[0]
    for k in range(3):
        cf[:, CF_CW + 24 * k: CF_CW + 24 * k + 24] = cw[k].reshape(24, 128).T
    cf[:, CF_CW + 72: CF_CW + 96] = cbias.reshape(24, 128).T
    cf[0:64, CF_FB + 0] = np.asarray(inp["hy_f_b1"])[0]
    cf[0:64, CF_FB + 1] = np.asarray(inp["hy_f_b2"])[0]
    cf[0:64, CF_FB + 2] = np.asarray(inp["hy_f_b3"])[0]
    cf[0:64, CF_FB + 3] = np.asarray(inp["hy_f_freq"])[0]
    t = np.linspace(0.0, 1.0, S, dtype=np.float32)
    for k in range(16):
        sidx = 2 * (128 * (k % 8) + np.arange(128)) + (k // 8)
        cf[:, CF_TNEG + k] = -t[sidx]
    cf[0:33, CF_FW1: CF_FW1 + 64] = np.asarray(inp["hy_f_w1"])[0]
    cf[0:64, CF_FW2: CF_FW2 + 64] = np.asarray(inp["hy_f_w2"])[0]
    cf[0:64, CF_FW3: CF_FW3 + 64] = np.asarray(inp["hy_f_w3"])[0]
    cb = np.zeros((128, CB_N), np.float32)
    cb[:, CB_ID:CB_ID + 128] = np.eye(128)
    cb[:, CB_ONES:CB_ONES + 128] = 1.0
    for q in range(1, 128):
        cb[128 - q, CB_JREV + q] = 1.0
    cb[0, CB_E00] = 1.0
    cb[:, CB_SGN] = np.where(np.arange(128) % 2 == 0, 1.0, -1.0)
    cb = cb.astype(ml_dtypes.bfloat16)
    zf = np.zeros((33, S), np.float32)
    zf[:, :] = _zfeat().T
    return cf, cb, zf


def build_program(stage=4):
    from contextlib import ExitStack
    nc = bass.Bass("TRN2", target_bir_lowering=False)
    dt = nc.dram_tensor
    xT = dt("xT", [D, S], F32, kind="ExternalInput").ap()
    cf_d = dt("cf", [128, CF_N], F32, kind="ExternalInput").ap()
    cb_d = dt("cb", [128, CB_N], BF16, kind="ExternalInput").ap()
    zf_d = dt("zf", [33, S], F32, kind="ExternalInput").ap()
    adl_d = dt("adl", [1, D], F32, kind="ExternalInput").ap()
    hyb_d = dt("hyb", [1, 2 * D], F32, kind="ExternalInput").ap()
    dftF = dt("dftF", [16, 128, 2, 16, 128], BF16, kind="ExternalInput").ap()
    dftI = dt("dftI", [16, 128, 2, 16, 128], BF16, kind="ExternalInput").ap()
    dftK = dt("dftK", [8, 128, 2, 16, 128], BF16, kind="ExternalInput").ap()
    w_in_d = dt("hy_w_in", [D, 3 * D], F32, kind="ExternalInput").ap()
    wout_f_d = dt("hy_f_wout", [64, 4 * D], F32, kind="ExternalInput").ap()
    w_out_d = dt("hy_w_out", [D, D], F32, kind="ExternalInput").ap()
    w_up_d = dt("w_up", [2, D, DFF], F32, kind="ExternalInput").ap()
    w_down_d = dt("w_down", [2, DFF, D], F32, kind="ExternalInput").ap()
    wqkv_d = dt("at_w_qkv", [D, 1536], F32, kind="ExternalInput").ap()
    wo2_d = dt("at_w_o", [D, D], F32, kind="ExternalInput").ap()
    sink_d = dt("sink", [1, 16], F32, kind="ExternalInput").ap()
    atc_d = dt("atc", [128, 4736], BF16, kind="ExternalInput").ap()
    kco_d = dt("kco", [2, 16, 128, 2, D], BF16, kind="Internal").ap()
    outT = dt("outT", [D, S], F32, kind="ExternalOutput").ap()

    stack = ExitStack()
    big = stack.enter_context(nc.sbuf_tensor("big", [128, SBUF_WORDS], F32))
    PS = stack.enter_context(nc.psum_tensor("PS", [128, 8, 512], F32))
    sch = Sched()

    def V(off, words, dtype=F32, pat=None, **kw):
        ap = big[:, off:off + words]
        if dtype != F32:
            ap = ap.bitcast(dtype)
        if pat is not None:
            ap = ap.rearrange(pat, **kw)
        return ap

    def psb(b):
        return PS[:, b, :]

    def psb16(b):
        return PS[:, b, :].bitcast(BF16)

    o = 0
    O_CF = o; o += CF_N
    O_CB = o; o += CB_N // 2
    O_RSTD = o; o += 2 * 512
    O_LN = o; o += 512
    O_SQ = o; o += 2048
    O_KNY = o; o += 1024
    P_H = o; o += 16384
    P_Z2T = o; o += 8192
    P_T0 = o; o += 4096
    P_T1 = o; o += 4096
    P_RAW = o; o += 2052
    P_TT = o; o += 2048
    P_TMP = o; o += 2048
    P_KST = o; o += 1024
    P_WIN = o; o += 2048
    P_PAN = o; o += 4096
    P_X = o; o += 1024
    P_Y = o; o += 256
    assert o <= SBUF_WORDS, o

    CF = V(O_CF, CF_N)
    CB = V(O_CB, CB_N // 2, BF16)
    IDENT = CB[:, CB_ID:CB_ID + 128]
    ONES = CB[:, CB_ONES:CB_ONES + 128]
    JREV = CB[:, CB_JREV:CB_JREV + 128]
    E00 = CB[:, CB_E00:CB_E00 + 128]
    SGNC = CB[:, CB_SGN:CB_SGN + 1]
    RSTD = [V(O_RSTD + 512 * i, 512) for i in range(2)]
    LNB = V(O_LN, 512)
    SQ = V(O_SQ, 2048, BF16, "p (j s) -> p j s", j=8)
    KNY = V(O_KNY, 1024, BF16)

    def gcol(layer, gi, j):
        c = CF_G + (layer * 4 + gi) * 8 + j
        return CF[:, c:c + 1]

    PAN = [V(P_PAN + 2048 * i, 2048, BF16, "p (a k j) -> p a k j", a=2, k=16) for i in range(2)]

    sch.op("sp", lambda e: e.dma_start(out=CF, in_=cf_d), w=["CF"], slot="cf")
    sch.op("sp", lambda e: e.dma_start(out=CB, in_=cb_d), w=["CB"], slot="cb")

    F0 = P_Z2T
    ZF = V(F0, 2048)
    HB = [V(F0 + 2048, 2048), V(F0 + 4096, 2048)]
    H3 = V(F0 + 6144, 1024, BF16)
    WF16 = V(F0 + 7168, 1024, BF16)
    WOF = V(F0 + 8192, 4096)
    WSUM = V(F0 + 12288, 1024, BF16)
    WDIF = V(F0 + 13312, 1024, BF16)
    ARG = V(F0 + 14336, 512)
    ADL = V(F0 + 14848, 1024)
    DEC = V(F0 + 15872, 1024)
    HYB = V(F0 + 16896, 2048)
    TROW = V(F0 + 18944, 512)
    ARG2 = V(F0 + 19456, 512)
    assert F0 + 19968 <= P_TMP
    sch.op("sp", lambda e: e.dma_start(out=ZF[0:33, :], in_=zf_d), w=["ZF"], slot="zf")
    sch.op("sp", lambda e: e.dma_start(out=WOF[0:64, :], in_=wout_f_d), w=["WOF"], slot="wof")
    sch.op("sp", lambda e: e.dma_start(out=ADL, in_=adl_d.partition_broadcast(128)), w=["ADL"], slot="adl")
    sch.op("sp", lambda e: e.dma_start(out=HYB[0:1, :], in_=hyb_d), w=["HYB"], slot="hyb")

    for l in range(3):
        sch.op("dve", lambda e, l=l: e.tensor_tensor(out=CF[0:64, CF_FBF + l:CF_FBF + l + 1],
                                                     in0=CF[0:64, CF_FB + l:CF_FB + l + 1],
                                                     in1=CF[0:64, CF_FB + 3:CF_FB + 4], op=ALU.mult),
               r=["CF"], w=[("fbf", l)])
    sch.op("dve", lambda e: e.tensor_tensor(out=WSUM[0:64, :], in0=WOF[0:64, 0:2048], in1=WOF[0:64, 2048:4096], op=ALU.add),
           r=["WOF"], w=["WSUM"])
    sch.op("dve", lambda e: e.tensor_tensor(out=WDIF[0:64, :], in0=WOF[0:64, 2048:4096], in1=WOF[0:64, 0:2048], op=ALU.subtract),
           r=["WOF"], w=["WDIF"])
    sch.op("dve", lambda e: e.tensor_copy(out=WF16[0:64, :], in_=WOF[0:64, 0:2048]), r=["WOF"], w=["WF16"])

    FWOFF = [CF_FW1, CF_FW2, CF_FW3]
    FK = [33, 64, 64]
    fcnt = 0
    for l in range(3):
        src = ZF if l == 0 else HB[(l - 1) % 2]
        srckey = "ZF" if l == 0 else ("HB", (l - 1) % 2)
        for sc in range(4):
            bank = 5 + (fcnt % 2)
            fcnt += 1
            sch.op("pe", lambda e, l=l, sc=sc, bank=bank, src=src: e.matmul(
                psb(bank)[0:64, :], CF[0:FK[l], FWOFF[l]:FWOFF[l] + 64], src[0:FK[l], sc * 512:(sc + 1) * 512],
                start=True, stop=True), r=["CF", (srckey, sc) if l else "ZF"], w=[("ps", bank)])
            sch.op("act", lambda e, l=l, bank=bank: e.activation(
                out=ARG[0:64, :], in_=psb(bank)[0:64, :], func=AF.Identity,
                scale=CF[0:64, CF_FB + 3:CF_FB + 4], bias=CF[0:64, CF_FBF + l:CF_FBF + l + 1]),
                r=[("ps", bank), ("fbf", l), "CF"], w=["ARG"])
            sch.op("dve", lambda e: e.tensor_scalar(out=ARG2[0:64, :], in0=ARG[0:64, :], scalar1=PI, scalar2=2 * PI,
                                                    op0=ALU.is_gt, op1=ALU.mult), r=["ARG"], w=["ARG2"])
            sch.op("dve", lambda e: e.tensor_tensor(out=ARG[0:64, :], in0=ARG[0:64, :], in1=ARG2[0:64, :], op=ALU.subtract),
                   r=["ARG", "ARG2"], w=["ARG"])
            sch.op("dve", lambda e: e.tensor_scalar(out=ARG2[0:64, :], in0=ARG[0:64, :], scalar1=-PI, scalar2=2 * PI,
                                                    op0=ALU.is_lt, op1=ALU.mult), r=["ARG"], w=["ARG2"])
            sch.op("dve", lambda e: e.tensor_tensor(out=ARG[0:64, :], in0=ARG[0:64, :], in1=ARG2[0:64, :], op=ALU.add),
                   r=["ARG", "ARG2"], w=["ARG"])
            if l < 2:
                dst = HB[l % 2][0:64, sc * 512:(sc + 1) * 512]
                dkey = (("HB", l % 2), sc)
            else:
                dst = H3[0:64, sc * 512:(sc + 1) * 512]
                dkey = ("H3", sc)
            sch.op("act", lambda e, dst=dst: e.activation(out=dst, in_=ARG[0:64, :], func=AF.Sin),
                   r=["ARG"], w=[dkey])


    ABv = [[V(P_H + 8192 * sl + 4096 * w_, 4096, BF16, "p (t c) -> p t c", t=16) for w_ in range(2)] for sl in range(2)]
    kcnt = [0]
    panel_seq = []
    for _p in range(4):
        panel_seq += [(dftK, m) for m in range(7, -1, -1)]
    for _h in range(2):
        for _c in range(2):
            panel_seq += [(dftF, m) for m in range(NT)]
            panel_seq += [(dftI, m) for m in range(NT)]
    pst = {"use": 0, "issued": 0}

    def _issue_panel():
        i = pst["issued"]
        if i >= len(panel_seq):
            return
        src_d, m = panel_seq[i]
        sl = i % 2
        pst["issued"] += 1
        sch.op("sp", lambda e, sl=sl, m=m, src_d=src_d: e.dma_start(out=PAN[sl], in_=src_d[m]), w=[("PAN", sl)], slot=("pan", sl))

    def load_panel(src_d, m):
        i = pst["use"]
        assert panel_seq[i][1] == m
        while pst["issued"] <= i:
            _issue_panel()
        if i == 0:
            _issue_panel()
        pst["use"] += 1
        return i % 2

    def filt_steps(pss, st):
        return [lambda: filt_gen_tile(pss, st, 0), lambda: filt_gen_tile(pss, st, 1)]

    def filt_gen_tile(pss, st, only=None):
        od, hf = pss // 2, pss % 2
        sl = pss % 2
        A_, B_ = ABv[sl]
        c0 = od * 1024 + hf * 512
        if only in (None, 0):
            dsl = st % 2
            sch.op("act", lambda e, st=st, hf=hf, dsl=dsl: e.activation(out=DEC[:, 512 * dsl:512 * dsl + 512], in_=ADL[:, hf * 512:(hf + 1) * 512], func=AF.Exp,
                                                                       scale=CF[:, CF_TNEG + st:CF_TNEG + st + 1]),
                   r=["ADL", "CF"], w=[("DEC", dsl)])
        dsl = st % 2
        for which in ((0, 1) if only is None else (only,)):
            bank = 5 + (kcnt[0] % 2)
            kcnt[0] += 1
            wsrc = WSUM if which == 0 else WDIF
            dst = (A_, B_)[which]
            sch.op("pe", lambda e, st=st, bank=bank, wsrc=wsrc, c0=c0: e.matmul(
                psb(bank), H3[0:64, 256 * (st % 8) + st // 8:256 * (st % 8) + st // 8 + 255:2], wsrc[0:64, c0:c0 + 512],
                start=True, stop=True), r=[("H3", (st % 8) // 2), "WSUM", "WDIF"], w=[("ps", bank)])
            sch.op("dve", lambda e, st=st, bank=bank, dst=dst, dsl=dsl: e.tensor_tensor(
                out=dst[:, st, :], in0=psb(bank), in1=DEC[:, 512 * dsl:512 * dsl + 512], op=ALU.mult),
                r=[("ps", bank), ("DEC", dsl)], w=[("AB", sl, which, st)])
        if st == 0 and only in (None, 1):
            bank = 5 + (kcnt[0] % 2)
            kcnt[0] += 1
            sch.op("pe", lambda e, bank=bank, c0=c0: e.matmul(
                psb(bank)[0:1, :], H3[0:64, 0:1], WF16[0:64, c0:c0 + 512], start=True, stop=True),
                r=[("H3", 0), "WF16"], w=[("ps", bank)])
            sch.op("dve", lambda e, bank=bank, c0=c0: e.tensor_tensor(
                out=TROW[0:1, :], in0=psb(bank)[0:1, :], in1=HYB[0:1, c0:c0 + 512], op=ALU.add),
                r=[("ps", bank), "HYB"], w=["TROW"])
            sch.op("dve", lambda e, A_=A_: e.tensor_copy(out=A_[0:1, 0, :], in_=TROW[0:1, :]),
                   r=["TROW"], w=[("AB", sl, 0, 0)])
            sch.op("dve", lambda e, B_=B_: e.tensor_scalar(
                out=B_[0:1, 0, :], in0=TROW[0:1, :], scalar1=-1.0, scalar2=None, op0=ALU.mult),
                r=["TROW"], w=[("AB", sl, 1, 0)])

    KB0 = P_TMP
    STGP = [V(KB0 + 512 * i, 512, BF16, "p (a c) -> p a c", a=2) for i in range(2)]
    STGM = [V(KB0 + 1024 + 512 * i, 512, BF16, "p (a c) -> p a c", a=2) for i in range(3)]
    STGR = [V(KB0 + 2560 + 512 * i, 512, BF16, "p (a c) -> p a c", a=2) for i in range(2)]
    OSB = [V(KB0 + 3584 + 512 * i, 512) for i in range(2)]
    SPEC = V(KB0 + 4608, 512, BF16, "p (a c) -> p a c", a=2)
    assert KB0 + 5120 <= P_PAN
    S11 = 2.0 ** -11
    for st in range(NT):
        filt_gen_tile(0, st)
    sch.op("dve", lambda e: e.memset(SPEC, 0.0), w=["SPEC"])
    gcount = [0]
    for pss in range(4):
        od, hf = pss // 2, pss % 2
        sl = pss % 2
        A_, B_ = ABv[sl]
        c0 = od * 1024 + hf * 512
        for a_, (src_, k0) in enumerate(((A_, 0), (B_, 8))):
            bk = 4 if a_ == 0 else 7
            for k in range(8):
                sch.op("pe", lambda e, k=k, k0=k0, src_=src_, bk=bk: e.matmul(
                    psb(bk)[0:1, :], SGNC, src_[:, k0 + k, :], start=(k == 0), stop=(k == 7)),
                    r=["CB", ("AB", sl, a_, k0 + k)], w=[("ps", bk)])
            sch.op("act", lambda e, a_=a_, bk=bk: e.activation(out=SPEC[0:1, a_, :], in_=psb(bk)[0:1, :], func=AF.Copy, scale=S11),
                   r=[("ps", bk)], w=["SPEC"])
        prev_m = None
        pending = None

        def emit_rev(pend):
            m_, ss_, sm_, prev_t, prev_keys = pend
            for a_ in range(2):
                bk = 4 if a_ == 0 else 7
                sch.op("pe", lambda e, a_=a_, bk=bk: e.matmul(psb(bk), JREV, STGM[sm_][:, a_, :], start=True, stop=False),
                       r=["CB", ("STGM", sm_, a_)], w=[("ps", bk)])
                sch.op("pe", lambda e, a_=a_, bk=bk: e.matmul(psb(bk), E00, prev_t[:, a_, :], start=False, stop=True),
                       r=["CB"] + prev_keys, w=[("ps", bk)])
                sch.op("act", lambda e, a_=a_, bk=bk: e.activation(out=STGR[ss_][:, a_, :], in_=psb(bk), func=AF.Copy),
                       r=[("ps", bk)], w=[("STGR", ss_, a_)])
            sch.op("act", lambda e, od_=od, hf_=hf: e.dma_start(out=kco_d[od_, 15 - m_, :, :, hf_ * 512:(hf_ + 1) * 512], in_=STGR[ss_]),
                   r=[("STGR", ss_, 0), ("STGR", ss_, 1)], w=[("kco", od, 15 - m_, hf)], slot=("kst_outr", ss_))

        fq = []
        if pss + 1 < 4:
            for st_ in range(NT):
                fq += filt_steps(pss + 1, st_)
        for m in range(7, -1, -1):
            psl = load_panel(dftK, m)
            g = gcount[0]
            gcount[0] += 1
            ss = g % 2
            sm = g % 3
            for a_, src_ in enumerate((A_, B_)):
                be, bo = 2 * a_, 2 * a_ + 1
                for k in range(8):
                    sch.op("pe", lambda e, k=k, psl=psl, be=be, a_=a_, src_=src_: e.matmul(
                        psb(be), PAN[psl][:, a_, k, :], src_[:, k, :], start=(k == 0), stop=(k == 7)),
                        r=[("PAN", psl), ("AB", sl, a_, k)], w=[("ps", be)])
                if fq:
                    fq.pop(0)()
                for k in range(8, 16):
                    sch.op("pe", lambda e, k=k, psl=psl, bo=bo, a_=a_, src_=src_: e.matmul(
                        psb(bo), PAN[psl][:, a_, k, :], src_[:, k, :], start=(k == 8), stop=(k == 15)),
                        r=[("PAN", psl), ("AB", sl, a_, k)], w=[("ps", bo)])
                if fq:
                    fq.pop(0)()
                if a_ == 1:
                    _issue_panel()
                    if pending is not None:
                        emit_rev(pending)
                        pending = None
                sch.op("act", lambda e, bo=bo, a_=a_: e.activation(out=OSB[a_], in_=psb(bo), func=AF.Copy, scale=S11),
                       r=[("ps", bo)], w=[("OSB", a_)])
                sch.op("dve", lambda e, be=be, a_=a_, ss=ss: e.scalar_tensor_tensor(
                    out=STGP[ss][:, a_, :], in0=psb(be), scalar=S11, in1=OSB[a_], op0=ALU.mult, op1=ALU.add),
                    r=[("ps", be), ("OSB", a_)], w=[("STGP", ss, a_)])
                if a_ == 0:
                    sch.op("dve", lambda e, be=be, sm=sm: e.scalar_tensor_tensor(
                        out=STGM[sm][:, 0, :], in0=psb(be), scalar=S11, in1=OSB[0], op0=ALU.mult, op1=ALU.subtract),
                        r=[("ps", be), ("OSB", 0)], w=[("STGM", sm, 0)])
                else:
                    sch.op("dve", lambda e, be=be, sm=sm: e.scalar_tensor_tensor(
                        out=STGM[sm][:, 1, :], in0=psb(be), scalar=-S11, in1=OSB[1], op0=ALU.mult, op1=ALU.add),
                        r=[("ps", be), ("OSB", 1)], w=[("STGM", sm, 1)])
            if m == 0:
                sch.op("dve", lambda e, ss=ss: e.tensor_scalar(out=STGP[ss][0:1, 0, :], in0=STGP[ss][0:1, 0, :],
                                                               scalar1=0.5, scalar2=None, op0=ALU.mult),
                       r=[("STGP", ss, 0)], w=[("STGP", ss, 0)])
                sch.op("dve", lambda e, ss=ss: e.memset(STGP[ss][0:1, 1, :], 0.0), w=[("STGP", ss, 1)])
                sch.op("dve", lambda e, sm=sm, c0=c0: e.tensor_scalar(out=KNY[0:1, c0:c0 + 512], in0=STGM[sm][0:1, 0, :],
                                                                      scalar1=0.5, scalar2=None, op0=ALU.mult),
                       r=[("STGM", sm, 0)], w=[("KNY", pss)])
            sch.op("act", lambda e, ss=ss, od=od, m=m, hf=hf: e.dma_start(
                out=kco_d[od, m, :, :, hf * 512:(hf + 1) * 512], in_=STGP[ss]),
                r=[("STGP", ss, 0), ("STGP", ss, 1)], w=[("kco", od, m, hf)], slot=("kst_out", ss))
            prev_t = SPEC if prev_m is None else STGM[prev_m]
            prev_keys = ["SPEC"] if prev_m is None else [("STGM", prev_m, 0), ("STGM", prev_m, 1)]
            pending = (m, ss, sm, prev_t, prev_keys)
            prev_m = sm
        while fq:
            fq.pop(0)()
        emit_rev(pending)

    if stage == 0:
        sch.op("sp", lambda e: e.dma_start(out=outT[0:128, 0:1024].bitcast(BF16).rearrange("p (a c) -> p a c", a=2),
                                           in_=kco_d[0, 1, :, :, :]), r=[("kco", 0, 1, 0), ("kco", 0, 1, 1)], w=["out"], slot="out")
        sch.op("sp", lambda e: e.dma_start(out=outT[128:256, 0:1024].bitcast(BF16).rearrange("p (a c) -> p a c", a=2),
                                           in_=kco_d[1, 0, :, :, :]), r=[("kco", 1, 0, 0), ("kco", 1, 0, 1)], w=["out2"], slot="out")
        sch.op("sp", lambda e: e.dma_start(out=outT[256:257, 0:16], in_=outT[257:258, 0:16]), r=["out", "out2"], slot="fin")
        sch.emit(nc, stack)
        return nc, stack


    sch.fence_all()
    HNT = V(P_H, 8192, BF16, "p (j s) -> p j s", j=8)
    YA = V(P_H + 8192, 4096, BF16, "p (t c) -> p t c", t=16)
    YB = V(P_H + 12288, 4096, BF16, "p (t c) -> p t c", t=16)
    Z2T = V(P_Z2T, 8192, BF16, "p (j s) -> p j s", j=8)
    T0 = V(P_T0, 4096, BF16, "p (t c) -> p t c", t=16)
    T1 = V(P_T1, 4096, BF16, "p (t c) -> p t c", t=16)
    RAW = V(P_RAW, 2052)
    TT = V(P_TT, 2048)
    TMP = [V(P_TMP + 512 * i, 512) for i in range(4)]
    KST = [V(P_KST + 512 * i, 512, BF16, "p (a c) -> p a c", a=2) for i in range(2)]
    WIN = [V(P_WIN + 1024 * i, 1024, BF16, "p (j c) -> p j c", j=8) for i in range(2)]
    XC = [V(P_T0 + 4096 * i, 4096, F32, "p (j s) -> p j s", j=8) for i in range(2)]
    H = V(P_H, 16384, F32, "p (j s) -> p j s", j=8)
    MB = V(P_PAN, 4096, F32, "p (j s) -> p j s", j=8)
    xT_v = xT.rearrange("(j p) s -> p j s", p=128)
    outT_v = outT.rearrange("(j p) s -> p j s", p=128)
    cnt = {"mb": 0, "rs": 0, "win": 0, "kst": 0, "pq": 0, "py": 0}

    def mbank():
        b = 6 + (cnt["mb"] % 2)
        cnt["mb"] += 1
        return b

    def norm_chunk(src, src_keys, layer, gi, dst_of_j, dst_keys_of_j):
        rs = cnt["rs"] % 2
        cnt["rs"] += 1
        sch.op("act", lambda e: e.activation(out=SQ, in_=src, func=AF.Square), r=src_keys, w=[("SQj", j) for j in range(8)])
        bank = mbank()
        for j in range(8):
            sch.op("pe", lambda e, j=j, bank=bank: e.matmul(psb(bank), ONES, SQ[:, j, :], start=(j == 0), stop=(j == 7)),
                   r=[("SQj", j), "CB"], w=[("ps", bank)])
        sch.op("act", lambda e, bank=bank: e.activation(out=LNB, in_=psb(bank), func=AF.Ln, scale=1.0 / D, bias=EPSC),
               r=[("ps", bank), "EPSC"], w=["LNB"])
        sch.op("act", lambda e, rs=rs: e.activation(out=RSTD[rs], in_=LNB, func=AF.Exp, scale=-0.5), r=["LNB"], w=[("RSTD", rs)])
        for j in range(8):
            sch.op("dve", lambda e, j=j, rs=rs: e.scalar_tensor_tensor(
                out=dst_of_j(j), in0=src[:, j, :], scalar=gcol(layer, gi, j), in1=RSTD[rs], op0=ALU.mult, op1=ALU.mult),
                r=src_keys + [("RSTD", rs), "CF"], w=dst_keys_of_j(j))

    def branch_finish(sc, m_keys_ready, layer, gi, res_src, res_keys, tok0):
        rs = cnt["rs"] % 2
        cnt["rs"] += 1
        bank = mbank()
        for j in range(8):
            sch.op("pe", lambda e, j=j, bank=bank: e.matmul(psb(bank), ONES, SQ[:, j, :], start=(j == 0), stop=(j == 7)),
                   r=[("SQj", j), "CB"], w=[("ps", bank)])
        sch.op("act", lambda e, bank=bank: e.activation(out=LNB, in_=psb(bank), func=AF.Ln, scale=1.0 / D, bias=EPSC),
               r=[("ps", bank), "EPSC"], w=["LNB"])
        sch.op("act", lambda e, rs=rs: e.activation(out=RSTD[rs], in_=LNB, func=AF.Exp, scale=-0.5), r=["LNB"], w=[("RSTD", rs)])
        for j in range(8):
            sch.op("dve", lambda e, j=j, rs=rs: e.scalar_tensor_tensor(
                out=MB[:, j, :], in0=MB[:, j, :], scalar=gcol(layer, gi, j), in1=RSTD[rs], op0=ALU.mult, op1=ALU.mult),
                r=[("MB", j), ("RSTD", rs), "CF"], w=[("MB", j)])
            sch.op("dve", lambda e, j=j: e.tensor_tensor(
                out=H[:, j, tok0:tok0 + 512], in0=res_src(j), in1=MB[:, j, :], op=ALU.add),
                r=[("MB", j)] + res_keys(j), w=[("H", j, tok0 // 512)])

    def evac_m(bank, j):
        sch.op("act", lambda e: e.activation(out=MB[:, j, :], in_=psb(bank), func=AF.Copy), r=[("ps", bank)], w=[("MB", j)])
        sch.op("act", lambda e: e.activation(out=SQ[:, j, :], in_=psb(bank), func=AF.Square), r=[("ps", bank)], w=[("SQj", j)])

    EPSC = V(O_LN + 0, 512)[:, 0:1] if False else CF[:, CF_FBF + 3:CF_FBF + 4]
    sch.op("dve", lambda e: e.memset(EPSC, EPS), r=["CF"], w=["EPSC"])

    for sc in range(4):
        xs = sc % 2
        sch.op("sp", lambda e, sc=sc, xs=xs: e.dma_start(out=XC[xs], in_=xT_v[:, :, sc * 512:(sc + 1) * 512]),
               w=[("XC", xs)], slot=("xc", xs))
        norm_chunk(XC[xs], [("XC", xs)], 0, 0,
                   lambda j, sc=sc: HNT[:, j, sc * 512:(sc + 1) * 512], lambda j, sc=sc: [("HNT", sc)])

    sch.op("dve", lambda e: e.memset(RAW[:, 0:1], 0.0), w=["RAWpad"])
    sch.op("dve", lambda e: e.memset(RAW[:, 2049:2050], 0.0), w=["RAWpad2"])
    w_in_v = w_in_d.rearrange("(j p) n -> p j n", p=128)

    def inproj(strm, hf):
        c30 = strm * 1024 + hf * 512
        for sb in range(2):
            ws = cnt["win"] % 2
            cnt["win"] += 1
            sch.op("pool", lambda e, ws=ws, c=c30 + 256 * sb: e.dma_start(out=WIN[ws], in_=w_in_v[:, :, c:c + 256]),
                   w=[("WIN", ws)], slot=("win", ws))
            for q2 in range(2):
                q4 = sb * 2 + q2
                q = c30 // 128 + q4
                for sc in range(4):
                    bank = mbank()
                    for j in range(8):
                        sch.op("pe", lambda e, j=j, ws=ws, q2=q2, sc=sc, bank=bank: e.matmul(
                            psb(bank), WIN[ws][:, j, q2 * 128:(q2 + 1) * 128], HNT[:, j, sc * 512:(sc + 1) * 512],
                            start=(j == 0), stop=(j == 7)), r=[("WIN", ws), ("HNT", sc)], w=[("ps", bank)])
                    sch.op("act", lambda e, sc=sc, bank=bank: e.activation(out=RAW[:, 1 + sc * 512:1 + (sc + 1) * 512], in_=psb(bank), func=AF.Copy),
                           r=[("ps", bank)], w=[("RAW", sc)])
                rawk = [("RAW", i) for i in range(4)] + ["RAWpad", "RAWpad2"]
                sch.op("act", lambda e, q=q: e.activation(out=TT, in_=RAW[:, 0:2048], func=AF.Identity,
                                                          scale=CF[:, CF_CW + q:CF_CW + q + 1], bias=CF[:, CF_CW + 72 + q:CF_CW + 72 + q + 1]),
                       r=rawk + ["CF"], w=["TT"])
                sch.op("dve", lambda e, q=q: e.scalar_tensor_tensor(out=TT, in0=RAW[:, 1:2049], scalar=CF[:, CF_CW + 24 + q:CF_CW + 24 + q + 1],
                                                                    in1=TT, op0=ALU.mult, op1=ALU.add), r=rawk + ["TT", "CF"], w=["TT"])
                sch.op("dve", lambda e, q=q, zc=hf * 4 + q4: e.scalar_tensor_tensor(
                    out=Z2T[:, zc, :], in0=RAW[:, 2:2050], scalar=CF[:, CF_CW + 48 + q:CF_CW + 48 + q + 1],
                    in1=TT, op0=ALU.mult, op1=ALU.add), r=rawk + ["TT", "CF"], w=[("Z2T", hf * 4 + q4)])

    def transp_to(hf, T, Tn):
        for q4 in range(4):
            for stg in range(2):
                bank = mbank()
                pv = psb16(bank)
                for i in range(8):
                    st = stg * 8 + i
                    sch.op("pe", lambda e, i=i, st=st, q4=q4, pv=pv: e.transpose(
                        pv[:, i * 128:(i + 1) * 128], Z2T[:, hf * 4 + q4, st * 128:(st + 1) * 128], IDENT),
                        r=[("Z2T", hf * 4 + q4), "CB"], w=[("ps", bank)])
                sch.op("act", lambda e, pv=pv, stg=stg, q4=q4: e.activation(
                    out=T[:, stg * 8:(stg + 1) * 8, q4 * 128:(q4 + 1) * 128], in_=pv.rearrange("p (i c) -> p i c", i=8), func=AF.Copy),
                    r=[("ps", bank)], w=[(Tn, st_) for st_ in range(stg * 8, stg * 8 + 8)])

    def transp_back(hf, T, Tn):
        for q4 in range(4):
            for stg in range(2):
                bank = mbank()
                pv = psb16(bank)
                for i in range(8):
                    st = stg * 8 + i
                    sch.op("pe", lambda e, i=i, st=st, q4=q4, pv=pv: e.transpose(
                        pv[:, i * 128:(i + 1) * 128], T[:, st, q4 * 128:(q4 + 1) * 128], IDENT),
                        r=[(Tn, st), "CB"], w=[("ps", bank)])
                sch.op("act", lambda e, pv=pv, stg=stg, q4=q4: e.activation(
                    out=Z2T[:, hf * 4 + q4, stg * 1024:(stg + 1) * 1024], in_=pv, func=AF.Copy),
                    r=[("ps", bank)], w=[("Z2T", hf * 4 + q4)])

    def fwd_dft(T, Tn, od, hf):
        c0 = od * 1024 + hf * 512
        for ft in range(NT):
            psl = load_panel(dftF, ft)
            ks = cnt["kst"] % 2
            cnt["kst"] += 1
            sch.op("sp", lambda e, ks=ks, ft=ft: e.dma_start(out=KST[ks], in_=kco_d[od, ft, :, :, hf * 512:(hf + 1) * 512]),
                   r=[("kco", od, ft, hf)], w=[("KST", ks)], slot=("kst", ks))
            pq = cnt["pq"] % 2
            cnt["pq"] += 1
            bp, bq = 2 * pq, 2 * pq + 1
            for st in range(NT):
                sch.op("pe", lambda e, st=st, psl=psl, bp=bp: e.matmul(
                    psb(bp), PAN[psl][:, 0, st, :], T[:, st, :], start=(st == 0), stop=(st == NT - 1)),
                    r=[("PAN", psl), (Tn, st)], w=[("ps", bp)])
            for st in range(NT):
                sch.op("pe", lambda e, st=st, psl=psl, bq=bq: e.matmul(
                    psb(bq), PAN[psl][:, 1, st, :], T[:, st, :], start=(st == 0), stop=(st == NT - 1)),
                    r=[("PAN", psl), (Tn, st)], w=[("ps", bq)])
            _issue_panel()
            Ka, Kb = KST[ks][:, 0, :], KST[ks][:, 1, :]
            kk = [("KST", ks)]
            tt = sch.op
            tt("dve", lambda e, bp=bp, Ka=Ka: e.tensor_tensor(out=TMP[0], in0=psb(bp), in1=Ka, op=ALU.mult), r=[("ps", bp)] + kk, w=[("TMP", 0)])
            tt("dve", lambda e, bq=bq, Kb=Kb: e.tensor_tensor(out=TMP[1], in0=psb(bq), in1=Kb, op=ALU.mult), r=[("ps", bq)] + kk, w=[("TMP", 1)])
            tt("dve", lambda e, ft=ft: e.tensor_tensor(out=YA[:, ft, :], in0=TMP[0], in1=TMP[1], op=ALU.add),
               r=[("TMP", 0), ("TMP", 1)], w=[("YA", ft)])
            tt("dve", lambda e, bq=bq, Ka=Ka: e.tensor_tensor(out=TMP[2], in0=psb(bq), in1=Ka, op=ALU.mult), r=[("ps", bq)] + kk, w=[("TMP", 2)])
            tt("dve", lambda e, bp=bp, Kb=Kb: e.tensor_tensor(out=TMP[3], in0=psb(bp), in1=Kb, op=ALU.mult), r=[("ps", bp)] + kk, w=[("TMP", 3)])
            tt("dve", lambda e, ft=ft: e.tensor_tensor(out=YB[:, ft, :], in0=TMP[2], in1=TMP[3], op=ALU.subtract),
               r=[("TMP", 2), ("TMP", 3)], w=[("YB", ft)])
            if ft == 0:
                tt("dve", lambda e, bq=bq: e.tensor_tensor(out=YB[0:1, 0, :], in0=psb(bq)[0:1, :], in1=KNY[0:1, c0:c0 + 512], op=ALU.mult),
                   r=[("ps", bq), ("KNY", od * 2 + hf)], w=[("YB", 0)])

    def inv_dft(T, Tn):
        for tt_ in range(NT):
            psl = load_panel(dftI, tt_)
            by = 4 + (cnt["py"] % 2)
            cnt["py"] += 1
            for ft in range(NT):
                sch.op("pe", lambda e, ft=ft, psl=psl, by=by: e.matmul(
                    psb(by), PAN[psl][:, 0, ft, :], YA[:, ft, :], start=(ft == 0), stop=False),
                    r=[("PAN", psl), ("YA", ft)], w=[("ps", by)])
                sch.op("pe", lambda e, ft=ft, psl=psl, by=by: e.matmul(
                    psb(by), PAN[psl][:, 1, ft, :], YB[:, ft, :], start=False, stop=(ft == NT - 1)),
                    r=[("PAN", psl), ("YB", ft)], w=[("ps", by)])
            _issue_panel()
            sch.op("dve", lambda e, tt_=tt_, by=by: e.tensor_tensor(out=T[:, tt_, :], in0=psb(by), in1=T[:, tt_, :], op=ALU.mult),
                   r=[("ps", by), (Tn, tt_)], w=[(Tn, tt_)])

    for hf in range(2):
        inproj(0, hf)
        transp_to(hf, T0, "T0")
        inproj(1, hf)
        fwd_dft(T0, "T0", 0, hf)
        transp_to(hf, T1, "T1")
        inv_dft(T1, "T1")
        inproj(2, hf)
        fwd_dft(T1, "T1", 1, hf)
        transp_to(hf, T0, "T0")
        inv_dft(T0, "T0")
        transp_back(hf, T0, "T0")

    sch.fence_all()
    WO = V(P_RAW, 4096, BF16, "p (j d) -> p j d", j=8)
    sch.op("pool", lambda e: e.dma_start(out=WO, in_=w_out_d.rearrange("(j p) d -> p j d", p=128)), w=["WO"], slot="wo")
    XC2 = [V(P_T0 + 4096 * i, 4096, F32, "p (j s) -> p j s", j=8) for i in range(2)]
    for sc in range(4):
        xs = sc % 2
        sch.op("sp", lambda e, sc=sc, xs=xs: e.dma_start(out=XC2[xs], in_=xT_v[:, :, sc * 512:(sc + 1) * 512]),
               w=[("XC2", xs)], slot=("xc2", xs))
        for jd in range(8):
            bank = mbank()
            for cj in range(8):
                sch.op("pe", lambda e, cj=cj, jd=jd, sc=sc, bank=bank: e.matmul(
                    psb(bank), WO[:, cj, jd * 128:(jd + 1) * 128], Z2T[:, cj, sc * 512:(sc + 1) * 512],
                    start=(cj == 0), stop=(cj == 7)), r=["WO", ("Z2T", cj)], w=[("ps", bank)])
            evac_m(bank, jd)
        branch_finish(sc, None, 0, 1, lambda j, xs=xs: XC2[xs][:, j, :], lambda j, xs=xs: [("XC2", xs)], sc * 512)

    def write_out():
        for j in range(8):
            sch.op("sp", lambda e, j=j: e.dma_start(out=outT_v[:, j, :], in_=H[:, j, :]),
                   r=[("H", j, i) for i in range(4)], w=[("out", j)], slot=("out", j % 4))
        sch.op("sp", lambda e: e.dma_start(out=kco_d[0, 0, 0:1, 0, 0:8], in_=kco_d[0, 0, 1:2, 0, 0:8]),
               r=[("out", j) for j in range(8)], slot="fin")

    if stage == 1:
        write_out()
        sch.emit(nc, stack)
        return nc, stack


    def mlp(layer):
        sch.fence_all()
        tag = "L%d" % layer
        HNC = V(P_RAW, 4096, BF16, "p (j s) -> p j s", j=8)
        ACTB = V(P_Z2T, 16384, BF16, "p (f s) -> p f s", f=32)
        WU = [V(o_, 1024, BF16, "p (j c) -> p j c", j=8) for o_ in (P_TMP, P_TMP + 1024, P_X)]
        WD = [V(P_KST + 1024 * i, 1024, BF16, "p (f d) -> p f d", f=4) for i in range(3)]
        RL = [V(O_KNY + 256 * i, 256, BF16) for i in range(2)]
        wu_v = w_up_d[layer].rearrange("(j p) f -> p j f", p=128)
        wd_v = w_down_d[layer].rearrange("(f p) d -> p f d", p=128)
        c = {"wu": 0, "wd": 0, "rl": 0, "ub": 0}
        def _norm(tc):
            t0 = tc * 1024
            for sc2 in range(2):
                tok = t0 + sc2 * 512
                norm_chunk(H[:, :, tok:tok + 512], [("H", j, tok // 512) for j in range(8)], layer, 2,
                           lambda j, sc2=sc2: HNC[:, j, sc2 * 512:(sc2 + 1) * 512], lambda j, sc2=sc2: [(tag + "HNC", sc2)])

        def _up(tc):
            t0 = tc * 1024
            for slab in range(16):
                ws = c["wu"] % 3
                c["wu"] += 1
                sch.op("pool", lambda e, ws=ws, slab=slab: e.dma_start(out=WU[ws], in_=wu_v[:, :, slab * 256:(slab + 1) * 256]),
                       w=[(tag + "WU", ws)], slot=(tag + "wu", ws))
                for q2 in range(2):
                    ffc = slab * 2 + q2
                    for sc2 in range(2):
                        bank = 4 + (c["ub"] % 2)
                        c["ub"] += 1
                        for j in range(8):
                            sch.op("pe", lambda e, j=j, ws=ws, q2=q2, sc2=sc2, bank=bank: e.matmul(
                                psb(bank), WU[ws][:, j, q2 * 128:(q2 + 1) * 128], HNC[:, j, sc2 * 512:(sc2 + 1) * 512],
                                start=(j == 0), stop=(j == 7)), r=[(tag + "WU", ws), (tag + "HNC", sc2)], w=[("ps", bank)])
                        rl = c["rl"] % 2
                        c["rl"] += 1
                        sch.op("act", lambda e, bank=bank, rl=rl: e.activation(out=RL[rl], in_=psb(bank), func=AF.Relu),
                               r=[("ps", bank)], w=[(tag + "RL", rl)])
                        sch.op("dve", lambda e, rl=rl, ffc=ffc, sc2=sc2: e.tensor_tensor(
                            out=ACTB[:, ffc, sc2 * 512:(sc2 + 1) * 512], in0=RL[rl], in1=RL[rl], op=ALU.mult),
                            r=[(tag + "RL", rl)], w=[(tag + "ACT", ffc, sc2)])

        def _down(tc):
            t0 = tc * 1024
            for sc2 in range(2):
                tok = t0 + sc2 * 512
                for jh in range(2):
                    for slab in range(8):
                        ws = c["wd"] % 3
                        c["wd"] += 1
                        sch.op("pool", lambda e, ws=ws, slab=slab, jh=jh: e.dma_start(
                            out=WD[ws], in_=wd_v[:, slab * 4:(slab + 1) * 4, jh * 512:(jh + 1) * 512]),
                            w=[(tag + "WD", ws)], slot=(tag + "wd", ws))
                        for f4 in range(4):
                            ffc = slab * 4 + f4
                            for jq in range(4):
                                sch.op("pe", lambda e, ws=ws, f4=f4, jq=jq, ffc=ffc, sc2=sc2: e.matmul(
                                    psb(jq), WD[ws][:, f4, jq * 128:(jq + 1) * 128], ACTB[:, ffc, sc2 * 512:(sc2 + 1) * 512],
                                    start=(ffc == 0), stop=(ffc == 31)), r=[(tag + "WD", ws), (tag + "ACT", ffc, sc2)], w=[("ps", jq)])
                    for jq in range(4):
                        evac_m(jq, jh * 4 + jq)
                branch_finish(None, None, layer, 3, lambda j, tok=tok: H[:, j, tok:tok + 512],
                              lambda j, tok=tok: [("H", j, tok // 512)], tok)


        _norm(0)
        _up(0)
        _norm(1)
        _down(0)
        _up(1)
        _down(1)

    mlp(0)
    if stage == 2:
        write_out()
        sch.emit(nc, stack)
        return nc, stack


    sch.fence_all()
    HNT2 = V(P_Z2T, 8192, BF16, "p (j s) -> p j s", j=8)
    QKT = V(P_T0, 12288, BF16, "p (c s) -> p c s", c=12)
    VTOK = V(P_TMP, 2112, BF16, "p (t g e) -> p t g e", t=16, g=4)
    WQ = [V(P_WIN + 1024 * i, 1024, BF16, "p (j c) -> p j c", j=8) for i in range(2)]
    ATC = V(P_PAN, 2368, BF16)
    MASK = ATC[:, 0:384]
    PERMR = ATC[:, 384:512]
    PERMH = ATC[:, 512:640]
    COS = ATC[:, 640:2688]
    SIN = ATC[:, 2688:4736]
    PTS = [V(P_PAN + 2368 + 192 * i, 192, BF16) for i in range(8)]
    ESK = V(P_PAN + 3904, 16)
    RDN = V(P_PAN + 3920, 16)
    XB = [V(O_KNY + 256 * i, 256, BF16) for i in range(2)] + [V(P_Y, 256, BF16)]
    XS = [V(O_KNY + 512 + 256 * i, 256, BF16) for i in range(2)]
    wqkv_v = wqkv_d.rearrange("(j p) n -> p j n", p=128)
    sch.op("sp", lambda e: e.dma_start(out=ATC, in_=atc_d), w=["ATC"], slot="atc")
    sch.op("sp", lambda e: e.dma_start(out=ESK, in_=sink_d.partition_broadcast(128)), w=["ESK"], slot="esk")
    sch.op("act", lambda e: e.activation(out=ESK, in_=ESK, func=AF.Exp), r=["ESK"], w=["ESK"])
    for sc in range(4):
        norm_chunk(H[:, :, sc * 512:(sc + 1) * 512], [("H", j, sc) for j in range(8)], 1, 0,
                   lambda j, sc=sc: HNT2[:, j, sc * 512:(sc + 1) * 512], lambda j, sc=sc: [("HNT2", sc)])
    ac = {"wq": 0, "xb": 0, "pt": 0, "sb": 0, "pv": 0}
    KZ = [[QKT[:, 8, :], QKT[:, 9, :]], [QKT[:, 10, :], QKT[:, 11, :]],
          [V(O_SQ, 1024, BF16), V(O_SQ + 1024, 1024, BF16)], [V(O_RSTD, 1024, BF16), V(P_X, 1024, BF16)]]
    sch.retire([("SQj", j) for j in range(8)] + [("RSTD", 0), ("RSTD", 1)],
               [("KZ", g_, v_, s_) for g_ in (2, 3) for v_ in range(2) for s_ in range(4)] + [("KZz", g_, v_) for g_ in (2, 3) for v_ in range(2)])
    for g_ in range(4):
        sch.op("dve", lambda e, g_=g_: e.memset(KZ[g_][0][64:128, :], 0.0), w=[("KZz", g_, 0)])
        sch.op("dve", lambda e, g_=g_: e.memset(KZ[g_][1][0:64, :], 0.0), w=[("KZz", g_, 1)])
    dst_chunk = [0, 1, 2, 3, 4, 5, 6, 7, 8, 10]
    rb = [0]

    def rbank():
        b_ = 4 + (rb[0] % 2)
        rb[0] += 1
        return b_

    def proj_unit(ws, q2, sc, xb):
        bank = mbank()
        for j in range(8):
            sch.op("pe", lambda e, j=j: e.matmul(
                psb(bank), WQ[ws][:, j, q2 * 128:(q2 + 1) * 128], HNT2[:, j, sc * 512:(sc + 1) * 512],
                start=(j == 0), stop=(j == 7)), r=[("WQ", ws), ("HNT2", sc)], w=[("ps", bank)])
        sch.op("act", lambda e: e.activation(out=XB[xb], in_=psb(bank), func=AF.Copy),
               r=[("ps", bank)], w=[("XB", xb)])

    def rope_unit(dc, sc, xb, xs):
        bank2 = rbank()
        sch.op("pe", lambda e: e.matmul(psb(bank2), PERMR, XB[xb], start=True, stop=True),
               r=[("XB", xb), "ATC"], w=[("ps", bank2)])
        sch.op("dve", lambda e: e.tensor_tensor(out=XS[xs], in0=psb(bank2), in1=SIN[:, sc * 512:(sc + 1) * 512], op=ALU.mult),
               r=[("ps", bank2), "ATC"], w=[("XS", xs)])
        sch.op("dve", lambda e: e.tensor_tensor(out=XB[xb], in0=XB[xb], in1=COS[:, sc * 512:(sc + 1) * 512], op=ALU.mult),
               r=[("XB", xb), "ATC"], w=[("XB", xb)])
        if dc < 8:
            sch.op("dve", lambda e: e.tensor_tensor(out=QKT[:, dc, sc * 512:(sc + 1) * 512], in0=XB[xb], in1=XS[xs], op=ALU.add),
                   r=[("XB", xb), ("XS", xs)], w=[("QKT", dc, sc)])
        else:
            g0, g1 = (0, 1) if dc == 8 else (2, 3)
            cs = slice(sc * 512, (sc + 1) * 512)
            sch.op("dve", lambda e: e.tensor_tensor(out=XS[xs], in0=XB[xb], in1=XS[xs], op=ALU.add),
                   r=[("XB", xb), ("XS", xs)], w=[("XS", xs)])
            sch.op("act", lambda e: e.activation(out=KZ[g0][0][0:64, cs], in_=XS[xs][0:64, :], func=AF.Copy),
                   r=[("XS", xs)], w=[("KZ", g0, 0, sc)])
            sch.op("act", lambda e: e.activation(out=KZ[g1][1][64:128, cs], in_=XS[xs][64:128, :], func=AF.Copy),
                   r=[("XS", xs)], w=[("KZ", g1, 1, sc)])
            bank3 = rbank()
            sch.op("pe", lambda e: e.matmul(psb(bank3), PERMH, XS[xs], start=True, stop=True),
                   r=[("XS", xs), "ATC"], w=[("ps", bank3)])
            sch.op("act", lambda e: e.activation(out=KZ[g1][0][0:64, cs], in_=psb(bank3)[0:64, :], func=AF.Copy),
                   r=[("ps", bank3)], w=[("KZ", g1, 0, sc)])
            sch.op("act", lambda e: e.activation(out=KZ[g0][1][64:128, cs], in_=psb(bank3)[64:128, :], func=AF.Copy),
                   r=[("ps", bank3)], w=[("KZ", g0, 1, sc)])

    pend_rope = None
    ucount = 0
    for slab in range(5):
        ws = ac["wq"] % 2
        ac["wq"] += 1
        sch.op("pool", lambda e, ws=ws, slab=slab: e.dma_start(out=WQ[ws], in_=wqkv_v[:, :, slab * 256:(slab + 1) * 256]),
               w=[("WQ", ws)], slot=("wq", ws))
        for q2 in range(2):
            dc = dst_chunk[slab * 2 + q2]
            for sc in range(4):
                xb = ucount % 3
                xs = ucount % 2
                ucount += 1
                proj_unit(ws, q2, sc, xb)
                if pend_rope is not None:
                    rope_unit(*pend_rope)
                pend_rope = (dc, sc, xb, xs)
    rope_unit(*pend_rope)
    ws = ac["wq"] % 2
    ac["wq"] += 1
    sch.op("pool", lambda e, ws=ws: e.dma_start(out=WQ[ws], in_=wqkv_v[:, :, 1280:1536]), w=[("WQ", ws)], slot=("wq", ws))
    sch.op("dve", lambda e: e.memset(VTOK[:, :, :, 64:66], 1.0), w=["VONE"])
    for st in range(NT):
        bank = mbank()
        for j in range(8):
            sch.op("pe", lambda e, j=j, ws=ws, st=st, bank=bank: e.matmul(
                psb(bank)[:, 0:256], HNT2[:, j, st * 128:(st + 1) * 128], WQ[ws][:, j, :],
                start=(j == 0), stop=(j == 7)), r=[("WQ", ws), ("HNT2", st // 4)], w=[("ps", bank)])
        sch.op("act", lambda e, bank=bank, st=st: e.activation(
            out=VTOK[:, st, :, 0:64], in_=psb(bank)[:, 0:256].rearrange("p (g e) -> p g e", g=4), func=AF.Copy),
            r=[("ps", bank)], w=[("VTOK", st)])

    OTOK = V(P_Z2T, 8192, BF16, "p (t c) -> p t c", t=16)
    sch.retire([("HNT2", i) for i in range(4)], [("OTOK", i) for i in range(16)])

    NSLOT = 8
    LAG = 3
    tidx = lambda h, j: h * NT + j

    def pv(h, i):
        g = h // 4
        kbs = [kb for kb in (i - 1, i, i + 1) if 0 <= kb < NT]
        pb = ac["pv"] % 4
        ac["pv"] += 1
        for n_, kb in enumerate(kbs):
            qlo = max(kb - 1, 0)
            slot = tidx(h, kb) % NSLOT
            off = (i - qlo) * 128
            sch.op("pe", lambda e, slot=slot, off=off, kb=kb, pb=pb, n_=n_: e.matmul(
                psb(pb)[:, 0:65], PTS[slot][:, off:off + 128], VTOK[:, kb, g, 0:65],
                start=(n_ == 0), stop=(n_ == len(kbs) - 1)),
                r=[("PT", slot), ("VTOK", kb), "VONE"], w=[("ps", pb)])
        rd = ac["pv"] % 2
        sch.op("dve", lambda e, pb=pb, h=h, rd=rd: e.tensor_scalar(out=RDN[:, 2 * rd:2 * rd + 1], in0=psb(pb)[:, 64:65], scalar1=ESK[:, h:h + 1],
                                                               scalar2=None, op0=ALU.add), r=[("ps", pb), "ESK"], w=[("RDN", rd)])
        sch.op("dve", lambda e, rd=rd: e.reciprocal(out=RDN[:, 2 * rd + 1:2 * rd + 2], in_=RDN[:, 2 * rd:2 * rd + 1]), r=[("RDN", rd)], w=[("RDN2", rd)])
        sch.op("dve", lambda e, pb=pb, h=h, i=i, rd=rd: e.tensor_scalar(out=OTOK[:, i, h * 64:(h + 1) * 64], in0=psb(pb)[:, 0:64],
                                                                       scalar1=RDN[:, 2 * rd + 1:2 * rd + 2], scalar2=None, op0=ALU.mult),
               r=[("ps", pb), ("RDN2", rd)], w=[("OTOK", i)])

    def scores(h, j):
        g = h // 4
        v = h % 2
        qc = h // 2
        qlo, qhi = max(j - 1, 0), min(j + 1, NT - 1)
        nq = qhi - qlo + 1
        bank = 4 + (ac["sb"] % 4)
        ac["sb"] += 1
        slot = tidx(h, j) % NSLOT
        sch.op("pe", lambda e: e.matmul(
            psb(bank)[:, 0:nq * 128], KZ[g][v][:, j * 128:(j + 1) * 128], QKT[:, qc, qlo * 128:(qhi + 1) * 128],
            start=True, stop=False),
            r=[("KZ", g, v, j // 4), ("KZz", g, v)] + [("QKT", qc, s_) for s_ in range(qlo // 4, qhi // 4 + 1)], w=[("ps", bank)])
        m0 = 128 if j == 0 else 0
        sch.op("pe", lambda e: e.matmul(psb(bank)[:, 0:nq * 128], IDENT, MASK[:, m0:m0 + nq * 128], start=False, stop=True),
               r=["ATC", "CB"], w=[("ps", bank)])
        sch.op("act", lambda e: e.activation(
            out=PTS[slot][:, 0:nq * 128], in_=psb(bank)[:, 0:nq * 128], func=AF.Exp, scale=0.125),
            r=[("ps", bank)], w=[("PT", slot)])

    pv_tasks = []
    for h in range(16):
        for i in range(NT):
            pv_tasks.append((tidx(h, min(i + 1, NT - 1)), h, i))
    pvi = 0
    for t in range(16 * NT + LAG):
        if t < 16 * NT:
            scores(t // NT, t % NT)
        while pvi < len(pv_tasks) and pv_tasks[pvi][0] + LAG <= t:
            pv(pv_tasks[pvi][1], pv_tasks[pvi][2])
            pvi += 1
    assert pvi == len(pv_tasks)

    OT = V(P_T0, 8192, BF16, "p (j s) -> p j s", j=8)
    sch.retire([("QKT", c, s_) for c in range(8) for s_ in range(4)] + [("KZ", g_, v_, s_) for g_ in range(2) for v_ in range(2) for s_ in range(4)]
               + [("KZz", g_, v_) for g_ in range(2) for v_ in range(2)], [("OT", j) for j in range(8)] + ["WO2"])
    WO2 = V(P_RAW, 4096, BF16, "p (j d) -> p j d", j=8)
    sch.op("pool", lambda e: e.dma_start(out=WO2, in_=wo2_d.rearrange("(j p) d -> p j d", p=128)), w=["WO2"], slot="wo2")
    for cj in range(8):
        for stg in range(2):
            bank = mbank()
            pv_ = psb16(bank)
            for i in range(8):
                st = stg * 8 + i
                sch.op("pe", lambda e, i=i, st=st, cj=cj, pv_=pv_: e.transpose(
                    pv_[:, i * 128:(i + 1) * 128], OTOK[:, st, cj * 128:(cj + 1) * 128], IDENT),
                    r=[("OTOK", st), "CB"], w=[("ps", bank)])
            sch.op("act", lambda e, pv_=pv_, stg=stg, cj=cj: e.activation(
                out=OT[:, cj, stg * 1024:(stg + 1) * 1024], in_=pv_, func=AF.Copy), r=[("ps", bank)], w=[("OT", cj)])
    sch.fence_all()
    for sc in range(4):
        for jd in range(8):
            bank = mbank()
            for cj in range(8):
                sch.op("pe", lambda e, cj=cj, jd=jd, sc=sc, bank=bank: e.matmul(
                    psb(bank), WO2[:, cj, jd * 128:(jd + 1) * 128], OT[:, cj, sc * 512:(sc + 1) * 512],
                    start=(cj == 0), stop=(cj == 7)), r=["WO2", ("OT", cj)], w=[("ps", bank)])
            evac_m(bank, jd)
        branch_finish(sc, None, 1, 1, lambda j, sc=sc: H[:, j, sc * 512:(sc + 1) * 512], lambda j, sc=sc: [("H", j, sc)], sc * 512)
    if stage == 3:
        write_out()
        sch.emit(nc, stack)
        return nc, stack
    mlp(1)
    write_out()
    sch.emit(nc, stack)
    return nc, stack


def make_in_maps(inp):
    cf, cb, zf = _host_consts(inp)
    fwd, inv, dk = _dft_tables()
    adl = _absdelta().reshape(1, D)
    hyb = np.ascontiguousarray(np.asarray(inp["hy_bias"], np.float32)[0].reshape(1, 2 * D))
    x = np.asarray(inp["x"], np.float32)
    common = {
        "cf": cf, "cb": cb, "zf": zf, "adl": adl, "hyb": hyb, "dftF": fwd, "dftI": inv, "dftK": dk,
        "hy_w_in": np.ascontiguousarray(np.asarray(inp["hy_w_in"], np.float32)[0]),
        "hy_f_wout": np.ascontiguousarray(np.asarray(inp["hy_f_wout"], np.float32)[0]),
        "hy_w_out": np.ascontiguousarray(np.asarray(inp["hy_w_out"], np.float32)[0]),
        "w_up": np.ascontiguousarray(np.asarray(inp["w_up"], np.float32)),
        "w_down": np.ascontiguousarray(np.asarray(inp["w_down"], np.float32)),
    }
    common["at_w_qkv"] = np.ascontiguousarray(np.asarray(inp["at_w_qkv"], np.float32)[0])
    common["at_w_o"] = np.ascontiguousarray(np.asarray(inp["at_w_o"], np.float32)[0])
    common["sink"] = np.ascontiguousarray(np.asarray(inp["at_sink"], np.float32)[0].reshape(1, 16))
    common["atc"] = _attn_consts()
    maps = []
    for c in range(NCORES):
        m = dict(common)
        m["xT"] = np.ascontiguousarray(x[c].T)
        maps.append(m)
    return maps


_PROG = {}


def kernel(**inputs):
    inp = {k: np.asarray(v) for k, v in inputs.items()}
    if "nc" not in _PROG:
        _PROG["nc"] = build_program(stage=4)
    nc, _stack = _PROG["nc"]
    maps = make_in_maps(inp)
    res = run_bass_kernel_spmd(nc, maps, core_ids=list(range(NCORES)))
    out = np.stack([np.ascontiguousarray(r["outT"].T) for r in res.results], axis=0)
    return out.astype(np.float32)
```

```python
import math
import bisect
import numpy as np
import ml_dtypes
import concourse.bass as bass
import concourse.mybir as mybir
from concourse.bass_utils import run_bass_kernel_spmd

F32 = mybir.dt.float32
BF16 = mybir.dt.bfloat16
AF = mybir.ActivationFunctionType
ALU = mybir.AluOpType

S = 2048
D = 1024
NT = 16
DFF = 4096
EPS = 1e-6
NCORES = 8
PI = math.pi

SBUF_WORDS = 52800


class Sched:
    ENGS = ("pe", "act", "dve", "pool", "sp")

    def __init__(self):
        self.ops = []
        self.lastw = {}
        self.readers = {}

    def op(self, eng, fn, r=(), w=(), slot=None):
        idx = len(self.ops)
        deps = set()
        if getattr(self, "fence", None):
            for k in w:
                if k not in self.known:
                    deps.update(self.fence)
                    self.known.add(k)
        for k in r:
            if k in self.lastw:
                deps.add(self.lastw[k])
        for k in w:
            if k in self.lastw:
                deps.add(self.lastw[k])
            deps.update(self.readers.get(k, ()))
        for k in r:
            self.readers.setdefault(k, []).append(idx)
        for k in w:
            self.lastw[k] = idx
            self.readers[k] = []
        deps.discard(idx)
        self.ops.append(dict(eng=eng, fn=fn, deps=deps, slot=slot, sem=None, val=None, sig=False))
        return idx

    def fence_all(self):
        last = {}
        for i, o in enumerate(self.ops):
            last[(o["eng"], o["slot"])] = i
        self.fence = set(last.values())
        self.known = set(self.lastw.keys()) | set(self.readers.keys())

    def retire(self, old_keys, new_keys):
        acc = set()
        for k in old_keys:
            if k in self.lastw:
                acc.add(self.lastw[k])
            acc.update(self.readers.get(k, ()))
        for k in new_keys:
            self.readers.setdefault(k, []).extend(acc)

    def emit(self, nc, stack, same_engine_sync=("act", "dve", "pool")):
        ops = self.ops
        for i, o in enumerate(ops):
            for d in o["deps"]:
                y = ops[d]
                if y["slot"] is not None:
                    continue
                if y["eng"] == o["eng"] and y["eng"] not in same_engine_sync:
                    continue
                y["sig"] = True
        SEM_MAX = 30000
        eng_sems = {}
        counters = {}
        for e in ("pe", "act", "dve", "pool"):
            eng_sems[e] = []
            counters[e] = SEM_MAX
        slot_sem = {}
        slot_list = {}
        for i, o in enumerate(ops):
            if o["slot"] is not None:
                sl = o["slot"]
                if sl not in slot_sem:
                    slot_sem[sl] = stack.enter_context(nc.semaphore("d_" + str(len(slot_sem))))
                    slot_list[sl] = []
                slot_list[sl].append(i)
                o["sem"] = slot_sem[sl]
                o["val"] = 16 * len(slot_list[sl])
            elif o["sig"]:
                e = o["eng"]
                if counters[e] >= SEM_MAX:
                    eng_sems[e].append(stack.enter_context(nc.semaphore("e_%s_%d" % (e, len(eng_sems[e])))))
                    counters[e] = 0
                counters[e] += 1
                o["sem"] = eng_sems[e][-1]
                o["val"] = counters[e]
        self.nsem = len(slot_sem) + sum(len(v) for v in eng_sems.values())
        block = stack.enter_context(nc.Block())
        per_eng = {e: [] for e in self.ENGS}
        for i, o in enumerate(ops):
            per_eng[o["eng"]].append(i)

        def run(engname, eng):
            waited = {}
            for i in per_eng[engname]:
                o = ops[i]
                need = {}
                for d in o["deps"]:
                    y = ops[d]
                    if y["slot"] is not None:
                        lst = slot_list[y["slot"]]
                        pos = bisect.bisect_left(lst, i)
                        sem, val = y["sem"], 16 * pos
                    else:
                        if y["eng"] == engname and engname not in same_engine_sync:
                            continue
                        sem, val = y["sem"], y["val"]
                    key = id(sem)
                    if key not in need or need[key][1] < val:
                        need[key] = (sem, val)
                for key, (sem, val) in need.items():
                    if waited.get(key, 0) >= val:
                        continue
                    eng.wait_ge(sem, val)
                    waited[key] = val
                ins = o["fn"](eng)
                if o["slot"] is not None:
                    ins.then_inc(o["sem"], 16)
                elif o["sig"]:
                    ins.then_inc(o["sem"], 1)

        @block.tensor
        def _(e):
            run("pe", e)

        @block.scalar
        def _(e):
            run("act", e)

        @block.vector
        def _(e):
            run("dve", e)

        @block.gpsimd
        def _(e):
            run("pool", e)

        @block.sync
        def _(e):
            run("sp", e)


_CONST_CACHE = {}


def _dft_tables():
    if "dft" in _CONST_CACHE:
        return _CONST_CACHE["dft"]
    n = np.arange(S, dtype=np.int64)
    prod = (n[:, None] * n[None, :]) % 4096
    ang = prod.astype(np.float64) * (2.0 * np.pi / 4096.0)
    C = np.cos(ang)
    Sm = np.sin(ang)
    sgn = np.where(n % 2 == 0, 1.0, -1.0)
    Sp = Sm.copy()
    Sp[:, 0] = sgn
    SpT = Sp.T.copy()

    def panelize(M):
        return M.reshape(16, 128, 16, 128).transpose(2, 1, 0, 3)

    fwd = np.stack([panelize(C), panelize(Sp)], axis=2)
    inv = np.stack([panelize(C), panelize(SpT)], axis=2)
    fwd = np.ascontiguousarray(fwd).astype(ml_dtypes.bfloat16)
    inv = np.ascontiguousarray(inv).astype(ml_dtypes.bfloat16)
    kk = np.arange(16)[:, None]
    pp = np.arange(128)[None, :]
    sperm = (2 * (128 * (kk % 8) + pp) + (kk // 8)).reshape(-1)
    f1 = np.arange(1024, dtype=np.int64)
    angk = ((sperm[:, None].astype(np.int64) * f1[None, :]) % 4096).astype(np.float64) * (2.0 * np.pi / 4096.0)

    def panelize_k(M):
        return M.reshape(16, 128, 8, 128).transpose(2, 1, 0, 3)

    dk = np.stack([panelize_k(np.cos(angk)), panelize_k(np.sin(angk))], axis=2)
    dk = np.ascontiguousarray(dk).astype(ml_dtypes.bfloat16)
    _CONST_CACHE["dft"] = (fwd, inv, dk)
    return fwd, inv, dk


def _zfeat():
    L = S
    t = np.linspace(0.0, 1.0, L, dtype=np.float32)[:, None]
    w = (np.float32(2.0 * math.pi / L) * np.arange(L, dtype=np.float32))[:, None]
    f = np.linspace(1e-4, 15.0, 16, dtype=np.float32)[None, :]
    z = np.concatenate([t, np.cos(f * w), -np.sin(f * w)], axis=-1).astype(np.float32)
    return z


def _attn_consts():
    a = np.zeros((128, 4736), np.float32)
    kl = np.arange(128)[:, None]
    ql = np.arange(128)[None, :]
    a[:, 0:128] = np.where(kl <= ql, 0.0, -30000.0)
    a[:, 128:256] = 0.0
    a[:, 256:384] = np.where(ql <= kl, 0.0, -30000.0)
    pr = np.zeros((128, 128), np.float32)
    for hb in (0, 64):
        for i in range(8):
            pr[hb + 8 + i, hb + i] = 1.0
            pr[hb + i, hb + 8 + i] = 1.0
    a[:, 384:512] = pr
    ph = np.zeros((128, 128), np.float32)
    for m in range(128):
        ph[(m + 64) % 128, m] = 1.0
    a[:, 512:640] = ph
    inv = (500000.0 ** (-np.arange(0, 16, 2, dtype=np.float32) / 16.0)).astype(np.float32)
    ang = np.arange(S, dtype=np.float32)[None, :] * inv[:, None]
    cos = np.ones((128, S), np.float32)
    sin = np.zeros((128, S), np.float32)
    for hb in (0, 64):
        cos[hb:hb + 8] = np.cos(ang); cos[hb + 8:hb + 16] = np.cos(ang)
        sin[hb:hb + 8] = -np.sin(ang); sin[hb + 8:hb + 16] = np.sin(ang)
    a[:, 640:2688] = cos
    a[:, 2688:4736] = sin
    return a.astype(ml_dtypes.bfloat16)


def _absdelta():
    max_decay = math.log(1e-2) / 0.3
    min_decay = math.log(1e-2) / 1.5
    deltas = np.linspace(min_decay, max_decay, D, dtype=np.float32)
    return np.abs(deltas).astype(np.float32)


CF_G = 0
CF_CW = 64
CF_FB = 160
CF_TNEG = 164
CF_FW1 = 180
CF_FW2 = 244
CF_FW3 = 308
CF_FBF = 372
CF_N = 384

CB_ID = 0
CB_ONES = 128
CB_JREV = 256
CB_E00 = 384
CB_SGN = 512
CB_N = 640


def _host_consts(inp):
    cf = np.zeros((128, CF_N), np.float32)
    gl = ["norm_mix_pre", "norm_mix_post", "norm_mlp_pre", "norm_mlp_post"]
    for layer in range(2):
        for gi, gname in enumerate(gl):
            v = np.asarray(inp[gname])[layer]
            cf[:, CF_G + (layer * 4 + gi) * 8: CF_G + (layer * 4 + gi) * 8 + 8] = v.reshape(8, 128).T
    cw = np.asarray(inp["hy_conv_w"])[0]
    cbias = np.asarray(inp["hy_conv_b"])[0]
    for k in range(3):
        cf[:, CF_CW + 24 * k: CF_CW + 24 * k + 24] = cw[k].reshape(24, 128).T
    cf[:, CF_CW + 72: CF_CW + 96] = cbias.reshape(24, 128).T
    cf[0:64, CF_FB + 0] = np.asarray(inp["hy_f_b1"])[0]
    cf[0:64, CF_FB + 1] = np.asarray(inp["hy_f_b2"])[0]
    cf[0:64, CF_FB + 2] = np.asarray(inp["hy_f_b3"])[0]
    cf[0:64, CF_FB + 3] = np.asarray(inp["hy_f_freq"])[0]
    t = np.linspace(0.0, 1.0, S, dtype=np.float32)
    for k in range(16):
        sidx = 2 * (128 * (k % 8) + np.arange(128)) + (k // 8)
        cf[:, CF_TNEG + k] = -t[sidx]
    cf[0:33, CF_FW1: CF_FW1 + 64] = np.asarray(inp["hy_f_w1"])[0]
    cf[0:64, CF_FW2: CF_FW2 + 64] = np.asarray(inp["hy_f_w2"])[0]
    cf[0:64, CF_FW3: CF_FW3 + 64] = np.asarray(inp["hy_f_w3"])[0]
    cb = np.zeros((128, CB_N), np.float32)
    cb[:, CB_ID:CB_ID + 128] = np.eye(128)
    cb[:, CB_ONES:CB_ONES + 128] = 1.0
    for q in range(1, 128):
        cb[128 - q, CB_JREV + q] = 1.0
    cb[0, CB_E00] = 1.0
    cb[:, CB_SGN] = np.where(np.arange(128) % 2 == 0, 1.0, -1.0)
    cb = cb.astype(ml_dtypes.bfloat16)
    zf = np.zeros((33, S), np.float32)
    zf[:, :] = _zfeat().T
    return cf, cb, zf


def build_program(stage=4):
    from contextlib import ExitStack
    nc = bass.Bass("TRN2", target_bir_lowering=False)
    dt = nc.dram_tensor
    xT = dt("xT", [D, S], F32, kind="ExternalInput").ap()
    cf_d = dt("cf", [128, CF_N], F32, kind="ExternalInput").ap()
    cb_d = dt("cb", [128, CB_N], BF16, kind="ExternalInput").ap()
    zf_d = dt("zf", [33, S], F32, kind="ExternalInput").ap()
    adl_d = dt("adl", [1, D], F32, kind="ExternalInput").ap()
    hyb_d = dt("hyb", [1, 2 * D], F32, kind="ExternalInput").ap()
    dftF = dt("dftF", [16, 128, 2, 16, 128], BF16, kind="ExternalInput").ap()
    dftI = dt("dftI", [16, 128, 2, 16, 128], BF16, kind="ExternalInput").ap()
    dftK = dt("dftK", [8, 128, 2, 16, 128], BF16, kind="ExternalInput").ap()
    w_in_d = dt("hy_w_in", [D, 3 * D], F32, kind="ExternalInput").ap()
    wout_f_d = dt("hy_f_wout", [64, 4 * D], F32, kind="ExternalInput").ap()
    w_out_d = dt("hy_w_out", [D, D], F32, kind="ExternalInput").ap()
    w_up_d = dt("w_up", [2, D, DFF], F32, kind="ExternalInput").ap()
    w_down_d = dt("w_down", [2, DFF, D], F32, kind="ExternalInput").ap()
    wqkv_d = dt("at_w_qkv", [D, 1536], F32, kind="ExternalInput").ap()
    wo2_d = dt("at_w_o", [D, D], F32, kind="ExternalInput").ap()
    sink_d = dt("sink", [1, 16], F32, kind="ExternalInput").ap()
    atc_d = dt("atc", [128, 4736], BF16, kind="ExternalInput").ap()
    kco_d = dt("kco", [2, 16, 128, 2, D], BF16, kind="Internal").ap()
    outT = dt("outT", [D, S], F32, kind="ExternalOutput").ap()

    stack = ExitStack()
    big = stack.enter_context(nc.sbuf_tensor("big", [128, SBUF_WORDS], F32))
    PS = stack.enter_context(nc.psum_tensor("PS", [128, 8, 512], F32))
    sch = Sched()

    def V(off, words, dtype=F32, pat=None, **kw):
        ap = big[:, off:off + words]
        if dtype != F32:
            ap = ap.bitcast(dtype)
        if pat is not None:
            ap = ap.rearrange(pat, **kw)
        return ap

    def psb(b):
        return PS[:, b, :]

    def psb16(b):
        return PS[:, b, :].bitcast(BF16)

    o = 0
    O_CF = o; o += CF_N
    O_CB = o; o += CB_N // 2
    O_RSTD = o; o += 2 * 512
    O_LN = o; o += 512
    O_SQ = o; o += 2048
    O_KNY = o; o += 1024
    P_H = o; o += 16384
    P_Z2T = o; o += 8192
    P_T0 = o; o += 4096
    P_T1 = o; o += 4096
    P_RAW = o; o += 2052
    P_TT = o; o += 2048
    P_TMP = o; o += 2048
    P_KST = o; o += 1024
    P_WIN = o; o += 2048
    P_PAN = o; o += 4096
    P_X = o; o += 1024
    P_Y = o; o += 256
    assert o <= SBUF_WORDS, o

    CF = V(O_CF, CF_N)
    CB = V(O_CB, CB_N // 2, BF16)
    IDENT = CB[:, CB_ID:CB_ID + 128]
    ONES = CB[:, CB_ONES:CB_ONES + 128]
    JREV = CB[:, CB_JREV:CB_JREV + 128]
    E00 = CB[:, CB_E00:CB_E00 + 128]
    SGNC = CB[:, CB_SGN:CB_SGN + 1]
    RSTD = [V(O_RSTD + 512 * i, 512) for i in range(2)]
    LNB = V(O_LN, 512)
    SQ = V(O_SQ, 2048, BF16, "p (j s) -> p j s", j=8)
    KNY = V(O_KNY, 1024, BF16)

    def gcol(layer, gi, j):
        c = CF_G + (layer * 4 + gi) * 8 + j
        return CF[:, c:c + 1]

    PAN = [V(P_PAN + 2048 * i, 2048, BF16, "p (a k j) -> p a k j", a=2, k=16) for i in range(2)]

    sch.op("sp", lambda e: e.dma_start(out=CF, in_=cf_d), w=["CF"], slot="cf")
    sch.op("sp", lambda e: e.dma_start(out=CB, in_=cb_d), w=["CB"], slot="cb")

    F0 = P_Z2T
    ZF = V(F0, 2048)
    HB = [V(F0 + 2048, 2048), V(F0 + 4096, 2048)]
    H3 = V(F0 + 6144, 1024, BF16)
    WF16 = V(F0 + 7168, 1024, BF16)
    WOF = V(F0 + 8192, 4096)
    WSUM = V(F0 + 12288, 1024, BF16)
    WDIF = V(F0 + 13312, 1024, BF16)
    ARG = V(F0 + 14336, 512)
    ADL = V(F0 + 14848, 1024)
    DEC = V(F0 + 15872, 1024)
    HYB = V(F0 + 16896, 2048)
    TROW = V(F0 + 18944, 512)
    ARG2 = V(F0 + 19456, 512)
    assert F0 + 19968 <= P_TMP
    sch.op("sp", lambda e: e.dma_start(out=ZF[0:33, :], in_=zf_d), w=["ZF"], slot="zf")
    sch.op("sp", lambda e: e.dma_start(out=WOF[0:64, :], in_=wout_f_d), w=["WOF"], slot="wof")
    sch.op("sp", lambda e: e.dma_start(out=ADL, in_=adl_d.partition_broadcast(128)), w=["ADL"], slot="adl")
    sch.op("sp", lambda e: e.dma_start(out=HYB[0:1, :], in_=hyb_d), w=["HYB"], slot="hyb")

    for l in range(3):
        sch.op("dve", lambda e, l=l: e.tensor_tensor(out=CF[0:64, CF_FBF + l:CF_FBF + l + 1],
                                                     in0=CF[0:64, CF_FB + l:CF_FB + l + 1],
                                                     in1=CF[0:64, CF_FB + 3:CF_FB + 4], op=ALU.mult),
               r=["CF"], w=[("fbf", l)])
    sch.op("dve", lambda e: e.tensor_tensor(out=WSUM[0:64, :], in0=WOF[0:64, 0:2048], in1=WOF[0:64, 2048:4096], op=ALU.add),
           r=["WOF"], w=["WSUM"])
    sch.op("dve", lambda e: e.tensor_tensor(out=WDIF[0:64, :], in0=WOF[0:64, 2048:4096], in1=WOF[0:64, 0:2048], op=ALU.subtract),
           r=["WOF"], w=["WDIF"])
    sch.op("dve", lambda e: e.tensor_copy(out=WF16[0:64, :], in_=WOF[0:64, 0:2048]), r=["WOF"], w=["WF16"])

    FWOFF = [CF_FW1, CF_FW2, CF_FW3]
    FK = [33, 64, 64]
    fcnt = 0
    for l in range(3):
        src = ZF if l == 0 else HB[(l - 1) % 2]
        srckey = "ZF" if l == 0 else ("HB", (l - 1) % 2)
        for sc in range(4):
            bank = 5 + (fcnt % 2)
            fcnt += 1
            sch.op("pe", lambda e, l=l, sc=sc, bank=bank, src=src: e.matmul(
                psb(bank)[0:64, :], CF[0:FK[l], FWOFF[l]:FWOFF[l] + 64], src[0:FK[l], sc * 512:(sc + 1) * 512],
                start=True, stop=True), r=["CF", (srckey, sc) if l else "ZF"], w=[("ps", bank)])
            sch.op("act", lambda e, l=l, bank=bank: e.activation(
                out=ARG[0:64, :], in_=psb(bank)[0:64, :], func=AF.Identity,
                scale=CF[0:64, CF_FB + 3:CF_FB + 4], bias=CF[0:64, CF_FBF + l:CF_FBF + l + 1]),
                r=[("ps", bank), ("fbf", l), "CF"], w=["ARG"])
            sch.op("dve", lambda e: e.tensor_scalar(out=ARG2[0:64, :], in0=ARG[0:64, :], scalar1=PI, scalar2=2 * PI,
                                                    op0=ALU.is_gt, op1=ALU.mult), r=["ARG"], w=["ARG2"])
            sch.op("dve", lambda e: e.tensor_tensor(out=ARG[0:64, :], in0=ARG[0:64, :], in1=ARG2[0:64, :], op=ALU.subtract),
                   r=["ARG", "ARG2"], w=["ARG"])
            sch.op("dve", lambda e: e.tensor_scalar(out=ARG2[0:64, :], in0=ARG[0:64, :], scalar1=-PI, scalar2=2 * PI,
                                                    op0=ALU.is_lt, op1=ALU.mult), r=["ARG"], w=["ARG2"])
            sch.op("dve", lambda e: e.tensor_tensor(out=ARG[0:64, :], in0=ARG[0:64, :], in1=ARG2[0:64, :], op=ALU.add),
                   r=["ARG", "ARG2"], w=["ARG"])
            if l < 2:
                dst = HB[l % 2][0:64, sc * 512:(sc + 1) * 512]
                dkey = (("HB", l % 2), sc)
            else:
                dst = H3[0:64, sc * 512:(sc + 1) * 512]
                dkey = ("H3", sc)
            sch.op("act", lambda e, dst=dst: e.activation(out=dst, in_=ARG[0:64, :], func=AF.Sin),
                   r=["ARG"], w=[dkey])


    ABv = [[V(P_H + 8192 * sl + 4096 * w_, 4096, BF16, "p (t c) -> p t c", t=16) for w_ in range(2)] for sl in range(2)]
    kcnt = [0]
    panel_seq = []
    for _p in range(4):
        panel_seq += [(dftK, m) for m in range(7, -1, -1)]
    for _h in range(2):
        for _c in range(2):
            panel_seq += [(dftF, m) for m in range(NT)]
            panel_seq += [(dftI, m) for m in range(NT)]
    pst = {"use": 0, "issued": 0}

    def _issue_panel():
        i = pst["issued"]
        if i >= len(panel_seq):
            return
        src_d, m = panel_seq[i]
        sl = i % 2
        pst["issued"] += 1
        sch.op("sp", lambda e, sl=sl, m=m, src_d=src_d: e.dma_start(out=PAN[sl], in_=src_d[m]), w=[("PAN", sl)], slot=("pan", sl))

    def load_panel(src_d, m):
        i = pst["use"]
        assert panel_seq[i][1] == m
        while pst["issued"] <= i:
            _issue_panel()
        if i == 0:
            _issue_panel()
        pst["use"] += 1
        return i % 2

    def filt_steps(pss, st):
        return [lambda: filt_gen_tile(pss, st, 0), lambda: filt_gen_tile(pss, st, 1)]

    def filt_gen_tile(pss, st, only=None):
        od, hf = pss // 2, pss % 2
        sl = pss % 2
        A_, B_ = ABv[sl]
        c0 = od * 1024 + hf * 512
        if only in (None, 0):
            dsl = st % 2
            sch.op("act", lambda e, st=st, hf=hf, dsl=dsl: e.activation(out=DEC[:, 512 * dsl:512 * dsl + 512], in_=ADL[:, hf * 512:(hf + 1) * 512], func=AF.Exp,
                                                                       scale=CF[:, CF_TNEG + st:CF_TNEG + st + 1]),
                   r=["ADL", "CF"], w=[("DEC", dsl)])
        dsl = st % 2
        for which in ((0, 1) if only is None else (only,)):
            bank = 5 + (kcnt[0] % 2)
            kcnt[0] += 1
            wsrc = WSUM if which == 0 else WDIF
            dst = (A_, B_)[which]
            sch.op("pe", lambda e, st=st, bank=bank, wsrc=wsrc, c0=c0: e.matmul(
                psb(bank), H3[0:64, 256 * (st % 8) + st // 8:256 * (st % 8) + st // 8 + 255:2], wsrc[0:64, c0:c0 + 512],
                start=True, stop=True), r=[("H3", (st % 8) // 2), "WSUM", "WDIF"], w=[("ps", bank)])
            sch.op("dve", lambda e, st=st, bank=bank, dst=dst, dsl=dsl: e.tensor_tensor(
                out=dst[:, st, :], in0=psb(bank), in1=DEC[:, 512 * dsl:512 * dsl + 512], op=ALU.mult),
                r=[("ps", bank), ("DEC", dsl)], w=[("AB", sl, which, st)])
        if st == 0 and only in (None, 1):
            bank = 5 + (kcnt[0] % 2)
            kcnt[0] += 1
            sch.op("pe", lambda e, bank=bank, c0=c0: e.matmul(
                psb(bank)[0:1, :], H3[0:64, 0:1], WF16[0:64, c0:c0 + 512], start=True, stop=True),
                r=[("H3", 0), "WF16"], w=[("ps", bank)])
            sch.op("dve", lambda e, bank=bank, c0=c0: e.tensor_tensor(
                out=TROW[0:1, :], in0=psb(bank)[0:1, :], in1=HYB[0:1, c0:c0 + 512], op=ALU.add),
                r=[("ps", bank), "HYB"], w=["TROW"])
            sch.op("dve", lambda e, A_=A_: e.tensor_copy(out=A_[0:1, 0, :], in_=TROW[0:1, :]),
                   r=["TROW"], w=[("AB", sl, 0, 0)])
            sch.op("dve", lambda e, B_=B_: e.tensor_scalar(
                out=B_[0:1, 0, :], in0=TROW[0:1, :], scalar1=-1.0, scalar2=None, op0=ALU.mult),
                r=["TROW"], w=[("AB", sl, 1, 0)])

    EPSC = CF[:, CF_FBF + 3:CF_FBF + 4]
    sch.op("dve", lambda e: e.memset(EPSC, EPS), r=["CF"], w=["EPSC"])
    HNT = V(P_H, 8192, BF16, "p (j s) -> p j s", j=8)
    XC = [V(P_T0 + 4096 * i, 4096, F32, "p (j s) -> p j s", j=8) for i in range(2)]
    xT_v = xT.rearrange("(j p) s -> p j s", p=128)
    cnt = {"mb": 0, "rs": 0, "win": 0, "kst": 0, "pq": 0, "py": 0}

    def mbank():
        b = 6 + (cnt["mb"] % 2)
        cnt["mb"] += 1
        return b

    def norm_chunk(src, src_keys, layer, gi, dst_of_j, dst_keys_of_j, bank=None):
        rs = cnt["rs"] % 2
        cnt["rs"] += 1
        sch.op("act", lambda e: e.activation(out=SQ, in_=src, func=AF.Square), r=src_keys, w=[("SQj", j) for j in range(8)])
        bank = mbank() if bank is None else bank
        for j in range(8):
            sch.op("pe", lambda e, j=j, bank=bank: e.matmul(psb(bank), ONES, SQ[:, j, :], start=(j == 0), stop=(j == 7)),
                   r=[("SQj", j), "CB"], w=[("ps", bank)])
        sch.op("act", lambda e, bank=bank: e.activation(out=LNB, in_=psb(bank), func=AF.Ln, scale=1.0 / D, bias=EPSC),
               r=[("ps", bank), "EPSC"], w=["LNB"])
        sch.op("act", lambda e, rs=rs: e.activation(out=RSTD[rs], in_=LNB, func=AF.Exp, scale=-0.5), r=["LNB"], w=[("RSTD", rs)])
        for j in range(8):
            sch.op("dve", lambda e, j=j, rs=rs: e.scalar_tensor_tensor(
                out=dst_of_j(j), in0=src[:, j, :], scalar=gcol(layer, gi, j), in1=RSTD[rs], op0=ALU.mult, op1=ALU.mult),
                r=src_keys + [("RSTD", rs), "CF"], w=dst_keys_of_j(j))


    def prenorm_steps():
        sch.retire(["WOF", "WSUM", "WDIF", "ARG", "ARG2", "ADL", ("DEC", 0), ("DEC", 1)], [("XC", 0), ("XC", 1)])
        sch.retire([("AB", 0, w_, st_) for w_ in range(2) for st_ in range(NT)], [("HNT", sc_) for sc_ in range(4)])
        steps = []
        for sc in range(4):
            def _st(sc=sc):
                xs = sc % 2
                sch.op("sp", lambda e: e.dma_start(out=XC[xs], in_=xT_v[:, :, sc * 512:(sc + 1) * 512]),
                       w=[("XC", xs)], slot=("xc", xs))
                norm_chunk(XC[xs], [("XC", xs)], 0, 0,
                           lambda j: HNT[:, j, sc * 512:(sc + 1) * 512], lambda j: [("HNT", sc)], bank=5 + sc % 2)
            steps.append(_st)
        return steps

    KB0 = P_TMP
    STGP = [V(KB0 + 512 * i, 512, BF16, "p (a c) -> p a c", a=2) for i in range(2)]
    STGM = [V(KB0 + 1024 + 512 * i, 512, BF16, "p (a c) -> p a c", a=2) for i in range(3)]
    STGR = [V(KB0 + 2560 + 512 * i, 512, BF16, "p (a c) -> p a c", a=2) for i in range(2)]
    OSB = [V(KB0 + 3584 + 512 * i, 512) for i in range(2)]
    SPEC = V(KB0 + 4608, 512, BF16, "p (a c) -> p a c", a=2)
    assert KB0 + 5120 <= P_PAN
    S11 = 2.0 ** -11
    for st in range(NT):
        filt_gen_tile(0, st)
    sch.op("dve", lambda e: e.memset(SPEC, 0.0), w=["SPEC"])
    gcount = [0]
    for pss in range(4):
        od, hf = pss // 2, pss % 2
        sl = pss % 2
        A_, B_ = ABv[sl]
        c0 = od * 1024 + hf * 512
        for a_, (src_, k0) in enumerate(((A_, 0), (B_, 8))):
            bk = 4 if a_ == 0 else 7
            for k in range(8):
                sch.op("pe", lambda e, k=k, k0=k0, src_=src_, bk=bk: e.matmul(
                    psb(bk)[0:1, :], SGNC, src_[:, k0 + k, :], start=(k == 0), stop=(k == 7)),
                    r=["CB", ("AB", sl, a_, k0 + k)], w=[("ps", bk)])
            sch.op("act", lambda e, a_=a_, bk=bk: e.activation(out=SPEC[0:1, a_, :], in_=psb(bk)[0:1, :], func=AF.Copy, scale=S11),
                   r=[("ps", bk)], w=["SPEC"])
        prev_m = None
        pending = None

        def emit_rev(pend):
            m_, ss_, sm_, prev_t, prev_keys = pend
            for a_ in range(2):
                bk = 4 if a_ == 0 else 7
                sch.op("pe", lambda e, a_=a_, bk=bk: e.matmul(psb(bk), JREV, STGM[sm_][:, a_, :], start=True, stop=False),
                       r=["CB", ("STGM", sm_, a_)], w=[("ps", bk)])
                sch.op("pe", lambda e, a_=a_, bk=bk: e.matmul(psb(bk), E00, prev_t[:, a_, :], start=False, stop=True),
                       r=["CB"] + prev_keys, w=[("ps", bk)])
                sch.op("act", lambda e, a_=a_, bk=bk: e.activation(out=STGR[ss_][:, a_, :], in_=psb(bk), func=AF.Copy),
                       r=[("ps", bk)], w=[("STGR", ss_, a_)])
            sch.op("act", lambda e, od_=od, hf_=hf: e.dma_start(out=kco_d[od_, 15 - m_, :, :, hf_ * 512:(hf_ + 1) * 512], in_=STGR[ss_]),
                   r=[("STGR", ss_, 0), ("STGR", ss_, 1)], w=[("kco", od, 15 - m_, hf)], slot=("kst_outr", ss_))

        fq = []
        if pss + 1 < 4:
            for st_ in range(NT):
                fq += filt_steps(pss + 1, st_)
        else:
            for stp in prenorm_steps():
                fq += [lambda: None] * 5 + [stp]
        for m in range(7, -1, -1):
            psl = load_panel(dftK, m)
            g = gcount[0]
            gcount[0] += 1
            ss = g % 2
            sm = g % 3
            for a_, src_ in enumerate((A_, B_)):
                be, bo = 2 * a_, 2 * a_ + 1
                for k in range(8):
                    sch.op("pe", lambda e, k=k, psl=psl, be=be, a_=a_, src_=src_: e.matmul(
                        psb(be), PAN[psl][:, a_, k, :], src_[:, k, :], start=(k == 0), stop=(k == 7)),
                        r=[("PAN", psl), ("AB", sl, a_, k)], w=[("ps", be)])
                if fq:
                    fq.pop(0)()
                for k in range(8, 16):
                    sch.op("pe", lambda e, k=k, psl=psl, bo=bo, a_=a_, src_=src_: e.matmul(
                        psb(bo), PAN[psl][:, a_, k, :], src_[:, k, :], start=(k == 8), stop=(k == 15)),
                        r=[("PAN", psl), ("AB", sl, a_, k)], w=[("ps", bo)])
                if fq:
                    fq.pop(0)()
                if a_ == 1:
                    _issue_panel()
                    if pending is not None:
                        emit_rev(pending)
                        pending = None
                sch.op("act", lambda e, bo=bo, a_=a_: e.activation(out=OSB[a_], in_=psb(bo), func=AF.Copy, scale=S11),
                       r=[("ps", bo)], w=[("OSB", a_)])
                sch.op("dve", lambda e, be=be, a_=a_, ss=ss: e.scalar_tensor_tensor(
                    out=STGP[ss][:, a_, :], in0=psb(be), scalar=S11, in1=OSB[a_], op0=ALU.mult, op1=ALU.add),
                    r=[("ps", be), ("OSB", a_)], w=[("STGP", ss, a_)])
                if a_ == 0:
                    sch.op("dve", lambda e, be=be, sm=sm: e.scalar_tensor_tensor(
                        out=STGM[sm][:, 0, :], in0=psb(be), scalar=S11, in1=OSB[0], op0=ALU.mult, op1=ALU.subtract),
                        r=[("ps", be), ("OSB", 0)], w=[("STGM", sm, 0)])
                else:
                    sch.op("dve", lambda e, be=be, sm=sm: e.scalar_tensor_tensor(
                        out=STGM[sm][:, 1, :], in0=psb(be), scalar=-S11, in1=OSB[1], op0=ALU.mult, op1=ALU.add),
                        r=[("ps", be), ("OSB", 1)], w=[("STGM", sm, 1)])
            if m == 0:
                sch.op("dve", lambda e, ss=ss: e.tensor_scalar(out=STGP[ss][0:1, 0, :], in0=STGP[ss][0:1, 0, :],
                                                               scalar1=0.5, scalar2=None, op0=ALU.mult),
                       r=[("STGP", ss, 0)], w=[("STGP", ss, 0)])
                sch.op("dve", lambda e, ss=ss: e.memset(STGP[ss][0:1, 1, :], 0.0), w=[("STGP", ss, 1)])
                sch.op("dve", lambda e, sm=sm, c0=c0: e.tensor_scalar(out=KNY[0:1, c0:c0 + 512], in0=STGM[sm][0:1, 0, :],
                                                                      scalar1=0.5, scalar2=None, op0=ALU.mult),
                       r=[("STGM", sm, 0)], w=[("KNY", pss)])
            sch.op("act", lambda e, ss=ss, od=od, m=m, hf=hf: e.dma_start(
                out=kco_d[od, m, :, :, hf * 512:(hf + 1) * 512], in_=STGP[ss]),
                r=[("STGP", ss, 0), ("STGP", ss, 1)], w=[("kco", od, m, hf)], slot=("kst_out", ss))
            prev_t = SPEC if prev_m is None else STGM[prev_m]
            prev_keys = ["SPEC"] if prev_m is None else [("STGM", prev_m, 0), ("STGM", prev_m, 1)]
            pending = (m, ss, sm, prev_t, prev_keys)
            prev_m = sm
        while fq:
            fq.pop(0)()
        emit_rev(pending)

    if stage == 0:
        sch.op("sp", lambda e: e.dma_start(out=outT[0:128, 0:1024].bitcast(BF16).rearrange("p (a c) -> p a c", a=2),
                                           in_=kco_d[0, 1, :, :, :]), r=[("kco", 0, 1, 0), ("kco", 0, 1, 1)], w=["out"], slot="out")
        sch.op("sp", lambda e: e.dma_start(out=outT[128:256, 0:1024].bitcast(BF16).rearrange("p (a c) -> p a c", a=2),
                                           in_=kco_d[1, 0, :, :, :]), r=[("kco", 1, 0, 0), ("kco", 1, 0, 1)], w=["out2"], slot="out")
        sch.op("sp", lambda e: e.dma_start(out=outT[256:257, 0:16], in_=outT[257:258, 0:16]), r=["out", "out2"], slot="fin")
        sch.emit(nc, stack)
        return nc, stack


    sch.fence_all()
    YA = V(P_H + 8192, 4096, BF16, "p (t c) -> p t c", t=16)
    YB = V(P_H + 12288, 4096, BF16, "p (t c) -> p t c", t=16)
    Z2T = V(P_Z2T, 8192, BF16, "p (j s) -> p j s", j=8)
    T0 = V(P_T0, 4096, BF16, "p (t c) -> p t c", t=16)
    T1 = V(P_T1, 4096, BF16, "p (t c) -> p t c", t=16)
    RAW = V(P_RAW, 2052)
    TT = V(P_TT, 2048)
    TMP = [V(P_TMP + 512 * i, 512) for i in range(4)]
    KST = [V(P_KST + 512 * i, 512, BF16, "p (a c) -> p a c", a=2) for i in range(2)]
    WIN = [V(P_WIN + 1024 * i, 1024, BF16, "p (j c) -> p j c", j=8) for i in range(2)]
    H = V(P_H, 16384, F32, "p (j s) -> p j s", j=8)
    MB = V(P_PAN, 4096, F32, "p (j s) -> p j s", j=8)
    outT_v = outT.rearrange("(j p) s -> p j s", p=128)
    def branch_finish(sc, m_keys_ready, layer, gi, res_src, res_keys, tok0):
        rs = cnt["rs"] % 2
        cnt["rs"] += 1
        bank = mbank()
        for j in range(8):
            sch.op("pe", lambda e, j=j, bank=bank: e.matmul(psb(bank), ONES, SQ[:, j, :], start=(j == 0), stop=(j == 7)),
                   r=[("SQj", j), "CB"], w=[("ps", bank)])
        sch.op("act", lambda e, bank=bank: e.activation(out=LNB, in_=psb(bank), func=AF.Ln, scale=1.0 / D, bias=EPSC),
               r=[("ps", bank), "EPSC"], w=["LNB"])
        sch.op("act", lambda e, rs=rs: e.activation(out=RSTD[rs], in_=LNB, func=AF.Exp, scale=-0.5), r=["LNB"], w=[("RSTD", rs)])
        for j in range(8):
            sch.op("dve", lambda e, j=j, rs=rs: e.scalar_tensor_tensor(
                out=MB[:, j, :], in0=MB[:, j, :], scalar=gcol(layer, gi, j), in1=RSTD[rs], op0=ALU.mult, op1=ALU.mult),
                r=[("MB", j), ("RSTD", rs), "CF"], w=[("MB", j)])
            sch.op("dve", lambda e, j=j: e.tensor_tensor(
                out=H[:, j, tok0:tok0 + 512], in0=res_src(j), in1=MB[:, j, :], op=ALU.add),
                r=[("MB", j)] + res_keys(j), w=[("H", j, tok0 // 512)])

    def evac_m(bank, j):
        sch.op("act", lambda e: e.activation(out=MB[:, j, :], in_=psb(bank), func=AF.Copy), r=[("ps", bank)], w=[("MB", j)])
        sch.op("act", lambda e: e.activation(out=SQ[:, j, :], in_=psb(bank), func=AF.Square), r=[("ps", bank)], w=[("SQj", j)])


    sch.op("dve", lambda e: e.memset(RAW[:, 0:1], 0.0), w=["RAWpad"])
    sch.op("dve", lambda e: e.memset(RAW[:, 2049:2050], 0.0), w=["RAWpad2"])
    w_in_v = w_in_d.rearrange("(j p) n -> p j n", p=128)

    def inproj(strm, hf):
        c30 = strm * 1024 + hf * 512
        for sb in range(2):
            ws = cnt["win"] % 2
            cnt["win"] += 1
            sch.op("pool", lambda e, ws=ws, c=c30 + 256 * sb: e.dma_start(out=WIN[ws], in_=w_in_v[:, :, c:c + 256]),
                   w=[("WIN", ws)], slot=("win", ws))
            for q2 in range(2):
                q4 = sb * 2 + q2
                q = c30 // 128 + q4
                for sc in range(4):
                    bank = mbank()
                    for j in range(8):
                        sch.op("pe", lambda e, j=j, ws=ws, q2=q2, sc=sc, bank=bank: e.matmul(
                            psb(bank), WIN[ws][:, j, q2 * 128:(q2 + 1) * 128], HNT[:, j, sc * 512:(sc + 1) * 512],
                            start=(j == 0), stop=(j == 7)), r=[("WIN", ws), ("HNT", sc)], w=[("ps", bank)])
                    sch.op("act", lambda e, sc=sc, bank=bank: e.activation(out=RAW[:, 1 + sc * 512:1 + (sc + 1) * 512], in_=psb(bank), func=AF.Copy),
                           r=[("ps", bank)], w=[("RAW", sc)])
                rawk = [("RAW", i) for i in range(4)] + ["RAWpad", "RAWpad2"]
                sch.op("act", lambda e, q=q: e.activation(out=TT, in_=RAW[:, 0:2048], func=AF.Identity,
                                                          scale=CF[:, CF_CW + q:CF_CW + q + 1], bias=CF[:, CF_CW + 72 + q:CF_CW + 72 + q + 1]),
                       r=rawk + ["CF"], w=["TT"])
                sch.op("dve", lambda e, q=q: e.scalar_tensor_tensor(out=TT, in0=RAW[:, 1:2049], scalar=CF[:, CF_CW + 24 + q:CF_CW + 24 + q + 1],
                                                                    in1=TT, op0=ALU.mult, op1=ALU.add), r=rawk + ["TT", "CF"], w=["TT"])
                sch.op("dve", lambda e, q=q, zc=hf * 4 + q4: e.scalar_tensor_tensor(
                    out=Z2T[:, zc, :], in0=RAW[:, 2:2050], scalar=CF[:, CF_CW + 48 + q:CF_CW + 48 + q + 1],
                    in1=TT, op0=ALU.mult, op1=ALU.add), r=rawk + ["TT", "CF"], w=[("Z2T", hf * 4 + q4)])

    def transp_to(hf, T, Tn):
        for q4 in range(4):
            for stg in range(2):
                bank = mbank()
                pv = psb16(bank)
                for i in range(8):
                    st = stg * 8 + i
                    sch.op("pe", lambda e, i=i, st=st, q4=q4, pv=pv: e.transpose(
                        pv[:, i * 128:(i + 1) * 128], Z2T[:, hf * 4 + q4, st * 128:(st + 1) * 128], IDENT),
                        r=[("Z2T", hf * 4 + q4), "CB"], w=[("ps", bank)])
                sch.op("act", lambda e, pv=pv, stg=stg, q4=q4: e.activation(
                    out=T[:, stg * 8:(stg + 1) * 8, q4 * 128:(q4 + 1) * 128], in_=pv.rearrange("p (i c) -> p i c", i=8), func=AF.Copy),
                    r=[("ps", bank)], w=[(Tn, st_) for st_ in range(stg * 8, stg * 8 + 8)])

    def transp_back(hf, T, Tn):
        for q4 in range(4):
            for stg in range(2):
                bank = mbank()
                pv = psb16(bank)
                for i in range(8):
                    st = stg * 8 + i
                    sch.op("pe", lambda e, i=i, st=st, q4=q4, pv=pv: e.transpose(
                        pv[:, i * 128:(i + 1) * 128], T[:, st, q4 * 128:(q4 + 1) * 128], IDENT),
                        r=[(Tn, st), "CB"], w=[("ps", bank)])
                sch.op("act", lambda e, pv=pv, stg=stg, q4=q4: e.activation(
                    out=Z2T[:, hf * 4 + q4, stg * 1024:(stg + 1) * 1024], in_=pv, func=AF.Copy),
                    r=[("ps", bank)], w=[("Z2T", hf * 4 + q4)])

    def fwd_dft(T, Tn, od, hf):
        c0 = od * 1024 + hf * 512
        for ft in range(NT):
            psl = load_panel(dftF, ft)
            ks = cnt["kst"] % 2
            cnt["kst"] += 1
            sch.op("sp", lambda e, ks=ks, ft=ft: e.dma_start(out=KST[ks], in_=kco_d[od, ft, :, :, hf * 512:(hf + 1) * 512]),
                   r=[("kco", od, ft, hf)], w=[("KST", ks)], slot=("kst", ks))
            pq = cnt["pq"] % 2
            cnt["pq"] += 1
            bp, bq = 2 * pq, 2 * pq + 1
            for st in range(NT):
                sch.op("pe", lambda e, st=st, psl=psl, bp=bp: e.matmul(
                    psb(bp), PAN[psl][:, 0, st, :], T[:, st, :], start=(st == 0), stop=(st == NT - 1)),
                    r=[("PAN", psl), (Tn, st)], w=[("ps", bp)])
            for st in range(NT):
                sch.op("pe", lambda e, st=st, psl=psl, bq=bq: e.matmul(
                    psb(bq), PAN[psl][:, 1, st, :], T[:, st, :], start=(st == 0), stop=(st == NT - 1)),
                    r=[("PAN", psl), (Tn, st)], w=[("ps", bq)])
            _issue_panel()
            Ka, Kb = KST[ks][:, 0, :], KST[ks][:, 1, :]
            kk = [("KST", ks)]
            tt = sch.op
            tt("dve", lambda e, bp=bp, Ka=Ka: e.tensor_tensor(out=TMP[0], in0=psb(bp), in1=Ka, op=ALU.mult), r=[("ps", bp)] + kk, w=[("TMP", 0)])
            tt("dve", lambda e, bq=bq, Kb=Kb: e.tensor_tensor(out=TMP[1], in0=psb(bq), in1=Kb, op=ALU.mult), r=[("ps", bq)] + kk, w=[("TMP", 1)])
            tt("dve", lambda e, ft=ft: e.tensor_tensor(out=YA[:, ft, :], in0=TMP[0], in1=TMP[1], op=ALU.add),
               r=[("TMP", 0), ("TMP", 1)], w=[("YA", ft)])
            tt("dve", lambda e, bq=bq, Ka=Ka: e.tensor_tensor(out=TMP[2], in0=psb(bq), in1=Ka, op=ALU.mult), r=[("ps", bq)] + kk, w=[("TMP", 2)])
            tt("dve", lambda e, bp=bp, Kb=Kb: e.tensor_tensor(out=TMP[3], in0=psb(bp), in1=Kb, op=ALU.mult), r=[("ps", bp)] + kk, w=[("TMP", 3)])
            tt("dve", lambda e, ft=ft: e.tensor_tensor(out=YB[:, ft, :], in0=TMP[2], in1=TMP[3], op=ALU.subtract),
               r=[("TMP", 2), ("TMP", 3)], w=[("YB", ft)])
            if ft == 0:
                tt("dve", lambda e, bq=bq: e.tensor_tensor(out=YB[0:1, 0, :], in0=psb(bq)[0:1, :], in1=KNY[0:1, c0:c0 + 512], op=ALU.mult),
                   r=[("ps", bq), ("KNY", od * 2 + hf)], w=[("YB", 0)])

    def inv_dft(T, Tn):
        for tt_ in range(NT):
            psl = load_panel(dftI, tt_)
            by = 4 + (cnt["py"] % 2)
            cnt["py"] += 1
            for ft in range(NT):
                sch.op("pe", lambda e, ft=ft, psl=psl, by=by: e.matmul(
                    psb(by), PAN[psl][:, 0, ft, :], YA[:, ft, :], start=(ft == 0), stop=False),
                    r=[("PAN", psl), ("YA", ft)], w=[("ps", by)])
                sch.op("pe", lambda e, ft=ft, psl=psl, by=by: e.matmul(
                    psb(by), PAN[psl][:, 1, ft, :], YB[:, ft, :], start=False, stop=(ft == NT - 1)),
                    r=[("PAN", psl), ("YB", ft)], w=[("ps", by)])
            _issue_panel()
            sch.op("dve", lambda e, tt_=tt_, by=by: e.tensor_tensor(out=T[:, tt_, :], in0=psb(by), in1=T[:, tt_, :], op=ALU.mult),
                   r=[("ps", by), (Tn, tt_)], w=[(Tn, tt_)])

    for hf in range(2):
        inproj(0, hf)
        transp_to(hf, T0, "T0")
        inproj(1, hf)
        fwd_dft(T0, "T0", 0, hf)
        transp_to(hf, T1, "T1")
        inv_dft(T1, "T1")
        inproj(2, hf)
        fwd_dft(T1, "T1", 1, hf)
        transp_to(hf, T0, "T0")
        inv_dft(T0, "T0")
        transp_back(hf, T0, "T0")

    sch.fence_all()
    WO = V(P_RAW, 4096, BF16, "p (j d) -> p j d", j=8)
    sch.op("pool", lambda e: e.dma_start(out=WO, in_=w_out_d.rearrange("(j p) d -> p j d", p=128)), w=["WO"], slot="wo")
    XC2 = [V(P_T0 + 4096 * i, 4096, F32, "p (j s) -> p j s", j=8) for i in range(2)]
    for sc in range(4):
        xs = sc % 2
        sch.op("sp", lambda e, sc=sc, xs=xs: e.dma_start(out=XC2[xs], in_=xT_v[:, :, sc * 512:(sc + 1) * 512]),
               w=[("XC2", xs)], slot=("xc2", xs))
        for jd in range(8):
            bank = mbank()
            for cj in range(8):
                sch.op("pe", lambda e, cj=cj, jd=jd, sc=sc, bank=bank: e.matmul(
                    psb(bank), WO[:, cj, jd * 128:(jd + 1) * 128], Z2T[:, cj, sc * 512:(sc + 1) * 512],
                    start=(cj == 0), stop=(cj == 7)), r=["WO", ("Z2T", cj)], w=[("ps", bank)])
            evac_m(bank, jd)
        branch_finish(sc, None, 0, 1, lambda j, xs=xs: XC2[xs][:, j, :], lambda j, xs=xs: [("XC2", xs)], sc * 512)

    def write_out():
        for j in range(8):
            sch.op("sp", lambda e, j=j: e.dma_start(out=outT_v[:, j, :], in_=H[:, j, :]),
                   r=[("H", j, i) for i in range(4)], w=[("out", j)], slot=("out", j % 4))
        sch.op("sp", lambda e: e.dma_start(out=kco_d[0, 0, 0:1, 0, 0:8], in_=kco_d[0, 0, 1:2, 0, 0:8]),
               r=[("out", j) for j in range(8)], slot="fin")

    if stage == 1:
        write_out()
        sch.emit(nc, stack)
        return nc, stack


    def mlp(layer):
        sch.fence_all()
        tag = "L%d" % layer
        HNC = V(P_RAW, 4096, BF16, "p (j s) -> p j s", j=8)
        ACTB = V(P_Z2T, 16384, BF16, "p (f s) -> p f s", f=32)
        WU = [V(o_, 1024, BF16, "p (j c) -> p j c", j=8) for o_ in (P_TMP, P_TMP + 1024, P_X)]
        WD = [V(P_KST + 1024 * i, 1024, BF16, "p (f d) -> p f d", f=4) for i in range(3)]
        RL = [V(O_KNY + 256 * i, 256, BF16) for i in range(2)]
        wu_v = w_up_d[layer].rearrange("(j p) f -> p j f", p=128)
        wd_v = w_down_d[layer].rearrange("(f p) d -> p f d", p=128)
        c = {"wu": 0, "wd": 0, "rl": 0, "ub": 0}
        def _norm(tc):
            t0 = tc * 1024
            for sc2 in range(2):
                tok = t0 + sc2 * 512
                norm_chunk(H[:, :, tok:tok + 512], [("H", j, tok // 512) for j in range(8)], layer, 2,
                           lambda j, sc2=sc2: HNC[:, j, sc2 * 512:(sc2 + 1) * 512], lambda j, sc2=sc2: [(tag + "HNC", sc2)])

        def _up(tc):
            t0 = tc * 1024
            for slab in range(16):
                ws = c["wu"] % 3
                c["wu"] += 1
                sch.op("pool", lambda e, ws=ws, slab=slab: e.dma_start(out=WU[ws], in_=wu_v[:, :, slab * 256:(slab + 1) * 256]),
                       w=[(tag + "WU", ws)], slot=(tag + "wu", ws))
                for q2 in range(2):
                    ffc = slab * 2 + q2
                    for sc2 in range(2):
                        bank = 4 + (c["ub"] % 2)
                        c["ub"] += 1
                        for j in range(8):
                            sch.op("pe", lambda e, j=j, ws=ws, q2=q2, sc2=sc2, bank=bank: e.matmul(
                                psb(bank), WU[ws][:, j, q2 * 128:(q2 + 1) * 128], HNC[:, j, sc2 * 512:(sc2 + 1) * 512],
                                start=(j == 0), stop=(j == 7)), r=[(tag + "WU", ws), (tag + "HNC", sc2)], w=[("ps", bank)])
                        rl = c["rl"] % 2
                        c["rl"] += 1
                        sch.op("act", lambda e, bank=bank, rl=rl: e.activation(out=RL[rl], in_=psb(bank), func=AF.Relu),
                               r=[("ps", bank)], w=[(tag + "RL", rl)])
                        sch.op("dve", lambda e, rl=rl, ffc=ffc, sc2=sc2: e.tensor_tensor(
                            out=ACTB[:, ffc, sc2 * 512:(sc2 + 1) * 512], in0=RL[rl], in1=RL[rl], op=ALU.mult),
                            r=[(tag + "RL", rl)], w=[(tag + "ACT", ffc, sc2)])

        def _down(tc):
            t0 = tc * 1024
            for sc2 in range(2):
                tok = t0 + sc2 * 512
                for jh in range(2):
                    for slab in range(8):
                        ws = c["wd"] % 3
                        c["wd"] += 1
                        sch.op("pool", lambda e, ws=ws, slab=slab, jh=jh: e.dma_start(
                            out=WD[ws], in_=wd_v[:, slab * 4:(slab + 1) * 4, jh * 512:(jh + 1) * 512]),
                            w=[(tag + "WD", ws)], slot=(tag + "wd", ws))
                        for f4 in range(4):
                            ffc = slab * 4 + f4
                            for jq in range(4):
                                sch.op("pe", lambda e, ws=ws, f4=f4, jq=jq, ffc=ffc, sc2=sc2: e.matmul(
                                    psb(jq), WD[ws][:, f4, jq * 128:(jq + 1) * 128], ACTB[:, ffc, sc2 * 512:(sc2 + 1) * 512],
                                    start=(ffc == 0), stop=(ffc == 31)), r=[(tag + "WD", ws), (tag + "ACT", ffc, sc2)], w=[("ps", jq)])
                    for jq in range(4):
                        evac_m(jq, jh * 4 + jq)
                branch_finish(None, None, layer, 3, lambda j, tok=tok: H[:, j, tok:tok + 512],
                              lambda j, tok=tok: [("H", j, tok // 512)], tok)


        _norm(0)
        _up(0)
        _norm(1)
        _down(0)
        _up(1)
        _down(1)

    mlp(0)
    if stage == 2:
        write_out()
        sch.emit(nc, stack)
        return nc, stack


    sch.fence_all()
    HNT2 = V(P_Z2T, 8192, BF16, "p (j s) -> p j s", j=8)
    QKT = V(P_T0, 12288, BF16, "p (c s) -> p c s", c=12)
    VTOK = V(P_TMP, 2112, BF16, "p (t g e) -> p t g e", t=16, g=4)
    WQ = [V(P_WIN + 1024 * i, 1024, BF16, "p (j c) -> p j c", j=8) for i in range(2)]
    ATC = V(P_PAN, 2368, BF16)
    MASK = ATC[:, 0:384]
    PERMR = ATC[:, 384:512]
    PERMH = ATC[:, 512:640]
    COS = ATC[:, 640:2688]
    SIN = ATC[:, 2688:4736]
    PTS = [V(P_PAN + 2368 + 192 * i, 192, BF16) for i in range(8)]
    ESK = V(P_PAN + 3904, 16)
    RDN = V(P_PAN + 3920, 16)
    XB = [V(O_KNY + 256 * i, 256, BF16) for i in range(2)] + [V(P_Y, 256, BF16)]
    XS = [V(O_KNY + 512 + 256 * i, 256, BF16) for i in range(2)]
    wqkv_v = wqkv_d.rearrange("(j p) n -> p j n", p=128)
    sch.op("sp", lambda e: e.dma_start(out=ATC, in_=atc_d), w=["ATC"], slot="atc")
    sch.op("sp", lambda e: e.dma_start(out=ESK, in_=sink_d.partition_broadcast(128)), w=["ESK"], slot="esk")
    sch.op("act", lambda e: e.activation(out=ESK, in_=ESK, func=AF.Exp), r=["ESK"], w=["ESK"])
    for sc in range(4):
        norm_chunk(H[:, :, sc * 512:(sc + 1) * 512], [("H", j, sc) for j in range(8)], 1, 0,
                   lambda j, sc=sc: HNT2[:, j, sc * 512:(sc + 1) * 512], lambda j, sc=sc: [("HNT2", sc)])
    ac = {"wq": 0, "xb": 0, "pt": 0, "sb": 0, "pv": 0}
    KZ = [[QKT[:, 8, :], QKT[:, 9, :]], [QKT[:, 10, :], QKT[:, 11, :]],
          [V(O_SQ, 1024, BF16), V(O_SQ + 1024, 1024, BF16)], [V(O_RSTD, 1024, BF16), V(P_X, 1024, BF16)]]
    sch.retire([("SQj", j) for j in range(8)] + [("RSTD", 0), ("RSTD", 1)],
               [("KZ", g_, v_, s_) for g_ in (2, 3) for v_ in range(2) for s_ in range(4)] + [("KZz", g_, v_) for g_ in (2, 3) for v_ in range(2)])
    for g_ in range(4):
        sch.op("dve", lambda e, g_=g_: e.memset(KZ[g_][0][64:128, :], 0.0), w=[("KZz", g_, 0)])
        sch.op("dve", lambda e, g_=g_: e.memset(KZ[g_][1][0:64, :], 0.0), w=[("KZz", g_, 1)])
    dst_chunk = [0, 1, 2, 3, 4, 5, 6, 7, 8, 10]
    rb = [0]

    def rbank():
        b_ = 4 + (rb[0] % 2)
        rb[0] += 1
        return b_

    def proj_unit(ws, q2, sc, xb):
        bank = mbank()
        for j in range(8):
            sch.op("pe", lambda e, j=j: e.matmul(
                psb(bank), WQ[ws][:, j, q2 * 128:(q2 + 1) * 128], HNT2[:, j, sc * 512:(sc + 1) * 512],
                start=(j == 0), stop=(j == 7)), r=[("WQ", ws), ("HNT2", sc)], w=[("ps", bank)])
        sch.op("act", lambda e: e.activation(out=XB[xb], in_=psb(bank), func=AF.Copy),
               r=[("ps", bank)], w=[("XB", xb)])

    def rope_unit(dc, sc, xb, xs):
        bank2 = rbank()
        sch.op("pe", lambda e: e.matmul(psb(bank2), PERMR, XB[xb], start=True, stop=True),
               r=[("XB", xb), "ATC"], w=[("ps", bank2)])
        sch.op("dve", lambda e: e.tensor_tensor(out=XS[xs], in0=psb(bank2), in1=SIN[:, sc * 512:(sc + 1) * 512], op=ALU.mult),
               r=[("ps", bank2), "ATC"], w=[("XS", xs)])
        sch.op("dve", lambda e: e.tensor_tensor(out=XB[xb], in0=XB[xb], in1=COS[:, sc * 512:(sc + 1) * 512], op=ALU.mult),
               r=[("XB", xb), "ATC"], w=[("XB", xb)])
        if dc < 8:
            sch.op("dve", lambda e: e.tensor_tensor(out=QKT[:, dc, sc * 512:(sc + 1) * 512], in0=XB[xb], in1=XS[xs], op=ALU.add),
                   r=[("XB", xb), ("XS", xs)], w=[("QKT", dc, sc)])
        else:
            g0, g1 = (0, 1) if dc == 8 else (2, 3)
            cs = slice(sc * 512, (sc + 1) * 512)
            sch.op("dve", lambda e: e.tensor_tensor(out=XS[xs], in0=XB[xb], in1=XS[xs], op=ALU.add),
                   r=[("XB", xb), ("XS", xs)], w=[("XS", xs)])
            sch.op("act", lambda e: e.activation(out=KZ[g0][0][0:64, cs], in_=XS[xs][0:64, :], func=AF.Copy),
                   r=[("XS", xs)], w=[("KZ", g0, 0, sc)])
            sch.op("act", lambda e: e.activation(out=KZ[g1][1][64:128, cs], in_=XS[xs][64:128, :], func=AF.Copy),
                   r=[("XS", xs)], w=[("KZ", g1, 1, sc)])
            bank3 = rbank()
            sch.op("pe", lambda e: e.matmul(psb(bank3), PERMH, XS[xs], start=True, stop=True),
                   r=[("XS", xs), "ATC"], w=[("ps", bank3)])
            sch.op("act", lambda e: e.activation(out=KZ[g1][0][0:64, cs], in_=psb(bank3)[0:64, :], func=AF.Copy),
                   r=[("ps", bank3)], w=[("KZ", g1, 0, sc)])
            sch.op("act", lambda e: e.activation(out=KZ[g0][1][64:128, cs], in_=psb(bank3)[64:128, :], func=AF.Copy),
                   r=[("ps", bank3)], w=[("KZ", g0, 1, sc)])

    pend_rope = None
    ucount = 0
    for slab in range(5):
        ws = ac["wq"] % 2
        ac["wq"] += 1
        sch.op("pool", lambda e, ws=ws, slab=slab: e.dma_start(out=WQ[ws], in_=wqkv_v[:, :, slab * 256:(slab + 1) * 256]),
               w=[("WQ", ws)], slot=("wq", ws))
        for q2 in range(2):
            dc = dst_chunk[slab * 2 + q2]
            for sc in range(4):
                xb = ucount % 3
                xs = ucount % 2
                ucount += 1
                proj_unit(ws, q2, sc, xb)
                if pend_rope is not None:
                    rope_unit(*pend_rope)
                pend_rope = (dc, sc, xb, xs)
    rope_unit(*pend_rope)
    ws = ac["wq"] % 2
    ac["wq"] += 1
    sch.op("pool", lambda e, ws=ws: e.dma_start(out=WQ[ws], in_=wqkv_v[:, :, 1280:1536]), w=[("WQ", ws)], slot=("wq", ws))
    sch.op("dve", lambda e: e.memset(VTOK[:, :, :, 64:66], 1.0), w=["VONE"])
    for st in range(NT):
        bank = mbank()
        for j in range(8):
            sch.op("pe", lambda e, j=j, ws=ws, st=st, bank=bank: e.matmul(
                psb(bank)[:, 0:256], HNT2[:, j, st * 128:(st + 1) * 128], WQ[ws][:, j, :],
                start=(j == 0), stop=(j == 7)), r=[("WQ", ws), ("HNT2", st // 4)], w=[("ps", bank)])
        sch.op("act", lambda e, bank=bank, st=st: e.activation(
            out=VTOK[:, st, :, 0:64], in_=psb(bank)[:, 0:256].rearrange("p (g e) -> p g e", g=4), func=AF.Copy),
            r=[("ps", bank)], w=[("VTOK", st)])

    OTOK = V(P_Z2T, 8192, BF16, "p (t c) -> p t c", t=16)
    sch.retire([("HNT2", i) for i in range(4)], [("OTOK", i) for i in range(16)])

    NSLOT = 8
    LAG = 3
    tidx = lambda h, j: h * NT + j

    def pv(h, i):
        g = h // 4
        kbs = [kb for kb in (i - 1, i, i + 1) if 0 <= kb < NT]
        pb = ac["pv"] % 4
        ac["pv"] += 1
        for n_, kb in enumerate(kbs):
            qlo = max(kb - 1, 0)
            slot = tidx(h, kb) % NSLOT
            off = (i - qlo) * 128
            sch.op("pe", lambda e, slot=slot, off=off, kb=kb, pb=pb, n_=n_: e.matmul(
                psb(pb)[:, 0:65], PTS[slot][:, off:off + 128], VTOK[:, kb, g, 0:65],
                start=(n_ == 0), stop=(n_ == len(kbs) - 1)),
                r=[("PT", slot), ("VTOK", kb), "VONE"], w=[("ps", pb)])
        rd = ac["pv"] % 4
        pv_pend.append((pb, h, i, rd))
        if len(pv_pend) >= 2:
            pv_flush()

    pv_pend = []

    def pv_flush():
        for (pb, h, i, rd) in pv_pend:
            sch.op("dve", lambda e, pb=pb, h=h, rd=rd: e.tensor_scalar(out=RDN[:, 2 * rd:2 * rd + 1], in0=psb(pb)[:, 64:65], scalar1=ESK[:, h:h + 1],
                                                                   scalar2=None, op0=ALU.add), r=[("ps", pb), "ESK"], w=[("RDN", rd)])
        for (pb, h, i, rd) in pv_pend:
            sch.op("dve", lambda e, rd=rd: e.reciprocal(out=RDN[:, 2 * rd + 1:2 * rd + 2], in_=RDN[:, 2 * rd:2 * rd + 1]), r=[("RDN", rd)], w=[("RDN2", rd)])
        for (pb, h, i, rd) in pv_pend:
            sch.op("dve", lambda e, pb=pb, h=h, i=i, rd=rd: e.tensor_scalar(out=OTOK[:, i, h * 64:(h + 1) * 64], in0=psb(pb)[:, 0:64],
                                                                           scalar1=RDN[:, 2 * rd + 1:2 * rd + 2], scalar2=None, op0=ALU.mult),
                   r=[("ps", pb), ("RDN2", rd)], w=[("OTOK", i)])
        del pv_pend[:]

    def scores(h, j):
        g = h // 4
        v = h % 2
        qc = h // 2
        qlo, qhi = max(j - 1, 0), min(j + 1, NT - 1)
        nq = qhi - qlo + 1
        bank = 4 + (ac["sb"] % 4)
        ac["sb"] += 1
        slot = tidx(h, j) % NSLOT
        sch.op("pe", lambda e: e.matmul(
            psb(bank)[:, 0:nq * 128], KZ[g][v][:, j * 128:(j + 1) * 128], QKT[:, qc, qlo * 128:(qhi + 1) * 128],
            start=True, stop=False),
            r=[("KZ", g, v, j // 4), ("KZz", g, v)] + [("QKT", qc, s_) for s_ in range(qlo // 4, qhi // 4 + 1)], w=[("ps", bank)])
        m0 = 128 if j == 0 else 0
        sch.op("pe", lambda e: e.matmul(psb(bank)[:, 0:nq * 128], IDENT, MASK[:, m0:m0 + nq * 128], start=False, stop=True),
               r=["ATC", "CB"], w=[("ps", bank)])
        sch.op("act", lambda e: e.activation(
            out=PTS[slot][:, 0:nq * 128], in_=psb(bank)[:, 0:nq * 128], func=AF.Exp, scale=0.125),
            r=[("ps", bank)], w=[("PT", slot)])

    pv_tasks = []
    for h in range(16):
        for i in range(NT):
            pv_tasks.append((tidx(h, min(i + 1, NT - 1)), h, i))
    pvi = 0
    for t in range(16 * NT + LAG):
        if t < 16 * NT:
            scores(t // NT, t % NT)
        while pvi < len(pv_tasks) and pv_tasks[pvi][0] + LAG <= t:
            pv(pv_tasks[pvi][1], pv_tasks[pvi][2])
            pvi += 1
    assert pvi == len(pv_tasks)
    if pv_pend:
        pv_flush()

    OT = V(P_T0, 8192, BF16, "p (j s) -> p j s", j=8)
    sch.retire([("QKT", c, s_) for c in range(8) for s_ in range(4)] + [("KZ", g_, v_, s_) for g_ in range(2) for v_ in range(2) for s_ in range(4)]
               + [("KZz", g_, v_) for g_ in range(2) for v_ in range(2)], [("OT", j) for j in range(8)] + ["WO2"])
    WO2 = V(P_RAW, 4096, BF16, "p (j d) -> p j d", j=8)
    sch.op("pool", lambda e: e.dma_start(out=WO2, in_=wo2_d.rearrange("(j p) d -> p j d", p=128)), w=["WO2"], slot="wo2")
    for cj in range(8):
        for stg in range(2):
            bank = mbank()
            pv_ = psb16(bank)
            for i in range(8):
                st = stg * 8 + i
                sch.op("pe", lambda e, i=i, st=st, cj=cj, pv_=pv_: e.transpose(
                    pv_[:, i * 128:(i + 1) * 128], OTOK[:, st, cj * 128:(cj + 1) * 128], IDENT),
                    r=[("OTOK", st), "CB"], w=[("ps", bank)])
            sch.op("act", lambda e, pv_=pv_, stg=stg, cj=cj: e.activation(
                out=OT[:, cj, stg * 1024:(stg + 1) * 1024], in_=pv_, func=AF.Copy), r=[("ps", bank)], w=[("OT", cj)])
    sch.fence_all()
    for sc in range(4):
        for jd in range(8):
            bank = mbank()
            for cj in range(8):
                sch.op("pe", lambda e, cj=cj, jd=jd, sc=sc, bank=bank: e.matmul(
                    psb(bank), WO2[:, cj, jd * 128:(jd + 1) * 128], OT[:, cj, sc * 512:(sc + 1) * 512],
                    start=(cj == 0), stop=(cj == 7)), r=["WO2", ("OT", cj)], w=[("ps", bank)])
            evac_m(bank, jd)
        branch_finish(sc, None, 1, 1, lambda j, sc=sc: H[:, j, sc * 512:(sc + 1) * 512], lambda j, sc=sc: [("H", j, sc)], sc * 512)
    if stage == 3:
        write_out()
        sch.emit(nc, stack)
        return nc, stack
    mlp(1)
    write_out()
    sch.emit(nc, stack)
    return nc, stack


def make_in_maps(inp):
    cf, cb, zf = _host_consts(inp)
    fwd, inv, dk = _dft_tables()
    adl = _absdelta().reshape(1, D)
    hyb = np.ascontiguousarray(np.asarray(inp["hy_bias"], np.float32)[0].reshape(1, 2 * D))
    x = np.asarray(inp["x"], np.float32)
    common = {
        "cf": cf, "cb": cb, "zf": zf, "adl": adl, "hyb": hyb, "dftF": fwd, "dftI": inv, "dftK": dk,
        "hy_w_in": np.ascontiguousarray(np.asarray(inp["hy_w_in"], np.float32)[0]),
        "hy_f_wout": np.ascontiguousarray(np.asarray(inp["hy_f_wout"], np.float32)[0]),
        "hy_w_out": np.ascontiguousarray(np.asarray(inp["hy_w_out"], np.float32)[0]),
        "w_up": np.ascontiguousarray(np.asarray(inp["w_up"], np.float32)),
        "w_down": np.ascontiguousarray(np.asarray(inp["w_down"], np.float32)),
    }
    common["at_w_qkv"] = np.ascontiguousarray(np.asarray(inp["at_w_qkv"], np.float32)[0])
    common["at_w_o"] = np.ascontiguousarray(np.asarray(inp["at_w_o"], np.float32)[0])
    common["sink"] = np.ascontiguousarray(np.asarray(inp["at_sink"], np.float32)[0].reshape(1, 16))
    common["atc"] = _attn_consts()
    maps = []
    for c in range(NCORES):
        m = dict(common)
        m["xT"] = np.ascontiguousarray(x[c].T)
        maps.append(m)
    return maps


_PROG = {}


def kernel(**inputs):
    inp = {k: np.asarray(v) for k, v in inputs.items()}
    if "nc" not in _PROG:
        _PROG["nc"] = build_program(stage=4)
    nc, _stack = _PROG["nc"]
    maps = make_in_maps(inp)
    res = run_bass_kernel_spmd(nc, maps, core_ids=list(range(NCORES)))
    out = np.stack([np.ascontiguousarray(r["outT"].T) for r in res.results], axis=0)
    return out.astype(np.float32)
```

```python
import math
import bisect
import numpy as np
import ml_dtypes
import concourse.bass as bass
import concourse.mybir as mybir
from concourse.bass_utils import run_bass_kernel_spmd

F32 = mybir.dt.float32
BF16 = mybir.dt.bfloat16
AF = mybir.ActivationFunctionType
ALU = mybir.AluOpType

S = 2048
D = 1024
NT = 16
DFF = 4096
EPS = 1e-6
NCORES = 8
PI = math.pi

SBUF_WORDS = 52800


class Sched:
    ENGS = ("pe", "act", "dve", "pool", "sp")

    def __init__(self):
        self.ops = []
        self.lastw = {}
        self.readers = {}

    def op(self, eng, fn, r=(), w=(), slot=None):
        idx = len(self.ops)
        deps = set()
        if getattr(self, "fence", None):
            for k in w:
                if k not in self.known:
                    deps.update(self.fence)
                    self.known.add(k)
        for k in r:
            if k in self.lastw:
                deps.add(self.lastw[k])
        for k in w:
            if k in self.lastw:
                deps.add(self.lastw[k])
            deps.update(self.readers.get(k, ()))
        for k in r:
            self.readers.setdefault(k, []).append(idx)
        for k in w:
            self.lastw[k] = idx
            self.readers[k] = []
        deps.discard(idx)
        self.ops.append(dict(eng=eng, fn=fn, deps=deps, slot=slot, sem=None, val=None, sig=False))
        return idx

    def fence_all(self):
        last = {}
        for i, o in enumerate(self.ops):
            last[(o["eng"], o["slot"])] = i
        self.fence = set(last.values())
        self.known = set(self.lastw.keys()) | set(self.readers.keys())

    def retire(self, old_keys, new_keys):
        acc = set()
        for k in old_keys:
            if k in self.lastw:
                acc.add(self.lastw[k])
            acc.update(self.readers.get(k, ()))
        for k in new_keys:
            self.readers.setdefault(k, []).extend(acc)

    def emit(self, nc, stack, same_engine_sync=("act", "dve", "pool")):
        ops = self.ops
        for i, o in enumerate(ops):
            for d in o["deps"]:
                y = ops[d]
                if y["slot"] is not None:
                    continue
                if y["eng"] == o["eng"] and y["eng"] not in same_engine_sync:
                    continue
                y["sig"] = True
        SEM_MAX = 30000
        eng_sems = {}
        counters = {}
        for e in ("pe", "act", "dve", "pool"):
            eng_sems[e] = []
            counters[e] = SEM_MAX
        slot_sem = {}
        slot_list = {}
        for i, o in enumerate(ops):
            if o["slot"] is not None:
                sl = o["slot"]
                if sl not in slot_sem:
                    slot_sem[sl] = stack.enter_context(nc.semaphore("d_" + str(len(slot_sem))))
                    slot_list[sl] = []
                slot_list[sl].append(i)
                o["sem"] = slot_sem[sl]
                o["val"] = 16 * len(slot_list[sl])
            elif o["sig"]:
                e = o["eng"]
                if counters[e] >= SEM_MAX:
                    eng_sems[e].append(stack.enter_context(nc.semaphore("e_%s_%d" % (e, len(eng_sems[e])))))
                    counters[e] = 0
                counters[e] += 1
                o["sem"] = eng_sems[e][-1]
                o["val"] = counters[e]
        self.nsem = len(slot_sem) + sum(len(v) for v in eng_sems.values())
        block = stack.enter_context(nc.Block())
        per_eng = {e: [] for e in self.ENGS}
        for i, o in enumerate(ops):
            per_eng[o["eng"]].append(i)

        def run(engname, eng):
            waited = {}
            for i in per_eng[engname]:
                o = ops[i]
                need = {}
                for d in o["deps"]:
                    y = ops[d]
                    if y["slot"] is not None:
                        lst = slot_list[y["slot"]]
                        pos = bisect.bisect_left(lst, i)
                        sem, val = y["sem"], 16 * pos
                    else:
                        if y["eng"] == engname and engname not in same_engine_sync:
                            continue
                        sem, val = y["sem"], y["val"]
                    key = id(sem)
                    if key not in need or need[key][1] < val:
                        need[key] = (sem, val)
                for key, (sem, val) in need.items():
                    if waited.get(key, 0) >= val:
                        continue
                    eng.wait_ge(sem, val)
                    waited[key] = val
                ins = o["fn"](eng)
                if o["slot"] is not None:
                    ins.then_inc(o["sem"], 16)
                elif o["sig"]:
                    ins.then_inc(o["sem"], 1)

        @block.tensor
        def _(e):
            run("pe", e)

        @block.scalar
        def _(e):
            run("act", e)

        @block.vector
        def _(e):
            run("dve", e)

        @block.gpsimd
        def _(e):
            run("pool", e)

        @block.sync
        def _(e):
            run("sp", e)


_CONST_CACHE = {}


def _dft_tables():
    if "dft" in _CONST_CACHE:
        return _CONST_CACHE["dft"]
    n = np.arange(S, dtype=np.int64)
    prod = (n[:, None] * n[None, :]) % 4096
    ang = prod.astype(np.float64) * (2.0 * np.pi / 4096.0)
    C = np.cos(ang)
    Sm = np.sin(ang)
    sgn = np.where(n % 2 == 0, 1.0, -1.0)
    Sp = Sm.copy()
    Sp[:, 0] = sgn
    SpT = Sp.T.copy()

    def panelize(M):
        return M.reshape(16, 128, 16, 128).transpose(2, 1, 0, 3)

    fwd = np.stack([panelize(C), panelize(Sp)], axis=2)
    inv = np.stack([panelize(C), panelize(SpT)], axis=2)
    fwd = np.ascontiguousarray(fwd).astype(ml_dtypes.bfloat16)
    inv = np.ascontiguousarray(inv).astype(ml_dtypes.bfloat16)
    kk = np.arange(16)[:, None]
    pp = np.arange(128)[None, :]
    sperm = (2 * (128 * (kk % 8) + pp) + (kk // 8)).reshape(-1)
    f1 = np.arange(1024, dtype=np.int64)
    angk = ((sperm[:, None].astype(np.int64) * f1[None, :]) % 4096).astype(np.float64) * (2.0 * np.pi / 4096.0)

    def panelize_k(M):
        return M.reshape(16, 128, 8, 128).transpose(2, 1, 0, 3)

    dk = np.stack([panelize_k(np.cos(angk)), panelize_k(np.sin(angk))], axis=2)
    dk = np.ascontiguousarray(dk).astype(ml_dtypes.bfloat16)
    _CONST_CACHE["dft"] = (fwd, inv, dk)
    return fwd, inv, dk


def _zfeat():
    L = S
    t = np.linspace(0.0, 1.0, L, dtype=np.float32)[:, None]
    w = (np.float32(2.0 * math.pi / L) * np.arange(L, dtype=np.float32))[:, None]
    f = np.linspace(1e-4, 15.0, 16, dtype=np.float32)[None, :]
    z = np.concatenate([t, np.cos(f * w), -np.sin(f * w)], axis=-1).astype(np.float32)
    return z


def _attn_consts():
    a = np.zeros((128, 4736), np.float32)
    kl = np.arange(128)[:, None]
    ql = np.arange(128)[None, :]
    a[:, 0:128] = np.where(kl <= ql, 0.0, -30000.0)
    a[:, 128:256] = 0.0
    a[:, 256:384] = np.where(ql <= kl, 0.0, -30000.0)
    pr = np.zeros((128, 128), np.float32)
    for hb in (0, 64):
        for i in range(8):
            pr[hb + 8 + i, hb + i] = 1.0
            pr[hb + i, hb + 8 + i] = 1.0
    a[:, 384:512] = pr
    ph = np.zeros((128, 128), np.float32)
    for m in range(128):
        ph[(m + 64) % 128, m] = 1.0
    a[:, 512:640] = ph
    inv = (500000.0 ** (-np.arange(0, 16, 2, dtype=np.float32) / 16.0)).astype(np.float32)
    ang = np.arange(S, dtype=np.float32)[None, :] * inv[:, None]
    cos = np.ones((128, S), np.float32)
    sin = np.zeros((128, S), np.float32)
    for hb in (0, 64):
        cos[hb:hb + 8] = np.cos(ang); cos[hb + 8:hb + 16] = np.cos(ang)
        sin[hb:hb + 8] = -np.sin(ang); sin[hb + 8:hb + 16] = np.sin(ang)
    a[:, 640:2688] = cos
    a[:, 2688:4736] = sin
    return a.astype(ml_dtypes.bfloat16)


def _absdelta():
    max_decay = math.log(1e-2) / 0.3
    min_decay = math.log(1e-2) / 1.5
    deltas = np.linspace(min_decay, max_decay, D, dtype=np.float32)
    return np.abs(deltas).astype(np.float32)


CF_G = 0
CF_CW = 64
CF_FB = 160
CF_TNEG = 164
CF_FW1 = 180
CF_FW2 = 244
CF_FW3 = 308
CF_FBF = 372
CF_N = 384

CB_ID = 0
CB_ONES = 128
CB_JREV = 256
CB_E00 = 384
CB_SGN = 512
CB_N = 640


def _host_consts(inp):
    cf = np.zeros((128, CF_N), np.float32)
    gl = ["norm_mix_pre", "norm_mix_post", "norm_mlp_pre", "norm_mlp_post"]
    for layer in range(2):
        for gi, gname in enumerate(gl):
            v = np.asarray(inp[gname])[layer]
            cf[:, CF_G + (layer * 4 + gi) * 8: CF_G + (layer * 4 + gi) * 8 + 8] = v.reshape(8, 128).T
    cw = np.asarray(inp["hy_conv_w"])[0]
    cbias = np.asarray(inp["hy_conv_b"])[0]
    for k in range(3):
        cf[:, CF_CW + 24 * k: CF_CW + 24 * k + 24] = cw[k].reshape(24, 128).T
    cf[:, CF_CW + 72: CF_CW + 96] = cbias.reshape(24, 128).T
    cf[0:64, CF_FB + 0] = np.asarray(inp["hy_f_b1"])[0]
    cf[0:64, CF_FB + 1] = np.asarray(inp["hy_f_b2"])[0]
    cf[0:64, CF_FB + 2] = np.asarray(inp["hy_f_b3"])[0]
    cf[0:64, CF_FB + 3] = np.asarray(inp["hy_f_freq"])[0]
    t = np.linspace(0.0, 1.0, S, dtype=np.float32)
    for k in range(16):
        sidx = 2 * (128 * (k % 8) + np.arange(128)) + (k // 8)
        cf[:, CF_TNEG + k] = -t[sidx]
    cf[0:33, CF_FW1: CF_FW1 + 64] = np.asarray(inp["hy_f_w1"])[0]
    cf[0:64, CF_FW2: CF_FW2 + 64] = np.asarray(inp["hy_f_w2"])[0]
    cf[0:64, CF_FW3: CF_FW3 + 64] = np.asarray(inp["hy_f_w3"])[0]
    cb = np.zeros((128, CB_N), np.float32)
    cb[:, CB_ID:CB_ID + 128] = np.eye(128)
    cb[:, CB_ONES:CB_ONES + 128] = 1.0
    for q in range(1, 128):
        cb[128 - q, CB_JREV + q] = 1.0
    cb[0, CB_E00] = 1.0
    cb[:, CB_SGN] = np.where(np.arange(128) % 2 == 0, 1.0, -1.0)
    cb = cb.astype(ml_dtypes.bfloat16)
    zf = np.zeros((33, S), np.float32)
    zf[:, :] = _zfeat().T
    return cf, cb, zf


def build_program(stage=4):
    from contextlib import ExitStack
    nc = bass.Bass("TRN2", target_bir_lowering=False)
    dt = nc.dram_tensor
    xT = dt("xT", [D, S], F32, kind="ExternalInput").ap()
    cf_d = dt("cf", [128, CF_N], F32, kind="ExternalInput").ap()
    cb_d = dt("cb", [128, CB_N], BF16, kind="ExternalInput").ap()
    zf_d = dt("zf", [33, S], F32, kind="ExternalInput").ap()
    adl_d = dt("adl", [1, D], F32, kind="ExternalInput").ap()
    hyb_d = dt("hyb", [1, 2 * D], F32, kind="ExternalInput").ap()
    dftF = dt("dftF", [16, 128, 2, 16, 128], BF16, kind="ExternalInput").ap()
    dftI = dt("dftI", [16, 128, 2, 16, 128], BF16, kind="ExternalInput").ap()
    dftK = dt("dftK", [8, 128, 2, 16, 128], BF16, kind="ExternalInput").ap()
    w_in_d = dt("hy_w_in", [D, 3 * D], F32, kind="ExternalInput").ap()
    wout_f_d = dt("hy_f_wout", [64, 4 * D], F32, kind="ExternalInput").ap()
    w_out_d = dt("hy_w_out", [D, D], F32, kind="ExternalInput").ap()
    w_up_d = dt("w_up", [2, D, DFF], F32, kind="ExternalInput").ap()
    w_down_d = dt("w_down", [2, DFF, D], F32, kind="ExternalInput").ap()
    wqkv_d = dt("at_w_qkv", [D, 1536], F32, kind="ExternalInput").ap()
    wo2_d = dt("at_w_o", [D, D], F32, kind="ExternalInput").ap()
    sink_d = dt("sink", [1, 16], F32, kind="ExternalInput").ap()
    atc_d = dt("atc", [128, 4736], BF16, kind="ExternalInput").ap()
    kco_d = dt("kco", [2, 16, 128, 2, D], BF16, kind="Internal").ap()
    outT = dt("outT", [D, S], F32, kind="ExternalOutput").ap()

    stack = ExitStack()
    big = stack.enter_context(nc.sbuf_tensor("big", [128, SBUF_WORDS], F32))
    PS = stack.enter_context(nc.psum_tensor("PS", [128, 8, 512], F32))
    sch = Sched()

    def V(off, words, dtype=F32, pat=None, **kw):
        ap = big[:, off:off + words]
        if dtype != F32:
            ap = ap.bitcast(dtype)
        if pat is not None:
            ap = ap.rearrange(pat, **kw)
        return ap

    def psb(b):
        return PS[:, b, :]

    def psb16(b):
        return PS[:, b, :].bitcast(BF16)

    o = 0
    O_CF = o; o += CF_N
    O_CB = o; o += CB_N // 2
    O_RSTD = o; o += 2 * 512
    O_LN = o; o += 512
    O_SQ = o; o += 2048
    O_KNY = o; o += 1024
    P_H = o; o += 16384
    P_Z2T = o; o += 8192
    P_T0 = o; o += 4096
    P_T1 = o; o += 4096
    P_RAW = o; o += 2052
    P_TT = o; o += 2048
    P_TMP = o; o += 2048
    P_KST = o; o += 1024
    P_WIN = o; o += 2048
    P_PAN = o; o += 4096
    P_X = o; o += 1024
    P_Y = o; o += 256
    assert o <= SBUF_WORDS, o

    CF = V(O_CF, CF_N)
    CB = V(O_CB, CB_N // 2, BF16)
    IDENT = CB[:, CB_ID:CB_ID + 128]
    ONES = CB[:, CB_ONES:CB_ONES + 128]
    JREV = CB[:, CB_JREV:CB_JREV + 128]
    E00 = CB[:, CB_E00:CB_E00 + 128]
    SGNC = CB[:, CB_SGN:CB_SGN + 1]
    RSTD = [V(O_RSTD + 512 * i, 512) for i in range(2)]
    LNB = V(O_LN, 512)
    SQ = V(O_SQ, 2048, BF16, "p (j s) -> p j s", j=8)
    KNY = V(O_KNY, 1024, BF16)

    def gcol(layer, gi, j):
        c = CF_G + (layer * 4 + gi) * 8 + j
        return CF[:, c:c + 1]

    PAN = [V(P_PAN + 2048 * i, 2048, BF16, "p (a k j) -> p a k j", a=2, k=16) for i in range(2)]

    sch.op("sp", lambda e: e.dma_start(out=CF, in_=cf_d), w=["CF"], slot="cf")
    sch.op("sp", lambda e: e.dma_start(out=CB, in_=cb_d), w=["CB"], slot="cb")

    F0 = P_Z2T
    ZF = V(F0, 2048)
    HB = [V(F0 + 2048, 2048), V(F0 + 4096, 2048)]
    H3 = V(F0 + 6144, 1024, BF16)
    WF16 = V(F0 + 7168, 1024, BF16)
    WOF = V(F0 + 8192, 4096)
    WSUM = V(F0 + 12288, 1024, BF16)
    WDIF = V(F0 + 13312, 1024, BF16)
    ARG = V(F0 + 14336, 512)
    ADL = V(F0 + 14848, 1024)
    DEC = V(F0 + 15872, 1024)
    HYB = V(F0 + 16896, 2048)
    TROW = V(F0 + 18944, 512)
    ARG2 = V(F0 + 19456, 512)
    assert F0 + 19968 <= P_TMP
    sch.op("sp", lambda e: e.dma_start(out=ZF[0:33, :], in_=zf_d), w=["ZF"], slot="zf")
    sch.op("sp", lambda e: e.dma_start(out=WOF[0:64, :], in_=wout_f_d), w=["WOF"], slot="wof")
    sch.op("sp", lambda e: e.dma_start(out=ADL, in_=adl_d.partition_broadcast(128)), w=["ADL"], slot="adl")
    sch.op("sp", lambda e: e.dma_start(out=HYB[0:1, :], in_=hyb_d), w=["HYB"], slot="hyb")

    for l in range(3):
        sch.op("dve", lambda e, l=l: e.tensor_tensor(out=CF[0:64, CF_FBF + l:CF_FBF + l + 1],
                                                     in0=CF[0:64, CF_FB + l:CF_FB + l + 1],
                                                     in1=CF[0:64, CF_FB + 3:CF_FB + 4], op=ALU.mult),
               r=["CF"], w=[("fbf", l)])
    sch.op("dve", lambda e: e.tensor_tensor(out=WSUM[0:64, :], in0=WOF[0:64, 0:2048], in1=WOF[0:64, 2048:4096], op=ALU.add),
           r=["WOF"], w=["WSUM"])
    sch.op("dve", lambda e: e.tensor_tensor(out=WDIF[0:64, :], in0=WOF[0:64, 2048:4096], in1=WOF[0:64, 0:2048], op=ALU.subtract),
           r=["WOF"], w=["WDIF"])
    sch.op("dve", lambda e: e.tensor_copy(out=WF16[0:64, :], in_=WOF[0:64, 0:2048]), r=["WOF"], w=["WF16"])

    FWOFF = [CF_FW1, CF_FW2, CF_FW3]
    FK = [33, 64, 64]
    fcnt = 0
    for l in range(3):
        src = ZF if l == 0 else HB[(l - 1) % 2]
        srckey = "ZF" if l == 0 else ("HB", (l - 1) % 2)
        for sc in range(4):
            bank = 5 + (fcnt % 2)
            fcnt += 1
            sch.op("pe", lambda e, l=l, sc=sc, bank=bank, src=src: e.matmul(
                psb(bank)[0:64, :], CF[0:FK[l], FWOFF[l]:FWOFF[l] + 64], src[0:FK[l], sc * 512:(sc + 1) * 512],
                start=True, stop=True), r=["CF", (srckey, sc) if l else "ZF"], w=[("ps", bank)])
            sch.op("act", lambda e, l=l, bank=bank: e.activation(
                out=ARG[0:64, :], in_=psb(bank)[0:64, :], func=AF.Identity,
                scale=CF[0:64, CF_FB + 3:CF_FB + 4], bias=CF[0:64, CF_FBF + l:CF_FBF + l + 1]),
                r=[("ps", bank), ("fbf", l), "CF"], w=["ARG"])
            sch.op("dve", lambda e: e.tensor_scalar(out=ARG2[0:64, :], in0=ARG[0:64, :], scalar1=PI, scalar2=2 * PI,
                                                    op0=ALU.is_gt, op1=ALU.mult), r=["ARG"], w=["ARG2"])
            sch.op("dve", lambda e: e.tensor_tensor(out=ARG[0:64, :], in0=ARG[0:64, :], in1=ARG2[0:64, :], op=ALU.subtract),
                   r=["ARG", "ARG2"], w=["ARG"])
            sch.op("dve", lambda e: e.tensor_scalar(out=ARG2[0:64, :], in0=ARG[0:64, :], scalar1=-PI, scalar2=2 * PI,
                                                    op0=ALU.is_lt, op1=ALU.mult), r=["ARG"], w=["ARG2"])
            sch.op("dve", lambda e: e.tensor_tensor(out=ARG[0:64, :], in0=ARG[0:64, :], in1=ARG2[0:64, :], op=ALU.add),
                   r=["ARG", "ARG2"], w=["ARG"])
            if l < 2:
                dst = HB[l % 2][0:64, sc * 512:(sc + 1) * 512]
                dkey = (("HB", l % 2), sc)
            else:
                dst = H3[0:64, sc * 512:(sc + 1) * 512]
                dkey = ("H3", sc)
            sch.op("act", lambda e, dst=dst: e.activation(out=dst, in_=ARG[0:64, :], func=AF.Sin),
                   r=["ARG"], w=[dkey])


    ABv = [[V(P_H + 8192 * sl + 4096 * w_, 4096, BF16, "p (t c) -> p t c", t=16) for w_ in range(2)] for sl in range(2)]
    kcnt = [0]
    panel_seq = []
    for _p in range(4):
        panel_seq += [(dftK, m) for m in range(7, -1, -1)]
    for _h in range(2):
        for _c in range(2):
            panel_seq += [(dftF, m) for m in range(NT)]
            panel_seq += [(dftI, m) for m in range(NT)]
    pst = {"use": 0, "issued": 0}

    def _issue_panel():
        i = pst["issued"]
        if i >= len(panel_seq):
            return
        src_d, m = panel_seq[i]
        sl = i % 2
        pst["issued"] += 1
        sch.op("sp", lambda e, sl=sl, m=m, src_d=src_d: e.dma_start(out=PAN[sl], in_=src_d[m]), w=[("PAN", sl)], slot=("pan", sl))

    def load_panel(src_d, m):
        i = pst["use"]
        assert panel_seq[i][1] == m
        while pst["issued"] <= i:
            _issue_panel()
        if i == 0:
            _issue_panel()
        pst["use"] += 1
        return i % 2

    def filt_steps(pss, st):
        return [lambda: filt_gen_tile(pss, st, 0), lambda: filt_gen_tile(pss, st, 1)]

    def filt_gen_tile(pss, st, only=None):
        od, hf = pss // 2, pss % 2
        sl = pss % 2
        A_, B_ = ABv[sl]
        c0 = od * 1024 + hf * 512
        if only in (None, 0):
            dsl = st % 2
            sch.op("act", lambda e, st=st, hf=hf, dsl=dsl: e.activation(out=DEC[:, 512 * dsl:512 * dsl + 512], in_=ADL[:, hf * 512:(hf + 1) * 512], func=AF.Exp,
                                                                       scale=CF[:, CF_TNEG + st:CF_TNEG + st + 1]),
                   r=["ADL", "CF"], w=[("DEC", dsl)])
        dsl = st % 2
        for which in ((0, 1) if only is None else (only,)):
            bank = 5 + (kcnt[0] % 2)
            kcnt[0] += 1
            wsrc = WSUM if which == 0 else WDIF
            dst = (A_, B_)[which]
            sch.op("pe", lambda e, st=st, bank=bank, wsrc=wsrc, c0=c0: e.matmul(
                psb(bank), H3[0:64, 256 * (st % 8) + st // 8:256 * (st % 8) + st // 8 + 255:2], wsrc[0:64, c0:c0 + 512],
                start=True, stop=True), r=[("H3", (st % 8) // 2), "WSUM", "WDIF"], w=[("ps", bank)])
            sch.op("dve", lambda e, st=st, bank=bank, dst=dst, dsl=dsl: e.tensor_tensor(
                out=dst[:, st, :], in0=psb(bank), in1=DEC[:, 512 * dsl:512 * dsl + 512], op=ALU.mult),
                r=[("ps", bank), ("DEC", dsl)], w=[("AB", sl, which, st)])
        if st == 0 and only in (None, 1):
            bank = 5 + (kcnt[0] % 2)
            kcnt[0] += 1
            sch.op("pe", lambda e, bank=bank, c0=c0: e.matmul(
                psb(bank)[0:1, :], H3[0:64, 0:1], WF16[0:64, c0:c0 + 512], start=True, stop=True),
                r=[("H3", 0), "WF16"], w=[("ps", bank)])
            sch.op("dve", lambda e, bank=bank, c0=c0: e.tensor_tensor(
                out=TROW[0:1, :], in0=psb(bank)[0:1, :], in1=HYB[0:1, c0:c0 + 512], op=ALU.add),
                r=[("ps", bank), "HYB"], w=["TROW"])
            sch.op("dve", lambda e, A_=A_: e.tensor_copy(out=A_[0:1, 0, :], in_=TROW[0:1, :]),
                   r=["TROW"], w=[("AB", sl, 0, 0)])
            sch.op("dve", lambda e, B_=B_: e.tensor_scalar(
                out=B_[0:1, 0, :], in0=TROW[0:1, :], scalar1=-1.0, scalar2=None, op0=ALU.mult),
                r=["TROW"], w=[("AB", sl, 1, 0)])

    EPSC = CF[:, CF_FBF + 3:CF_FBF + 4]
    sch.op("dve", lambda e: e.memset(EPSC, EPS), r=["CF"], w=["EPSC"])
    HNT = V(P_H, 8192, BF16, "p (j s) -> p j s", j=8)
    XC = [V(P_T0 + 4096 * i, 4096, F32, "p (j s) -> p j s", j=8) for i in range(2)]
    xT_v = xT.rearrange("(j p) s -> p j s", p=128)
    cnt = {"mb": 0, "rs": 0, "win": 0, "kst": 0, "pq": 0, "py": 0}

    def mbank():
        b = 6 + (cnt["mb"] % 2)
        cnt["mb"] += 1
        return b

    def norm_chunk(src, src_keys, layer, gi, dst_of_j, dst_keys_of_j, bank=None, part=None):
        if part in (None, "a"):
            sch.op("act", lambda e: e.activation(out=SQ, in_=src, func=AF.Square), r=src_keys, w=[("SQj", j) for j in range(8)])
        if part == "a":
            return
        rs = cnt["rs"] % 2
        cnt["rs"] += 1
        bank = mbank() if bank is None else bank
        for j in range(8):
            sch.op("pe", lambda e, j=j, bank=bank: e.matmul(psb(bank), ONES, SQ[:, j, :], start=(j == 0), stop=(j == 7)),
                   r=[("SQj", j), "CB"], w=[("ps", bank)])
        sch.op("act", lambda e, bank=bank: e.activation(out=LNB, in_=psb(bank), func=AF.Ln, scale=1.0 / D, bias=EPSC),
               r=[("ps", bank), "EPSC"], w=["LNB"])
        sch.op("act", lambda e, rs=rs: e.activation(out=RSTD[rs], in_=LNB, func=AF.Exp, scale=-0.5), r=["LNB"], w=[("RSTD", rs)])
        for j in range(8):
            sch.op("dve", lambda e, j=j, rs=rs: e.scalar_tensor_tensor(
                out=dst_of_j(j), in0=src[:, j, :], scalar=gcol(layer, gi, j), in1=RSTD[rs], op0=ALU.mult, op1=ALU.mult),
                r=src_keys + [("RSTD", rs), "CF"], w=dst_keys_of_j(j))


    def prenorm_steps():
        sch.retire(["WOF", "WSUM", "WDIF", "ARG", "ARG2", "ADL", ("DEC", 0), ("DEC", 1)], [("XC", 0), ("XC", 1)])
        sch.retire([("AB", 0, w_, st_) for w_ in range(2) for st_ in range(NT)], [("HNT", sc_) for sc_ in range(4)])
        steps = []
        for sc in range(4):
            def _sta(sc=sc):
                xs = sc % 2
                sch.op("sp", lambda e: e.dma_start(out=XC[xs], in_=xT_v[:, :, sc * 512:(sc + 1) * 512]),
                       w=[("XC", xs)], slot=("xc", xs))
                norm_chunk(XC[xs], [("XC", xs)], 0, 0, None, None, part="a")

            def _stb(sc=sc):
                xs = sc % 2
                norm_chunk(XC[xs], [("XC", xs)], 0, 0,
                           lambda j: HNT[:, j, sc * 512:(sc + 1) * 512], lambda j: [("HNT", sc)], bank=5 + sc % 2, part="b")
            steps.append((_sta, _stb))
        return steps

    KB0 = P_TMP
    STGP = [V(KB0 + 512 * i, 512, BF16, "p (a c) -> p a c", a=2) for i in range(2)]
    STGM = [V(KB0 + 1024 + 512 * i, 512, BF16, "p (a c) -> p a c", a=2) for i in range(3)]
    STGR = [V(KB0 + 2560 + 512 * i, 512, BF16, "p (a c) -> p a c", a=2) for i in range(2)]
    OSB = [V(KB0 + 3584 + 512 * i, 512) for i in range(2)]
    SPEC = V(KB0 + 4608, 512, BF16, "p (a c) -> p a c", a=2)
    assert KB0 + 5120 <= P_PAN
    S11 = 2.0 ** -11
    for st in range(NT):
        filt_gen_tile(0, st)
    sch.op("dve", lambda e: e.memset(SPEC, 0.0), w=["SPEC"])
    gcount = [0]
    for pss in range(4):
        od, hf = pss // 2, pss % 2
        sl = pss % 2
        A_, B_ = ABv[sl]
        c0 = od * 1024 + hf * 512
        for a_, (src_, k0) in enumerate(((A_, 0), (B_, 8))):
            bk = 4 if a_ == 0 else 7
            for k in range(8):
                sch.op("pe", lambda e, k=k, k0=k0, src_=src_, bk=bk: e.matmul(
                    psb(bk)[0:1, :], SGNC, src_[:, k0 + k, :], start=(k == 0), stop=(k == 7)),
                    r=["CB", ("AB", sl, a_, k0 + k)], w=[("ps", bk)])
            sch.op("act", lambda e, a_=a_, bk=bk: e.activation(out=SPEC[0:1, a_, :], in_=psb(bk)[0:1, :], func=AF.Copy, scale=S11),
                   r=[("ps", bk)], w=["SPEC"])
        prev_m = None
        pending = None

        def emit_rev(pend):
            m_, ss_, sm_, prev_t, prev_keys = pend
            for a_ in range(2):
                bk = 4 if a_ == 0 else 7
                sch.op("pe", lambda e, a_=a_, bk=bk: e.matmul(psb(bk), JREV, STGM[sm_][:, a_, :], start=True, stop=False),
                       r=["CB", ("STGM", sm_, a_)], w=[("ps", bk)])
                sch.op("pe", lambda e, a_=a_, bk=bk: e.matmul(psb(bk), E00, prev_t[:, a_, :], start=False, stop=True),
                       r=["CB"] + prev_keys, w=[("ps", bk)])
                sch.op("act", lambda e, a_=a_, bk=bk: e.activation(out=STGR[ss_][:, a_, :], in_=psb(bk), func=AF.Copy),
                       r=[("ps", bk)], w=[("STGR", ss_, a_)])
            sch.op("act", lambda e, od_=od, hf_=hf: e.dma_start(out=kco_d[od_, 15 - m_, :, :, hf_ * 512:(hf_ + 1) * 512], in_=STGR[ss_]),
                   r=[("STGR", ss_, 0), ("STGR", ss_, 1)], w=[("kco", od, 15 - m_, hf)], slot=("kst_outr", ss_))

        fq = []
        if pss + 1 < 4:
            for st_ in range(NT):
                fq += filt_steps(pss + 1, st_)
        else:
            pn = prenorm_steps()
            nop = lambda: None
            fq += [pn[0][0], nop, nop, nop, pn[0][1], pn[1][0], nop, nop, nop, pn[1][1], pn[2][0], nop, nop, nop, pn[2][1],
                   pn[3][0], nop, nop, nop, pn[3][1]]
        for m in range(7, -1, -1):
            psl = load_panel(dftK, m)
            g = gcount[0]
            gcount[0] += 1
            ss = g % 2
            sm = g % 3
            for a_, src_ in enumerate((A_, B_)):
                be, bo = 2 * a_, 2 * a_ + 1
                for k in range(8):
                    sch.op("pe", lambda e, k=k, psl=psl, be=be, a_=a_, src_=src_: e.matmul(
                        psb(be), PAN[psl][:, a_, k, :], src_[:, k, :], start=(k == 0), stop=(k == 7)),
                        r=[("PAN", psl), ("AB", sl, a_, k)], w=[("ps", be)])
                if fq:
                    fq.pop(0)()
                for k in range(8, 16):
                    sch.op("pe", lambda e, k=k, psl=psl, bo=bo, a_=a_, src_=src_: e.matmul(
                        psb(bo), PAN[psl][:, a_, k, :], src_[:, k, :], start=(k == 8), stop=(k == 15)),
                        r=[("PAN", psl), ("AB", sl, a_, k)], w=[("ps", bo)])
                if fq:
                    fq.pop(0)()
                if a_ == 1:
                    _issue_panel()
                    if pending is not None:
                        emit_rev(pending)
                        pending = None
                sch.op("act", lambda e, bo=bo, a_=a_: e.activation(out=OSB[a_], in_=psb(bo), func=AF.Copy, scale=S11),
                       r=[("ps", bo)], w=[("OSB", a_)])
                sch.op("dve", lambda e, be=be, a_=a_, ss=ss: e.scalar_tensor_tensor(
                    out=STGP[ss][:, a_, :], in0=psb(be), scalar=S11, in1=OSB[a_], op0=ALU.mult, op1=ALU.add),
                    r=[("ps", be), ("OSB", a_)], w=[("STGP", ss, a_)])
                if a_ == 0:
                    sch.op("dve", lambda e, be=be, sm=sm: e.scalar_tensor_tensor(
                        out=STGM[sm][:, 0, :], in0=psb(be), scalar=S11, in1=OSB[0], op0=ALU.mult, op1=ALU.subtract),
                        r=[("ps", be), ("OSB", 0)], w=[("STGM", sm, 0)])
                else:
                    sch.op("dve", lambda e, be=be, sm=sm: e.scalar_tensor_tensor(
                        out=STGM[sm][:, 1, :], in0=psb(be), scalar=-S11, in1=OSB[1], op0=ALU.mult, op1=ALU.add),
                        r=[("ps", be), ("OSB", 1)], w=[("STGM", sm, 1)])
            if m == 0:
                sch.op("dve", lambda e, ss=ss: e.tensor_scalar(out=STGP[ss][0:1, 0, :], in0=STGP[ss][0:1, 0, :],
                                                               scalar1=0.5, scalar2=None, op0=ALU.mult),
                       r=[("STGP", ss, 0)], w=[("STGP", ss, 0)])
                sch.op("dve", lambda e, ss=ss: e.memset(STGP[ss][0:1, 1, :], 0.0), w=[("STGP", ss, 1)])
                sch.op("dve", lambda e, sm=sm, c0=c0: e.tensor_scalar(out=KNY[0:1, c0:c0 + 512], in0=STGM[sm][0:1, 0, :],
                                                                      scalar1=0.5, scalar2=None, op0=ALU.mult),
                       r=[("STGM", sm, 0)], w=[("KNY", pss)])
            sch.op("act", lambda e, ss=ss, od=od, m=m, hf=hf: e.dma_start(
                out=kco_d[od, m, :, :, hf * 512:(hf + 1) * 512], in_=STGP[ss]),
                r=[("STGP", ss, 0), ("STGP", ss, 1)], w=[("kco", od, m, hf)], slot=("kst_out", ss))
            prev_t = SPEC if prev_m is None else STGM[prev_m]
            prev_keys = ["SPEC"] if prev_m is None else [("STGM", prev_m, 0), ("STGM", prev_m, 1)]
            pending = (m, ss, sm, prev_t, prev_keys)
            prev_m = sm
        while fq:
            fq.pop(0)()
        emit_rev(pending)

    if stage == 0:
        sch.op("sp", lambda e: e.dma_start(out=outT[0:128, 0:1024].bitcast(BF16).rearrange("p (a c) -> p a c", a=2),
                                           in_=kco_d[0, 1, :, :, :]), r=[("kco", 0, 1, 0), ("kco", 0, 1, 1)], w=["out"], slot="out")
        sch.op("sp", lambda e: e.dma_start(out=outT[128:256, 0:1024].bitcast(BF16).rearrange("p (a c) -> p a c", a=2),
                                           in_=kco_d[1, 0, :, :, :]), r=[("kco", 1, 0, 0), ("kco", 1, 0, 1)], w=["out2"], slot="out")
        sch.op("sp", lambda e: e.dma_start(out=outT[256:257, 0:16], in_=outT[257:258, 0:16]), r=["out", "out2"], slot="fin")
        sch.emit(nc, stack)
        return nc, stack


    sch.fence_all()
    YA = V(P_H + 8192, 4096, BF16, "p (t c) -> p t c", t=16)
    YB = V(P_H + 12288, 4096, BF16, "p (t c) -> p t c", t=16)
    Z2T = V(P_Z2T, 8192, BF16, "p (j s) -> p j s", j=8)
    T0 = V(P_T0, 4096, BF16, "p (t c) -> p t c", t=16)
    T1 = V(P_T1, 4096, BF16, "p (t c) -> p t c", t=16)
    RAW = V(P_RAW, 2052)
    TT = V(P_TT, 2048)
    TMP = [V(P_TMP + 512 * i, 512) for i in range(4)]
    KST = [V(P_KST + 512 * i, 512, BF16, "p (a c) -> p a c", a=2) for i in range(2)]
    WIN = [V(P_WIN + 1024 * i, 1024, BF16, "p (j c) -> p j c", j=8) for i in range(2)]
    H = V(P_H, 16384, F32, "p (j s) -> p j s", j=8)
    MB = V(P_PAN, 4096, F32, "p (j s) -> p j s", j=8)
    outT_v = outT.rearrange("(j p) s -> p j s", p=128)
    def branch_finish(sc, m_keys_ready, layer, gi, res_src, res_keys, tok0):
        rs = cnt["rs"] % 2
        cnt["rs"] += 1
        bank = mbank()
        for j in range(8):
            sch.op("pe", lambda e, j=j, bank=bank: e.matmul(psb(bank), ONES, SQ[:, j, :], start=(j == 0), stop=(j == 7)),
                   r=[("SQj", j), "CB"], w=[("ps", bank)])
        sch.op("act", lambda e, bank=bank: e.activation(out=LNB, in_=psb(bank), func=AF.Ln, scale=1.0 / D, bias=EPSC),
               r=[("ps", bank), "EPSC"], w=["LNB"])
        sch.op("act", lambda e, rs=rs: e.activation(out=RSTD[rs], in_=LNB, func=AF.Exp, scale=-0.5), r=["LNB"], w=[("RSTD", rs)])
        for j in range(8):
            sch.op("dve", lambda e, j=j, rs=rs: e.scalar_tensor_tensor(
                out=MB[:, j, :], in0=MB[:, j, :], scalar=gcol(layer, gi, j), in1=RSTD[rs], op0=ALU.mult, op1=ALU.mult),
                r=[("MB", j), ("RSTD", rs), "CF"], w=[("MB", j)])
            sch.op("dve", lambda e, j=j: e.tensor_tensor(
                out=H[:, j, tok0:tok0 + 512], in0=res_src(j), in1=MB[:, j, :], op=ALU.add),
                r=[("MB", j)] + res_keys(j), w=[("H", j, tok0 // 512)])

    def evac_m(bank, j):
        sch.op("act", lambda e: e.activation(out=MB[:, j, :], in_=psb(bank), func=AF.Copy), r=[("ps", bank)], w=[("MB", j)])
        sch.op("act", lambda e: e.activation(out=SQ[:, j, :], in_=psb(bank), func=AF.Square), r=[("ps", bank)], w=[("SQj", j)])


    sch.op("dve", lambda e: e.memset(RAW[:, 0:1], 0.0), w=["RAWpad"])
    sch.op("dve", lambda e: e.memset(RAW[:, 2049:2050], 0.0), w=["RAWpad2"])
    w_in_v = w_in_d.rearrange("(j p) n -> p j n", p=128)

    def inproj(strm, hf):
        c30 = strm * 1024 + hf * 512
        for sb in range(2):
            ws = cnt["win"] % 2
            cnt["win"] += 1
            sch.op("pool", lambda e, ws=ws, c=c30 + 256 * sb: e.dma_start(out=WIN[ws], in_=w_in_v[:, :, c:c + 256]),
                   w=[("WIN", ws)], slot=("win", ws))
            for q2 in range(2):
                q4 = sb * 2 + q2
                q = c30 // 128 + q4
                for sc in range(4):
                    bank = mbank()
                    for j in range(8):
                        sch.op("pe", lambda e, j=j, ws=ws, q2=q2, sc=sc, bank=bank: e.matmul(
                            psb(bank), WIN[ws][:, j, q2 * 128:(q2 + 1) * 128], HNT[:, j, sc * 512:(sc + 1) * 512],
                            start=(j == 0), stop=(j == 7)), r=[("WIN", ws), ("HNT", sc)], w=[("ps", bank)])
                    sch.op("act", lambda e, sc=sc, bank=bank: e.activation(out=RAW[:, 1 + sc * 512:1 + (sc + 1) * 512], in_=psb(bank), func=AF.Copy),
                           r=[("ps", bank)], w=[("RAW", sc)])
                rawk = [("RAW", i) for i in range(4)] + ["RAWpad", "RAWpad2"]
                sch.op("act", lambda e, q=q: e.activation(out=TT, in_=RAW[:, 0:2048], func=AF.Identity,
                                                          scale=CF[:, CF_CW + q:CF_CW + q + 1], bias=CF[:, CF_CW + 72 + q:CF_CW + 72 + q + 1]),
                       r=rawk + ["CF"], w=["TT"])
                sch.op("dve", lambda e, q=q: e.scalar_tensor_tensor(out=TT, in0=RAW[:, 1:2049], scalar=CF[:, CF_CW + 24 + q:CF_CW + 24 + q + 1],
                                                                    in1=TT, op0=ALU.mult, op1=ALU.add), r=rawk + ["TT", "CF"], w=["TT"])
                sch.op("dve", lambda e, q=q, zc=hf * 4 + q4: e.scalar_tensor_tensor(
                    out=Z2T[:, zc, :], in0=RAW[:, 2:2050], scalar=CF[:, CF_CW + 48 + q:CF_CW + 48 + q + 1],
                    in1=TT, op0=ALU.mult, op1=ALU.add), r=rawk + ["TT", "CF"], w=[("Z2T", hf * 4 + q4)])

    def transp_to(hf, T, Tn):
        for q4 in range(4):
            for stg in range(2):
                bank = mbank()
                pv = psb16(bank)
                for i in range(8):
                    st = stg * 8 + i
                    sch.op("pe", lambda e, i=i, st=st, q4=q4, pv=pv: e.transpose(
                        pv[:, i * 128:(i + 1) * 128], Z2T[:, hf * 4 + q4, st * 128:(st + 1) * 128], IDENT),
                        r=[("Z2T", hf * 4 + q4), "CB"], w=[("ps", bank)])
                sch.op("act", lambda e, pv=pv, stg=stg, q4=q4: e.activation(
                    out=T[:, stg * 8:(stg + 1) * 8, q4 * 128:(q4 + 1) * 128], in_=pv.rearrange("p (i c) -> p i c", i=8), func=AF.Copy),
                    r=[("ps", bank)], w=[(Tn, st_) for st_ in range(stg * 8, stg * 8 + 8)])

    def transp_back(hf, T, Tn):
        for q4 in range(4):
            for stg in range(2):
                bank = mbank()
                pv = psb16(bank)
                for i in range(8):
                    st = stg * 8 + i
                    sch.op("pe", lambda e, i=i, st=st, q4=q4, pv=pv: e.transpose(
                        pv[:, i * 128:(i + 1) * 128], T[:, st, q4 * 128:(q4 + 1) * 128], IDENT),
                        r=[(Tn, st), "CB"], w=[("ps", bank)])
                sch.op("act", lambda e, pv=pv, stg=stg, q4=q4: e.activation(
                    out=Z2T[:, hf * 4 + q4, stg * 1024:(stg + 1) * 1024], in_=pv, func=AF.Copy),
                    r=[("ps", bank)], w=[("Z2T", hf * 4 + q4)])

    def fwd_dft(T, Tn, od, hf):
        c0 = od * 1024 + hf * 512
        for ft in range(NT):
            psl = load_panel(dftF, ft)
            ks = cnt["kst"] % 2
            cnt["kst"] += 1
            sch.op("sp", lambda e, ks=ks, ft=ft: e.dma_start(out=KST[ks], in_=kco_d[od, ft, :, :, hf * 512:(hf + 1) * 512]),
                   r=[("kco", od, ft, hf)], w=[("KST", ks)], slot=("kst", ks))
            pq = cnt["pq"] % 2
            cnt["pq"] += 1
            bp, bq = 2 * pq, 2 * pq + 1
            for st in range(NT):
                sch.op("pe", lambda e, st=st, psl=psl, bp=bp: e.matmul(
                    psb(bp), PAN[psl][:, 0, st, :], T[:, st, :], start=(st == 0), stop=(st == NT - 1)),
                    r=[("PAN", psl), (Tn, st)], w=[("ps", bp)])
            for st in range(NT):
                sch.op("pe", lambda e, st=st, psl=psl, bq=bq: e.matmul(
                    psb(bq), PAN[psl][:, 1, st, :], T[:, st, :], start=(st == 0), stop=(st == NT - 1)),
                    r=[("PAN", psl), (Tn, st)], w=[("ps", bq)])
            _issue_panel()
            Ka, Kb = KST[ks][:, 0, :], KST[ks][:, 1, :]
            kk = [("KST", ks)]
            tt = sch.op
            tt("dve", lambda e, bp=bp, Ka=Ka: e.tensor_tensor(out=TMP[0], in0=psb(bp), in1=Ka, op=ALU.mult), r=[("ps", bp)] + kk, w=[("TMP", 0)])
            tt("dve", lambda e, bq=bq, Kb=Kb: e.tensor_tensor(out=TMP[1], in0=psb(bq), in1=Kb, op=ALU.mult), r=[("ps", bq)] + kk, w=[("TMP", 1)])
            tt("dve", lambda e, ft=ft: e.tensor_tensor(out=YA[:, ft, :], in0=TMP[0], in1=TMP[1], op=ALU.add),
               r=[("TMP", 0), ("TMP", 1)], w=[("YA", ft)])
            tt("dve", lambda e, bq=bq, Ka=Ka: e.tensor_tensor(out=TMP[2], in0=psb(bq), in1=Ka, op=ALU.mult), r=[("ps", bq)] + kk, w=[("TMP", 2)])
            tt("dve", lambda e, bp=bp, Kb=Kb: e.tensor_tensor(out=TMP[3], in0=psb(bp), in1=Kb, op=ALU.mult), r=[("ps", bp)] + kk, w=[("TMP", 3)])
            tt("dve", lambda e, ft=ft: e.tensor_tensor(out=YB[:, ft, :], in0=TMP[2], in1=TMP[3], op=ALU.subtract),
               r=[("TMP", 2), ("TMP", 3)], w=[("YB", ft)])
            if ft == 0:
                tt("dve", lambda e, bq=bq: e.tensor_tensor(out=YB[0:1, 0, :], in0=psb(bq)[0:1, :], in1=KNY[0:1, c0:c0 + 512], op=ALU.mult),
                   r=[("ps", bq), ("KNY", od * 2 + hf)], w=[("YB", 0)])

    def inv_dft(T, Tn):
        for tt_ in range(NT):
            psl = load_panel(dftI, tt_)
            by = 4 + (cnt["py"] % 2)
            cnt["py"] += 1
            for ft in range(NT):
                sch.op("pe", lambda e, ft=ft, psl=psl, by=by: e.matmul(
                    psb(by), PAN[psl][:, 0, ft, :], YA[:, ft, :], start=(ft == 0), stop=False),
                    r=[("PAN", psl), ("YA", ft)], w=[("ps", by)])
                sch.op("pe", lambda e, ft=ft, psl=psl, by=by: e.matmul(
                    psb(by), PAN[psl][:, 1, ft, :], YB[:, ft, :], start=False, stop=(ft == NT - 1)),
                    r=[("PAN", psl), ("YB", ft)], w=[("ps", by)])
            _issue_panel()
            sch.op("dve", lambda e, tt_=tt_, by=by: e.tensor_tensor(out=T[:, tt_, :], in0=psb(by), in1=T[:, tt_, :], op=ALU.mult),
                   r=[("ps", by), (Tn, tt_)], w=[(Tn, tt_)])

    for hf in range(2):
        inproj(0, hf)
        transp_to(hf, T0, "T0")
        inproj(1, hf)
        fwd_dft(T0, "T0", 0, hf)
        transp_to(hf, T1, "T1")
        inv_dft(T1, "T1")
        inproj(2, hf)
        fwd_dft(T1, "T1", 1, hf)
        transp_to(hf, T0, "T0")
        inv_dft(T0, "T0")
        transp_back(hf, T0, "T0")

    sch.fence_all()
    WO = V(P_RAW, 4096, BF16, "p (j d) -> p j d", j=8)
    sch.op("pool", lambda e: e.dma_start(out=WO, in_=w_out_d.rearrange("(j p) d -> p j d", p=128)), w=["WO"], slot="wo")
    XC2 = [V(P_T0 + 4096 * i, 4096, F32, "p (j s) -> p j s", j=8) for i in range(2)]
    for sc in range(4):
        xs = sc % 2
        sch.op("sp", lambda e, sc=sc, xs=xs: e.dma_start(out=XC2[xs], in_=xT_v[:, :, sc * 512:(sc + 1) * 512]),
               w=[("XC2", xs)], slot=("xc2", xs))
        for jd in range(8):
            bank = mbank()
            for cj in range(8):
                sch.op("pe", lambda e, cj=cj, jd=jd, sc=sc, bank=bank: e.matmul(
                    psb(bank), WO[:, cj, jd * 128:(jd + 1) * 128], Z2T[:, cj, sc * 512:(sc + 1) * 512],
                    start=(cj == 0), stop=(cj == 7)), r=["WO", ("Z2T", cj)], w=[("ps", bank)])
            evac_m(bank, jd)
        branch_finish(sc, None, 0, 1, lambda j, xs=xs: XC2[xs][:, j, :], lambda j, xs=xs: [("XC2", xs)], sc * 512)

    def write_out():
        for j in range(8):
            sch.op("sp", lambda e, j=j: e.dma_start(out=outT_v[:, j, :], in_=H[:, j, :]),
                   r=[("H", j, i) for i in range(4)], w=[("out", j)], slot=("out", j % 4))
        sch.op("sp", lambda e: e.dma_start(out=kco_d[0, 0, 0:1, 0, 0:8], in_=kco_d[0, 0, 1:2, 0, 0:8]),
               r=[("out", j) for j in range(8)], slot="fin")

    if stage == 1:
        write_out()
        sch.emit(nc, stack)
        return nc, stack


    def mlp(layer):
        sch.fence_all()
        tag = "L%d" % layer
        HNC = V(P_RAW, 4096, BF16, "p (j s) -> p j s", j=8)
        ACTB = V(P_Z2T, 16384, BF16, "p (f s) -> p f s", f=32)
        WU = [V(o_, 1024, BF16, "p (j c) -> p j c", j=8) for o_ in (P_TMP, P_TMP + 1024, P_X)]
        WD = [V(P_KST + 1024 * i, 1024, BF16, "p (f d) -> p f d", f=4) for i in range(3)]
        RL = [V(O_KNY + 256 * i, 256, BF16) for i in range(2)]
        wu_v = w_up_d[layer].rearrange("(j p) f -> p j f", p=128)
        wd_v = w_down_d[layer].rearrange("(f p) d -> p f d", p=128)
        c = {"wu": 0, "wd": 0, "rl": 0, "ub": 0}
        def _norm(tc):
            t0 = tc * 1024
            for sc2 in range(2):
                tok = t0 + sc2 * 512
                norm_chunk(H[:, :, tok:tok + 512], [("H", j, tok // 512) for j in range(8)], layer, 2,
                           lambda j, sc2=sc2: HNC[:, j, sc2 * 512:(sc2 + 1) * 512], lambda j, sc2=sc2: [(tag + "HNC", sc2)])

        def _up(tc):
            t0 = tc * 1024
            for slab in range(16):
                ws = c["wu"] % 3
                c["wu"] += 1
                sch.op("pool", lambda e, ws=ws, slab=slab: e.dma_start(out=WU[ws], in_=wu_v[:, :, slab * 256:(slab + 1) * 256]),
                       w=[(tag + "WU", ws)], slot=(tag + "wu", ws))
                for q2 in range(2):
                    ffc = slab * 2 + q2
                    for sc2 in range(2):
                        bank = 4 + (c["ub"] % 2)
                        c["ub"] += 1
                        for j in range(8):
                            sch.op("pe", lambda e, j=j, ws=ws, q2=q2, sc2=sc2, bank=bank: e.matmul(
                                psb(bank), WU[ws][:, j, q2 * 128:(q2 + 1) * 128], HNC[:, j, sc2 * 512:(sc2 + 1) * 512],
                                start=(j == 0), stop=(j == 7)), r=[(tag + "WU", ws), (tag + "HNC", sc2)], w=[("ps", bank)])
                        rl = c["rl"] % 2
                        c["rl"] += 1
                        sch.op("act", lambda e, bank=bank, rl=rl: e.activation(out=RL[rl], in_=psb(bank), func=AF.Relu),
                               r=[("ps", bank)], w=[(tag + "RL", rl)])
                        sch.op("dve", lambda e, rl=rl, ffc=ffc, sc2=sc2: e.tensor_tensor(
                            out=ACTB[:, ffc, sc2 * 512:(sc2 + 1) * 512], in0=RL[rl], in1=RL[rl], op=ALU.mult),
                            r=[(tag + "RL", rl)], w=[(tag + "ACT", ffc, sc2)])

        def _down(tc):
            t0 = tc * 1024
            for sc2 in range(2):
                tok = t0 + sc2 * 512
                for jh in range(2):
                    for slab in range(8):
                        ws = c["wd"] % 3
                        c["wd"] += 1
                        sch.op("pool", lambda e, ws=ws, slab=slab, jh=jh: e.dma_start(
                            out=WD[ws], in_=wd_v[:, slab * 4:(slab + 1) * 4, jh * 512:(jh + 1) * 512]),
                            w=[(tag + "WD", ws)], slot=(tag + "wd", ws))
                        for f4 in range(4):
                            ffc = slab * 4 + f4
                            for jq in range(4):
                                sch.op("pe", lambda e, ws=ws, f4=f4, jq=jq, ffc=ffc, sc2=sc2: e.matmul(
                                    psb(jq), WD[ws][:, f4, jq * 128:(jq + 1) * 128], ACTB[:, ffc, sc2 * 512:(sc2 + 1) * 512],
                                    start=(ffc == 0), stop=(ffc == 31)), r=[(tag + "WD", ws), (tag + "ACT", ffc, sc2)], w=[("ps", jq)])
                    for jq in range(4):
                        evac_m(jq, jh * 4 + jq)
                branch_finish(None, None, layer, 3, lambda j, tok=tok: H[:, j, tok:tok + 512],
                              lambda j, tok=tok: [("H", j, tok // 512)], tok)


        _norm(0)
        _up(0)
        _norm(1)
        _down(0)
        _up(1)
        _down(1)

    mlp(0)
    if stage == 2:
        write_out()
        sch.emit(nc, stack)
        return nc, stack


    sch.fence_all()
    HNT2 = V(P_Z2T, 8192, BF16, "p (j s) -> p j s", j=8)
    QKT = V(P_T0, 12288, BF16, "p (c s) -> p c s", c=12)
    VTOK = V(P_TMP, 2112, BF16, "p (t g e) -> p t g e", t=16, g=4)
    WQ = [V(P_WIN + 1024 * i, 1024, BF16, "p (j c) -> p j c", j=8) for i in range(2)]
    ATC = V(P_PAN, 2368, BF16)
    MASK = ATC[:, 0:384]
    PERMR = ATC[:, 384:512]
    PERMH = ATC[:, 512:640]
    COS = ATC[:, 640:2688]
    SIN = ATC[:, 2688:4736]
    PTS = [V(P_PAN + 2368 + 192 * i, 192, BF16) for i in range(8)]
    ESK = V(P_PAN + 3904, 16)
    RDN = V(P_PAN + 3920, 16)
    XB = [V(O_KNY + 256 * i, 256, BF16) for i in range(2)] + [V(P_Y, 256, BF16)]
    XS = [V(O_KNY + 512 + 256 * i, 256, BF16) for i in range(2)]
    wqkv_v = wqkv_d.rearrange("(j p) n -> p j n", p=128)
    sch.op("sp", lambda e: e.dma_start(out=ATC, in_=atc_d), w=["ATC"], slot="atc")
    sch.op("sp", lambda e: e.dma_start(out=ESK, in_=sink_d.partition_broadcast(128)), w=["ESK"], slot="esk")
    sch.op("act", lambda e: e.activation(out=ESK, in_=ESK, func=AF.Exp), r=["ESK"], w=["ESK"])
    for sc in range(4):
        norm_chunk(H[:, :, sc * 512:(sc + 1) * 512], [("H", j, sc) for j in range(8)], 1, 0,
                   lambda j, sc=sc: HNT2[:, j, sc * 512:(sc + 1) * 512], lambda j, sc=sc: [("HNT2", sc)])
    ac = {"wq": 0, "xb": 0, "pt": 0, "sb": 0, "pv": 0}
    KZ = [[QKT[:, 8, :], QKT[:, 9, :]], [QKT[:, 10, :], QKT[:, 11, :]],
          [V(O_SQ, 1024, BF16), V(O_SQ + 1024, 1024, BF16)], [V(O_RSTD, 1024, BF16), V(P_X, 1024, BF16)]]
    sch.retire([("SQj", j) for j in range(8)] + [("RSTD", 0), ("RSTD", 1)],
               [("KZ", g_, v_, s_) for g_ in (2, 3) for v_ in range(2) for s_ in range(4)] + [("KZz", g_, v_) for g_ in (2, 3) for v_ in range(2)])
    for g_ in range(4):
        sch.op("dve", lambda e, g_=g_: e.memset(KZ[g_][0][64:128, :], 0.0), w=[("KZz", g_, 0)])
        sch.op("dve", lambda e, g_=g_: e.memset(KZ[g_][1][0:64, :], 0.0), w=[("KZz", g_, 1)])
    dst_chunk = [0, 1, 2, 3, 4, 5, 6, 7, 8, 10]
    rb = [0]

    def rbank():
        b_ = 4 + (rb[0] % 2)
        rb[0] += 1
        return b_

    def proj_unit(ws, q2, sc, xb):
        bank = mbank()
        for j in range(8):
            sch.op("pe", lambda e, j=j: e.matmul(
                psb(bank), WQ[ws][:, j, q2 * 128:(q2 + 1) * 128], HNT2[:, j, sc * 512:(sc + 1) * 512],
                start=(j == 0), stop=(j == 7)), r=[("WQ", ws), ("HNT2", sc)], w=[("ps", bank)])
        sch.op("act", lambda e: e.activation(out=XB[xb], in_=psb(bank), func=AF.Copy),
               r=[("ps", bank)], w=[("XB", xb)])

    def rope_unit(dc, sc, xb, xs):
        bank2 = rbank()
        sch.op("pe", lambda e: e.matmul(psb(bank2), PERMR, XB[xb], start=True, stop=True),
               r=[("XB", xb), "ATC"], w=[("ps", bank2)])
        sch.op("dve", lambda e: e.tensor_tensor(out=XS[xs], in0=psb(bank2), in1=SIN[:, sc * 512:(sc + 1) * 512], op=ALU.mult),
               r=[("ps", bank2), "ATC"], w=[("XS", xs)])
        sch.op("pool", lambda e: e.tensor_tensor(out=XB[xb], in0=XB[xb], in1=COS[:, sc * 512:(sc + 1) * 512], op=ALU.mult),
               r=[("XB", xb), "ATC"], w=[("XB", xb)])
        if dc < 8:
            sch.op("dve", lambda e: e.tensor_tensor(out=QKT[:, dc, sc * 512:(sc + 1) * 512], in0=XB[xb], in1=XS[xs], op=ALU.add),
                   r=[("XB", xb), ("XS", xs)], w=[("QKT", dc, sc)])
        else:
            g0, g1 = (0, 1) if dc == 8 else (2, 3)
            cs = slice(sc * 512, (sc + 1) * 512)
            sch.op("dve", lambda e: e.tensor_tensor(out=XS[xs], in0=XB[xb], in1=XS[xs], op=ALU.add),
                   r=[("XB", xb), ("XS", xs)], w=[("XS", xs)])
            sch.op("act", lambda e: e.activation(out=KZ[g0][0][0:64, cs], in_=XS[xs][0:64, :], func=AF.Copy),
                   r=[("XS", xs)], w=[("KZ", g0, 0, sc)])
            sch.op("act", lambda e: e.activation(out=KZ[g1][1][64:128, cs], in_=XS[xs][64:128, :], func=AF.Copy),
                   r=[("XS", xs)], w=[("KZ", g1, 1, sc)])
            bank3 = rbank()
            sch.op("pe", lambda e: e.matmul(psb(bank3), PERMH, XS[xs], start=True, stop=True),
                   r=[("XS", xs), "ATC"], w=[("ps", bank3)])
            sch.op("act", lambda e: e.activation(out=KZ[g1][0][0:64, cs], in_=psb(bank3)[0:64, :], func=AF.Copy),
                   r=[("ps", bank3)], w=[("KZ", g1, 0, sc)])
            sch.op("act", lambda e: e.activation(out=KZ[g0][1][64:128, cs], in_=psb(bank3)[64:128, :], func=AF.Copy),
                   r=[("ps", bank3)], w=[("KZ", g0, 1, sc)])

    pend_rope = None
    ucount = 0
    for slab in range(5):
        ws = ac["wq"] % 2
        ac["wq"] += 1
        sch.op("pool", lambda e, ws=ws, slab=slab: e.dma_start(out=WQ[ws], in_=wqkv_v[:, :, slab * 256:(slab + 1) * 256]),
               w=[("WQ", ws)], slot=("wq", ws))
        for q2 in range(2):
            dc = dst_chunk[slab * 2 + q2]
            for sc in range(4):
                xb = ucount % 3
                xs = ucount % 2
                ucount += 1
                proj_unit(ws, q2, sc, xb)
                if pend_rope is not None:
                    rope_unit(*pend_rope)
                pend_rope = (dc, sc, xb, xs)
    rope_unit(*pend_rope)
    ws = ac["wq"] % 2
    ac["wq"] += 1
    sch.op("pool", lambda e, ws=ws: e.dma_start(out=WQ[ws], in_=wqkv_v[:, :, 1280:1536]), w=[("WQ", ws)], slot=("wq", ws))
    sch.op("dve", lambda e: e.memset(VTOK[:, :, :, 64:66], 1.0), w=["VONE"])
    for st in range(NT):
        bank = mbank()
        for j in range(8):
            sch.op("pe", lambda e, j=j, ws=ws, st=st, bank=bank: e.matmul(
                psb(bank)[:, 0:256], HNT2[:, j, st * 128:(st + 1) * 128], WQ[ws][:, j, :],
                start=(j == 0), stop=(j == 7)), r=[("WQ", ws), ("HNT2", st // 4)], w=[("ps", bank)])
        sch.op("act", lambda e, bank=bank, st=st: e.activation(
            out=VTOK[:, st, :, 0:64], in_=psb(bank)[:, 0:256].rearrange("p (g e) -> p g e", g=4), func=AF.Copy),
            r=[("ps", bank)], w=[("VTOK", st)])

    OTOK = V(P_Z2T, 8192, BF16, "p (t c) -> p t c", t=16)
    sch.retire([("HNT2", i) for i in range(4)], [("OTOK", i) for i in range(16)])

    NSLOT = 8
    LAG = 3
    tidx = lambda h, j: h * NT + j

    def pv(h, i):
        g = h // 4
        kbs = [kb for kb in (i - 1, i, i + 1) if 0 <= kb < NT]
        pb = ac["pv"] % 4
        ac["pv"] += 1
        for n_, kb in enumerate(kbs):
            qlo = max(kb - 1, 0)
            slot = tidx(h, kb) % NSLOT
            off = (i - qlo) * 128
            sch.op("pe", lambda e, slot=slot, off=off, kb=kb, pb=pb, n_=n_: e.matmul(
                psb(pb)[:, 0:65], PTS[slot][:, off:off + 128], VTOK[:, kb, g, 0:65],
                start=(n_ == 0), stop=(n_ == len(kbs) - 1)),
                r=[("PT", slot), ("VTOK", kb), "VONE"], w=[("ps", pb)])
        rd = ac["pv"] % 4
        pv_pend.append((pb, h, i, rd))
        if len(pv_pend) >= 2:
            pv_flush()

    pv_pend = []

    def pv_flush():
        for (pb, h, i, rd) in pv_pend:
            sch.op("dve", lambda e, pb=pb, h=h, rd=rd: e.tensor_scalar(out=RDN[:, 2 * rd:2 * rd + 1], in0=psb(pb)[:, 64:65], scalar1=ESK[:, h:h + 1],
                                                                   scalar2=None, op0=ALU.add), r=[("ps", pb), "ESK"], w=[("RDN", rd)])
        for (pb, h, i, rd) in pv_pend:
            sch.op("dve", lambda e, rd=rd: e.reciprocal(out=RDN[:, 2 * rd + 1:2 * rd + 2], in_=RDN[:, 2 * rd:2 * rd + 1]), r=[("RDN", rd)], w=[("RDN2", rd)])
        for (pb, h, i, rd) in pv_pend:
            sch.op("dve", lambda e, pb=pb, h=h, i=i, rd=rd: e.tensor_scalar(out=OTOK[:, i, h * 64:(h + 1) * 64], in0=psb(pb)[:, 0:64],
                                                                           scalar1=RDN[:, 2 * rd + 1:2 * rd + 2], scalar2=None, op0=ALU.mult),
                   r=[("ps", pb), ("RDN2", rd)], w=[("OTOK", i)])
        del pv_pend[:]

    def scores(h, j):
        g = h // 4
        v = h % 2
        qc = h // 2
        qlo, qhi = max(j - 1, 0), min(j + 1, NT - 1)
        nq = qhi - qlo + 1
        bank = 4 + (ac["sb"] % 4)
        ac["sb"] += 1
        slot = tidx(h, j) % NSLOT
        sch.op("pe", lambda e: e.matmul(
            psb(bank)[:, 0:nq * 128], KZ[g][v][:, j * 128:(j + 1) * 128], QKT[:, qc, qlo * 128:(qhi + 1) * 128],
            start=True, stop=False),
            r=[("KZ", g, v, j // 4), ("KZz", g, v)] + [("QKT", qc, s_) for s_ in range(qlo // 4, qhi // 4 + 1)], w=[("ps", bank)])
        m0 = 128 if j == 0 else 0
        sch.op("pe", lambda e: e.matmul(psb(bank)[:, 0:nq * 128], IDENT, MASK[:, m0:m0 + nq * 128], start=False, stop=True),
               r=["ATC", "CB"], w=[("ps", bank)])
        sch.op("act", lambda e: e.activation(
            out=PTS[slot][:, 0:nq * 128], in_=psb(bank)[:, 0:nq * 128], func=AF.Exp, scale=0.125),
            r=[("ps", bank)], w=[("PT", slot)])

    pv_tasks = []
    for h in range(16):
        for i in range(NT):
            pv_tasks.append((tidx(h, min(i + 1, NT - 1)), h, i))
    pvi = 0
    for t in range(16 * NT + LAG):
        if t < 16 * NT:
            scores(t // NT, t % NT)
        while pvi < len(pv_tasks) and pv_tasks[pvi][0] + LAG <= t:
            pv(pv_tasks[pvi][1], pv_tasks[pvi][2])
            pvi += 1
    assert pvi == len(pv_tasks)
    if pv_pend:
        pv_flush()

    OT = V(P_T0, 8192, BF16, "p (j s) -> p j s", j=8)
    sch.retire([("QKT", c, s_) for c in range(8) for s_ in range(4)] + [("KZ", g_, v_, s_) for g_ in range(2) for v_ in range(2) for s_ in range(4)]
               + [("KZz", g_, v_) for g_ in range(2) for v_ in range(2)], [("OT", j) for j in range(8)] + ["WO2"])
    WO2 = V(P_RAW, 4096, BF16, "p (j d) -> p j d", j=8)
    sch.op("pool", lambda e: e.dma_start(out=WO2, in_=wo2_d.rearrange("(j p) d -> p j d", p=128)), w=["WO2"], slot="wo2")
    for cj in range(8):
        for stg in range(2):
            bank = mbank()
            pv_ = psb16(bank)
            for i in range(8):
                st = stg * 8 + i
                sch.op("pe", lambda e, i=i, st=st, cj=cj, pv_=pv_: e.transpose(
                    pv_[:, i * 128:(i + 1) * 128], OTOK[:, st, cj * 128:(cj + 1) * 128], IDENT),
                    r=[("OTOK", st), "CB"], w=[("ps", bank)])
            sch.op("act", lambda e, pv_=pv_, stg=stg, cj=cj: e.activation(
                out=OT[:, cj, stg * 1024:(stg + 1) * 1024], in_=pv_, func=AF.Copy), r=[("ps", bank)], w=[("OT", cj)])
    sch.fence_all()
    for sc in range(4):
        for jd in range(8):
            bank = mbank()
            for cj in range(8):
                sch.op("pe", lambda e, cj=cj, jd=jd, sc=sc, bank=bank: e.matmul(
                    psb(bank), WO2[:, cj, jd * 128:(jd + 1) * 128], OT[:, cj, sc * 512:(sc + 1) * 512],
                    start=(cj == 0), stop=(cj == 7)), r=["WO2", ("OT", cj)], w=[("ps", bank)])
            evac_m(bank, jd)
        branch_finish(sc, None, 1, 1, lambda j, sc=sc: H[:, j, sc * 512:(sc + 1) * 512], lambda j, sc=sc: [("H", j, sc)], sc * 512)
    if stage == 3:
        write_out()
        sch.emit(nc, stack)
        return nc, stack
    mlp(1)
    write_out()
    sch.emit(nc, stack)
    return nc, stack


def make_in_maps(inp):
    cf, cb, zf = _host_consts(inp)
    fwd, inv, dk = _dft_tables()
    adl = _absdelta().reshape(1, D)
    hyb = np.ascontiguousarray(np.asarray(inp["hy_bias"], np.float32)[0].reshape(1, 2 * D))
    x = np.asarray(inp["x"], np.float32)
    common = {
        "cf": cf, "cb": cb, "zf": zf, "adl": adl, "hyb": hyb, "dftF": fwd, "dftI": inv, "dftK": dk,
        "hy_w_in": np.ascontiguousarray(np.asarray(inp["hy_w_in"], np.float32)[0]),
        "hy_f_wout": np.ascontiguousarray(np.asarray(inp["hy_f_wout"], np.float32)[0]),
        "hy_w_out": np.ascontiguousarray(np.asarray(inp["hy_w_out"], np.float32)[0]),
        "w_up": np.ascontiguousarray(np.asarray(inp["w_up"], np.float32)),
        "w_down": np.ascontiguousarray(np.asarray(inp["w_down"], np.float32)),
    }
    common["at_w_qkv"] = np.ascontiguousarray(np.asarray(inp["at_w_qkv"], np.float32)[0])
    common["at_w_o"] = np.ascontiguousarray(np.asarray(inp["at_w_o"], np.float32)[0])
    common["sink"] = np.ascontiguousarray(np.asarray(inp["at_sink"], np.float32)[0].reshape(1, 16))
    common["atc"] = _attn_consts()
    maps = []
    for c in range(NCORES):
        m = dict(common)
        m["xT"] = np.ascontiguousarray(x[c].T)
        maps.append(m)
    return maps


_PROG = {}


def kernel(**inputs):
    inp = {k: np.asarray(v) for k, v in inputs.items()}
    if "nc" not in _PROG:
        _PROG["nc"] = build_program(stage=4)
    nc, _stack = _PROG["nc"]
    maps = make_in_maps(inp)
    res = run_bass_kernel_spmd(nc, maps, core_ids=list(range(NCORES)))
    out = np.stack([np.ascontiguousarray(r["outT"].T) for r in res.results], axis=0)
    return out.astype(np.float32)
```

```python
import math
import bisect
import numpy as np
import ml_dtypes
import concourse.bass as bass
import concourse.mybir as mybir
from concourse.bass_utils import run_bass_kernel_spmd

F32 = mybir.dt.float32
BF16 = mybir.dt.bfloat16
AF = mybir.ActivationFunctionType
ALU = mybir.AluOpType

S = 2048
D = 1024
NT = 16
DFF = 4096
EPS = 1e-6
NCORES = 8
PI = math.pi

SBUF_WORDS = 52800


class Sched:
    ENGS = ("pe", "act", "dve", "pool", "sp")

    def __init__(self):
        self.ops = []
        self.lastw = {}
        self.readers = {}

    def op(self, eng, fn, r=(), w=(), slot=None):
        idx = len(self.ops)
        deps = set()
        if getattr(self, "fence", None):
            for k in w:
                if k not in self.known:
                    deps.update(self.fence)
                    self.known.add(k)
        for k in r:
            if k in self.lastw:
                deps.add(self.lastw[k])
        for k in w:
            if k in self.lastw:
                deps.add(self.lastw[k])
            deps.update(self.readers.get(k, ()))
        for k in r:
            self.readers.setdefault(k, []).append(idx)
        for k in w:
            self.lastw[k] = idx
            self.readers[k] = []
        deps.discard(idx)
        self.ops.append(dict(eng=eng, fn=fn, deps=deps, slot=slot, sem=None, val=None, sig=False))
        return idx

    def fence_all(self):
        last = {}
        for i, o in enumerate(self.ops):
            last[(o["eng"], o["slot"])] = i
        self.fence = set(last.values())
        self.known = set(self.lastw.keys()) | set(self.readers.keys())

    def retire(self, old_keys, new_keys):
        acc = set()
        for k in old_keys:
            if k in self.lastw:
                acc.add(self.lastw[k])
            acc.update(self.readers.get(k, ()))
        for k in new_keys:
            self.readers.setdefault(k, []).extend(acc)

    def emit(self, nc, stack, same_engine_sync=("act", "dve", "pool")):
        ops = self.ops
        for i, o in enumerate(ops):
            for d in o["deps"]:
                y = ops[d]
                if y["slot"] is not None:
                    continue
                if y["eng"] == o["eng"] and y["eng"] not in same_engine_sync:
                    continue
                y["sig"] = True
        SEM_MAX = 30000
        eng_sems = {}
        counters = {}
        for e in ("pe", "act", "dve", "pool"):
            eng_sems[e] = []
            counters[e] = SEM_MAX
        slot_sem = {}
        slot_list = {}
        for i, o in enumerate(ops):
            if o["slot"] is not None:
                sl = o["slot"]
                if sl not in slot_sem:
                    slot_sem[sl] = stack.enter_context(nc.semaphore("d_" + str(len(slot_sem))))
                    slot_list[sl] = []
                slot_list[sl].append(i)
                o["sem"] = slot_sem[sl]
                o["val"] = 16 * len(slot_list[sl])
            elif o["sig"]:
                e = o["eng"]
                if counters[e] >= SEM_MAX:
                    eng_sems[e].append(stack.enter_context(nc.semaphore("e_%s_%d" % (e, len(eng_sems[e])))))
                    counters[e] = 0
                counters[e] += 1
                o["sem"] = eng_sems[e][-1]
                o["val"] = counters[e]
        self.nsem = len(slot_sem) + sum(len(v) for v in eng_sems.values())
        block = stack.enter_context(nc.Block())
        per_eng = {e: [] for e in self.ENGS}
        for i, o in enumerate(ops):
            per_eng[o["eng"]].append(i)

        def run(engname, eng):
            waited = {}
            for i in per_eng[engname]:
                o = ops[i]
                need = {}
                for d in o["deps"]:
                    y = ops[d]
                    if y["slot"] is not None:
                        lst = slot_list[y["slot"]]
                        pos = bisect.bisect_left(lst, i)
                        sem, val = y["sem"], 16 * pos
                    else:
                        if y["eng"] == engname and engname not in same_engine_sync:
                            continue
                        sem, val = y["sem"], y["val"]
                    key = id(sem)
                    if key not in need or need[key][1] < val:
                        need[key] = (sem, val)
                for key, (sem, val) in need.items():
                    if waited.get(key, 0) >= val:
                        continue
                    eng.wait_ge(sem, val)
                    waited[key] = val
                ins = o["fn"](eng)
                if o["slot"] is not None:
                    ins.then_inc(o["sem"], 16)
                elif o["sig"]:
                    ins.then_inc(o["sem"], 1)

        @block.tensor
        def _(e):
            run("pe", e)

        @block.scalar
        def _(e):
            run("act", e)

        @block.vector
        def _(e):
            run("dve", e)

        @block.gpsimd
        def _(e):
            run("pool", e)

        @block.sync
        def _(e):
            run("sp", e)


_CONST_CACHE = {}


def _dft_tables():
    if "dft" in _CONST_CACHE:
        return _CONST_CACHE["dft"]
    n = np.arange(S, dtype=np.int64)
    prod = (n[:, None] * n[None, :]) % 4096
    ang = prod.astype(np.float64) * (2.0 * np.pi / 4096.0)
    C = np.cos(ang)
    Sm = np.sin(ang)
    sgn = np.where(n % 2 == 0, 1.0, -1.0)
    Sp = Sm.copy()
    Sp[:, 0] = sgn
    SpT = Sp.T.copy()

    def panelize(M):
        return M.reshape(16, 128, 16, 128).transpose(2, 1, 0, 3)

    fwd = np.stack([panelize(C), panelize(Sp)], axis=2)
    inv = np.stack([panelize(C), panelize(SpT)], axis=2)
    fwd = np.ascontiguousarray(fwd).astype(ml_dtypes.bfloat16)
    inv = np.ascontiguousarray(inv).astype(ml_dtypes.bfloat16)
    kk = np.arange(16)[:, None]
    pp = np.arange(128)[None, :]
    sperm = (2 * (128 * (kk % 8) + pp) + (kk // 8)).reshape(-1)
    f1 = np.arange(1024, dtype=np.int64)
    angk = ((sperm[:, None].astype(np.int64) * f1[None, :]) % 4096).astype(np.float64) * (2.0 * np.pi / 4096.0)

    def panelize_k(M):
        return M.reshape(16, 128, 8, 128).transpose(2, 1, 0, 3)

    dk = np.stack([panelize_k(np.cos(angk)), panelize_k(np.sin(angk))], axis=2)
    dk = np.ascontiguousarray(dk).astype(ml_dtypes.bfloat16)
    _CONST_CACHE["dft"] = (fwd, inv, dk)
    return fwd, inv, dk


def _zfeat():
    L = S
    t = np.linspace(0.0, 1.0, L, dtype=np.float32)[:, None]
    w = (np.float32(2.0 * math.pi / L) * np.arange(L, dtype=np.float32))[:, None]
    f = np.linspace(1e-4, 15.0, 16, dtype=np.float32)[None, :]
    z = np.concatenate([t, np.cos(f * w), -np.sin(f * w)], axis=-1).astype(np.float32)
    return z


def _attn_consts():
    a = np.zeros((128, 4736), np.float32)
    kl = np.arange(128)[:, None]
    ql = np.arange(128)[None, :]
    a[:, 0:128] = np.where(kl <= ql, 0.0, -30000.0)
    a[:, 128:256] = 0.0
    a[:, 256:384] = np.where(ql <= kl, 0.0, -30000.0)
    pr = np.zeros((128, 128), np.float32)
    for hb in (0, 64):
        for i in range(8):
            pr[hb + 8 + i, hb + i] = 1.0
            pr[hb + i, hb + 8 + i] = 1.0
    a[:, 384:512] = pr
    ph = np.zeros((128, 128), np.float32)
    for m in range(128):
        ph[(m + 64) % 128, m] = 1.0
    a[:, 512:640] = ph
    inv = (500000.0 ** (-np.arange(0, 16, 2, dtype=np.float32) / 16.0)).astype(np.float32)
    ang = np.arange(S, dtype=np.float32)[None, :] * inv[:, None]
    cos = np.ones((128, S), np.float32)
    sin = np.zeros((128, S), np.float32)
    for hb in (0, 64):
        cos[hb:hb + 8] = np.cos(ang); cos[hb + 8:hb + 16] = np.cos(ang)
        sin[hb:hb + 8] = -np.sin(ang); sin[hb + 8:hb + 16] = np.sin(ang)
    a[:, 640:2688] = cos
    a[:, 2688:4736] = sin
    return a.astype(ml_dtypes.bfloat16)


def _absdelta():
    max_decay = math.log(1e-2) / 0.3
    min_decay = math.log(1e-2) / 1.5
    deltas = np.linspace(min_decay, max_decay, D, dtype=np.float32)
    return np.abs(deltas).astype(np.float32)


CF_G = 0
CF_CW = 64
CF_FB = 160
CF_TNEG = 164
CF_FW1 = 180
CF_FW2 = 244
CF_FW3 = 308
CF_FBF = 372
CF_N = 384

CB_ID = 0
CB_ONES = 128
CB_JREV = 256
CB_E00 = 384
CB_SGN = 512
CB_N = 640


def _host_consts(inp):
    cf = np.zeros((128, CF_N), np.float32)
    gl = ["norm_mix_pre", "norm_mix_post", "norm_mlp_pre", "norm_mlp_post"]
    for layer in range(2):
        for gi, gname in enumerate(gl):
            v = np.asarray(inp[gname])[layer]
            cf[:, CF_G + (layer * 4 + gi) * 8: CF_G + (layer * 4 + gi) * 8 + 8] = v.reshape(8, 128).T
    cw = np.asarray(inp["hy_conv_w"])[0]
    cbias = np.asarray(inp["hy_conv_b"])[0]
    for k in range(3):
        cf[:, CF_CW + 24 * k: CF_CW + 24 * k + 24] = cw[k].reshape(24, 128).T
    cf[:, CF_CW + 72: CF_CW + 96] = cbias.reshape(24, 128).T
    cf[0:64, CF_FB + 0] = np.asarray(inp["hy_f_b1"])[0]
    cf[0:64, CF_FB + 1] = np.asarray(inp["hy_f_b2"])[0]
    cf[0:64, CF_FB + 2] = np.asarray(inp["hy_f_b3"])[0]
    cf[0:64, CF_FB + 3] = np.asarray(inp["hy_f_freq"])[0]
    t = np.linspace(0.0, 1.0, S, dtype=np.float32)
    for k in range(16):
        sidx = 2 * (128 * (k % 8) + np.arange(128)) + (k // 8)
        cf[:, CF_TNEG + k] = -t[sidx]
    cf[0:33, CF_FW1: CF_FW1 + 64] = np.asarray(inp["hy_f_w1"])[0]
    cf[0:64, CF_FW2: CF_FW2 + 64] = np.asarray(inp["hy_f_w2"])[0]
    cf[0:64, CF_FW3: CF_FW3 + 64] = np.asarray(inp["hy_f_w3"])[0]
    cb = np.zeros((128, CB_N), np.float32)
    cb[:, CB_ID:CB_ID + 128] = np.eye(128)
    cb[:, CB_ONES:CB_ONES + 128] = 1.0
    for q in range(1, 128):
        cb[128 - q, CB_JREV + q] = 1.0
    cb[0, CB_E00] = 1.0
    cb[:, CB_SGN] = np.where(np.arange(128) % 2 == 0, 1.0, -1.0)
    cb = cb.astype(ml_dtypes.bfloat16)
    zf = np.zeros((33, S), np.float32)
    zf[:, :] = _zfeat().T
    return cf, cb, zf


def build_program(stage=4):
    from contextlib import ExitStack
    nc = bass.Bass("TRN2", target_bir_lowering=False)
    dt = nc.dram_tensor
    xT = dt("xT", [D, S], F32, kind="ExternalInput").ap()
    cf_d = dt("cf", [128, CF_N], F32, kind="ExternalInput").ap()
    cb_d = dt("cb", [128, CB_N], BF16, kind="ExternalInput").ap()
    zf_d = dt("zf", [33, S], F32, kind="ExternalInput").ap()
    adl_d = dt("adl", [1, D], F32, kind="ExternalInput").ap()
    hyb_d = dt("hyb", [1, 2 * D], F32, kind="ExternalInput").ap()
    dftF = dt("dftF", [16, 128, 2, 16, 128], BF16, kind="ExternalInput").ap()
    dftI = dt("dftI", [16, 128, 2, 16, 128], BF16, kind="ExternalInput").ap()
    dftK = dt("dftK", [8, 128, 2, 16, 128], BF16, kind="ExternalInput").ap()
    w_in_d = dt("hy_w_in", [D, 3 * D], F32, kind="ExternalInput").ap()
    wout_f_d = dt("hy_f_wout", [64, 4 * D], F32, kind="ExternalInput").ap()
    w_out_d = dt("hy_w_out", [D, D], F32, kind="ExternalInput").ap()
    w_up_d = dt("w_up", [2, D, DFF], F32, kind="ExternalInput").ap()
    w_down_d = dt("w_down", [2, DFF, D], F32, kind="ExternalInput").ap()
    wqkv_d = dt("at_w_qkv", [D, 1536], F32, kind="ExternalInput").ap()
    wo2_d = dt("at_w_o", [D, D], F32, kind="ExternalInput").ap()
    sink_d = dt("sink", [1, 16], F32, kind="ExternalInput").ap()
    atc_d = dt("atc", [128, 4736], BF16, kind="ExternalInput").ap()
    kco_d = dt("kco", [2, 16, 128, 2, D], BF16, kind="Internal").ap()
    outT = dt("outT", [D, S], F32, kind="ExternalOutput").ap()

    stack = ExitStack()
    big = stack.enter_context(nc.sbuf_tensor("big", [128, SBUF_WORDS], F32))
    PS = stack.enter_context(nc.psum_tensor("PS", [128, 8, 512], F32))
    sch = Sched()

    def V(off, words, dtype=F32, pat=None, **kw):
        ap = big[:, off:off + words]
        if dtype != F32:
            ap = ap.bitcast(dtype)
        if pat is not None:
            ap = ap.rearrange(pat, **kw)
        return ap

    def psb(b):
        return PS[:, b, :]

    def psb16(b):
        return PS[:, b, :].bitcast(BF16)

    o = 0
    O_CF = o; o += CF_N
    O_CB = o; o += CB_N // 2
    O_RSTD = o; o += 2 * 512
    O_LN = o; o += 512
    O_SQ = o; o += 2048
    O_KNY = o; o += 1024
    P_H = o; o += 16384
    P_Z2T = o; o += 8192
    P_T0 = o; o += 4096
    P_T1 = o; o += 4096
    P_RAW = o; o += 2052
    P_TT = o; o += 2048
    P_TMP = o; o += 2048
    P_KST = o; o += 1024
    P_WIN = o; o += 2048
    P_PAN = o; o += 4096
    P_X = o; o += 1024
    P_Y = o; o += 256
    assert o <= SBUF_WORDS, o

    CF = V(O_CF, CF_N)
    CB = V(O_CB, CB_N // 2, BF16)
    IDENT = CB[:, CB_ID:CB_ID + 128]
    ONES = CB[:, CB_ONES:CB_ONES + 128]
    JREV = CB[:, CB_JREV:CB_JREV + 128]
    E00 = CB[:, CB_E00:CB_E00 + 128]
    SGNC = CB[:, CB_SGN:CB_SGN + 1]
    RSTD = [V(O_RSTD + 512 * i, 512) for i in range(2)]
    LNB = V(O_LN, 512)
    SQ = V(O_SQ, 2048, BF16, "p (j s) -> p j s", j=8)
    KNY = V(O_KNY, 1024, BF16)

    def gcol(layer, gi, j):
        c = CF_G + (layer * 4 + gi) * 8 + j
        return CF[:, c:c + 1]

    PAN = [V(P_PAN + 2048 * i, 2048, BF16, "p (a k j) -> p a k j", a=2, k=16) for i in range(2)]

    sch.op("sp", lambda e: e.dma_start(out=CF, in_=cf_d), w=["CF"], slot="cf")
    sch.op("sp", lambda e: e.dma_start(out=CB, in_=cb_d), w=["CB"], slot="cb")

    F0 = P_Z2T
    ZF = V(F0, 2048)
    HB = [V(F0 + 2048, 2048), V(F0 + 4096, 2048)]
    H3 = V(F0 + 6144, 1024, BF16)
    WF16 = V(F0 + 7168, 1024, BF16)
    WOF = V(F0 + 8192, 4096)
    WSUM = V(F0 + 12288, 1024, BF16)
    WDIF = V(F0 + 13312, 1024, BF16)
    ARG = V(F0 + 14336, 512)
    ADL = V(F0 + 14848, 1024)
    DEC = V(F0 + 15872, 1024)
    HYB = V(F0 + 16896, 2048)
    TROW = V(F0 + 18944, 512)
    ARG2 = V(F0 + 19456, 512)
    assert F0 + 19968 <= P_TMP
    sch.op("sp", lambda e: e.dma_start(out=ZF[0:33, :], in_=zf_d), w=["ZF"], slot="zf")
    sch.op("sp", lambda e: e.dma_start(out=WOF[0:64, :], in_=wout_f_d), w=["WOF"], slot="wof")
    sch.op("sp", lambda e: e.dma_start(out=ADL, in_=adl_d.partition_broadcast(128)), w=["ADL"], slot="adl")
    sch.op("sp", lambda e: e.dma_start(out=HYB[0:1, :], in_=hyb_d), w=["HYB"], slot="hyb")

    for l in range(3):
        sch.op("dve", lambda e, l=l: e.tensor_tensor(out=CF[0:64, CF_FBF + l:CF_FBF + l + 1],
                                                     in0=CF[0:64, CF_FB + l:CF_FB + l + 1],
                                                     in1=CF[0:64, CF_FB + 3:CF_FB + 4], op=ALU.mult),
               r=["CF"], w=[("fbf", l)])
    sch.op("dve", lambda e: e.tensor_tensor(out=WSUM[0:64, :], in0=WOF[0:64, 0:2048], in1=WOF[0:64, 2048:4096], op=ALU.add),
           r=["WOF"], w=["WSUM"])
    sch.op("dve", lambda e: e.tensor_tensor(out=WDIF[0:64, :], in0=WOF[0:64, 2048:4096], in1=WOF[0:64, 0:2048], op=ALU.subtract),
           r=["WOF"], w=["WDIF"])
    sch.op("dve", lambda e: e.tensor_copy(out=WF16[0:64, :], in_=WOF[0:64, 0:2048]), r=["WOF"], w=["WF16"])

    FWOFF = [CF_FW1, CF_FW2, CF_FW3]
    FK = [33, 64, 64]
    fcnt = 0
    for l in range(3):
        src = ZF if l == 0 else HB[(l - 1) % 2]
        srckey = "ZF" if l == 0 else ("HB", (l - 1) % 2)
        for sc in range(4):
            bank = 5 + (fcnt % 2)
            fcnt += 1
            sch.op("pe", lambda e, l=l, sc=sc, bank=bank, src=src: e.matmul(
                psb(bank)[0:64, :], CF[0:FK[l], FWOFF[l]:FWOFF[l] + 64], src[0:FK[l], sc * 512:(sc + 1) * 512],
                start=True, stop=True), r=["CF", (srckey, sc) if l else "ZF"], w=[("ps", bank)])
            sch.op("act", lambda e, l=l, bank=bank: e.activation(
                out=ARG[0:64, :], in_=psb(bank)[0:64, :], func=AF.Identity,
                scale=CF[0:64, CF_FB + 3:CF_FB + 4], bias=CF[0:64, CF_FBF + l:CF_FBF + l + 1]),
                r=[("ps", bank), ("fbf", l), "CF"], w=["ARG"])
            sch.op("dve", lambda e: e.tensor_scalar(out=ARG2[0:64, :], in0=ARG[0:64, :], scalar1=PI, scalar2=2 * PI,
                                                    op0=ALU.is_gt, op1=ALU.mult), r=["ARG"], w=["ARG2"])
            sch.op("dve", lambda e: e.tensor_tensor(out=ARG[0:64, :], in0=ARG[0:64, :], in1=ARG2[0:64, :], op=ALU.subtract),
                   r=["ARG", "ARG2"], w=["ARG"])
            sch.op("dve", lambda e: e.tensor_scalar(out=ARG2[0:64, :], in0=ARG[0:64, :], scalar1=-PI, scalar2=2 * PI,
                                                    op0=ALU.is_lt, op1=ALU.mult), r=["ARG"], w=["ARG2"])
            sch.op("dve", lambda e: e.tensor_tensor(out=ARG[0:64, :], in0=ARG[0:64, :], in1=ARG2[0:64, :], op=ALU.add),
                   r=["ARG", "ARG2"], w=["ARG"])
            if l < 2:
                dst = HB[l % 2][0:64, sc * 512:(sc + 1) * 512]
                dkey = (("HB", l % 2), sc)
            else:
                dst = H3[0:64, sc * 512:(sc + 1) * 512]
                dkey = ("H3", sc)
            sch.op("act", lambda e, dst=dst: e.activation(out=dst, in_=ARG[0:64, :], func=AF.Sin),
                   r=["ARG"], w=[dkey])


    ABv = [[V(P_H + 8192 * sl + 4096 * w_, 4096, BF16, "p (t c) -> p t c", t=16) for w_ in range(2)] for sl in range(2)]
    kcnt = [0]
    panel_seq = []
    for _p in range(4):
        panel_seq += [(dftK, m) for m in range(7, -1, -1)]
    for _h in range(2):
        for _c in range(2):
            panel_seq += [(dftF, m) for m in range(NT)]
            panel_seq += [(dftI, m) for m in range(NT)]
    pst = {"use": 0, "issued": 0}

    def _issue_panel():
        i = pst["issued"]
        if i >= len(panel_seq):
            return
        src_d, m = panel_seq[i]
        sl = i % 2
        pst["issued"] += 1
        sch.op("sp", lambda e, sl=sl, m=m, src_d=src_d: e.dma_start(out=PAN[sl], in_=src_d[m]), w=[("PAN", sl)], slot=("pan", sl))

    def load_panel(src_d, m):
        i = pst["use"]
        assert panel_seq[i][1] == m
        while pst["issued"] <= i:
            _issue_panel()
        if i == 0:
            _issue_panel()
        pst["use"] += 1
        return i % 2

    def filt_steps(pss, st):
        return [lambda: filt_gen_tile(pss, st, 0), lambda: filt_gen_tile(pss, st, 1)]

    def filt_gen_tile(pss, st, only=None):
        od, hf = pss // 2, pss % 2
        sl = pss % 2
        A_, B_ = ABv[sl]
        c0 = od * 1024 + hf * 512
        if only in (None, 0):
            dsl = st % 2
            sch.op("act", lambda e, st=st, hf=hf, dsl=dsl: e.activation(out=DEC[:, 512 * dsl:512 * dsl + 512], in_=ADL[:, hf * 512:(hf + 1) * 512], func=AF.Exp,
                                                                       scale=CF[:, CF_TNEG + st:CF_TNEG + st + 1]),
                   r=["ADL", "CF"], w=[("DEC", dsl)])
        dsl = st % 2
        for which in ((0, 1) if only is None else (only,)):
            bank = 5 + (kcnt[0] % 2)
            kcnt[0] += 1
            wsrc = WSUM if which == 0 else WDIF
            dst = (A_, B_)[which]
            sch.op("pe", lambda e, st=st, bank=bank, wsrc=wsrc, c0=c0: e.matmul(
                psb(bank), H3[0:64, 256 * (st % 8) + st // 8:256 * (st % 8) + st // 8 + 255:2], wsrc[0:64, c0:c0 + 512],
                start=True, stop=True), r=[("H3", (st % 8) // 2), "WSUM", "WDIF"], w=[("ps", bank)])
            sch.op("dve", lambda e, st=st, bank=bank, dst=dst, dsl=dsl: e.tensor_tensor(
                out=dst[:, st, :], in0=psb(bank), in1=DEC[:, 512 * dsl:512 * dsl + 512], op=ALU.mult),
                r=[("ps", bank), ("DEC", dsl)], w=[("AB", sl, which, st)])
        if st == 0 and only in (None, 1):
            bank = 5 + (kcnt[0] % 2)
            kcnt[0] += 1
            sch.op("pe", lambda e, bank=bank, c0=c0: e.matmul(
                psb(bank)[0:1, :], H3[0:64, 0:1], WF16[0:64, c0:c0 + 512], start=True, stop=True),
                r=[("H3", 0), "WF16"], w=[("ps", bank)])
            sch.op("dve", lambda e, bank=bank, c0=c0: e.tensor_tensor(
                out=TROW[0:1, :], in0=psb(bank)[0:1, :], in1=HYB[0:1, c0:c0 + 512], op=ALU.add),
                r=[("ps", bank), "HYB"], w=["TROW"])
            sch.op("dve", lambda e, A_=A_: e.tensor_copy(out=A_[0:1, 0, :], in_=TROW[0:1, :]),
                   r=["TROW"], w=[("AB", sl, 0, 0)])
            sch.op("dve", lambda e, B_=B_: e.tensor_scalar(
                out=B_[0:1, 0, :], in0=TROW[0:1, :], scalar1=-1.0, scalar2=None, op0=ALU.mult),
                r=["TROW"], w=[("AB", sl, 1, 0)])

    EPSC = CF[:, CF_FBF + 3:CF_FBF + 4]
    sch.op("dve", lambda e: e.memset(EPSC, EPS), r=["CF"], w=["EPSC"])
    HNT = V(P_H, 8192, BF16, "p (j s) -> p j s", j=8)
    XC = [V(P_T0 + 4096 * i, 4096, F32, "p (j s) -> p j s", j=8) for i in range(2)]
    xT_v = xT.rearrange("(j p) s -> p j s", p=128)
    cnt = {"mb": 0, "rs": 0, "win": 0, "kst": 0, "pq": 0, "py": 0}

    def mbank():
        b = 6 + (cnt["mb"] % 2)
        cnt["mb"] += 1
        return b

    def norm_chunk(src, src_keys, layer, gi, dst_of_j, dst_keys_of_j, bank=None, part=None):
        if part in (None, "a"):
            sch.op("act", lambda e: e.activation(out=SQ, in_=src, func=AF.Square), r=src_keys, w=[("SQj", j) for j in range(8)])
        if part == "a":
            return
        rs = cnt["rs"] % 2
        cnt["rs"] += 1
        bank = mbank() if bank is None else bank
        for j in range(8):
            sch.op("pe", lambda e, j=j, bank=bank: e.matmul(psb(bank), ONES, SQ[:, j, :], start=(j == 0), stop=(j == 7)),
                   r=[("SQj", j), "CB"], w=[("ps", bank)])
        sch.op("act", lambda e, bank=bank: e.activation(out=LNB, in_=psb(bank), func=AF.Ln, scale=1.0 / D, bias=EPSC),
               r=[("ps", bank), "EPSC"], w=["LNB"])
        sch.op("act", lambda e, rs=rs: e.activation(out=RSTD[rs], in_=LNB, func=AF.Exp, scale=-0.5), r=["LNB"], w=[("RSTD", rs)])
        for j in range(8):
            sch.op("dve", lambda e, j=j, rs=rs: e.scalar_tensor_tensor(
                out=dst_of_j(j), in0=src[:, j, :], scalar=gcol(layer, gi, j), in1=RSTD[rs], op0=ALU.mult, op1=ALU.mult),
                r=src_keys + [("RSTD", rs), "CF"], w=dst_keys_of_j(j))


    def prenorm_steps():
        sch.retire(["WOF", "WSUM", "WDIF", "ARG", "ARG2", "ADL", ("DEC", 0), ("DEC", 1)], [("XC", 0), ("XC", 1)])
        sch.retire([("AB", 0, w_, st_) for w_ in range(2) for st_ in range(NT)], [("HNT", sc_) for sc_ in range(4)])
        steps = []
        for sc in range(4):
            def _sta(sc=sc):
                xs = sc % 2
                sch.op("sp", lambda e: e.dma_start(out=XC[xs], in_=xT_v[:, :, sc * 512:(sc + 1) * 512]),
                       w=[("XC", xs)], slot=("xc", xs))
                norm_chunk(XC[xs], [("XC", xs)], 0, 0, None, None, part="a")

            def _stb(sc=sc):
                xs = sc % 2
                norm_chunk(XC[xs], [("XC", xs)], 0, 0,
                           lambda j: HNT[:, j, sc * 512:(sc + 1) * 512], lambda j: [("HNT", sc)], bank=5 + sc % 2, part="b")
            steps.append((_sta, _stb))
        return steps

    KB0 = P_TMP
    STGP = [V(KB0 + 512 * i, 512, BF16, "p (a c) -> p a c", a=2) for i in range(2)]
    STGM = [V(KB0 + 1024 + 512 * i, 512, BF16, "p (a c) -> p a c", a=2) for i in range(3)]
    STGR = [V(KB0 + 2560 + 512 * i, 512, BF16, "p (a c) -> p a c", a=2) for i in range(2)]
    OSB = [V(KB0 + 3584 + 512 * i, 512) for i in range(2)]
    SPEC = V(KB0 + 4608, 512, BF16, "p (a c) -> p a c", a=2)
    assert KB0 + 5120 <= P_PAN
    S11 = 2.0 ** -11
    for st in range(NT):
        filt_gen_tile(0, st)
    sch.op("dve", lambda e: e.memset(SPEC, 0.0), w=["SPEC"])
    gcount = [0]
    for pss in range(4):
        od, hf = pss // 2, pss % 2
        sl = pss % 2
        A_, B_ = ABv[sl]
        c0 = od * 1024 + hf * 512
        for a_, (src_, k0) in enumerate(((A_, 0), (B_, 8))):
            bk = 4 if a_ == 0 else 7
            for k in range(8):
                sch.op("pe", lambda e, k=k, k0=k0, src_=src_, bk=bk: e.matmul(
                    psb(bk)[0:1, :], SGNC, src_[:, k0 + k, :], start=(k == 0), stop=(k == 7)),
                    r=["CB", ("AB", sl, a_, k0 + k)], w=[("ps", bk)])
            sch.op("act", lambda e, a_=a_, bk=bk: e.activation(out=SPEC[0:1, a_, :], in_=psb(bk)[0:1, :], func=AF.Copy, scale=S11),
                   r=[("ps", bk)], w=["SPEC"])
        prev_m = None
        pending = None

        def emit_rev(pend):
            m_, ss_, sm_, prev_t, prev_keys = pend
            for a_ in range(2):
                bk = 4 if a_ == 0 else 7
                sch.op("pe", lambda e, a_=a_, bk=bk: e.matmul(psb(bk), JREV, STGM[sm_][:, a_, :], start=True, stop=False),
                       r=["CB", ("STGM", sm_, a_)], w=[("ps", bk)])
                sch.op("pe", lambda e, a_=a_, bk=bk: e.matmul(psb(bk), E00, prev_t[:, a_, :], start=False, stop=True),
                       r=["CB"] + prev_keys, w=[("ps", bk)])
                sch.op("act", lambda e, a_=a_, bk=bk: e.activation(out=STGR[ss_][:, a_, :], in_=psb(bk), func=AF.Copy),
                       r=[("ps", bk)], w=[("STGR", ss_, a_)])
            sch.op("act", lambda e, od_=od, hf_=hf: e.dma_start(out=kco_d[od_, 15 - m_, :, :, hf_ * 512:(hf_ + 1) * 512], in_=STGR[ss_]),
                   r=[("STGR", ss_, 0), ("STGR", ss_, 1)], w=[("kco", od, 15 - m_, hf)], slot=("kst_outr", ss_))

        fq = []
        if pss + 1 < 4:
            for st_ in range(NT):
                fq += filt_steps(pss + 1, st_)
        else:
            pn = prenorm_steps()
            nop = lambda: None
            fq += [pn[0][0], nop, nop, nop, pn[0][1], pn[1][0], nop, nop, nop, pn[1][1], pn[2][0], nop, nop, nop, pn[2][1],
                   pn[3][0], nop, nop, nop, pn[3][1]]
        for m in range(7, -1, -1):
            psl = load_panel(dftK, m)
            g = gcount[0]
            gcount[0] += 1
            ss = g % 2
            sm = g % 3
            for a_, src_ in enumerate((A_, B_)):
                be, bo = 2 * a_, 2 * a_ + 1
                for k in range(8):
                    sch.op("pe", lambda e, k=k, psl=psl, be=be, a_=a_, src_=src_: e.matmul(
                        psb(be), PAN[psl][:, a_, k, :], src_[:, k, :], start=(k == 0), stop=(k == 7)),
                        r=[("PAN", psl), ("AB", sl, a_, k)], w=[("ps", be)])
                if fq:
                    fq.pop(0)()
                for k in range(8, 16):
                    sch.op("pe", lambda e, k=k, psl=psl, bo=bo, a_=a_, src_=src_: e.matmul(
                        psb(bo), PAN[psl][:, a_, k, :], src_[:, k, :], start=(k == 8), stop=(k == 15)),
                        r=[("PAN", psl), ("AB", sl, a_, k)], w=[("ps", bo)])
                if fq:
                    fq.pop(0)()
                if a_ == 1:
                    _issue_panel()
                    if pending is not None:
                        emit_rev(pending)
                        pending = None
                sch.op("act", lambda e, bo=bo, a_=a_: e.activation(out=OSB[a_], in_=psb(bo), func=AF.Copy, scale=S11),
                       r=[("ps", bo)], w=[("OSB", a_)])
                sch.op("dve", lambda e, be=be, a_=a_, ss=ss: e.scalar_tensor_tensor(
                    out=STGP[ss][:, a_, :], in0=psb(be), scalar=S11, in1=OSB[a_], op0=ALU.mult, op1=ALU.add),
                    r=[("ps", be), ("OSB", a_)], w=[("STGP", ss, a_)])
                if a_ == 0:
                    sch.op("dve", lambda e, be=be, sm=sm: e.scalar_tensor_tensor(
                        out=STGM[sm][:, 0, :], in0=psb(be), scalar=S11, in1=OSB[0], op0=ALU.mult, op1=ALU.subtract),
                        r=[("ps", be), ("OSB", 0)], w=[("STGM", sm, 0)])
                else:
                    sch.op("dve", lambda e, be=be, sm=sm: e.scalar_tensor_tensor(
                        out=STGM[sm][:, 1, :], in0=psb(be), scalar=-S11, in1=OSB[1], op0=ALU.mult, op1=ALU.add),
                        r=[("ps", be), ("OSB", 1)], w=[("STGM", sm, 1)])
            if m == 0:
                sch.op("dve", lambda e, ss=ss: e.tensor_scalar(out=STGP[ss][0:1, 0, :], in0=STGP[ss][0:1, 0, :],
                                                               scalar1=0.5, scalar2=None, op0=ALU.mult),
                       r=[("STGP", ss, 0)], w=[("STGP", ss, 0)])
                sch.op("dve", lambda e, ss=ss: e.memset(STGP[ss][0:1, 1, :], 0.0), w=[("STGP", ss, 1)])
                sch.op("dve", lambda e, sm=sm, c0=c0: e.tensor_scalar(out=KNY[0:1, c0:c0 + 512], in0=STGM[sm][0:1, 0, :],
                                                                      scalar1=0.5, scalar2=None, op0=ALU.mult),
                       r=[("STGM", sm, 0)], w=[("KNY", pss)])
            sch.op("act", lambda e, ss=ss, od=od, m=m, hf=hf: e.dma_start(
                out=kco_d[od, m, :, :, hf * 512:(hf + 1) * 512], in_=STGP[ss]),
                r=[("STGP", ss, 0), ("STGP", ss, 1)], w=[("kco", od, m, hf)], slot=("kst_out", ss))
            prev_t = SPEC if prev_m is None else STGM[prev_m]
            prev_keys = ["SPEC"] if prev_m is None else [("STGM", prev_m, 0), ("STGM", prev_m, 1)]
            pending = (m, ss, sm, prev_t, prev_keys)
            prev_m = sm
        while fq:
            fq.pop(0)()
        emit_rev(pending)

    if stage == 0:
        sch.op("sp", lambda e: e.dma_start(out=outT[0:128, 0:1024].bitcast(BF16).rearrange("p (a c) -> p a c", a=2),
                                           in_=kco_d[0, 1, :, :, :]), r=[("kco", 0, 1, 0), ("kco", 0, 1, 1)], w=["out"], slot="out")
        sch.op("sp", lambda e: e.dma_start(out=outT[128:256, 0:1024].bitcast(BF16).rearrange("p (a c) -> p a c", a=2),
                                           in_=kco_d[1, 0, :, :, :]), r=[("kco", 1, 0, 0), ("kco", 1, 0, 1)], w=["out2"], slot="out")
        sch.op("sp", lambda e: e.dma_start(out=outT[256:257, 0:16], in_=outT[257:258, 0:16]), r=["out", "out2"], slot="fin")
        sch.emit(nc, stack)
        return nc, stack


    sch.fence_all()
    YA = V(P_H + 8192, 4096, BF16, "p (t c) -> p t c", t=16)
    YB = V(P_H + 12288, 4096, BF16, "p (t c) -> p t c", t=16)
    Z2T = V(P_Z2T, 8192, BF16, "p (j s) -> p j s", j=8)
    T0 = V(P_T0, 4096, BF16, "p (t c) -> p t c", t=16)
    T1 = V(P_T1, 4096, BF16, "p (t c) -> p t c", t=16)
    RAWS = [V(P_RAW + 1026 * i, 1026, BF16) for i in range(2)]
    TTS = [V(P_TT + 1024 * i, 1024, BF16) for i in range(2)]
    TMP = [V(P_TMP + 512 * i, 512) for i in range(4)]
    KST = [V(P_KST + 512 * i, 512, BF16, "p (a c) -> p a c", a=2) for i in range(2)]
    WIN = [V(P_WIN + 1024 * i, 1024, BF16, "p (j c) -> p j c", j=8) for i in range(2)]
    H = V(P_H, 16384, F32, "p (j s) -> p j s", j=8)
    MB = V(P_PAN, 4096, F32, "p (j s) -> p j s", j=8)
    outT_v = outT.rearrange("(j p) s -> p j s", p=128)
    def branch_finish(sc, m_keys_ready, layer, gi, res_src, res_keys, tok0):
        rs = cnt["rs"] % 2
        cnt["rs"] += 1
        bank = mbank()
        for j in range(8):
            sch.op("pe", lambda e, j=j, bank=bank: e.matmul(psb(bank), ONES, SQ[:, j, :], start=(j == 0), stop=(j == 7)),
                   r=[("SQj", j), "CB"], w=[("ps", bank)])
        sch.op("act", lambda e, bank=bank: e.activation(out=LNB, in_=psb(bank), func=AF.Ln, scale=1.0 / D, bias=EPSC),
               r=[("ps", bank), "EPSC"], w=["LNB"])
        sch.op("act", lambda e, rs=rs: e.activation(out=RSTD[rs], in_=LNB, func=AF.Exp, scale=-0.5), r=["LNB"], w=[("RSTD", rs)])
        for j in range(8):
            sch.op("dve", lambda e, j=j, rs=rs: e.scalar_tensor_tensor(
                out=MB[:, j, :], in0=MB[:, j, :], scalar=gcol(layer, gi, j), in1=RSTD[rs], op0=ALU.mult, op1=ALU.mult),
                r=[("MB", j), ("RSTD", rs), "CF"], w=[("MB", j)])
            sch.op("dve", lambda e, j=j: e.tensor_tensor(
                out=H[:, j, tok0:tok0 + 512], in0=res_src(j), in1=MB[:, j, :], op=ALU.add),
                r=[("MB", j)] + res_keys(j), w=[("H", j, tok0 // 512)])

    def evac_m(bank, j):
        sch.op("act", lambda e: e.activation(out=MB[:, j, :], in_=psb(bank), func=AF.Copy), r=[("ps", bank)], w=[("MB", j)])
        sch.op("act", lambda e: e.activation(out=SQ[:, j, :], in_=psb(bank), func=AF.Square), r=[("ps", bank)], w=[("SQj", j)])


    for rb_ in range(2):
        sch.op("dve", lambda e, rb_=rb_: e.memset(RAWS[rb_][:, 0:1], 0.0), w=[("RAWpad", rb_)])
        sch.op("dve", lambda e, rb_=rb_: e.memset(RAWS[rb_][:, 2049:2050], 0.0), w=[("RAWpad2", rb_)])
    cnt["raw"] = 0
    w_in_v = w_in_d.rearrange("(j p) n -> p j n", p=128)

    def inproj(strm, hf):
        c30 = strm * 1024 + hf * 512
        for sb in range(2):
            ws = cnt["win"] % 2
            cnt["win"] += 1
            sch.op("pool", lambda e, ws=ws, c=c30 + 256 * sb: e.dma_start(out=WIN[ws], in_=w_in_v[:, :, c:c + 256]),
                   w=[("WIN", ws)], slot=("win", ws))
            for q2 in range(2):
                q4 = sb * 2 + q2
                q = c30 // 128 + q4
                rbi = cnt["raw"] % 2
                cnt["raw"] += 1
                RAW, TT = RAWS[rbi], TTS[rbi]
                for sc in range(4):
                    bank = mbank()
                    for j in range(8):
                        sch.op("pe", lambda e, j=j, ws=ws, q2=q2, sc=sc, bank=bank: e.matmul(
                            psb(bank), WIN[ws][:, j, q2 * 128:(q2 + 1) * 128], HNT[:, j, sc * 512:(sc + 1) * 512],
                            start=(j == 0), stop=(j == 7)), r=[("WIN", ws), ("HNT", sc)], w=[("ps", bank)])
                    sch.op("act", lambda e, sc=sc, bank=bank, RAW=RAW: e.activation(out=RAW[:, 1 + sc * 512:1 + (sc + 1) * 512], in_=psb(bank), func=AF.Copy),
                           r=[("ps", bank)], w=[("RAW", rbi, sc)])
                rawk = [("RAW", rbi, i) for i in range(4)] + [("RAWpad", rbi), ("RAWpad2", rbi)]
                sch.op("act", lambda e, q=q, RAW=RAW, TT=TT: e.activation(out=TT, in_=RAW[:, 0:2048], func=AF.Identity,
                                                                          scale=CF[:, CF_CW + q:CF_CW + q + 1], bias=CF[:, CF_CW + 72 + q:CF_CW + 72 + q + 1]),
                       r=rawk + ["CF"], w=[("TT", rbi)])
                sch.op("dve", lambda e, q=q, RAW=RAW, TT=TT: e.scalar_tensor_tensor(out=TT, in0=RAW[:, 1:2049], scalar=CF[:, CF_CW + 24 + q:CF_CW + 24 + q + 1],
                                                                                    in1=TT, op0=ALU.mult, op1=ALU.add), r=rawk + [("TT", rbi), "CF"], w=[("TT", rbi)])
                sch.op("dve", lambda e, q=q, zc=hf * 4 + q4, RAW=RAW, TT=TT: e.scalar_tensor_tensor(
                    out=Z2T[:, zc, :], in0=RAW[:, 2:2050], scalar=CF[:, CF_CW + 48 + q:CF_CW + 48 + q + 1],
                    in1=TT, op0=ALU.mult, op1=ALU.add), r=rawk + [("TT", rbi), "CF"], w=[("Z2T", hf * 4 + q4)])

    def transp_to(hf, T, Tn):
        for q4 in range(4):
            for stg in range(2):
                bank = mbank()
                pv = psb16(bank)
                for i in range(8):
                    st = stg * 8 + i
                    sch.op("pe", lambda e, i=i, st=st, q4=q4, pv=pv: e.transpose(
                        pv[:, i * 128:(i + 1) * 128], Z2T[:, hf * 4 + q4, st * 128:(st + 1) * 128], IDENT),
                        r=[("Z2T", hf * 4 + q4), "CB"], w=[("ps", bank)])
                sch.op("act", lambda e, pv=pv, stg=stg, q4=q4: e.activation(
                    out=T[:, stg * 8:(stg + 1) * 8, q4 * 128:(q4 + 1) * 128], in_=pv.rearrange("p (i c) -> p i c", i=8), func=AF.Copy),
                    r=[("ps", bank)], w=[(Tn, st_) for st_ in range(stg * 8, stg * 8 + 8)])

    def transp_back(hf, T, Tn):
        for q4 in range(4):
            for stg in range(2):
                bank = mbank()
                pv = psb16(bank)
                for i in range(8):
                    st = stg * 8 + i
                    sch.op("pe", lambda e, i=i, st=st, q4=q4, pv=pv: e.transpose(
                        pv[:, i * 128:(i + 1) * 128], T[:, st, q4 * 128:(q4 + 1) * 128], IDENT),
                        r=[(Tn, st), "CB"], w=[("ps", bank)])
                sch.op("act", lambda e, pv=pv, stg=stg, q4=q4: e.activation(
                    out=Z2T[:, hf * 4 + q4, stg * 1024:(stg + 1) * 1024], in_=pv, func=AF.Copy),
                    r=[("ps", bank)], w=[("Z2T", hf * 4 + q4)])

    def fwd_dft(T, Tn, od, hf):
        c0 = od * 1024 + hf * 512
        for ft in range(NT):
            psl = load_panel(dftF, ft)
            ks = cnt["kst"] % 2
            cnt["kst"] += 1
            sch.op("sp", lambda e, ks=ks, ft=ft: e.dma_start(out=KST[ks], in_=kco_d[od, ft, :, :, hf * 512:(hf + 1) * 512]),
                   r=[("kco", od, ft, hf)], w=[("KST", ks)], slot=("kst", ks))
            pq = cnt["pq"] % 2
            cnt["pq"] += 1
            bp, bq = 2 * pq, 2 * pq + 1
            for st in range(NT):
                sch.op("pe", lambda e, st=st, psl=psl, bp=bp: e.matmul(
                    psb(bp), PAN[psl][:, 0, st, :], T[:, st, :], start=(st == 0), stop=(st == NT - 1)),
                    r=[("PAN", psl), (Tn, st)], w=[("ps", bp)])
            for st in range(NT):
                sch.op("pe", lambda e, st=st, psl=psl, bq=bq: e.matmul(
                    psb(bq), PAN[psl][:, 1, st, :], T[:, st, :], start=(st == 0), stop=(st == NT - 1)),
                    r=[("PAN", psl), (Tn, st)], w=[("ps", bq)])
            _issue_panel()
            Ka, Kb = KST[ks][:, 0, :], KST[ks][:, 1, :]
            kk = [("KST", ks)]
            tt = sch.op
            tt("dve", lambda e, bp=bp, Ka=Ka: e.tensor_tensor(out=TMP[0], in0=psb(bp), in1=Ka, op=ALU.mult), r=[("ps", bp)] + kk, w=[("TMP", 0)])
            tt("dve", lambda e, bq=bq, Kb=Kb: e.tensor_tensor(out=TMP[1], in0=psb(bq), in1=Kb, op=ALU.mult), r=[("ps", bq)] + kk, w=[("TMP", 1)])
            tt("dve", lambda e, ft=ft: e.tensor_tensor(out=YA[:, ft, :], in0=TMP[0], in1=TMP[1], op=ALU.add),
               r=[("TMP", 0), ("TMP", 1)], w=[("YA", ft)])
            tt("dve", lambda e, bq=bq, Ka=Ka: e.tensor_tensor(out=TMP[2], in0=psb(bq), in1=Ka, op=ALU.mult), r=[("ps", bq)] + kk, w=[("TMP", 2)])
            tt("dve", lambda e, bp=bp, Kb=Kb: e.tensor_tensor(out=TMP[3], in0=psb(bp), in1=Kb, op=ALU.mult), r=[("ps", bp)] + kk, w=[("TMP", 3)])
            tt("dve", lambda e, ft=ft: e.tensor_tensor(out=YB[:, ft, :], in0=TMP[2], in1=TMP[3], op=ALU.subtract),
               r=[("TMP", 2), ("TMP", 3)], w=[("YB", ft)])
            if ft == 0:
                tt("dve", lambda e, bq=bq: e.tensor_tensor(out=YB[0:1, 0, :], in0=psb(bq)[0:1, :], in1=KNY[0:1, c0:c0 + 512], op=ALU.mult),
                   r=[("ps", bq), ("KNY", od * 2 + hf)], w=[("YB", 0)])

    def inv_dft(T, Tn):
        for tt_ in range(NT):
            psl = load_panel(dftI, tt_)
            by = 4 + (cnt["py"] % 2)
            cnt["py"] += 1
            for ft in range(NT):
                sch.op("pe", lambda e, ft=ft, psl=psl, by=by: e.matmul(
                    psb(by), PAN[psl][:, 0, ft, :], YA[:, ft, :], start=(ft == 0), stop=False),
                    r=[("PAN", psl), ("YA", ft)], w=[("ps", by)])
                sch.op("pe", lambda e, ft=ft, psl=psl, by=by: e.matmul(
                    psb(by), PAN[psl][:, 1, ft, :], YB[:, ft, :], start=False, stop=(ft == NT - 1)),
                    r=[("PAN", psl), ("YB", ft)], w=[("ps", by)])
            _issue_panel()
            sch.op("dve", lambda e, tt_=tt_, by=by: e.tensor_tensor(out=T[:, tt_, :], in0=psb(by), in1=T[:, tt_, :], op=ALU.mult),
                   r=[("ps", by), (Tn, tt_)], w=[(Tn, tt_)])

    WO = V(P_RAW, 4096, BF16, "p (j d) -> p j d", j=8)
    for hf in range(2):
        inproj(0, hf)
        transp_to(hf, T0, "T0")
        inproj(1, hf)
        fwd_dft(T0, "T0", 0, hf)
        transp_to(hf, T1, "T1")
        inv_dft(T1, "T1")
        inproj(2, hf)
        if hf == 1:
            sch.retire([("RAW", b_, i_) for b_ in range(2) for i_ in range(4)] + [("TT", 0), ("TT", 1)]
                       + [("RAWpad", 0), ("RAWpad", 1), ("RAWpad2", 0), ("RAWpad2", 1)], ["WO"])
            sch.op("pool", lambda e: e.dma_start(out=WO, in_=w_out_d.rearrange("(j p) d -> p j d", p=128)), w=["WO"], slot="wo")
        fwd_dft(T1, "T1", 1, hf)
        transp_to(hf, T0, "T0")
        inv_dft(T0, "T0")
        transp_back(hf, T0, "T0")

    sch.fence_all()
    XC2 = [V(P_T0 + 4096 * i, 4096, F32, "p (j s) -> p j s", j=8) for i in range(2)]
    for sc in range(4):
        xs = sc % 2
        sch.op("sp", lambda e, sc=sc, xs=xs: e.dma_start(out=XC2[xs], in_=xT_v[:, :, sc * 512:(sc + 1) * 512]),
               w=[("XC2", xs)], slot=("xc2", xs))
        for jd in range(8):
            bank = mbank()
            for cj in range(8):
                sch.op("pe", lambda e, cj=cj, jd=jd, sc=sc, bank=bank: e.matmul(
                    psb(bank), WO[:, cj, jd * 128:(jd + 1) * 128], Z2T[:, cj, sc * 512:(sc + 1) * 512],
                    start=(cj == 0), stop=(cj == 7)), r=["WO", ("Z2T", cj)], w=[("ps", bank)])
            evac_m(bank, jd)
        branch_finish(sc, None, 0, 1, lambda j, xs=xs: XC2[xs][:, j, :], lambda j, xs=xs: [("XC2", xs)], sc * 512)

    def write_out():
        for j in range(8):
            sch.op("sp", lambda e, j=j: e.dma_start(out=outT_v[:, j, :], in_=H[:, j, :]),
                   r=[("H", j, i) for i in range(4)], w=[("out", j)], slot=("out", j % 4))
        sch.op("sp", lambda e: e.dma_start(out=kco_d[0, 0, 0:1, 0, 0:8], in_=kco_d[0, 0, 1:2, 0, 0:8]),
               r=[("out", j) for j in range(8)], slot="fin")

    if stage == 1:
        write_out()
        sch.emit(nc, stack)
        return nc, stack


    def mlp(layer):
        sch.fence_all()
        tag = "L%d" % layer
        HNC = V(P_RAW, 4096, BF16, "p (j s) -> p j s", j=8)
        ACTB = V(P_Z2T, 16384, BF16, "p (f s) -> p f s", f=32)
        WU = [V(o_, 1024, BF16, "p (j c) -> p j c", j=8) for o_ in (P_TMP, P_TMP + 1024, P_X)]
        WD = [V(P_KST + 1024 * i, 1024, BF16, "p (f d) -> p f d", f=4) for i in range(3)]
        RL = [V(O_KNY + 256 * i, 256, BF16) for i in range(2)]
        wu_v = w_up_d[layer].rearrange("(j p) f -> p j f", p=128)
        wd_v = w_down_d[layer].rearrange("(f p) d -> p f d", p=128)
        c = {"wu": 0, "wd": 0, "rl": 0, "ub": 0}
        def _norm(tc):
            t0 = tc * 1024
            for sc2 in range(2):
                tok = t0 + sc2 * 512
                norm_chunk(H[:, :, tok:tok + 512], [("H", j, tok // 512) for j in range(8)], layer, 2,
                           lambda j, sc2=sc2: HNC[:, j, sc2 * 512:(sc2 + 1) * 512], lambda j, sc2=sc2: [(tag + "HNC", sc2)])

        def _up(tc):
            t0 = tc * 1024
            for slab in range(16):
                ws = c["wu"] % 3
                c["wu"] += 1
                sch.op("pool", lambda e, ws=ws, slab=slab: e.dma_start(out=WU[ws], in_=wu_v[:, :, slab * 256:(slab + 1) * 256]),
                       w=[(tag + "WU", ws)], slot=(tag + "wu", ws))
                for q2 in range(2):
                    ffc = slab * 2 + q2
                    for sc2 in range(2):
                        bank = 4 + (c["ub"] % 2)
                        c["ub"] += 1
                        for j in range(8):
                            sch.op("pe", lambda e, j=j, ws=ws, q2=q2, sc2=sc2, bank=bank: e.matmul(
                                psb(bank), WU[ws][:, j, q2 * 128:(q2 + 1) * 128], HNC[:, j, sc2 * 512:(sc2 + 1) * 512],
                                start=(j == 0), stop=(j == 7)), r=[(tag + "WU", ws), (tag + "HNC", sc2)], w=[("ps", bank)])
                        rl = c["rl"] % 2
                        c["rl"] += 1
                        sch.op("act", lambda e, bank=bank, rl=rl: e.activation(out=RL[rl], in_=psb(bank), func=AF.Relu),
                               r=[("ps", bank)], w=[(tag + "RL", rl)])
                        sch.op("dve", lambda e, rl=rl, ffc=ffc, sc2=sc2: e.tensor_tensor(
                            out=ACTB[:, ffc, sc2 * 512:(sc2 + 1) * 512], in0=RL[rl], in1=RL[rl], op=ALU.mult),
                            r=[(tag + "RL", rl)], w=[(tag + "ACT", ffc, sc2)])

        def _down(tc):
            t0 = tc * 1024
            for sc2 in range(2):
                tok = t0 + sc2 * 512
                for jh in range(2):
                    for slab in range(8):
                        ws = c["wd"] % 3
                        c["wd"] += 1
                        sch.op("pool", lambda e, ws=ws, slab=slab, jh=jh: e.dma_start(
                            out=WD[ws], in_=wd_v[:, slab * 4:(slab + 1) * 4, jh * 512:(jh + 1) * 512]),
                            w=[(tag + "WD", ws)], slot=(tag + "wd", ws))
                        for f4 in range(4):
                            ffc = slab * 4 + f4
                            for jq in range(4):
                                sch.op("pe", lambda e, ws=ws, f4=f4, jq=jq, ffc=ffc, sc2=sc2: e.matmul(
                                    psb(jq), WD[ws][:, f4, jq * 128:(jq + 1) * 128], ACTB[:, ffc, sc2 * 512:(sc2 + 1) * 512],
                                    start=(ffc == 0), stop=(ffc == 31)), r=[(tag + "WD", ws), (tag + "ACT", ffc, sc2)], w=[("ps", jq)])
                    for jq in range(4):
                        evac_m(jq, jh * 4 + jq)
                branch_finish(None, None, layer, 3, lambda j, tok=tok: H[:, j, tok:tok + 512],
                              lambda j, tok=tok: [("H", j, tok // 512)], tok)


        _norm(0)
        _up(0)
        _norm(1)
        _down(0)
        _up(1)
        _down(1)

    mlp(0)
    if stage == 2:
        write_out()
        sch.emit(nc, stack)
        return nc, stack


    sch.fence_all()
    HNT2 = V(P_Z2T, 8192, BF16, "p (j s) -> p j s", j=8)
    QKT = V(P_T0, 12288, BF16, "p (c s) -> p c s", c=12)
    VTOK = V(P_TMP, 2112, BF16, "p (t g e) -> p t g e", t=16, g=4)
    WQ = [V(P_WIN + 1024 * i, 1024, BF16, "p (j c) -> p j c", j=8) for i in range(2)]
    ATC = V(P_PAN, 2368, BF16)
    MASK = ATC[:, 0:384]
    PERMR = ATC[:, 384:512]
    PERMH = ATC[:, 512:640]
    COS = ATC[:, 640:2688]
    SIN = ATC[:, 2688:4736]
    PTS = [V(P_PAN + 2368 + 192 * i, 192, BF16) for i in range(8)]
    ESK = V(P_PAN + 3904, 16)
    RDN = V(P_PAN + 3920, 16)
    XB = [V(O_KNY + 256 * i, 256, BF16) for i in range(2)] + [V(P_Y, 256, BF16)]
    XS = [V(O_KNY + 512 + 256 * i, 256, BF16) for i in range(2)]
    wqkv_v = wqkv_d.rearrange("(j p) n -> p j n", p=128)
    sch.op("sp", lambda e: e.dma_start(out=ATC, in_=atc_d), w=["ATC"], slot="atc")
    sch.op("sp", lambda e: e.dma_start(out=ESK, in_=sink_d.partition_broadcast(128)), w=["ESK"], slot="esk")
    sch.op("act", lambda e: e.activation(out=ESK, in_=ESK, func=AF.Exp), r=["ESK"], w=["ESK"])
    for sc in range(4):
        norm_chunk(H[:, :, sc * 512:(sc + 1) * 512], [("H", j, sc) for j in range(8)], 1, 0,
                   lambda j, sc=sc: HNT2[:, j, sc * 512:(sc + 1) * 512], lambda j, sc=sc: [("HNT2", sc)])
    ac = {"wq": 0, "xb": 0, "pt": 0, "sb": 0, "pv": 0}
    KZ = [[QKT[:, 8, :], QKT[:, 9, :]], [QKT[:, 10, :], QKT[:, 11, :]],
          [V(O_SQ, 1024, BF16), V(O_SQ + 1024, 1024, BF16)], [V(O_RSTD, 1024, BF16), V(P_X, 1024, BF16)]]
    sch.retire([("SQj", j) for j in range(8)] + [("RSTD", 0), ("RSTD", 1)],
               [("KZ", g_, v_, s_) for g_ in (2, 3) for v_ in range(2) for s_ in range(4)] + [("KZz", g_, v_) for g_ in (2, 3) for v_ in range(2)])
    for g_ in range(4):
        sch.op("dve", lambda e, g_=g_: e.memset(KZ[g_][0][64:128, :], 0.0), w=[("KZz", g_, 0)])
        sch.op("dve", lambda e, g_=g_: e.memset(KZ[g_][1][0:64, :], 0.0), w=[("KZz", g_, 1)])
    dst_chunk = [0, 1, 2, 3, 4, 5, 6, 7, 8, 10]
    rb = [0]

    def rbank():
        b_ = 4 + (rb[0] % 2)
        rb[0] += 1
        return b_

    def proj_unit(ws, q2, sc, xb):
        bank = mbank()
        for j in range(8):
            sch.op("pe", lambda e, j=j: e.matmul(
                psb(bank), WQ[ws][:, j, q2 * 128:(q2 + 1) * 128], HNT2[:, j, sc * 512:(sc + 1) * 512],
                start=(j == 0), stop=(j == 7)), r=[("WQ", ws), ("HNT2", sc)], w=[("ps", bank)])
        sch.op("act", lambda e: e.activation(out=XB[xb], in_=psb(bank), func=AF.Copy),
               r=[("ps", bank)], w=[("XB", xb)])

    def rope_unit(dc, sc, xb, xs):
        bank2 = rbank()
        sch.op("pe", lambda e: e.matmul(psb(bank2), PERMR, XB[xb], start=True, stop=True),
               r=[("XB", xb), "ATC"], w=[("ps", bank2)])
        sch.op("dve", lambda e: e.tensor_tensor(out=XS[xs], in0=psb(bank2), in1=SIN[:, sc * 512:(sc + 1) * 512], op=ALU.mult),
               r=[("ps", bank2), "ATC"], w=[("XS", xs)])
        sch.op("dve", lambda e: e.tensor_tensor(out=XB[xb], in0=XB[xb], in1=COS[:, sc * 512:(sc + 1) * 512], op=ALU.mult),
               r=[("XB", xb), "ATC"], w=[("XB", xb)])
        if dc < 8:
            sch.op("dve", lambda e: e.tensor_tensor(out=QKT[:, dc, sc * 512:(sc + 1) * 512], in0=XB[xb], in1=XS[xs], op=ALU.add),
                   r=[("XB", xb), ("XS", xs)], w=[("QKT", dc, sc)])
        else:
            g0, g1 = (0, 1) if dc == 8 else (2, 3)
            cs = slice(sc * 512, (sc + 1) * 512)
            sch.op("dve", lambda e: e.tensor_tensor(out=XS[xs], in0=XB[xb], in1=XS[xs], op=ALU.add),
                   r=[("XB", xb), ("XS", xs)], w=[("XS", xs)])
            sch.op("act", lambda e: e.activation(out=KZ[g0][0][0:64, cs], in_=XS[xs][0:64, :], func=AF.Copy),
                   r=[("XS", xs)], w=[("KZ", g0, 0, sc)])
            sch.op("act", lambda e: e.activation(out=KZ[g1][1][64:128, cs], in_=XS[xs][64:128, :], func=AF.Copy),
                   r=[("XS", xs)], w=[("KZ", g1, 1, sc)])
            bank3 = rbank()
            sch.op("pe", lambda e: e.matmul(psb(bank3), PERMH, XS[xs], start=True, stop=True),
                   r=[("XS", xs), "ATC"], w=[("ps", bank3)])
            sch.op("act", lambda e: e.activation(out=KZ[g1][0][0:64, cs], in_=psb(bank3)[0:64, :], func=AF.Copy),
                   r=[("ps", bank3)], w=[("KZ", g1, 0, sc)])
            sch.op("act", lambda e: e.activation(out=KZ[g0][1][64:128, cs], in_=psb(bank3)[64:128, :], func=AF.Copy),
                   r=[("ps", bank3)], w=[("KZ", g0, 1, sc)])

    pend_rope = None
    ucount = 0
    for slab in range(5):
        ws = ac["wq"] % 2
        ac["wq"] += 1
        sch.op("pool", lambda e, ws=ws, slab=slab: e.dma_start(out=WQ[ws], in_=wqkv_v[:, :, slab * 256:(slab + 1) * 256]),
               w=[("WQ", ws)], slot=("wq", ws))
        for q2 in range(2):
            dc = dst_chunk[slab * 2 + q2]
            for sc in range(4):
                xb = ucount % 3
                xs = ucount % 2
                ucount += 1
                proj_unit(ws, q2, sc, xb)
                if pend_rope is not None:
                    rope_unit(*pend_rope)
                pend_rope = (dc, sc, xb, xs)
    rope_unit(*pend_rope)
    ws = ac["wq"] % 2
    ac["wq"] += 1
    sch.op("pool", lambda e, ws=ws: e.dma_start(out=WQ[ws], in_=wqkv_v[:, :, 1280:1536]), w=[("WQ", ws)], slot=("wq", ws))
    sch.op("dve", lambda e: e.memset(VTOK[:, :, :, 64:66], 1.0), w=["VONE"])
    for st in range(NT):
        bank = mbank()
        for j in range(8):
            sch.op("pe", lambda e, j=j, ws=ws, st=st, bank=bank: e.matmul(
                psb(bank)[:, 0:256], HNT2[:, j, st * 128:(st + 1) * 128], WQ[ws][:, j, :],
                start=(j == 0), stop=(j == 7)), r=[("WQ", ws), ("HNT2", st // 4)], w=[("ps", bank)])
        sch.op("act", lambda e, bank=bank, st=st: e.activation(
            out=VTOK[:, st, :, 0:64], in_=psb(bank)[:, 0:256].rearrange("p (g e) -> p g e", g=4), func=AF.Copy),
            r=[("ps", bank)], w=[("VTOK", st)])

    OTOK = V(P_Z2T, 8192, BF16, "p (t c) -> p t c", t=16)
    sch.retire([("HNT2", i) for i in range(4)], [("OTOK", i) for i in range(16)])

    NSLOT = 8
    LAG = 3
    tidx = lambda h, j: h * NT + j

    def pv(h, i):
        g = h // 4
        kbs = [kb for kb in (i - 1, i, i + 1) if 0 <= kb < NT]
        pb = ac["pv"] % 4
        ac["pv"] += 1
        for n_, kb in enumerate(kbs):
            qlo = max(kb - 1, 0)
            slot = tidx(h, kb) % NSLOT
            off = (i - qlo) * 128
            sch.op("pe", lambda e, slot=slot, off=off, kb=kb, pb=pb, n_=n_: e.matmul(
                psb(pb)[:, 0:65], PTS[slot][:, off:off + 128], VTOK[:, kb, g, 0:65],
                start=(n_ == 0), stop=(n_ == len(kbs) - 1)),
                r=[("PT", slot), ("VTOK", kb), "VONE"], w=[("ps", pb)])
        rd = ac["pv"] % 4
        pv_pend.append((pb, h, i, rd))
        if len(pv_pend) >= 2:
            pv_flush()

    pv_pend = []

    def pv_flush():
        for (pb, h, i, rd) in pv_pend:
            sch.op("dve", lambda e, pb=pb, h=h, rd=rd: e.tensor_scalar(out=RDN[:, 2 * rd:2 * rd + 1], in0=psb(pb)[:, 64:65], scalar1=ESK[:, h:h + 1],
                                                                   scalar2=None, op0=ALU.add), r=[("ps", pb), "ESK"], w=[("RDN", rd)])
        for (pb, h, i, rd) in pv_pend:
            sch.op("dve", lambda e, rd=rd: e.reciprocal(out=RDN[:, 2 * rd + 1:2 * rd + 2], in_=RDN[:, 2 * rd:2 * rd + 1]), r=[("RDN", rd)], w=[("RDN2", rd)])
        for (pb, h, i, rd) in pv_pend:
            sch.op("dve", lambda e, pb=pb, h=h, i=i, rd=rd: e.tensor_scalar(out=OTOK[:, i, h * 64:(h + 1) * 64], in0=psb(pb)[:, 0:64],
                                                                           scalar1=RDN[:, 2 * rd + 1:2 * rd + 2], scalar2=None, op0=ALU.mult),
                   r=[("ps", pb), ("RDN2", rd)], w=[("OTOK", i)])
        del pv_pend[:]

    def scores(h, j):
        g = h // 4
        v = h % 2
        qc = h // 2
        qlo, qhi = max(j - 1, 0), min(j + 1, NT - 1)
        nq = qhi - qlo + 1
        bank = 4 + (ac["sb"] % 4)
        ac["sb"] += 1
        slot = tidx(h, j) % NSLOT
        sch.op("pe", lambda e: e.matmul(
            psb(bank)[:, 0:nq * 128], KZ[g][v][:, j * 128:(j + 1) * 128], QKT[:, qc, qlo * 128:(qhi + 1) * 128],
            start=True, stop=False),
            r=[("KZ", g, v, j // 4), ("KZz", g, v)] + [("QKT", qc, s_) for s_ in range(qlo // 4, qhi // 4 + 1)], w=[("ps", bank)])
        m0 = 128 if j == 0 else 0
        sch.op("pe", lambda e: e.matmul(psb(bank)[:, 0:nq * 128], IDENT, MASK[:, m0:m0 + nq * 128], start=False, stop=True),
               r=["ATC", "CB"], w=[("ps", bank)])
        sch.op("act", lambda e: e.activation(
            out=PTS[slot][:, 0:nq * 128], in_=psb(bank)[:, 0:nq * 128], func=AF.Exp, scale=0.125),
            r=[("ps", bank)], w=[("PT", slot)])

    pv_tasks = []
    for h in range(16):
        for i in range(NT):
            pv_tasks.append((tidx(h, min(i + 1, NT - 1)), h, i))
    pvi = 0
    for t in range(16 * NT + LAG):
        if t < 16 * NT:
            scores(t // NT, t % NT)
        while pvi < len(pv_tasks) and pv_tasks[pvi][0] + LAG <= t:
            pv(pv_tasks[pvi][1], pv_tasks[pvi][2])
            pvi += 1
    assert pvi == len(pv_tasks)
    if pv_pend:
        pv_flush()

    OT = V(P_T0, 8192, BF16, "p (j s) -> p j s", j=8)
    sch.retire([("QKT", c, s_) for c in range(8) for s_ in range(4)] + [("KZ", g_, v_, s_) for g_ in range(2) for v_ in range(2) for s_ in range(4)]
               + [("KZz", g_, v_) for g_ in range(2) for v_ in range(2)], [("OT", j) for j in range(8)] + ["WO2"])
    WO2 = V(P_RAW, 4096, BF16, "p (j d) -> p j d", j=8)
    sch.op("pool", lambda e: e.dma_start(out=WO2, in_=wo2_d.rearrange("(j p) d -> p j d", p=128)), w=["WO2"], slot="wo2")
    for cj in range(8):
        for stg in range(2):
            bank = mbank()
            pv_ = psb16(bank)
            for i in range(8):
                st = stg * 8 + i
                sch.op("pe", lambda e, i=i, st=st, cj=cj, pv_=pv_: e.transpose(
                    pv_[:, i * 128:(i + 1) * 128], OTOK[:, st, cj * 128:(cj + 1) * 128], IDENT),
                    r=[("OTOK", st), "CB"], w=[("ps", bank)])
            sch.op("act", lambda e, pv_=pv_, stg=stg, cj=cj: e.activation(
                out=OT[:, cj, stg * 1024:(stg + 1) * 1024], in_=pv_, func=AF.Copy), r=[("ps", bank)], w=[("OT", cj)])
    sch.fence_all()
    for sc in range(4):
        for jd in range(8):
            bank = mbank()
            for cj in range(8):
                sch.op("pe", lambda e, cj=cj, jd=jd, sc=sc, bank=bank: e.matmul(
                    psb(bank), WO2[:, cj, jd * 128:(jd + 1) * 128], OT[:, cj, sc * 512:(sc + 1) * 512],
                    start=(cj == 0), stop=(cj == 7)), r=["WO2", ("OT", cj)], w=[("ps", bank)])
            evac_m(bank, jd)
        branch_finish(sc, None, 1, 1, lambda j, sc=sc: H[:, j, sc * 512:(sc + 1) * 512], lambda j, sc=sc: [("H", j, sc)], sc * 512)
    if stage == 3:
        write_out()
        sch.emit(nc, stack)
        return nc, stack
    mlp(1)
    write_out()
    sch.emit(nc, stack)
    return nc, stack


def make_in_maps(inp):
    cf, cb, zf = _host_consts(inp)
    fwd, inv, dk = _dft_tables()
    adl = _absdelta().reshape(1, D)
    hyb = np.ascontiguousarray(np.asarray(inp["hy_bias"], np.float32)[0].reshape(1, 2 * D))
    x = np.asarray(inp["x"], np.float32)
    common = {
        "cf": cf, "cb": cb, "zf": zf, "adl": adl, "hyb": hyb, "dftF": fwd, "dftI": inv, "dftK": dk,
        "hy_w_in": np.ascontiguousarray(np.asarray(inp["hy_w_in"], np.float32)[0]),
        "hy_f_wout": np.ascontiguousarray(np.asarray(inp["hy_f_wout"], np.float32)[0]),
        "hy_w_out": np.ascontiguousarray(np.asarray(inp["hy_w_out"], np.float32)[0]),
        "w_up": np.ascontiguousarray(np.asarray(inp["w_up"], np.float32)),
        "w_down": np.ascontiguousarray(np.asarray(inp["w_down"], np.float32)),
    }
    common["at_w_qkv"] = np.ascontiguousarray(np.asarray(inp["at_w_qkv"], np.float32)[0])
    common["at_w_o"] = np.ascontiguousarray(np.asarray(inp["at_w_o"], np.float32)[0])
    common["sink"] = np.ascontiguousarray(np.asarray(inp["at_sink"], np.float32)[0].reshape(1, 16))
    common["atc"] = _attn_consts()
    maps = []
    for c in range(NCORES):
        m = dict(common)
        m["xT"] = np.ascontiguousarray(x[c].T)
        maps.append(m)
    return maps


_PROG = {}


def kernel(**inputs):
    inp = {k: np.asarray(v) for k, v in inputs.items()}
    if "nc" not in _PROG:
        _PROG["nc"] = build_program(stage=4)
    nc, _stack = _PROG["nc"]
    maps = make_in_maps(inp)
    res = run_bass_kernel_spmd(nc, maps, core_ids=list(range(NCORES)))
    out = np.stack([np.ascontiguousarray(r["outT"].T) for r in res.results], axis=0)
    return out.astype(np.float32)
```

```python
import math
import bisect
import numpy as np
import ml_dtypes
import concourse.bass as bass
import concourse.mybir as mybir
from concourse.bass_utils import run_bass_kernel_spmd

F32 = mybir.dt.float32
BF16 = mybir.dt.bfloat16
AF = mybir.ActivationFunctionType
ALU = mybir.AluOpType

S = 2048
D = 1024
NT = 16
DFF = 4096
EPS = 1e-6
NCORES = 8
PI = math.pi

SBUF_WORDS = 52800


class Sched:
    ENGS = ("pe", "act", "dve", "pool", "sp")

    def __init__(self):
        self.ops = []
        self.lastw = {}
        self.readers = {}

    def op(self, eng, fn, r=(), w=(), slot=None):
        idx = len(self.ops)
        deps = set()
        if getattr(self, "fence", None):
            for k in w:
                if k not in self.known:
                    deps.update(self.fence)
                    self.known.add(k)
        for k in r:
            if k in self.lastw:
                deps.add(self.lastw[k])
        for k in w:
            if k in self.lastw:
                deps.add(self.lastw[k])
            deps.update(self.readers.get(k, ()))
        for k in r:
            self.readers.setdefault(k, []).append(idx)
        for k in w:
            self.lastw[k] = idx
            self.readers[k] = []
        deps.discard(idx)
        self.ops.append(dict(eng=eng, fn=fn, deps=deps, slot=slot, sem=None, val=None, sig=False))
        return idx

    def fence_all(self):
        last = {}
        for i, o in enumerate(self.ops):
            last[(o["eng"], o["slot"])] = i
        self.fence = set(last.values())
        self.known = set(self.lastw.keys()) | set(self.readers.keys())

    def retire(self, old_keys, new_keys):
        acc = set()
        for k in old_keys:
            if k in self.lastw:
                acc.add(self.lastw[k])
            acc.update(self.readers.get(k, ()))
        for k in new_keys:
            self.readers.setdefault(k, []).extend(acc)

    def emit(self, nc, stack, same_engine_sync=("act", "dve", "pool")):
        ops = self.ops
        for i, o in enumerate(ops):
            for d in o["deps"]:
                y = ops[d]
                if y["slot"] is not None:
                    continue
                if y["eng"] == o["eng"] and y["eng"] not in same_engine_sync:
                    continue
                y["sig"] = True
        SEM_MAX = 30000
        eng_sems = {}
        counters = {}
        for e in ("pe", "act", "dve", "pool"):
            eng_sems[e] = []
            counters[e] = SEM_MAX
        slot_sem = {}
        slot_list = {}
        for i, o in enumerate(ops):
            if o["slot"] is not None:
                sl = o["slot"]
                if sl not in slot_sem:
                    slot_sem[sl] = stack.enter_context(nc.semaphore("d_" + str(len(slot_sem))))
                    slot_list[sl] = []
                slot_list[sl].append(i)
                o["sem"] = slot_sem[sl]
                o["val"] = 16 * len(slot_list[sl])
            elif o["sig"]:
                e = o["eng"]
                if counters[e] >= SEM_MAX:
                    eng_sems[e].append(stack.enter_context(nc.semaphore("e_%s_%d" % (e, len(eng_sems[e])))))
                    counters[e] = 0
                counters[e] += 1
                o["sem"] = eng_sems[e][-1]
                o["val"] = counters[e]
        self.nsem = len(slot_sem) + sum(len(v) for v in eng_sems.values())
        block = stack.enter_context(nc.Block())
        per_eng = {e: [] for e in self.ENGS}
        for i, o in enumerate(ops):
            per_eng[o["eng"]].append(i)

        def run(engname, eng):
            waited = {}
            for i in per_eng[engname]:
                o = ops[i]
                need = {}
                for d in o["deps"]:
                    y = ops[d]
                    if y["slot"] is not None:
                        lst = slot_list[y["slot"]]
                        pos = bisect.bisect_left(lst, i)
                        sem, val = y["sem"], 16 * pos
                    else:
                        if y["eng"] == engname and engname not in same_engine_sync:
                            continue
                        sem, val = y["sem"], y["val"]
                    key = id(sem)
                    if key not in need or need[key][1] < val:
                        need[key] = (sem, val)
                for key, (sem, val) in need.items():
                    if waited.get(key, 0) >= val:
                        continue
                    eng.wait_ge(sem, val)
                    waited[key] = val
                ins = o["fn"](eng)
                if o["slot"] is not None:
                    ins.then_inc(o["sem"], 16)
                elif o["sig"]:
                    ins.then_inc(o["sem"], 1)

        @block.tensor
        def _(e):
            run("pe", e)

        @block.scalar
        def _(e):
            run("act", e)

        @block.vector
        def _(e):
            run("dve", e)

        @block.gpsimd
        def _(e):
            run("pool", e)

        @block.sync
        def _(e):
            run("sp", e)


_CONST_CACHE = {}


def _dft_tables():
    if "dft" in _CONST_CACHE:
        return _CONST_CACHE["dft"]
    n = np.arange(S, dtype=np.int64)
    prod = (n[:, None] * n[None, :]) % 4096
    ang = prod.astype(np.float64) * (2.0 * np.pi / 4096.0)
    C = np.cos(ang)
    Sm = np.sin(ang)
    sgn = np.where(n % 2 == 0, 1.0, -1.0)
    Sp = Sm.copy()
    Sp[:, 0] = sgn
    SpT = Sp.T.copy()

    def panelize(M):
        return M.reshape(16, 128, 16, 128).transpose(2, 1, 0, 3)

    fwd = np.stack([panelize(C), panelize(Sp)], axis=2)
    inv = np.stack([panelize(C), panelize(SpT)], axis=2)
    fwd = np.ascontiguousarray(fwd).astype(ml_dtypes.bfloat16)
    inv = np.ascontiguousarray(inv).astype(ml_dtypes.bfloat16)
    kk = np.arange(16)[:, None]
    pp = np.arange(128)[None, :]
    sperm = (2 * (128 * (kk % 8) + pp) + (kk // 8)).reshape(-1)
    f1 = np.arange(1024, dtype=np.int64)
    angk = ((sperm[:, None].astype(np.int64) * f1[None, :]) % 4096).astype(np.float64) * (2.0 * np.pi / 4096.0)

    def panelize_k(M):
        return M.reshape(16, 128, 8, 128).transpose(2, 1, 0, 3)

    dk = np.stack([panelize_k(np.cos(angk)), panelize_k(np.sin(angk))], axis=2)
    dk = np.ascontiguousarray(dk).astype(ml_dtypes.bfloat16)
    _CONST_CACHE["dft"] = (fwd, inv, dk)
    return fwd, inv, dk


def _zfeat():
    L = S
    t = np.linspace(0.0, 1.0, L, dtype=np.float32)[:, None]
    w = (np.float32(2.0 * math.pi / L) * np.arange(L, dtype=np.float32))[:, None]
    f = np.linspace(1e-4, 15.0, 16, dtype=np.float32)[None, :]
    z = np.concatenate([t, np.cos(f * w), -np.sin(f * w)], axis=-1).astype(np.float32)
    return z


def _attn_consts():
    a = np.zeros((128, 4736), np.float32)
    kl = np.arange(128)[:, None]
    ql = np.arange(128)[None, :]
    a[:, 0:128] = np.where(kl <= ql, 0.0, -30000.0)
    a[:, 128:256] = 0.0
    a[:, 256:384] = np.where(ql <= kl, 0.0, -30000.0)
    pr = np.zeros((128, 128), np.float32)
    for hb in (0, 64):
        for i in range(8):
            pr[hb + 8 + i, hb + i] = 1.0
            pr[hb + i, hb + 8 + i] = 1.0
    a[:, 384:512] = pr
    ph = np.zeros((128, 128), np.float32)
    for m in range(128):
        ph[(m + 64) % 128, m] = 1.0
    a[:, 512:640] = ph
    inv = (500000.0 ** (-np.arange(0, 16, 2, dtype=np.float32) / 16.0)).astype(np.float32)
    ang = np.arange(S, dtype=np.float32)[None, :] * inv[:, None]
    cos = np.ones((128, S), np.float32)
    sin = np.zeros((128, S), np.float32)
    for hb in (0, 64):
        cos[hb:hb + 8] = np.cos(ang); cos[hb + 8:hb + 16] = np.cos(ang)
        sin[hb:hb + 8] = -np.sin(ang); sin[hb + 8:hb + 16] = np.sin(ang)
    a[:, 640:2688] = cos
    a[:, 2688:4736] = sin
    return a.astype(ml_dtypes.bfloat16)


def _absdelta():
    max_decay = math.log(1e-2) / 0.3
    min_decay = math.log(1e-2) / 1.5
    deltas = np.linspace(min_decay, max_decay, D, dtype=np.float32)
    return np.abs(deltas).astype(np.float32)


CF_G = 0
CF_CW = 64
CF_FB = 160
CF_TNEG = 164
CF_FW1 = 180
CF_FW2 = 244
CF_FW3 = 308
CF_FBF = 372
CF_N = 384

CB_ID = 0
CB_ONES = 128
CB_JREV = 256
CB_E00 = 384
CB_SGN = 512
CB_N = 640


def _host_consts(inp):
    cf = np.zeros((128, CF_N), np.float32)
    gl = ["norm_mix_pre", "norm_mix_post", "norm_mlp_pre", "norm_mlp_post"]
    for layer in range(2):
        for gi, gname in enumerate(gl):
            v = np.asarray(inp[gname])[layer]
            cf[:, CF_G + (layer * 4 + gi) * 8: CF_G + (layer * 4 + gi) * 8 + 8] = v.reshape(8, 128).T
    cw = np.asarray(inp["hy_conv_w"])[0]
    cbias = np.asarray(inp["hy_conv_b"])[0]
    for k in range(3):
        cf[:, CF_CW + 24 * k: CF_CW + 24 * k + 24] = cw[k].reshape(24, 128).T
    cf[:, CF_CW + 72: CF_CW + 96] = cbias.reshape(24, 128).T
    cf[0:64, CF_FB + 0] = np.asarray(inp["hy_f_b1"])[0]
    cf[0:64, CF_FB + 1] = np.asarray(inp["hy_f_b2"])[0]
    cf[0:64, CF_FB + 2] = np.asarray(inp["hy_f_b3"])[0]
    cf[0:64, CF_FB + 3] = np.asarray(inp["hy_f_freq"])[0]
    t = np.linspace(0.0, 1.0, S, dtype=np.float32)
    for k in range(16):
        sidx = 2 * (128 * (k % 8) + np.arange(128)) + (k // 8)
        cf[:, CF_TNEG + k] = -t[sidx]
    cf[0:33, CF_FW1: CF_FW1 + 64] = np.asarray(inp["hy_f_w1"])[0]
    cf[0:64, CF_FW2: CF_FW2 + 64] = np.asarray(inp["hy_f_w2"])[0]
    cf[0:64, CF_FW3: CF_FW3 + 64] = np.asarray(inp["hy_f_w3"])[0]
    cb = np.zeros((128, CB_N), np.float32)
    cb[:, CB_ID:CB_ID + 128] = np.eye(128)
    cb[:, CB_ONES:CB_ONES + 128] = 1.0
    for q in range(1, 128):
        cb[128 - q, CB_JREV + q] = 1.0
    cb[0, CB_E00] = 1.0
    cb[:, CB_SGN] = np.where(np.arange(128) % 2 == 0, 1.0, -1.0)
    cb = cb.astype(ml_dtypes.bfloat16)
    zf = np.zeros((33, S), np.float32)
    zf[:, :] = _zfeat().T
    return cf, cb, zf


def build_program(stage=4):
    from contextlib import ExitStack
    nc = bass.Bass("TRN2", target_bir_lowering=False)
    dt = nc.dram_tensor
    xT = dt("xT", [D, S], F32, kind="ExternalInput").ap()
    cf_d = dt("cf", [128, CF_N], F32, kind="ExternalInput").ap()
    cb_d = dt("cb", [128, CB_N], BF16, kind="ExternalInput").ap()
    zf_d = dt("zf", [33, S], F32, kind="ExternalInput").ap()
    adl_d = dt("adl", [1, D], F32, kind="ExternalInput").ap()
    hyb_d = dt("hyb", [1, 2 * D], F32, kind="ExternalInput").ap()
    dftF = dt("dftF", [16, 128, 2, 16, 128], BF16, kind="ExternalInput").ap()
    dftI = dt("dftI", [16, 128, 2, 16, 128], BF16, kind="ExternalInput").ap()
    dftK = dt("dftK", [8, 128, 2, 16, 128], BF16, kind="ExternalInput").ap()
    w_in_d = dt("hy_w_in", [D, 3 * D], F32, kind="ExternalInput").ap()
    wout_f_d = dt("hy_f_wout", [64, 4 * D], F32, kind="ExternalInput").ap()
    w_out_d = dt("hy_w_out", [D, D], F32, kind="ExternalInput").ap()
    w_up_d = dt("w_up", [2, D, DFF], F32, kind="ExternalInput").ap()
    w_down_d = dt("w_down", [2, DFF, D], F32, kind="ExternalInput").ap()
    wqkv_d = dt("at_w_qkv", [D, 1536], F32, kind="ExternalInput").ap()
    wo2_d = dt("at_w_o", [D, D], F32, kind="ExternalInput").ap()
    sink_d = dt("sink", [1, 16], F32, kind="ExternalInput").ap()
    atc_d = dt("atc", [128, 4736], BF16, kind="ExternalInput").ap()
    kco_d = dt("kco", [2, 16, 128, 2, D], BF16, kind="Internal").ap()
    outT = dt("outT", [D, S], F32, kind="ExternalOutput").ap()

    stack = ExitStack()
    big = stack.enter_context(nc.sbuf_tensor("big", [128, SBUF_WORDS], F32))
    PS = stack.enter_context(nc.psum_tensor("PS", [128, 8, 512], F32))
    sch = Sched()

    def V(off, words, dtype=F32, pat=None, **kw):
        ap = big[:, off:off + words]
        if dtype != F32:
            ap = ap.bitcast(dtype)
        if pat is not None:
            ap = ap.rearrange(pat, **kw)
        return ap

    def psb(b):
        return PS[:, b, :]

    def psb16(b):
        return PS[:, b, :].bitcast(BF16)

    o = 0
    O_CF = o; o += CF_N
    O_CB = o; o += CB_N // 2
    O_RSTD = o; o += 2 * 512
    O_LN = o; o += 512
    O_SQ = o; o += 2048
    O_KNY = o; o += 1024
    P_H = o; o += 16384
    P_Z2T = o; o += 8192
    P_T0 = o; o += 4096
    P_T1 = o; o += 4096
    P_RAW = o; o += 2052
    P_TT = o; o += 2048
    P_TMP = o; o += 2048
    P_KST = o; o += 1024
    P_WIN = o; o += 2048
    P_PAN = o; o += 4096
    P_X = o; o += 1024
    P_Y = o; o += 256
    assert o <= SBUF_WORDS, o

    CF = V(O_CF, CF_N)
    CB = V(O_CB, CB_N // 2, BF16)
    IDENT = CB[:, CB_ID:CB_ID + 128]
    ONES = CB[:, CB_ONES:CB_ONES + 128]
    JREV = CB[:, CB_JREV:CB_JREV + 128]
    E00 = CB[:, CB_E00:CB_E00 + 128]
    SGNC = CB[:, CB_SGN:CB_SGN + 1]
    RSTD = [V(O_RSTD + 512 * i, 512) for i in range(2)]
    LNB = V(O_LN, 512)
    SQ = V(O_SQ, 2048, BF16, "p (j s) -> p j s", j=8)
    KNY = V(O_KNY, 1024, BF16)

    def gcol(layer, gi, j):
        c = CF_G + (layer * 4 + gi) * 8 + j
        return CF[:, c:c + 1]

    PAN = [V(P_PAN + 2048 * i, 2048, BF16, "p (a k j) -> p a k j", a=2, k=16) for i in range(2)]

    sch.op("sp", lambda e: e.dma_start(out=CF, in_=cf_d), w=["CF"], slot="cf")
    sch.op("sp", lambda e: e.dma_start(out=CB, in_=cb_d), w=["CB"], slot="cb")

    F0 = P_Z2T
    ZF = V(F0, 2048)
    HB = [V(F0 + 2048, 2048), V(F0 + 4096, 2048)]
    H3 = V(F0 + 6144, 1024, BF16)
    WF16 = V(F0 + 7168, 1024, BF16)
    WOF = V(F0 + 8192, 4096)
    WSUM = V(F0 + 12288, 1024, BF16)
    WDIF = V(F0 + 13312, 1024, BF16)
    ARG = V(F0 + 14336, 512)
    ADL = V(F0 + 14848, 1024)
    DEC = V(F0 + 15872, 1024)
    HYB = V(F0 + 16896, 2048)
    TROW = V(F0 + 18944, 512)
    ARG2 = V(F0 + 19456, 512)
    assert F0 + 19968 <= P_TMP
    sch.op("sp", lambda e: e.dma_start(out=ZF[0:33, :], in_=zf_d), w=["ZF"], slot="zf")
    sch.op("sp", lambda e: e.dma_start(out=WOF[0:64, :], in_=wout_f_d), w=["WOF"], slot="wof")
    sch.op("sp", lambda e: e.dma_start(out=ADL, in_=adl_d.partition_broadcast(128)), w=["ADL"], slot="adl")
    sch.op("sp", lambda e: e.dma_start(out=HYB[0:1, :], in_=hyb_d), w=["HYB"], slot="hyb")

    for l in range(3):
        sch.op("dve", lambda e, l=l: e.tensor_tensor(out=CF[0:64, CF_FBF + l:CF_FBF + l + 1],
                                                     in0=CF[0:64, CF_FB + l:CF_FB + l + 1],
                                                     in1=CF[0:64, CF_FB + 3:CF_FB + 4], op=ALU.mult),
               r=["CF"], w=[("fbf", l)])
    sch.op("dve", lambda e: e.tensor_tensor(out=WSUM[0:64, :], in0=WOF[0:64, 0:2048], in1=WOF[0:64, 2048:4096], op=ALU.add),
           r=["WOF"], w=["WSUM"])
    sch.op("dve", lambda e: e.tensor_tensor(out=WDIF[0:64, :], in0=WOF[0:64, 2048:4096], in1=WOF[0:64, 0:2048], op=ALU.subtract),
           r=["WOF"], w=["WDIF"])
    sch.op("dve", lambda e: e.tensor_copy(out=WF16[0:64, :], in_=WOF[0:64, 0:2048]), r=["WOF"], w=["WF16"])

    FWOFF = [CF_FW1, CF_FW2, CF_FW3]
    FK = [33, 64, 64]
    ARG2S = [ARG2, V(F0 + 19968, 512)]
    assert F0 + 20480 <= P_TMP
    fcnt = 0
    for l in range(3):
        src = ZF if l == 0 else HB[(l - 1) % 2]
        srckey = "ZF" if l == 0 else ("HB", (l - 1) % 2)
        for sc in range(4):
            bank = 5 + (fcnt % 2)
            a2 = ARG2S[fcnt % 2]
            a2k = ("ARG2", fcnt % 2)
            fcnt += 1
            cs = slice(sc * 512, (sc + 1) * 512)
            ab = HB[l % 2][0:64, cs] if l < 2 else HB[0][0:64, cs]
            abk = (("HB", l % 2), sc) if l < 2 else (("HB", 0), sc)
            sch.op("pe", lambda e, l=l, cs=cs, bank=bank, src=src: e.matmul(
                psb(bank)[0:64, :], CF[0:FK[l], FWOFF[l]:FWOFF[l] + 64], src[0:FK[l], cs],
                start=True, stop=True), r=["CF", (srckey, sc) if l else "ZF"], w=[("ps", bank)])
            sch.op("act", lambda e, l=l, bank=bank, ab=ab: e.activation(
                out=ab, in_=psb(bank)[0:64, :], func=AF.Identity,
                scale=CF[0:64, CF_FB + 3:CF_FB + 4], bias=CF[0:64, CF_FBF + l:CF_FBF + l + 1]),
                r=[("ps", bank), ("fbf", l), "CF"], w=[abk])
            sch.op("dve", lambda e, ab=ab, a2=a2: e.tensor_scalar(out=a2[0:64, :], in0=ab, scalar1=PI, scalar2=2 * PI,
                                                                op0=ALU.is_gt, op1=ALU.mult), r=[abk], w=[a2k])
            sch.op("dve", lambda e, ab=ab, a2=a2: e.tensor_tensor(out=ab, in0=ab, in1=a2[0:64, :], op=ALU.subtract),
                   r=[abk, a2k], w=[abk])
            sch.op("dve", lambda e, ab=ab, a2=a2: e.tensor_scalar(out=a2[0:64, :], in0=ab, scalar1=-PI, scalar2=2 * PI,
                                                                op0=ALU.is_lt, op1=ALU.mult), r=[abk], w=[a2k])
            sch.op("dve", lambda e, ab=ab, a2=a2: e.tensor_tensor(out=ab, in0=ab, in1=a2[0:64, :], op=ALU.add),
                   r=[abk, a2k], w=[abk])
            if l < 2:
                sch.op("act", lambda e, ab=ab: e.activation(out=ab, in_=ab, func=AF.Sin), r=[abk], w=[abk])
            else:
                sch.op("act", lambda e, ab=ab, cs=cs: e.activation(out=H3[0:64, cs], in_=ab, func=AF.Sin),
                       r=[abk], w=[("H3", sc)])


    ABv = [[V(P_H + 8192 * sl + 4096 * w_, 4096, BF16, "p (t c) -> p t c", t=16) for w_ in range(2)] for sl in range(2)]
    kcnt = [0]
    panel_seq = []
    for _p in range(4):
        panel_seq += [(dftK, m) for m in range(7, -1, -1)]
    for _h in range(2):
        for _c in range(2):
            panel_seq += [(dftF, m) for m in range(NT)]
            panel_seq += [(dftI, m) for m in range(NT)]
    pst = {"use": 0, "issued": 0}

    def _issue_panel():
        i = pst["issued"]
        if i >= len(panel_seq):
            return
        src_d, m = panel_seq[i]
        sl = i % 2
        pst["issued"] += 1
        sch.op("sp", lambda e, sl=sl, m=m, src_d=src_d: e.dma_start(out=PAN[sl], in_=src_d[m]), w=[("PAN", sl)], slot=("pan", sl))

    def load_panel(src_d, m):
        i = pst["use"]
        assert panel_seq[i][1] == m
        while pst["issued"] <= i:
            _issue_panel()
        if i == 0:
            _issue_panel()
        pst["use"] += 1
        return i % 2

    def filt_steps(pss, st):
        return [lambda: filt_gen_tile(pss, st, 0), lambda: filt_gen_tile(pss, st, 1)]

    def filt_gen_tile(pss, st, only=None):
        od, hf = pss // 2, pss % 2
        sl = pss % 2
        A_, B_ = ABv[sl]
        c0 = od * 1024 + hf * 512
        if only in (None, 0):
            dsl = st % 2
            sch.op("act", lambda e, st=st, hf=hf, dsl=dsl: e.activation(out=DEC[:, 512 * dsl:512 * dsl + 512], in_=ADL[:, hf * 512:(hf + 1) * 512], func=AF.Exp,
                                                                       scale=CF[:, CF_TNEG + st:CF_TNEG + st + 1]),
                   r=["ADL", "CF"], w=[("DEC", dsl)])
        dsl = st % 2
        for which in ((0, 1) if only is None else (only,)):
            bank = 5 + (kcnt[0] % 2)
            kcnt[0] += 1
            wsrc = WSUM if which == 0 else WDIF
            dst = (A_, B_)[which]
            sch.op("pe", lambda e, st=st, bank=bank, wsrc=wsrc, c0=c0: e.matmul(
                psb(bank), H3[0:64, 256 * (st % 8) + st // 8:256 * (st % 8) + st // 8 + 255:2], wsrc[0:64, c0:c0 + 512],
                start=True, stop=True), r=[("H3", (st % 8) // 2), "WSUM", "WDIF"], w=[("ps", bank)])
            sch.op("dve", lambda e, st=st, bank=bank, dst=dst, dsl=dsl: e.tensor_tensor(
                out=dst[:, st, :], in0=psb(bank), in1=DEC[:, 512 * dsl:512 * dsl + 512], op=ALU.mult),
                r=[("ps", bank), ("DEC", dsl)], w=[("AB", sl, which, st)])
        if st == 0 and only in (None, 1):
            bank = 5 + (kcnt[0] % 2)
            kcnt[0] += 1
            sch.op("pe", lambda e, bank=bank, c0=c0: e.matmul(
                psb(bank)[0:1, :], H3[0:64, 0:1], WF16[0:64, c0:c0 + 512], start=True, stop=True),
                r=[("H3", 0), "WF16"], w=[("ps", bank)])
            sch.op("dve", lambda e, bank=bank, c0=c0: e.tensor_tensor(
                out=TROW[0:1, :], in0=psb(bank)[0:1, :], in1=HYB[0:1, c0:c0 + 512], op=ALU.add),
                r=[("ps", bank), "HYB"], w=["TROW"])
            sch.op("dve", lambda e, A_=A_: e.tensor_copy(out=A_[0:1, 0, :], in_=TROW[0:1, :]),
                   r=["TROW"], w=[("AB", sl, 0, 0)])
            sch.op("dve", lambda e, B_=B_: e.tensor_scalar(
                out=B_[0:1, 0, :], in0=TROW[0:1, :], scalar1=-1.0, scalar2=None, op0=ALU.mult),
                r=["TROW"], w=[("AB", sl, 1, 0)])

    EPSC = CF[:, CF_FBF + 3:CF_FBF + 4]
    sch.op("dve", lambda e: e.memset(EPSC, EPS), r=["CF"], w=["EPSC"])
    HNT = V(P_H, 8192, BF16, "p (j s) -> p j s", j=8)
    XC = [V(P_T0 + 4096 * i, 4096, F32, "p (j s) -> p j s", j=8) for i in range(2)]
    xT_v = xT.rearrange("(j p) s -> p j s", p=128)
    cnt = {"mb": 0, "rs": 0, "win": 0, "kst": 0, "pq": 0, "py": 0}

    def mbank():
        b = 6 + (cnt["mb"] % 2)
        cnt["mb"] += 1
        return b

    def norm_chunk(src, src_keys, layer, gi, dst_of_j, dst_keys_of_j, bank=None, part=None):
        if part in (None, "a"):
            sch.op("act", lambda e: e.activation(out=SQ, in_=src, func=AF.Square), r=src_keys, w=[("SQj", j) for j in range(8)])
        if part == "a":
            return
        rs = cnt["rs"] % 2
        cnt["rs"] += 1
        bank = mbank() if bank is None else bank
        for j in range(8):
            sch.op("pe", lambda e, j=j, bank=bank: e.matmul(psb(bank), ONES, SQ[:, j, :], start=(j == 0), stop=(j == 7)),
                   r=[("SQj", j), "CB"], w=[("ps", bank)])
        sch.op("act", lambda e, bank=bank: e.activation(out=LNB, in_=psb(bank), func=AF.Ln, scale=1.0 / D, bias=EPSC),
               r=[("ps", bank), "EPSC"], w=["LNB"])
        sch.op("act", lambda e, rs=rs: e.activation(out=RSTD[rs], in_=LNB, func=AF.Exp, scale=-0.5), r=["LNB"], w=[("RSTD", rs)])
        for j in range(8):
            sch.op("dve", lambda e, j=j, rs=rs: e.scalar_tensor_tensor(
                out=dst_of_j(j), in0=src[:, j, :], scalar=gcol(layer, gi, j), in1=RSTD[rs], op0=ALU.mult, op1=ALU.mult),
                r=src_keys + [("RSTD", rs), "CF"], w=dst_keys_of_j(j))


    def prenorm_steps():
        sch.retire(["WOF", "WSUM", "WDIF", "ARG", "ARG2", "ADL", ("DEC", 0), ("DEC", 1)], [("XC", 0), ("XC", 1)])
        sch.retire([("AB", 0, w_, st_) for w_ in range(2) for st_ in range(NT)], [("HNT", sc_) for sc_ in range(4)])
        steps = []
        for sc in range(4):
            def _sta(sc=sc):
                xs = sc % 2
                sch.op("sp", lambda e: e.dma_start(out=XC[xs], in_=xT_v[:, :, sc * 512:(sc + 1) * 512]),
                       w=[("XC", xs)], slot=("xc", xs))
                norm_chunk(XC[xs], [("XC", xs)], 0, 0, None, None, part="a")

            def _stb(sc=sc):
                xs = sc % 2
                norm_chunk(XC[xs], [("XC", xs)], 0, 0,
                           lambda j: HNT[:, j, sc * 512:(sc + 1) * 512], lambda j: [("HNT", sc)], bank=5 + sc % 2, part="b")
            steps.append((_sta, _stb))
        return steps

    KB0 = P_TMP
    STGP = [V(KB0 + 512 * i, 512, BF16, "p (a c) -> p a c", a=2) for i in range(2)]
    STGM = [V(KB0 + 1024 + 512 * i, 512, BF16, "p (a c) -> p a c", a=2) for i in range(3)]
    STGR = [V(KB0 + 2560 + 512 * i, 512, BF16, "p (a c) -> p a c", a=2) for i in range(2)]
    OSB = [V(KB0 + 3584 + 512 * i, 512) for i in range(2)]
    SPEC = V(KB0 + 4608, 512, BF16, "p (a c) -> p a c", a=2)
    assert KB0 + 5120 <= P_PAN
    S11 = 2.0 ** -11
    for st in range(NT):
        filt_gen_tile(0, st)
    sch.op("dve", lambda e: e.memset(SPEC, 0.0), w=["SPEC"])
    gcount = [0]
    for pss in range(4):
        od, hf = pss // 2, pss % 2
        sl = pss % 2
        A_, B_ = ABv[sl]
        c0 = od * 1024 + hf * 512
        for a_, (src_, k0) in enumerate(((A_, 0), (B_, 8))):
            bk = 4 if a_ == 0 else 7
            for k in range(8):
                sch.op("pe", lambda e, k=k, k0=k0, src_=src_, bk=bk: e.matmul(
                    psb(bk)[0:1, :], SGNC, src_[:, k0 + k, :], start=(k == 0), stop=(k == 7)),
                    r=["CB", ("AB", sl, a_, k0 + k)], w=[("ps", bk)])
            sch.op("act", lambda e, a_=a_, bk=bk: e.activation(out=SPEC[0:1, a_, :], in_=psb(bk)[0:1, :], func=AF.Copy, scale=S11),
                   r=[("ps", bk)], w=["SPEC"])
        prev_m = None
        pending = None

        def emit_rev(pend):
            m_, ss_, sm_, prev_t, prev_keys = pend
            for a_ in range(2):
                bk = 4 if a_ == 0 else 7
                sch.op("pe", lambda e, a_=a_, bk=bk: e.matmul(psb(bk), JREV, STGM[sm_][:, a_, :], start=True, stop=False),
                       r=["CB", ("STGM", sm_, a_)], w=[("ps", bk)])
                sch.op("pe", lambda e, a_=a_, bk=bk: e.matmul(psb(bk), E00, prev_t[:, a_, :], start=False, stop=True),
                       r=["CB"] + prev_keys, w=[("ps", bk)])
                sch.op("act", lambda e, a_=a_, bk=bk: e.activation(out=STGR[ss_][:, a_, :], in_=psb(bk), func=AF.Copy),
                       r=[("ps", bk)], w=[("STGR", ss_, a_)])
            sch.op("act", lambda e, od_=od, hf_=hf: e.dma_start(out=kco_d[od_, 15 - m_, :, :, hf_ * 512:(hf_ + 1) * 512], in_=STGR[ss_]),
                   r=[("STGR", ss_, 0), ("STGR", ss_, 1)], w=[("kco", od, 15 - m_, hf)], slot=("kst_outr", ss_))

        fq = []
        if pss + 1 < 4:
            for st_ in range(NT):
                fq += filt_steps(pss + 1, st_)
        else:
            pn = prenorm_steps()
            nop = lambda: None
            fq += [pn[0][0], nop, nop, nop, pn[0][1], pn[1][0], nop, nop, nop, pn[1][1], pn[2][0], nop, nop, nop, pn[2][1],
                   pn[3][0], nop, nop, nop, pn[3][1]]
        for m in range(7, -1, -1):
            psl = load_panel(dftK, m)
            g = gcount[0]
            gcount[0] += 1
            ss = g % 2
            sm = g % 3
            for a_, src_ in enumerate((A_, B_)):
                be, bo = 2 * a_, 2 * a_ + 1
                for k in range(8):
                    sch.op("pe", lambda e, k=k, psl=psl, be=be, a_=a_, src_=src_: e.matmul(
                        psb(be), PAN[psl][:, a_, k, :], src_[:, k, :], start=(k == 0), stop=(k == 7)),
                        r=[("PAN", psl), ("AB", sl, a_, k)], w=[("ps", be)])
                if fq:
                    fq.pop(0)()
                for k in range(8, 16):
                    sch.op("pe", lambda e, k=k, psl=psl, bo=bo, a_=a_, src_=src_: e.matmul(
                        psb(bo), PAN[psl][:, a_, k, :], src_[:, k, :], start=(k == 8), stop=(k == 15)),
                        r=[("PAN", psl), ("AB", sl, a_, k)], w=[("ps", bo)])
                if fq:
                    fq.pop(0)()
                if a_ == 1:
                    _issue_panel()
                    if pending is not None:
                        emit_rev(pending)
                        pending = None
                sch.op("act", lambda e, bo=bo, a_=a_: e.activation(out=OSB[a_], in_=psb(bo), func=AF.Copy, scale=S11),
                       r=[("ps", bo)], w=[("OSB", a_)])
                sch.op("dve", lambda e, be=be, a_=a_, ss=ss: e.scalar_tensor_tensor(
                    out=STGP[ss][:, a_, :], in0=psb(be), scalar=S11, in1=OSB[a_], op0=ALU.mult, op1=ALU.add),
                    r=[("ps", be), ("OSB", a_)], w=[("STGP", ss, a_)])
                if a_ == 0:
                    sch.op("dve", lambda e, be=be, sm=sm: e.scalar_tensor_tensor(
                        out=STGM[sm][:, 0, :], in0=psb(be), scalar=S11, in1=OSB[0], op0=ALU.mult, op1=ALU.subtract),
                        r=[("ps", be), ("OSB", 0)], w=[("STGM", sm, 0)])
                else:
                    sch.op("dve", lambda e, be=be, sm=sm: e.scalar_tensor_tensor(
                        out=STGM[sm][:, 1, :], in0=psb(be), scalar=-S11, in1=OSB[1], op0=ALU.mult, op1=ALU.add),
                        r=[("ps", be), ("OSB", 1)], w=[("STGM", sm, 1)])
            if m == 0:
                sch.op("dve", lambda e, ss=ss: e.tensor_scalar(out=STGP[ss][0:1, 0, :], in0=STGP[ss][0:1, 0, :],
                                                               scalar1=0.5, scalar2=None, op0=ALU.mult),
                       r=[("STGP", ss, 0)], w=[("STGP", ss, 0)])
                sch.op("dve", lambda e, ss=ss: e.memset(STGP[ss][0:1, 1, :], 0.0), w=[("STGP", ss, 1)])
                sch.op("dve", lambda e, sm=sm, c0=c0: e.tensor_scalar(out=KNY[0:1, c0:c0 + 512], in0=STGM[sm][0:1, 0, :],
                                                                      scalar1=0.5, scalar2=None, op0=ALU.mult),
                       r=[("STGM", sm, 0)], w=[("KNY", pss)])
            sch.op("act", lambda e, ss=ss, od=od, m=m, hf=hf: e.dma_start(
                out=kco_d[od, m, :, :, hf * 512:(hf + 1) * 512], in_=STGP[ss]),
                r=[("STGP", ss, 0), ("STGP", ss, 1)], w=[("kco", od, m, hf)], slot=("kst_out", ss))
            prev_t = SPEC if prev_m is None else STGM[prev_m]
            prev_keys = ["SPEC"] if prev_m is None else [("STGM", prev_m, 0), ("STGM", prev_m, 1)]
            pending = (m, ss, sm, prev_t, prev_keys)
            prev_m = sm
        while fq:
            fq.pop(0)()
        emit_rev(pending)

    if stage == 0:
        sch.op("sp", lambda e: e.dma_start(out=outT[0:128, 0:1024].bitcast(BF16).rearrange("p (a c) -> p a c", a=2),
                                           in_=kco_d[0, 1, :, :, :]), r=[("kco", 0, 1, 0), ("kco", 0, 1, 1)], w=["out"], slot="out")
        sch.op("sp", lambda e: e.dma_start(out=outT[128:256, 0:1024].bitcast(BF16).rearrange("p (a c) -> p a c", a=2),
                                           in_=kco_d[1, 0, :, :, :]), r=[("kco", 1, 0, 0), ("kco", 1, 0, 1)], w=["out2"], slot="out")
        sch.op("sp", lambda e: e.dma_start(out=outT[256:257, 0:16], in_=outT[257:258, 0:16]), r=["out", "out2"], slot="fin")
        sch.emit(nc, stack)
        return nc, stack


    sch.fence_all()
    YA = V(P_H + 8192, 4096, BF16, "p (t c) -> p t c", t=16)
    YB = V(P_H + 12288, 4096, BF16, "p (t c) -> p t c", t=16)
    Z2T = V(P_Z2T, 8192, BF16, "p (j s) -> p j s", j=8)
    T0 = V(P_T0, 4096, BF16, "p (t c) -> p t c", t=16)
    T1 = V(P_T1, 4096, BF16, "p (t c) -> p t c", t=16)
    RAWS = [V(P_RAW + 1026 * i, 1026, BF16) for i in range(2)]
    TTS = [V(P_TT + 1024 * i, 1024, BF16) for i in range(2)]
    TMP = [V(P_TMP + 512 * i, 512) for i in range(4)]
    KST = [V(P_KST + 512 * i, 512, BF16, "p (a c) -> p a c", a=2) for i in range(2)]
    WIN = [V(P_WIN + 1024 * i, 1024, BF16, "p (j c) -> p j c", j=8) for i in range(2)]
    H = V(P_H, 16384, F32, "p (j s) -> p j s", j=8)
    MB = V(P_PAN, 4096, F32, "p (j s) -> p j s", j=8)
    outT_v = outT.rearrange("(j p) s -> p j s", p=128)
    def branch_finish(sc, m_keys_ready, layer, gi, res_src, res_keys, tok0):
        rs = cnt["rs"] % 2
        cnt["rs"] += 1
        bank = mbank()
        for j in range(8):
            sch.op("pe", lambda e, j=j, bank=bank: e.matmul(psb(bank), ONES, SQ[:, j, :], start=(j == 0), stop=(j == 7)),
                   r=[("SQj", j), "CB"], w=[("ps", bank)])
        sch.op("act", lambda e, bank=bank: e.activation(out=LNB, in_=psb(bank), func=AF.Ln, scale=1.0 / D, bias=EPSC),
               r=[("ps", bank), "EPSC"], w=["LNB"])
        sch.op("act", lambda e, rs=rs: e.activation(out=RSTD[rs], in_=LNB, func=AF.Exp, scale=-0.5), r=["LNB"], w=[("RSTD", rs)])
        for j in range(8):
            sch.op("dve", lambda e, j=j, rs=rs: e.scalar_tensor_tensor(
                out=MB[:, j, :], in0=MB[:, j, :], scalar=gcol(layer, gi, j), in1=RSTD[rs], op0=ALU.mult, op1=ALU.mult),
                r=[("MB", j), ("RSTD", rs), "CF"], w=[("MB", j)])
            sch.op("dve", lambda e, j=j: e.tensor_tensor(
                out=H[:, j, tok0:tok0 + 512], in0=res_src(j), in1=MB[:, j, :], op=ALU.add),
                r=[("MB", j)] + res_keys(j), w=[("H", j, tok0 // 512)])

    def evac_m(bank, j):
        sch.op("act", lambda e: e.activation(out=MB[:, j, :], in_=psb(bank), func=AF.Copy), r=[("ps", bank)], w=[("MB", j)])
        sch.op("act", lambda e: e.activation(out=SQ[:, j, :], in_=psb(bank), func=AF.Square), r=[("ps", bank)], w=[("SQj", j)])


    for rb_ in range(2):
        sch.op("dve", lambda e, rb_=rb_: e.memset(RAWS[rb_][:, 0:1], 0.0), w=[("RAWpad", rb_)])
        sch.op("dve", lambda e, rb_=rb_: e.memset(RAWS[rb_][:, 2049:2050], 0.0), w=[("RAWpad2", rb_)])
    cnt["raw"] = 0
    w_in_v = w_in_d.rearrange("(j p) n -> p j n", p=128)

    def inproj(strm, hf):
        c30 = strm * 1024 + hf * 512
        for sb in range(2):
            ws = cnt["win"] % 2
            cnt["win"] += 1
            sch.op("pool", lambda e, ws=ws, c=c30 + 256 * sb: e.dma_start(out=WIN[ws], in_=w_in_v[:, :, c:c + 256]),
                   w=[("WIN", ws)], slot=("win", ws))
            for q2 in range(2):
                q4 = sb * 2 + q2
                q = c30 // 128 + q4
                rbi = cnt["raw"] % 2
                cnt["raw"] += 1
                RAW, TT = RAWS[rbi], TTS[rbi]
                for sc in range(4):
                    bank = mbank()
                    for j in range(8):
                        sch.op("pe", lambda e, j=j, ws=ws, q2=q2, sc=sc, bank=bank: e.matmul(
                            psb(bank), WIN[ws][:, j, q2 * 128:(q2 + 1) * 128], HNT[:, j, sc * 512:(sc + 1) * 512],
                            start=(j == 0), stop=(j == 7)), r=[("WIN", ws), ("HNT", sc)], w=[("ps", bank)])
                    sch.op("act", lambda e, sc=sc, bank=bank, RAW=RAW: e.activation(out=RAW[:, 1 + sc * 512:1 + (sc + 1) * 512], in_=psb(bank), func=AF.Copy),
                           r=[("ps", bank)], w=[("RAW", rbi, sc)])
                rawk = [("RAW", rbi, i) for i in range(4)] + [("RAWpad", rbi), ("RAWpad2", rbi)]
                sch.op("act", lambda e, q=q, RAW=RAW, TT=TT: e.activation(out=TT, in_=RAW[:, 0:2048], func=AF.Identity,
                                                                          scale=CF[:, CF_CW + q:CF_CW + q + 1], bias=CF[:, CF_CW + 72 + q:CF_CW + 72 + q + 1]),
                       r=rawk + ["CF"], w=[("TT", rbi)])
                sch.op("dve", lambda e, q=q, RAW=RAW, TT=TT: e.scalar_tensor_tensor(out=TT, in0=RAW[:, 1:2049], scalar=CF[:, CF_CW + 24 + q:CF_CW + 24 + q + 1],
                                                                                    in1=TT, op0=ALU.mult, op1=ALU.add), r=rawk + [("TT", rbi), "CF"], w=[("TT", rbi)])
                sch.op("dve", lambda e, q=q, zc=hf * 4 + q4, RAW=RAW, TT=TT: e.scalar_tensor_tensor(
                    out=Z2T[:, zc, :], in0=RAW[:, 2:2050], scalar=CF[:, CF_CW + 48 + q:CF_CW + 48 + q + 1],
                    in1=TT, op0=ALU.mult, op1=ALU.add), r=rawk + [("TT", rbi), "CF"], w=[("Z2T", hf * 4 + q4)])

    def transp_to(hf, T, Tn):
        for q4 in range(4):
            for stg in range(2):
                bank = mbank()
                pv = psb16(bank)
                for i in range(8):
                    st = stg * 8 + i
                    sch.op("pe", lambda e, i=i, st=st, q4=q4, pv=pv: e.transpose(
                        pv[:, i * 128:(i + 1) * 128], Z2T[:, hf * 4 + q4, st * 128:(st + 1) * 128], IDENT),
                        r=[("Z2T", hf * 4 + q4), "CB"], w=[("ps", bank)])
                sch.op("act", lambda e, pv=pv, stg=stg, q4=q4: e.activation(
                    out=T[:, stg * 8:(stg + 1) * 8, q4 * 128:(q4 + 1) * 128], in_=pv.rearrange("p (i c) -> p i c", i=8), func=AF.Copy),
                    r=[("ps", bank)], w=[(Tn, st_) for st_ in range(stg * 8, stg * 8 + 8)])

    def transp_back(hf, T, Tn):
        for q4 in range(4):
            for stg in range(2):
                bank = mbank()
                pv = psb16(bank)
                for i in range(8):
                    st = stg * 8 + i
                    sch.op("pe", lambda e, i=i, st=st, q4=q4, pv=pv: e.transpose(
                        pv[:, i * 128:(i + 1) * 128], T[:, st, q4 * 128:(q4 + 1) * 128], IDENT),
                        r=[(Tn, st), "CB"], w=[("ps", bank)])
                sch.op("act", lambda e, pv=pv, stg=stg, q4=q4: e.activation(
                    out=Z2T[:, hf * 4 + q4, stg * 1024:(stg + 1) * 1024], in_=pv, func=AF.Copy),
                    r=[("ps", bank)], w=[("Z2T", hf * 4 + q4)])

    def fwd_dft(T, Tn, od, hf):
        c0 = od * 1024 + hf * 512
        for ft in range(NT):
            psl = load_panel(dftF, ft)
            ks = cnt["kst"] % 2
            cnt["kst"] += 1
            sch.op("sp", lambda e, ks=ks, ft=ft: e.dma_start(out=KST[ks], in_=kco_d[od, ft, :, :, hf * 512:(hf + 1) * 512]),
                   r=[("kco", od, ft, hf)], w=[("KST", ks)], slot=("kst", ks))
            pq = cnt["pq"] % 2
            cnt["pq"] += 1
            bp, bq = 2 * pq, 2 * pq + 1
            for st in range(NT):
                sch.op("pe", lambda e, st=st, psl=psl, bp=bp: e.matmul(
                    psb(bp), PAN[psl][:, 0, st, :], T[:, st, :], start=(st == 0), stop=(st == NT - 1)),
                    r=[("PAN", psl), (Tn, st)], w=[("ps", bp)])
            for st in range(NT):
                sch.op("pe", lambda e, st=st, psl=psl, bq=bq: e.matmul(
                    psb(bq), PAN[psl][:, 1, st, :], T[:, st, :], start=(st == 0), stop=(st == NT - 1)),
                    r=[("PAN", psl), (Tn, st)], w=[("ps", bq)])
            _issue_panel()
            Ka, Kb = KST[ks][:, 0, :], KST[ks][:, 1, :]
            kk = [("KST", ks)]
            tt = sch.op
            tt("dve", lambda e, bp=bp, Ka=Ka: e.tensor_tensor(out=TMP[0], in0=psb(bp), in1=Ka, op=ALU.mult), r=[("ps", bp)] + kk, w=[("TMP", 0)])
            tt("dve", lambda e, bq=bq, Kb=Kb: e.tensor_tensor(out=TMP[1], in0=psb(bq), in1=Kb, op=ALU.mult), r=[("ps", bq)] + kk, w=[("TMP", 1)])
            tt("dve", lambda e, ft=ft: e.tensor_tensor(out=YA[:, ft, :], in0=TMP[0], in1=TMP[1], op=ALU.add),
               r=[("TMP", 0), ("TMP", 1)], w=[("YA", ft)])
            tt("dve", lambda e, bq=bq, Ka=Ka: e.tensor_tensor(out=TMP[2], in0=psb(bq), in1=Ka, op=ALU.mult), r=[("ps", bq)] + kk, w=[("TMP", 2)])
            tt("dve", lambda e, bp=bp, Kb=Kb: e.tensor_tensor(out=TMP[3], in0=psb(bp), in1=Kb, op=ALU.mult), r=[("ps", bp)] + kk, w=[("TMP", 3)])
            tt("dve", lambda e, ft=ft: e.tensor_tensor(out=YB[:, ft, :], in0=TMP[2], in1=TMP[3], op=ALU.subtract),
               r=[("TMP", 2), ("TMP", 3)], w=[("YB", ft)])
            if ft == 0:
                tt("dve", lambda e, bq=bq: e.tensor_tensor(out=YB[0:1, 0, :], in0=psb(bq)[0:1, :], in1=KNY[0:1, c0:c0 + 512], op=ALU.mult),
                   r=[("ps", bq), ("KNY", od * 2 + hf)], w=[("YB", 0)])

    def inv_dft(T, Tn):
        for tt_ in range(NT):
            psl = load_panel(dftI, tt_)
            by = 4 + (cnt["py"] % 2)
            cnt["py"] += 1
            for ft in range(NT):
                sch.op("pe", lambda e, ft=ft, psl=psl, by=by: e.matmul(
                    psb(by), PAN[psl][:, 0, ft, :], YA[:, ft, :], start=(ft == 0), stop=False),
                    r=[("PAN", psl), ("YA", ft)], w=[("ps", by)])
                sch.op("pe", lambda e, ft=ft, psl=psl, by=by: e.matmul(
                    psb(by), PAN[psl][:, 1, ft, :], YB[:, ft, :], start=False, stop=(ft == NT - 1)),
                    r=[("PAN", psl), ("YB", ft)], w=[("ps", by)])
            _issue_panel()
            sch.op("dve", lambda e, tt_=tt_, by=by: e.tensor_tensor(out=T[:, tt_, :], in0=psb(by), in1=T[:, tt_, :], op=ALU.mult),
                   r=[("ps", by), (Tn, tt_)], w=[(Tn, tt_)])

    WO = V(P_RAW, 4096, BF16, "p (j d) -> p j d", j=8)
    for hf in range(2):
        inproj(0, hf)
        transp_to(hf, T0, "T0")
        inproj(1, hf)
        fwd_dft(T0, "T0", 0, hf)
        transp_to(hf, T1, "T1")
        inv_dft(T1, "T1")
        inproj(2, hf)
        if hf == 1:
            sch.retire([("RAW", b_, i_) for b_ in range(2) for i_ in range(4)] + [("TT", 0), ("TT", 1)]
                       + [("RAWpad", 0), ("RAWpad", 1), ("RAWpad2", 0), ("RAWpad2", 1)], ["WO"])
            sch.op("pool", lambda e: e.dma_start(out=WO, in_=w_out_d.rearrange("(j p) d -> p j d", p=128)), w=["WO"], slot="wo")
        fwd_dft(T1, "T1", 1, hf)
        transp_to(hf, T0, "T0")
        inv_dft(T0, "T0")
        transp_back(hf, T0, "T0")

    sch.fence_all()
    XC2 = [V(P_T0 + 4096 * i, 4096, F32, "p (j s) -> p j s", j=8) for i in range(2)]
    for sc in range(4):
        xs = sc % 2
        sch.op("sp", lambda e, sc=sc, xs=xs: e.dma_start(out=XC2[xs], in_=xT_v[:, :, sc * 512:(sc + 1) * 512]),
               w=[("XC2", xs)], slot=("xc2", xs))
        for jd in range(8):
            bank = mbank()
            for cj in range(8):
                sch.op("pe", lambda e, cj=cj, jd=jd, sc=sc, bank=bank: e.matmul(
                    psb(bank), WO[:, cj, jd * 128:(jd + 1) * 128], Z2T[:, cj, sc * 512:(sc + 1) * 512],
                    start=(cj == 0), stop=(cj == 7)), r=["WO", ("Z2T", cj)], w=[("ps", bank)])
            evac_m(bank, jd)
        branch_finish(sc, None, 0, 1, lambda j, xs=xs: XC2[xs][:, j, :], lambda j, xs=xs: [("XC2", xs)], sc * 512)

    def write_out():
        for sc_ in range(4):
            for j in range(8):
                sch.op("sp", lambda e, j=j, sc_=sc_: e.dma_start(out=outT_v[:, j, sc_ * 512:(sc_ + 1) * 512], in_=H[:, j, sc_ * 512:(sc_ + 1) * 512]),
                       r=[("H", j, sc_)], w=[("out", j)], slot=("out", j % 4))
        sch.op("sp", lambda e: e.dma_start(out=kco_d[0, 0, 0:1, 0, 0:8], in_=kco_d[0, 0, 1:2, 0, 0:8]),
               r=[("out", j) for j in range(8)], slot="fin")

    if stage == 1:
        write_out()
        sch.emit(nc, stack)
        return nc, stack


    def mlp(layer):
        sch.fence_all()
        tag = "L%d" % layer
        HNC = V(P_RAW, 4096, BF16, "p (j s) -> p j s", j=8)
        ACTB = V(P_Z2T, 16384, BF16, "p (f s) -> p f s", f=32)
        WU = [V(o_, 1024, BF16, "p (j c) -> p j c", j=8) for o_ in (P_TMP, P_TMP + 1024, P_X)]
        WD = [V(P_KST + 1024 * i, 1024, BF16, "p (f d) -> p f d", f=4) for i in range(3)]
        RL = [V(O_KNY + 256 * i, 256, BF16) for i in range(2)]
        wu_v = w_up_d[layer].rearrange("(j p) f -> p j f", p=128)
        wd_v = w_down_d[layer].rearrange("(f p) d -> p f d", p=128)
        c = {"wu": 0, "wd": 0, "rl": 0, "ub": 0}
        def _norm(tc):
            t0 = tc * 1024
            for sc2 in range(2):
                tok = t0 + sc2 * 512
                norm_chunk(H[:, :, tok:tok + 512], [("H", j, tok // 512) for j in range(8)], layer, 2,
                           lambda j, sc2=sc2: HNC[:, j, sc2 * 512:(sc2 + 1) * 512], lambda j, sc2=sc2: [(tag + "HNC", sc2)])

        def _up(tc):
            t0 = tc * 1024
            for slab in range(16):
                ws = c["wu"] % 3
                c["wu"] += 1
                sch.op("pool", lambda e, ws=ws, slab=slab: e.dma_start(out=WU[ws], in_=wu_v[:, :, slab * 256:(slab + 1) * 256]),
                       w=[(tag + "WU", ws)], slot=(tag + "wu", ws))
                for q2 in range(2):
                    ffc = slab * 2 + q2
                    for sc2 in range(2):
                        bank = 4 + (c["ub"] % 2)
                        c["ub"] += 1
                        for j in range(8):
                            sch.op("pe", lambda e, j=j, ws=ws, q2=q2, sc2=sc2, bank=bank: e.matmul(
                                psb(bank), WU[ws][:, j, q2 * 128:(q2 + 1) * 128], HNC[:, j, sc2 * 512:(sc2 + 1) * 512],
                                start=(j == 0), stop=(j == 7)), r=[(tag + "WU", ws), (tag + "HNC", sc2)], w=[("ps", bank)])
                        rl = c["rl"] % 2
                        c["rl"] += 1
                        sch.op("act", lambda e, bank=bank, rl=rl: e.activation(out=RL[rl], in_=psb(bank), func=AF.Relu),
                               r=[("ps", bank)], w=[(tag + "RL", rl)])
                        sch.op("dve", lambda e, rl=rl, ffc=ffc, sc2=sc2: e.tensor_tensor(
                            out=ACTB[:, ffc, sc2 * 512:(sc2 + 1) * 512], in0=RL[rl], in1=RL[rl], op=ALU.mult),
                            r=[(tag + "RL", rl)], w=[(tag + "ACT", ffc, sc2)])

        def _down(tc):
            t0 = tc * 1024
            for sc2 in range(2):
                tok = t0 + sc2 * 512
                for jh in range(2):
                    for slab in range(8):
                        ws = c["wd"] % 3
                        c["wd"] += 1
                        sch.op("pool", lambda e, ws=ws, slab=slab, jh=jh: e.dma_start(
                            out=WD[ws], in_=wd_v[:, slab * 4:(slab + 1) * 4, jh * 512:(jh + 1) * 512]),
                            w=[(tag + "WD", ws)], slot=(tag + "wd", ws))
                        for f4 in range(4):
                            ffc = slab * 4 + f4
                            for jq in range(4):
                                sch.op("pe", lambda e, ws=ws, f4=f4, jq=jq, ffc=ffc, sc2=sc2: e.matmul(
                                    psb(jq), WD[ws][:, f4, jq * 128:(jq + 1) * 128], ACTB[:, ffc, sc2 * 512:(sc2 + 1) * 512],
                                    start=(ffc == 0), stop=(ffc == 31)), r=[(tag + "WD", ws), (tag + "ACT", ffc, sc2)], w=[("ps", jq)])
                    for jq in range(4):
                        evac_m(jq, jh * 4 + jq)
                branch_finish(None, None, layer, 3, lambda j, tok=tok: H[:, j, tok:tok + 512],
                              lambda j, tok=tok: [("H", j, tok // 512)], tok)


        _norm(0)
        _up(0)
        _norm(1)
        _down(0)
        _up(1)
        _down(1)

    mlp(0)
    if stage == 2:
        write_out()
        sch.emit(nc, stack)
        return nc, stack


    sch.fence_all()
    HNT2 = V(P_Z2T, 8192, BF16, "p (j s) -> p j s", j=8)
    QKT = V(P_T0, 12288, BF16, "p (c s) -> p c s", c=12)
    VTOK = V(P_TMP, 2112, BF16, "p (t g e) -> p t g e", t=16, g=4)
    WQ = [V(P_WIN + 1024 * i, 1024, BF16, "p (j c) -> p j c", j=8) for i in range(2)]
    ATC = V(P_PAN, 2368, BF16)
    MASK = ATC[:, 0:384]
    PERMR = ATC[:, 384:512]
    PERMH = ATC[:, 512:640]
    COS = ATC[:, 640:2688]
    SIN = ATC[:, 2688:4736]
    PTS = [V(P_PAN + 2368 + 192 * i, 192, BF16) for i in range(8)]
    ESK = V(P_PAN + 3904, 16)
    RDN = V(P_PAN + 3920, 16)
    XB = [V(O_KNY + 256 * i, 256, BF16) for i in range(2)] + [V(P_Y, 256, BF16)]
    XS = [V(O_KNY + 512 + 256 * i, 256, BF16) for i in range(2)]
    wqkv_v = wqkv_d.rearrange("(j p) n -> p j n", p=128)
    sch.op("sp", lambda e: e.dma_start(out=ATC, in_=atc_d), w=["ATC"], slot="atc")
    sch.op("sp", lambda e: e.dma_start(out=ESK, in_=sink_d.partition_broadcast(128)), w=["ESK"], slot="esk")
    sch.op("act", lambda e: e.activation(out=ESK, in_=ESK, func=AF.Exp), r=["ESK"], w=["ESK"])
    for sc in range(4):
        norm_chunk(H[:, :, sc * 512:(sc + 1) * 512], [("H", j, sc) for j in range(8)], 1, 0,
                   lambda j, sc=sc: HNT2[:, j, sc * 512:(sc + 1) * 512], lambda j, sc=sc: [("HNT2", sc)])
    ac = {"wq": 0, "xb": 0, "pt": 0, "sb": 0, "pv": 0}
    KZ = [[QKT[:, 8, :], QKT[:, 9, :]], [QKT[:, 10, :], QKT[:, 11, :]],
          [V(O_SQ, 1024, BF16), V(O_SQ + 1024, 1024, BF16)], [V(O_RSTD, 1024, BF16), V(P_X, 1024, BF16)]]
    sch.retire([("SQj", j) for j in range(8)] + [("RSTD", 0), ("RSTD", 1)],
               [("KZ", g_, v_, s_) for g_ in (2, 3) for v_ in range(2) for s_ in range(4)] + [("KZz", g_, v_) for g_ in (2, 3) for v_ in range(2)])
    for g_ in range(4):
        sch.op("dve", lambda e, g_=g_: e.memset(KZ[g_][0][64:128, :], 0.0), w=[("KZz", g_, 0)])
        sch.op("dve", lambda e, g_=g_: e.memset(KZ[g_][1][0:64, :], 0.0), w=[("KZz", g_, 1)])
    dst_chunk = [0, 1, 2, 3, 4, 5, 6, 7, 8, 10]
    rb = [0]

    def rbank():
        b_ = 4 + (rb[0] % 2)
        rb[0] += 1
        return b_

    def proj_unit(ws, q2, sc, xb):
        bank = mbank()
        for j in range(8):
            sch.op("pe", lambda e, j=j: e.matmul(
                psb(bank), WQ[ws][:, j, q2 * 128:(q2 + 1) * 128], HNT2[:, j, sc * 512:(sc + 1) * 512],
                start=(j == 0), stop=(j == 7)), r=[("WQ", ws), ("HNT2", sc)], w=[("ps", bank)])
        sch.op("act", lambda e: e.activation(out=XB[xb], in_=psb(bank), func=AF.Copy),
               r=[("ps", bank)], w=[("XB", xb)])

    def rope_unit(dc, sc, xb, xs):
        bank2 = rbank()
        sch.op("pe", lambda e: e.matmul(psb(bank2), PERMR, XB[xb], start=True, stop=True),
               r=[("XB", xb), "ATC"], w=[("ps", bank2)])
        sch.op("dve", lambda e: e.tensor_tensor(out=XS[xs], in0=psb(bank2), in1=SIN[:, sc * 512:(sc + 1) * 512], op=ALU.mult),
               r=[("ps", bank2), "ATC"], w=[("XS", xs)])
        sch.op("dve", lambda e: e.tensor_tensor(out=XB[xb], in0=XB[xb], in1=COS[:, sc * 512:(sc + 1) * 512], op=ALU.mult),
               r=[("XB", xb), "ATC"], w=[("XB", xb)])
        if dc < 8:
            sch.op("dve", lambda e: e.tensor_tensor(out=QKT[:, dc, sc * 512:(sc + 1) * 512], in0=XB[xb], in1=XS[xs], op=ALU.add),
                   r=[("XB", xb), ("XS", xs)], w=[("QKT", dc, sc)])
        else:
            g0, g1 = (0, 1) if dc == 8 else (2, 3)
            cs = slice(sc * 512, (sc + 1) * 512)
            sch.op("dve", lambda e: e.tensor_tensor(out=XS[xs], in0=XB[xb], in1=XS[xs], op=ALU.add),
                   r=[("XB", xb), ("XS", xs)], w=[("XS", xs)])
            sch.op("act", lambda e: e.activation(out=KZ[g0][0][0:64, cs], in_=XS[xs][0:64, :], func=AF.Copy),
                   r=[("XS", xs)], w=[("KZ", g0, 0, sc)])
            sch.op("act", lambda e: e.activation(out=KZ[g1][1][64:128, cs], in_=XS[xs][64:128, :], func=AF.Copy),
                   r=[("XS", xs)], w=[("KZ", g1, 1, sc)])
            bank3 = rbank()
            sch.op("pe", lambda e: e.matmul(psb(bank3), PERMH, XS[xs], start=True, stop=True),
                   r=[("XS", xs), "ATC"], w=[("ps", bank3)])
            sch.op("act", lambda e: e.activation(out=KZ[g1][0][0:64, cs], in_=psb(bank3)[0:64, :], func=AF.Copy),
                   r=[("ps", bank3)], w=[("KZ", g1, 0, sc)])
            sch.op("act", lambda e: e.activation(out=KZ[g0][1][64:128, cs], in_=psb(bank3)[64:128, :], func=AF.Copy),
                   r=[("ps", bank3)], w=[("KZ", g0, 1, sc)])

    pend_rope = None
    ucount = 0
    for slab in range(5):
        ws = ac["wq"] % 2
        ac["wq"] += 1
        sch.op("pool", lambda e, ws=ws, slab=slab: e.dma_start(out=WQ[ws], in_=wqkv_v[:, :, slab * 256:(slab + 1) * 256]),
               w=[("WQ", ws)], slot=("wq", ws))
        for sc in range(4):
            for q2 in range(2):
                dc = dst_chunk[slab * 2 + q2]
                xb = ucount % 3
                xs = ucount % 2
                ucount += 1
                proj_unit(ws, q2, sc, xb)
                if pend_rope is not None:
                    rope_unit(*pend_rope)
                pend_rope = (dc, sc, xb, xs)
    rope_unit(*pend_rope)
    ws = ac["wq"] % 2
    ac["wq"] += 1
    sch.op("pool", lambda e, ws=ws: e.dma_start(out=WQ[ws], in_=wqkv_v[:, :, 1280:1536]), w=[("WQ", ws)], slot=("wq", ws))
    sch.op("dve", lambda e: e.memset(VTOK[:, :, :, 64:66], 1.0), w=["VONE"])
    for st in range(NT):
        bank = mbank()
        for j in range(8):
            sch.op("pe", lambda e, j=j, ws=ws, st=st, bank=bank: e.matmul(
                psb(bank)[:, 0:256], HNT2[:, j, st * 128:(st + 1) * 128], WQ[ws][:, j, :],
                start=(j == 0), stop=(j == 7)), r=[("WQ", ws), ("HNT2", st // 4)], w=[("ps", bank)])
        sch.op("act", lambda e, bank=bank, st=st: e.activation(
            out=VTOK[:, st, :, 0:64], in_=psb(bank)[:, 0:256].rearrange("p (g e) -> p g e", g=4), func=AF.Copy),
            r=[("ps", bank)], w=[("VTOK", st)])

    OTOK = V(P_Z2T, 8192, BF16, "p (t c) -> p t c", t=16)
    sch.retire([("HNT2", i) for i in range(4)], [("OTOK", i) for i in range(16)])

    NSLOT = 8
    LAG = 3
    tidx = lambda h, j: h * NT + j

    def pv(h, i):
        g = h // 4
        kbs = [kb for kb in (i - 1, i, i + 1) if 0 <= kb < NT]
        pb = ac["pv"] % 4
        ac["pv"] += 1
        for n_, kb in enumerate(kbs):
            qlo = max(kb - 1, 0)
            slot = tidx(h, kb) % NSLOT
            off = (i - qlo) * 128
            sch.op("pe", lambda e, slot=slot, off=off, kb=kb, pb=pb, n_=n_: e.matmul(
                psb(pb)[:, 0:65], PTS[slot][:, off:off + 128], VTOK[:, kb, g, 0:65],
                start=(n_ == 0), stop=(n_ == len(kbs) - 1)),
                r=[("PT", slot), ("VTOK", kb), "VONE"], w=[("ps", pb)])
        rd = ac["pv"] % 4
        pv_pend.append((pb, h, i, rd))
        if len(pv_pend) >= 2:
            pv_flush()

    pv_pend = []

    def pv_flush():
        for (pb, h, i, rd) in pv_pend:
            sch.op("dve", lambda e, pb=pb, h=h, rd=rd: e.tensor_scalar(out=RDN[:, 2 * rd:2 * rd + 1], in0=psb(pb)[:, 64:65], scalar1=ESK[:, h:h + 1],
                                                                   scalar2=None, op0=ALU.add), r=[("ps", pb), "ESK"], w=[("RDN", rd)])
        for (pb, h, i, rd) in pv_pend:
            sch.op("dve", lambda e, rd=rd: e.reciprocal(out=RDN[:, 2 * rd + 1:2 * rd + 2], in_=RDN[:, 2 * rd:2 * rd + 1]), r=[("RDN", rd)], w=[("RDN2", rd)])
        for (pb, h, i, rd) in pv_pend:
            sch.op("dve", lambda e, pb=pb, h=h, i=i, rd=rd: e.tensor_scalar(out=OTOK[:, i, h * 64:(h + 1) * 64], in0=psb(pb)[:, 0:64],
                                                                           scalar1=RDN[:, 2 * rd + 1:2 * rd + 2], scalar2=None, op0=ALU.mult),
                   r=[("ps", pb), ("RDN2", rd)], w=[("OTOK", i)])
        del pv_pend[:]

    def scores(h, j):
        g = h // 4
        v = h % 2
        qc = h // 2
        qlo, qhi = max(j - 1, 0), min(j + 1, NT - 1)
        nq = qhi - qlo + 1
        bank = 4 + (ac["sb"] % 4)
        ac["sb"] += 1
        slot = tidx(h, j) % NSLOT
        sch.op("pe", lambda e: e.matmul(
            psb(bank)[:, 0:nq * 128], KZ[g][v][:, j * 128:(j + 1) * 128], QKT[:, qc, qlo * 128:(qhi + 1) * 128],
            start=True, stop=False),
            r=[("KZ", g, v, j // 4), ("KZz", g, v)] + [("QKT", qc, s_) for s_ in range(qlo // 4, qhi // 4 + 1)], w=[("ps", bank)])
        m0 = 128 if j == 0 else 0
        sch.op("pe", lambda e: e.matmul(psb(bank)[:, 0:nq * 128], IDENT, MASK[:, m0:m0 + nq * 128], start=False, stop=True),
               r=["ATC", "CB"], w=[("ps", bank)])
        sch.op("act", lambda e: e.activation(
            out=PTS[slot][:, 0:nq * 128], in_=psb(bank)[:, 0:nq * 128], func=AF.Exp, scale=0.125),
            r=[("ps", bank)], w=[("PT", slot)])

    pv_tasks = []
    for h in range(16):
        for i in range(NT):
            pv_tasks.append((tidx(h, min(i + 1, NT - 1)), h, i))
    pvi = 0
    for t in range(16 * NT + LAG):
        if t < 16 * NT:
            scores(t // NT, t % NT)
        while pvi < len(pv_tasks) and pv_tasks[pvi][0] + LAG <= t:
            pv(pv_tasks[pvi][1], pv_tasks[pvi][2])
            pvi += 1
    assert pvi == len(pv_tasks)
    if pv_pend:
        pv_flush()

    OT = V(P_T0, 8192, BF16, "p (j s) -> p j s", j=8)
    sch.retire([("QKT", c, s_) for c in range(8) for s_ in range(4)] + [("KZ", g_, v_, s_) for g_ in range(2) for v_ in range(2) for s_ in range(4)]
               + [("KZz", g_, v_) for g_ in range(2) for v_ in range(2)], [("OT", j) for j in range(8)] + ["WO2"])
    WO2 = V(P_RAW, 4096, BF16, "p (j d) -> p j d", j=8)
    sch.op("pool", lambda e: e.dma_start(out=WO2, in_=wo2_d.rearrange("(j p) d -> p j d", p=128)), w=["WO2"], slot="wo2")
    for cj in range(8):
        for stg in range(2):
            bank = mbank()
            pv_ = psb16(bank)
            for i in range(8):
                st = stg * 8 + i
                sch.op("pe", lambda e, i=i, st=st, cj=cj, pv_=pv_: e.transpose(
                    pv_[:, i * 128:(i + 1) * 128], OTOK[:, st, cj * 128:(cj + 1) * 128], IDENT),
                    r=[("OTOK", st), "CB"], w=[("ps", bank)])
            sch.op("act", lambda e, pv_=pv_, stg=stg, cj=cj: e.activation(
                out=OT[:, cj, stg * 1024:(stg + 1) * 1024], in_=pv_, func=AF.Copy), r=[("ps", bank)], w=[("OT", cj)])
    sch.fence_all()
    for sc in range(4):
        for jd in range(8):
            bank = mbank()
            for cj in range(8):
                sch.op("pe", lambda e, cj=cj, jd=jd, sc=sc, bank=bank: e.matmul(
                    psb(bank), WO2[:, cj, jd * 128:(jd + 1) * 128], OT[:, cj, sc * 512:(sc + 1) * 512],
                    start=(cj == 0), stop=(cj == 7)), r=["WO2", ("OT", cj)], w=[("ps", bank)])
            evac_m(bank, jd)
        branch_finish(sc, None, 1, 1, lambda j, sc=sc: H[:, j, sc * 512:(sc + 1) * 512], lambda j, sc=sc: [("H", j, sc)], sc * 512)
    if stage == 3:
        write_out()
        sch.emit(nc, stack)
        return nc, stack
    mlp(1)
    write_out()
    sch.emit(nc, stack)
    return nc, stack


def make_in_maps(inp):
    cf, cb, zf = _host_consts(inp)
    fwd, inv, dk = _dft_tables()
    adl = _absdelta().reshape(1, D)
    hyb = np.ascontiguousarray(np.asarray(inp["hy_bias"], np.float32)[0].reshape(1, 2 * D))
    x = np.asarray(inp["x"], np.float32)
    common = {
        "cf": cf, "cb": cb, "zf": zf, "adl": adl, "hyb": hyb, "dftF": fwd, "dftI": inv, "dftK": dk,
        "hy_w_in": np.ascontiguousarray(np.asarray(inp["hy_w_in"], np.float32)[0]),
        "hy_f_wout": np.ascontiguousarray(np.asarray(inp["hy_f_wout"], np.float32)[0]),
        "hy_w_out": np.ascontiguousarray(np.asarray(inp["hy_w_out"], np.float32)[0]),
        "w_up": np.ascontiguousarray(np.asarray(inp["w_up"], np.float32)),
        "w_down": np.ascontiguousarray(np.asarray(inp["w_down"], np.float32)),
    }
    common["at_w_qkv"] = np.ascontiguousarray(np.asarray(inp["at_w_qkv"], np.float32)[0])
    common["at_w_o"] = np.ascontiguousarray(np.asarray(inp["at_w_o"], np.float32)[0])
    common["sink"] = np.ascontiguousarray(np.asarray(inp["at_sink"], np.float32)[0].reshape(1, 16))
    common["atc"] = _attn_consts()
    maps = []
    for c in range(NCORES):
        m = dict(common)
        m["xT"] = np.ascontiguousarray(x[c].T)
        maps.append(m)
    return maps


_PROG = {}


def kernel(**inputs):
    inp = {k: np.asarray(v) for k, v in inputs.items()}
    if "nc" not in _PROG:
        _PROG["nc"] = build_program(stage=4)
    nc, _stack = _PROG["nc"]
    maps = make_in_maps(inp)
    res = run_bass_kernel_spmd(nc, maps, core_ids=list(range(NCORES)))
    out = np.stack([np.ascontiguousarray(r["outT"].T) for r in res.results], axis=0)
    return out.astype(np.float32)
```
